# Optimizing a Trainium2 kernel written in Bass

```python
import math
import jax, jax.numpy as jnp
from jax import lax
import numpy as np

D_MODEL = 2048
BATCH = 4
SEQ = 2048
DEPTH = 2
DEC_BATCH = 128
DEC_SEQ = 1
PAST_LEN = 16384
PAGE_SIZE = 128

N_META = 16
CHUNK = 64
D_FF = 4 * D_MODEL
H_A = 4
W_A = D_MODEL // 2
DH_A = W_A // H_A
H_B = 8
W_B = D_MODEL // 2
DH_B = W_B // H_B
CONV_W = 4
DH_C = 64
W_C = D_MODEL // 2
H_C = W_C // DH_C
R_W = D_MODEL // 32
R_A = D_MODEL // 32
R_G = D_MODEL // 16
N_A = 4 * W_A + 2 * H_A
N_B = 4 * W_B + 2 * H_B
N_C = 3 * W_C + R_W + R_A + R_G
N_GATE = 3 * D_MODEL
N_IN = N_A + N_B + N_C + N_GATE
RMS_EPS = 1e-6
GN_EPS = 64e-5
L2_EPS = 1e-12

kernel_name = "hybrid_mlstm_gdn_rwkv7_step"


def _split(x, sizes):
    offs, o = [], 0
    for s in sizes[:-1]:
        o += s
        offs.append(o)
    return jnp.split(x, offs, axis=-1)


def _rms(x, w, eps=RMS_EPS):
    xf = x.astype(jnp.float32)
    y = xf * lax.rsqrt(jnp.mean(xf * xf, axis=-1, keepdims=True) + eps)
    return (y * w.astype(jnp.float32)).astype(x.dtype)


def _l2norm(x):
    return x * lax.rsqrt(jnp.sum(x * x, axis=-1, keepdims=True) + L2_EPS)


def _run_chunks(chunk_fn, state, xs, lead, chunk):
    ys = []
    if lead > 0:
        state, y = chunk_fn(state, tuple(x[:, :lead] for x in xs))
        ys.append(y)
    rest = xs[0].shape[1] - lead
    if rest > 0:
        n = rest // chunk
        blocks = tuple(jnp.swapaxes(x[:, lead:].reshape(x.shape[0], n, chunk, *x.shape[2:]), 0, 1) for x in xs)
        state, yb = lax.scan(chunk_fn, state, blocks)
        yb = jnp.swapaxes(yb, 0, 1)
        ys.append(yb.reshape(yb.shape[0], n * chunk, *yb.shape[3:]))
    return state, jnp.concatenate(ys, axis=1)


def _mlstm_chunk(state, xs):
    c, n, m = state
    q, k, v, logi, logf = xs
    L = q.shape[1]
    b = jnp.swapaxes(jnp.cumsum(logf, axis=1), 1, 2)
    li = jnp.swapaxes(logi, 1, 2)
    causal = jnp.tril(jnp.ones((L, L), dtype=bool))
    d_mat = jnp.where(causal, b[..., :, None] - b[..., None, :] + li[..., None, :], -jnp.inf)
    inter = b + m[..., None]
    m_row = jnp.maximum(inter, jnp.max(d_mat, axis=-1))
    w_intra = jnp.exp(d_mat - m_row[..., None])
    w_inter = jnp.exp(inter - m_row)
    s = jnp.einsum('blhd,bshd->bhls', q, k) * w_intra
    num = (jnp.einsum('bhls,bshd->blhd', s, v)
           + jnp.swapaxes(w_inter, 1, 2)[..., None] * jnp.einsum('blhk,bhkv->blhv', q, c))
    den = jnp.sum(s, axis=-1) + w_inter * jnp.einsum('blhk,bhk->bhl', q, n)
    den = jnp.maximum(jnp.abs(den), jnp.exp(-m_row))
    h = num / jnp.swapaxes(den, 1, 2)[..., None]
    b_last = b[..., -1]
    g = b_last[..., None] - b + li
    m_new = jnp.maximum(b_last + m, jnp.max(g, axis=-1))
    w_state = jnp.exp(g - m_new[..., None])
    decay = jnp.exp(b_last + m - m_new)
    c = decay[..., None, None] * c + jnp.einsum('bhs,bshk,bshv->bhkv', w_state, k, v)
    n = decay[..., None] * n + jnp.einsum('bhs,bshk->bhk', w_state, k)
    return (c, n, m_new), h


def _gdn_chunk(s, xs):
    q, k, v, log_alpha, beta = xs
    L = q.shape[1]
    q, k, v = (jnp.swapaxes(t, 1, 2) for t in (q, k, v))
    g = jnp.swapaxes(jnp.cumsum(log_alpha, axis=1), 1, 2)
    beta = jnp.swapaxes(beta, 1, 2)
    incl = jnp.tril(jnp.ones((L, L), dtype=bool))
    strict = jnp.tril(jnp.ones((L, L), dtype=bool), -1)
    decay = jnp.exp(jnp.where(incl, g[..., :, None] - g[..., None, :], -jnp.inf))
    a_mat = jnp.where(strict, beta[..., :, None] * jnp.einsum('bhld,bhsd->bhls', k, k) * decay, 0.0)
    eye = jnp.eye(L, dtype=q.dtype)
    rhs = jnp.concatenate([v * beta[..., None], k * (beta * jnp.exp(g))[..., None]], axis=-1)
    sol = lax.linalg.triangular_solve(a_mat + eye, rhs, left_side=True, lower=True, unit_diagonal=True)
    u, w = sol[..., :v.shape[-1]], sol[..., v.shape[-1]:]
    v_new = u - jnp.einsum('bhlk,bhkv->bhlv', w, s)
    attn = jnp.where(incl, jnp.einsum('bhld,bhsd->bhls', q, k) * decay, 0.0)
    o = (jnp.einsum('bhlk,bhkv->bhlv', q * jnp.exp(g)[..., None], s)
         + jnp.einsum('bhls,bhsv->bhlv', attn, v_new))
    g_last = g[..., -1]
    s = (jnp.exp(g_last)[..., None, None] * s
         + jnp.einsum('bhsk,bhsv->bhkv', k * jnp.exp(g_last[..., None] - g)[..., None], v_new))
    return s, jnp.swapaxes(o, 1, 2)


def _rwkv_step(s, xs):
    r, w, k, v, a_vec, b_vec = xs
    sa = jnp.einsum('bhvk,bhk->bhv', s, a_vec)
    s = s * w[:, :, None, :] + sa[..., None] * b_vec[:, :, None, :] + v[..., None] * k[:, :, None, :]
    return s, jnp.einsum('bhvk,bhk->bhv', s, r)


def _causal_conv(buf, u, w):
    T = u.shape[1]
    full = jnp.concatenate([buf, u], axis=1)
    y = sum(full[:, j:j + T] * w[j] for j in range(CONV_W))
    return y, full[:, T:]


def _mlstm_branch(pa, c0, n0, m0, b_i, b_f, norm_w, lead, chunk):
    B, T, _ = pa.shape
    q, k, v, o, ig, fg = _split(pa, (W_A, W_A, W_A, W_A, H_A, H_A))
    heads = lambda t: t.reshape(B, T, H_A, DH_A)
    logi = ig + b_i
    logf = jax.nn.log_sigmoid(fg + b_f)
    (c, n, m), h = _run_chunks(_mlstm_chunk, (c0, n0, m0),
                               (heads(q), heads(k) * DH_A ** -0.5, heads(v), logi, logf), lead, chunk)
    h = _rms(h, norm_w.reshape(H_A, DH_A))
    return jax.nn.sigmoid(o) * h.reshape(B, T, W_A), c, n, m


def _gdn_branch(pb, conv_buf, s0, conv_w, a_log, dt_bias, norm_w, lead, chunk):
    B, T, _ = pb.shape
    qkv, z, a, b = _split(pb, (3 * W_B, W_B, H_B, H_B))
    qkv, new_buf = _causal_conv(conv_buf, qkv, conv_w)
    q, k, v = [t.reshape(B, T, H_B, DH_B) for t in _split(jax.nn.silu(qkv), (W_B, W_B, W_B))]
    q = _l2norm(q) * DH_B ** -0.5
    k = _l2norm(k)
    log_alpha = -jnp.exp(a_log) * jax.nn.softplus(a + dt_bias)
    beta = jax.nn.sigmoid(b)
    s, o = _run_chunks(_gdn_chunk, s0, (q, k, v, log_alpha, beta), lead, chunk)
    o = _rms(o, norm_w) * jax.nn.silu(z.reshape(B, T, H_B, DH_B))
    return o.reshape(B, T, W_B), s, new_buf


def _rwkv_branch(pc, shift_buf, s0, mu, w0, w2, a0, a2, g2, k_k, k_a, r_k, ln_w, ln_b):
    B, T, _ = pc.shape
    prev = jnp.concatenate([shift_buf[:, None], pc[:, :-1]], axis=1)
    xm = pc + (prev - pc) * mu
    r, k, v, xw, xa, xg = _split(xm, (W_C, W_C, W_C, R_W, R_A, R_G))
    w_log = -jax.nn.softplus(-(w0 + jnp.tanh(xw) @ w2)) - 0.5
    decay = jnp.exp(-jnp.exp(w_log))
    a = jax.nn.sigmoid(a0 + xa @ a2)
    g = jax.nn.sigmoid(xg) @ g2
    heads = lambda t: t.reshape(B, T, H_C, DH_C)
    kk = _l2norm(heads(k * k_k))
    k = k * (1.0 + (a - 1.0) * k_a)
    r_h, k_h, v_h, a_h = heads(r), heads(k), heads(v), heads(a)
    tm = lambda t: jnp.swapaxes(t, 0, 1)
    s, y = lax.scan(_rwkv_step, s0, (tm(r_h), tm(heads(decay)), tm(k_h), tm(v_h), tm(-kk), tm(kk * a_h)))
    y = tm(y)
    mean = jnp.mean(y, axis=-1, keepdims=True)
    var = jnp.mean(jnp.square(y - mean), axis=-1, keepdims=True)
    y = (y - mean) * lax.rsqrt(var + GN_EPS) * ln_w.reshape(H_C, DH_C) + ln_b.reshape(H_C, DH_C)
    y = y + jnp.sum(r_h * k_h * r_k, axis=-1, keepdims=True) * v_h
    return y.reshape(B, T, W_C) * g, s, pc[:, -1]


def _hybrid_layer(x, st, lp, lead, chunk):
    f32 = lambda t: t.astype(jnp.float32)
    c0, n0, m0, gs0, gconv0, rs0, rshift0 = [f32(s) for s in st]
    B, T, _ = x.shape
    dt = x.dtype
    h = _rms(x, lp["norm_mix_w"])
    proj = f32(h @ lp["w_in"])
    pa, pb, pc, pg = _split(proj, (N_A, N_B, N_C, N_GATE))
    ya, c, n, m = _mlstm_branch(pa, c0, n0, m0, f32(lp["mlstm_b_i"]), f32(lp["mlstm_b_f"]),
                                f32(lp["mlstm_norm_w"]), lead, chunk)
    yb, gs, gconv = _gdn_branch(pb, gconv0, gs0, f32(lp["gdn_conv_w"]), f32(lp["gdn_a_log"]),
                                f32(lp["gdn_dt_bias"]), f32(lp["gdn_norm_w"]), lead, chunk)
    yc, rs, rshift = _rwkv_branch(pc, rshift0, rs0, f32(lp["rwkv_mu"]), f32(lp["rwkv_w0"]), f32(lp["rwkv_w2"]),
                                  f32(lp["rwkv_a0"]), f32(lp["rwkv_a2"]), f32(lp["rwkv_g2"]), f32(lp["rwkv_k_k"]),
                                  f32(lp["rwkv_k_a"]), f32(lp["rwkv_r_k"]), f32(lp["rwkv_ln_w"]), f32(lp["rwkv_ln_b"]))
    gates = jax.nn.sigmoid(pg).astype(dt).reshape(B, T, 3, D_MODEL)
    merged = (gates[:, :, 0] * (ya.astype(dt) @ lp["w_branch_a"])
              + gates[:, :, 1] * (yb.astype(dt) @ lp["w_branch_b"])
              + gates[:, :, 2] * (yc.astype(dt) @ lp["w_branch_c"]))
    x = x + merged @ lp["w_out"]
    u = jax.nn.relu(_rms(x, lp["norm_mlp_w"]) @ lp["w_up"])
    x = x + (u * u) @ lp["w_down"]
    return x, (c, n, m, gs, gconv, rs, rshift)


def _trunk(x, states, layers, final_norm_w, lead, chunk):
    new = [[] for _ in states]
    for l in range(DEPTH):
        lp = {name: p[l] for name, p in layers.items()}
        x, st = _hybrid_layer(x, [s[l] for s in states], lp, lead, chunk)
        for acc, s in zip(new, st):
            acc.append(s)
    return _rms(x, final_norm_w), [jnp.stack(acc) for acc in new]


def _zero_states(b):
    z = lambda *shape: jnp.zeros((DEPTH, b) + shape, jnp.float32)
    return [z(H_A, DH_A, DH_A), z(H_A, DH_A), z(H_A), z(H_B, DH_B, DH_B),
            z(CONV_W - 1, 3 * W_B), z(H_C, DH_C, DH_C), z(N_C)]


def setup_inputs(seed: int = 0) -> dict:
    key = jax.random.key(seed)
    keys = jax.random.split(key, 48)
    counter = iter(range(48))
    nxt = lambda: keys[next(counter)]
    nrm = lambda shape, scale=1.0: scale * jax.random.normal(nxt(), shape, jnp.float32)
    uni = lambda shape, lo, hi: jax.random.uniform(nxt(), shape, jnp.float32, lo, hi)
    gain = lambda shape: 1.0 + 0.02 * jax.random.normal(nxt(), shape, jnp.float32)
    L = DEPTH
    out = {
        "x_prompt": nrm((BATCH, SEQ, D_MODEL)),
        "x_sample": nrm((DEC_BATCH, DEC_SEQ, D_MODEL)),
        "state_mlstm_c": nrm((L, DEC_BATCH, H_A, DH_A, DH_A), 0.1),
        "state_mlstm_n": nrm((L, DEC_BATCH, H_A, DH_A)),
        "state_mlstm_m": nrm((L, DEC_BATCH, H_A), 0.5),
        "state_gdn_s": nrm((L, DEC_BATCH, H_B, DH_B, DH_B), 0.1),
        "state_gdn_conv": nrm((L, DEC_BATCH, CONV_W - 1, 3 * W_B)),
        "state_rwkv_s": nrm((L, DEC_BATCH, H_C, DH_C, DH_C), 0.1),
        "state_rwkv_shift": nrm((L, DEC_BATCH, N_C)),
        "meta_tokens": nrm((N_META, D_MODEL)),
        "norm_mix_w": gain((L, D_MODEL)),
        "w_in": nrm((L, D_MODEL, N_IN), D_MODEL ** -0.5),
        "mlstm_b_i": nrm((L, H_A), 0.1),
        "mlstm_b_f": 3.0 + nrm((L, H_A), 0.5),
        "mlstm_norm_w": gain((L, W_A)),
        "gdn_conv_w": nrm((L, CONV_W, 3 * W_B), CONV_W ** -0.5),
        "gdn_a_log": jnp.log(uni((L, H_B), 1.0, 16.0)),
        "gdn_dt_bias": jnp.log(jnp.expm1(jnp.exp(uni((L, H_B), math.log(1e-3), math.log(1e-1))))),
        "gdn_norm_w": gain((L, DH_B)),
        "rwkv_mu": uni((L, N_C), 0.0, 1.0),
        "rwkv_w0": uni((L, W_C), -6.0, 1.0),
        "rwkv_w2": nrm((L, R_W, W_C), 0.1 * R_W ** -0.5),
        "rwkv_a0": nrm((L, W_C), 0.1),
        "rwkv_a2": nrm((L, R_A, W_C), 0.5 * R_A ** -0.5),
        "rwkv_g2": nrm((L, R_G, W_C), R_G ** -0.5),
        "rwkv_k_k": 0.85 + nrm((L, W_C), 0.05),
        "rwkv_k_a": uni((L, W_C), 0.5, 1.0),
        "rwkv_r_k": nrm((L, H_C, DH_C), 0.1),
        "rwkv_ln_w": gain((L, W_C)),
        "rwkv_ln_b": nrm((L, W_C), 0.01),
        "w_branch_a": nrm((L, W_A, D_MODEL), W_A ** -0.5),
        "w_branch_b": nrm((L, W_B, D_MODEL), W_B ** -0.5),
        "w_branch_c": nrm((L, W_C, D_MODEL), W_C ** -0.5),
        "w_out": nrm((L, D_MODEL, D_MODEL), D_MODEL ** -0.5),
        "norm_mlp_w": gain((L, D_MODEL)),
        "w_up": nrm((L, D_MODEL, D_FF), D_MODEL ** -0.5),
        "w_down": nrm((L, D_FF, D_MODEL), D_FF ** -0.5),
        "final_norm_w": gain((D_MODEL,)),
    }
    return out


def reference(x_prompt, x_sample, state_mlstm_c, state_mlstm_n, state_mlstm_m, state_gdn_s, state_gdn_conv,
              state_rwkv_s, state_rwkv_shift, meta_tokens, norm_mix_w, w_in, mlstm_b_i, mlstm_b_f, mlstm_norm_w,
              gdn_conv_w, gdn_a_log, gdn_dt_bias, gdn_norm_w, rwkv_mu, rwkv_w0, rwkv_w2, rwkv_a0, rwkv_a2,
              rwkv_g2, rwkv_k_k, rwkv_k_a, rwkv_r_k, rwkv_ln_w, rwkv_ln_b, w_branch_a, w_branch_b, w_branch_c,
              w_out, norm_mlp_w, w_up, w_down, final_norm_w):
    layers = {
        "norm_mix_w": norm_mix_w, "w_in": w_in, "mlstm_b_i": mlstm_b_i, "mlstm_b_f": mlstm_b_f,
        "mlstm_norm_w": mlstm_norm_w, "gdn_conv_w": gdn_conv_w, "gdn_a_log": gdn_a_log,
        "gdn_dt_bias": gdn_dt_bias, "gdn_norm_w": gdn_norm_w, "rwkv_mu": rwkv_mu, "rwkv_w0": rwkv_w0,
        "rwkv_w2": rwkv_w2, "rwkv_a0": rwkv_a0, "rwkv_a2": rwkv_a2, "rwkv_g2": rwkv_g2,
        "rwkv_k_k": rwkv_k_k, "rwkv_k_a": rwkv_k_a, "rwkv_r_k": rwkv_r_k, "rwkv_ln_w": rwkv_ln_w,
        "rwkv_ln_b": rwkv_ln_b, "w_branch_a": w_branch_a, "w_branch_b": w_branch_b,
        "w_branch_c": w_branch_c, "w_out": w_out, "norm_mlp_w": norm_mlp_w, "w_up": w_up, "w_down": w_down,
    }
    b = x_prompt.shape[0]
    meta = jnp.broadcast_to(meta_tokens.astype(x_prompt.dtype)[None], (b, N_META, D_MODEL))
    xp = jnp.concatenate([meta, x_prompt], axis=1)
    yp, sp = _trunk(xp, _zero_states(b), layers, final_norm_w, N_META, CHUNK)
    y_prompt = yp[:, N_META:]
    y_sample, ss = _trunk(x_sample, [state_mlstm_c, state_mlstm_n, state_mlstm_m, state_gdn_s, state_gdn_conv,
                                     state_rwkv_s, state_rwkv_shift],
                          layers, final_norm_w, x_sample.shape[1], CHUNK)
    p_c, p_n, p_m, p_gs, p_gconv, p_rs, p_rshift = sp
    s_c, s_n, s_m, s_gs, s_gconv, s_rs, s_rshift = ss
    return (y_prompt, y_sample, p_c, p_n, p_m, p_gs, p_gconv, p_rs, p_rshift,
            s_c, s_n, s_m, s_gs, s_gconv, s_rs, s_rshift)
```

```python
import numpy as np
import concourse.bass as bass
import concourse.mybir as mybir
from concourse.bass_utils import run_bass_kernel_spmd
from contextlib import ExitStack

F32 = mybir.dt.float32
BF16 = mybir.dt.bfloat16
AF = mybir.ActivationFunctionType
ALU = mybir.AluOpType
AX = mybir.AxisListType

D = 2048
KC = 16
NS = 16
NMETA = 16
SEQ = 2048
NROWS = NS + NMETA + SEQ
N_IN = 17688
DEPTH = 2
A0, B0, C0, G0 = 0, 4104, 8216, 11544
STAGES = ("mlstm", "gdn", "rwkv")


class Buf:
    __slots__ = ("name", "lw", "rd", "dsem", "dcnt")

    def __init__(self, name):
        self.name = name
        self.lw = None
        self.rd = {}
        self.dsem = None
        self.dcnt = 0


class Eng:
    def __init__(self, name, obj, sem):
        self.name = name
        self.obj = obj
        self.sem = sem
        self.cnt = 0
        self.known = {}


class KB:
    def __init__(self, nc, es):
        self.nc = nc
        self.es = es
        self.E = {}
        for nm, obj in (("pe", nc.tensor), ("act", nc.scalar), ("dve", nc.vector), ("pool", nc.gpsimd),
                        ("sp", nc.sync)):
            self.E[nm] = Eng(nm, obj, es.enter_context(nc.semaphore("sem_" + nm)))
        self.semid = {}
        self.dpool = []
        self.nd = 0
        self.out_tokens = {}
        self.n_inst = 0

    def buf(self, name):
        return Buf(name)

    def _dsem(self, b):
        if b.dsem is None:
            if self.dpool:
                b.dsem, b.dcnt = self.dpool.pop()
            else:
                b.dsem = self.es.enter_context(self.nc.semaphore("d%d" % self.nd))
                self.nd += 1
        return b.dsem

    def recycle(self, bufs):
        for b in bufs:
            if b.dsem is not None:
                self.dpool.append((b.dsem, b.dcnt))
                b.dsem = None

    def _waits(self, e, R, W):
        deps = {}
        for b in R:
            if b.lw is not None:
                s, v = b.lw
                if deps.get(s, 0) < v:
                    deps[s] = v
        for b in W:
            if b.lw is not None:
                s, v = b.lw
                if deps.get(s, 0) < v:
                    deps[s] = v
            for s, v in b.rd.items():
                if deps.get(s, 0) < v:
                    deps[s] = v
        for s, v in deps.items():
            if s is e.sem:
                if e.name == "pe" or e.name == "sp":
                    continue
                if e.cnt - v >= 2:
                    continue
            if e.known.get(s, 0) >= v:
                continue
            e.obj.wait_ge(s, v)
            e.known[s] = v
            self.n_inst += 1

    def op(self, en, fn, R=(), W=()):
        e = self.E[en]
        if e.cnt >= 30000:
            e.sem = self.es.enter_context(self.nc.semaphore("sem_%s_%d" % (en, self.n_inst)))
            e.cnt = 0
        self._waits(e, R, W)
        ins = fn(e.obj)
        e.cnt += 1
        ins.then_inc(e.sem, 1)
        tok = (e.sem, e.cnt)
        for b in R:
            if b.rd.get(tok[0], 0) < tok[1]:
                b.rd[tok[0]] = tok[1]
        for b in W:
            b.lw = tok
            b.rd = {}
        self.n_inst += 1
        return ins

    def dma(self, q, out, in_, R=(), W=(), semb=None, slow=False):
        e = self.E[q]
        self._waits(e, R, W)
        if semb is None:
            semb = W[0] if W else R[0]
        s = self._dsem(semb)
        if slow:
            ins = e.obj.dma_start(out=out, in_=in_, allow_slow_non_contiguous=True)
        else:
            ins = e.obj.dma_start(out=out, in_=in_)
        semb.dcnt += 16
        ins.then_inc(s, 16)
        tok = (s, semb.dcnt)
        for b in R:
            if b.rd.get(s, 0) < tok[1]:
                b.rd[s] = tok[1]
        for b in W:
            b.lw = tok
            b.rd = {}
        self.n_inst += 1
        return tok

    def barrier(self, bufs=()):
        bufs = list(bufs)
        for nm in ("pe", "act", "dve", "pool", "sp"):
            self._waits(self.E[nm], [], bufs)

    def finish(self, bufs):
        e = self.E["sp"]
        self._waits(e, list(bufs), list(bufs))


def build():
    nc = bass.Bass("TRN2", target_bir_lowering=False)
    es = ExitStack()
    with es:
        _build(nc, es)
    return nc


def tile_plan():
    tiles = []
    ch = [("s", b, 1, b) for b in range(NS)]
    ch.append(("p", NS, NMETA, 0))
    for j in range(5):
        ch.append(("p", NS + NMETA + 64 * j, 64, 0))
    tiles.append((0, NS + NMETA + 320, ch))
    r0 = NS + NMETA + 320
    for nchk in (6, 6, 6, 6, 3):
        ch = [("p", 64 * j, 64, 0) for j in range(nchk)]
        tiles.append((r0, 64 * nchk, ch))
        r0 += 64 * nchk
    assert r0 == NROWS
    return tiles


NMAX = 384
NWB = 8


def _build(nc, es):
    kb = KB(nc, es)
    op, dma = kb.op, kb.dma

    def dram_in(name, shape):
        return nc.dram_tensor(name, list(shape), F32, kind="ExternalInput").ap()

    def dram_out(name, shape):
        return nc.dram_tensor(name, list(shape), F32, kind="ExternalOutput").ap()

    xin = dram_in("xin", [NROWS, D])
    consts_d = dram_in("consts", [128, CONST_COLS])
    st_c = dram_in("st_c", [DEPTH, NS, 4, 256, 256])
    st_n = dram_in("st_n", [DEPTH, NS, 4, 256])
    st_m = dram_in("st_m", [DEPTH, NS, 4])
    st_gs = dram_in("st_gs", [DEPTH, NS, 8, 128, 128])
    st_gconv = dram_in("st_gconv", [DEPTH, NS, 3, 3072])
    st_rs = dram_in("st_rs", [DEPTH, NS, 16, 64, 64])
    st_rshift = dram_in("st_rshift", [DEPTH, NS, 3328])
    W = {}
    for nm, shp in PARAM_SHAPES.items():
        W[nm] = dram_in(nm, shp)
    y_out = dram_out("y", [NROWS, D])
    o_pc = dram_out("p_c", [DEPTH, 4, 256, 256])
    o_pn = dram_out("p_n", [DEPTH, 4, 256])
    o_pm = dram_out("p_m", [DEPTH, 4])
    o_pgs = dram_out("p_gs", [DEPTH, 8, 128, 128])
    o_pgconv = dram_out("p_gconv", [DEPTH, 3, 3072])
    o_prs = dram_out("p_rs", [DEPTH, 16, 64, 64])
    o_prshift = dram_out("p_rshift", [DEPTH, 3328])
    o_sc = dram_out("s_c", [DEPTH, NS, 4, 256, 256])
    o_sn = dram_out("s_n", [DEPTH, NS, 4, 256])
    o_sm = dram_out("s_m", [DEPTH, NS, 4])
    o_sgs = dram_out("s_gs", [DEPTH, NS, 8, 128, 128])
    o_sgconv = dram_out("s_gconv", [DEPTH, NS, 3, 3072])
    o_srs = dram_out("s_rs", [DEPTH, NS, 16, 64, 64])
    o_srshift = dram_out("s_rshift", [DEPTH, NS, 3328])

    outbufs = []
    cur = [es]

    uid = [0]

    def sb(name, shape, dt=F32):
        uid[0] += 1
        name = "%s_%d" % (name, uid[0])
        t = cur[0].enter_context(nc.sbuf_tensor(name, list(shape), dt))
        b = kb.buf(name)
        if cur[0] is not es:
            arena_bufs.append(b)
        return t, b
    arena_bufs = []

    class Arena:
        def __enter__(self):
            self.st = ExitStack()
            self.st.__enter__()
            cur[0] = self.st
            del arena_bufs[:]
            return self

        def __exit__(self, *a):
            kb.barrier(arena_bufs)
            kb.recycle(arena_bufs)
            cur[0] = es
            return self.st.__exit__(*a)

    cst, cstB = sb("cst", [128, CONST_COLS])
    dma("sp", cst[:, :], consts_d[:, :], W=[cstB])
    ident = cst[:, CO["ident"]:CO["ident"] + 128]
    ones = cst[:, CO["ones"]:CO["ones"] + 128]

    def cview(name, p, shape):
        n = int(np.prod(shape))
        v = cst[0:p, CO[name]:CO[name] + n]
        return v.rearrange("p (a b) -> p a b", b=shape[1])
    nmT_incl = cst[0:64, CO["nmT_incl"]:CO["nmT_incl"] + 64]
    selh = cview("selh", 8, (8, 64))
    sellast = cview("sellast", 64, (3, 128))
    LV = {64: 0, 16: 1, 1: 2}

    ident_bf, ident_bfB = sb("ident_bf", [128, 128], BF16)
    op("dve", lambda e: e.tensor_copy(out=ident_bf[:, :], in_=ident), R=[cstB], W=[ident_bfB])

    PS = []
    for i in range(8):
        t = es.enter_context(nc.psum_tensor("ps%d" % i, [128, 512], F32))
        PS.append((t, kb.buf("ps%d" % i)))
    psi = [0]

    def psum():
        t = PS[psi[0] % 8]
        psi[0] += 1
        return t

    def load_fm_vec(name, src, nblk):
        t, b = sb(name, [128, DEPTH, nblk])
        for l in range(DEPTH):
            dma("sp", t[:, l, :], src[l].rearrange("(k p) -> p k", p=128), W=[b], slow=True)
        return t, b

    nw_mix, nw_mixB = load_fm_vec("nw_mix", W["norm_mix_w"], 16)
    nw_mlp, nw_mlpB = load_fm_vec("nw_mlp", W["norm_mlp_w"], 16)
    nw_fin, nw_finB = sb("nw_fin", [128, 16])
    dma("sp", nw_fin[:, :], W["final_norm_w"].rearrange("(k p) -> p k", p=128), W=[nw_finB], slow=True)

    epst, epsB = sb("epst", [128, 4])
    op("dve", lambda e: e.memset(epst[:, 0:1], 1e-6), W=[epsB])
    op("dve", lambda e: e.memset(epst[:, 1:2], 1e-12), W=[epsB])
    op("dve", lambda e: e.memset(epst[:, 2:3], 64e-5), W=[epsB])
    op("dve", lambda e: e.memset(epst[:, 3:4], 1.0), W=[epsB])
    zrow, zrowB = sb("zrow", [8, 64])
    op("dve", lambda e: e.memset(zrow[:, :], 0.0), W=[zrowB])
    zcol, zcolB = sb("zcol", [128, 64])
    op("dve", lambda e: e.memset(zcol[:, :], 0.0), W=[zcolB])

    xF, xFB = sb("xF", [128, KC, NMAX])
    xnF, xnFB = sb("xnF", [128, KC, NMAX], BF16)
    wbuf = [sb("wb%d" % i, [128, 2048], BF16) for i in range(NWB)]
    wbi = [0]
    yF = [sb("y%sF" % c, [128, 8, NMAX], BF16) for c in "abc"]
    rstd, rstdB = sb("rstd", [128, NMAX])
    tmpA, tmpAB = sb("tmpA", [128, NMAX])

    def load_w(src2d, c0, ncol, kchunks):
        t, b = wbuf[wbi[0] % NWB]
        wbi[0] += 1
        v = t[:, 0:kchunks * ncol].rearrange("p (k c) -> p k c", c=ncol)
        src = src2d.rearrange("(k p) n -> p k n", p=128)[:, :, c0:c0 + ncol]
        dma("pool", v, src, W=[b])
        return v, b

    def rmsnorm(N, nw_ap_fn, nwB, inplace=False):
        ps, psB = psum()
        for k in range(KC):
            op("act", lambda e, k=k: e.activation(out=tmpA[:, 0:N], in_=xF[:, k, 0:N], func=AF.Square),
               R=[xFB], W=[tmpAB])
            op("pe", lambda e, k=k: e.matmul(ps[:, 0:N], lhsT=ones, rhs=tmpA[:, 0:N], start=(k == 0),
                                             stop=(k == KC - 1)), R=[tmpAB, cstB], W=[psB])
        op("act", lambda e: e.activation(out=rstd[:, 0:N], in_=ps[:, 0:N], func=AF.Ln, scale=1.0 / D, bias=epst[:, 0:1]),
           R=[psB, epsB], W=[rstdB])
        op("act", lambda e: e.activation(out=rstd[:, 0:N], in_=rstd[:, 0:N], func=AF.Exp, scale=-0.5),
           R=[rstdB], W=[rstdB])
        for k in range(KC):
            if inplace:
                op("dve", lambda e, k=k: e.scalar_tensor_tensor(out=xF[:, k, 0:N], in0=xF[:, k, 0:N], scalar=nw_ap_fn(k),
                                                                in1=rstd[:, 0:N], op0=ALU.mult, op1=ALU.mult),
                   R=[xFB, rstdB, nwB], W=[xFB])
            else:
                op("dve", lambda e, k=k: e.scalar_tensor_tensor(out=xnF[:, k, 0:N], in0=xF[:, k, 0:N], scalar=nw_ap_fn(k),
                                                                in1=rstd[:, 0:N], op0=ALU.mult, op1=ALU.mult),
                   R=[xFB, rstdB, nwB], W=[xnFB])

    def proj_fm(src2d, c0, ncols, N, consume, kchunks=KC, rhs_fn=None, rhsB=None):
        rhs_fn = rhs_fn or (lambda k: xnF[:, k, 0:N])
        rhsB = rhsB or xnFB
        done = 0
        j = 0
        while done < ncols:
            m = min(128, ncols - done)
            wv, wB = load_w(src2d, c0 + done, m, kchunks)
            ps, psB = psum()
            for k in range(kchunks):
                op("pe", lambda e, k=k: e.matmul(ps[0:m, 0:N], lhsT=wv[:, k, 0:m], rhs=rhs_fn(k), start=(k == 0),
                                                 stop=(k == kchunks - 1)), R=[wB, rhsB], W=[psB])
            consume(j, ps, psB, m)
            j += 1
            done += m

    Cext = []
    for l in range(DEPTH):
        t, b = sb("Cext%d" % l, [128, 4, 2, 257])
        op("dve", lambda e, t=t: e.memset(t[:, :, :, :], 0.0), W=[b])
        Cext.append((t, b))
    mprev = []
    for l in range(DEPTH):
        t, b = sb("mprev%d" % l, [4, 1])
        op("dve", lambda e, t=t: e.memset(t[:, :], 0.0), W=[b])
        mprev.append((t, b))
    mb_i, mb_iB = sb("mb_i", [4, DEPTH])
    mb_f, mb_fB = sb("mb_f", [4, DEPTH])
    dma("sp", mb_i[:, :], W["mlstm_b_i"].rearrange("l h -> h l"), W=[mb_iB], slow=True)
    dma("sp", mb_f[:, :], W["mlstm_b_f"].rearrange("l h -> h l"), W=[mb_fB], slow=True)
    nmb_f, nmb_fB = sb("nmb_f", [4, DEPTH])
    op("dve", lambda e: e.tensor_scalar(out=nmb_f[:, :], in0=mb_f[:, :], scalar1=-1.0, scalar2=None, op0=ALU.mult),
       R=[mb_fB], W=[nmb_fB])

    def to_fm(src, srcB, L, cs, dst):
        dt, dB = dst
        ps, psB = psum()
        pv = ps.bitcast(BF16)
        for blk in range(8):
            op("pe", lambda e, blk=blk: e.transpose(out=pv[:, blk * 64:blk * 64 + L], in_=src[0:L, blk * 128:(blk + 1) * 128],
                                                    identity=ident_bf[0:L, 0:L]), R=[srcB, ident_bfB], W=[psB])
        for blk in range(8):
            if blk % 2 == 0:
                op("act", lambda e, blk=blk: e.activation(out=dt[:, blk, cs], in_=pv[:, blk * 64:blk * 64 + L], func=AF.Copy),
                   R=[psB], W=[dB])
            else:
                op("dve", lambda e, blk=blk: e.tensor_copy(out=dt[:, blk, cs], in_=pv[:, blk * 64:blk * 64 + L]),
                   R=[psB], W=[dB])

    def mlstm_phase(l, tile):
        r0, N, chunks = tile
        CeP, CePB = Cext[l]
        w_in = W["w_in"][l]
        qF, qFB = sb("m_qF", [128, 8, NMAX], BF16)
        kF, kFB = sb("m_kF", [128, 8, NMAX], BF16)
        vF, vFB = sb("m_vF", [128, 8, NMAX], BF16)
        oF, oFB = sb("m_oF", [128, 8, NMAX], BF16)
        gi, giB = sb("m_gi", [4, NMAX])
        gf, gfB = sb("m_gf", [4, NMAX])
        gm, gmB = sb("m_gm", [4, NMAX])
        gb, gbB = sb("m_gb", [4, NMAX])
        gu, guB = sb("m_gu", [4, NMAX])
        gR, gRB = sb("m_gR", [128, NMAX])
        msamp, msampB = sb("m_msamp", [4, NS])
        tokT, tokTB = sb("m_tokT", [64, 128])
        kT, kTB = sb("m_kT", [64, 1024], BF16)
        vext, vextB = sb("m_vext", [64, 4, 257], BF16)
        osg, osgB = sb("m_osg", [64, 1024], BF16)
        Et, EtB = sb("m_E", [64, 64])
        PT, PTB = sb("m_PT", [64, 64], BF16)
        Asb, AsbB = sb("m_Asb", [64, 257])
        Rt, RtB = sb("m_R", [64, 257])
        den, denB = sb("m_den", [64, 2])
        hT, hTB = sb("m_hT", [64, 1024])
        ktil, ktilB = sb("m_ktil", [64, 256], BF16)
        dbc, dbcB = sb("m_dbc", [128, 4])
        ssq, ssqB = sb("m_ssq", [64, 8])
        yT, yTB = sb("m_yT", [64, 1024], BF16)
        sq_junk, sq_junkB = sb("m_sqj", [64, 256])
        CextS, CextSB = sb("CextS", [128, 4, 2, 257])
        Cbf, CbfB = sb("Cbf", [128, 4, 2, 257], BF16)
        bc1024, bc1024B = sb("bc1024", [64, 1024])
        op("dve", lambda e: e.memset(gR[:, :], 0.0), W=[gRB])
        op("dve", lambda e: e.memset(vext[:, :, 256:257], 1.0), W=[vextB])

        def cons_to(dst, dstB, scale=1.0, func=AF.Copy):
            def f(j, ps, psB, m):
                op("act", lambda e: e.activation(out=dst[0:m, j, 0:N], in_=ps[0:m, 0:N], func=func, scale=scale),
                   R=[psB], W=[dstB])
            return f
        proj_fm(w_in, A0 + 0, 1024, N, cons_to(qF, qFB))
        proj_fm(w_in, A0 + 1024, 1024, N, cons_to(kF, kFB, scale=1.0 / 16.0))
        proj_fm(w_in, A0 + 2048, 1024, N, cons_to(vF, vFB))
        proj_fm(w_in, A0 + 3072, 1024, N, cons_to(oF, oFB, func=AF.Sigmoid))

        def cons_i(j, ps, psB, m):
            op("act", lambda e: e.activation(out=gi[:, 0:N], in_=ps[0:4, 0:N], func=AF.Identity, bias=mb_i[:, l:l + 1]),
               R=[psB, mb_iB], W=[giB])
        proj_fm(w_in, A0 + 4096, 4, N, cons_i)

        def cons_f(j, ps, psB, m):
            op("act", lambda e: e.activation(out=gf[:, 0:N], in_=ps[0:4, 0:N], func=AF.Exp, scale=-1.0,
                                             bias=nmb_f[:, l:l + 1]), R=[psB, nmb_fB], W=[gfB])
            op("act", lambda e: e.activation(out=gf[:, 0:N], in_=gf[:, 0:N], func=AF.Ln, bias=epst[0:4, 3:4]),
               R=[gfB, epsB], W=[gfB])
            op("dve", lambda e: e.tensor_scalar(out=gf[:, 0:N], in0=gf[:, 0:N], scalar1=-1.0, scalar2=None,
                                                op0=ALU.mult), R=[gfB], W=[gfB])
        proj_fm(w_in, A0 + 4100, 4, N, cons_f)

        mP, mPB = mprev[l]
        has_s = any(c[0] == "s" for c in chunks)
        pc0 = NS if has_s else 0
        if has_s:
            dma("sp", msamp[:, :], st_m[l].rearrange("b h -> h b"), W=[msampB], slow=True)
            op("dve", lambda e: e.tensor_tensor(out=gm[:, 0:NS], in0=gf[:, 0:NS], in1=msamp[:, :], op=ALU.add),
               R=[gfB, msampB], W=[gmB])
            op("dve", lambda e: e.tensor_tensor(out=gm[:, 0:NS], in0=gm[:, 0:NS], in1=gi[:, 0:NS], op=ALU.max),
               R=[gmB, giB], W=[gmB])
            op("dve", lambda e: e.tensor_copy(out=gb[:, 0:NS], in_=gf[:, 0:NS]), R=[gfB], W=[gbB])
        op("dve", lambda e: e.tensor_tensor_scan(out=gm[:, pc0:N], data0=gf[:, pc0:N], data1=gi[:, pc0:N],
                                                 initial=mP[:, 0:1], op0=ALU.add, op1=ALU.max),
           R=[gfB, giB, mPB], W=[gmB])
        for (kind, c0, L, sidx) in chunks:
            if kind != "p":
                continue
            op("dve", lambda e, c0=c0, L=L: e.tensor_tensor_scan(out=gb[:, c0:c0 + L], data0=gf[:, c0:c0 + L],
                                                                 data1=zrow[0:4, 0:L], initial=0.0,
                                                                 op0=ALU.add, op1=ALU.add),
               R=[gfB, zrowB], W=[gbB])
        op("dve", lambda e: e.tensor_tensor(out=gu[:, 0:N], in0=gb[:, 0:N], in1=gm[:, 0:N], op=ALU.subtract),
           R=[gbB, gmB], W=[guB])
        op("dve", lambda e: e.tensor_tensor(out=gR[0:4, 0:N], in0=gi[:, 0:N], in1=gb[:, 0:N], op=ALU.subtract),
           R=[giB, gbB], W=[gRB])
        op("act", lambda e: e.activation(out=gR[64:68, 0:N], in_=gm[:, 0:N], func=AF.Exp, scale=-1.0),
           R=[gmB], W=[gRB])
        if has_s:
            op("dve", lambda e: e.tensor_tensor(out=gR[32:36, 0:NS], in0=gu[:, 0:NS], in1=msamp[:, :], op=ALU.add),
               R=[guB, msampB], W=[gRB])
            op("act", lambda e: e.activation(out=gR[32:36, 0:NS], in_=gR[32:36, 0:NS], func=AF.Exp), R=[gRB], W=[gRB])
            op("dve", lambda e: e.tensor_tensor(out=gR[96:100, 0:NS], in0=gR[0:4, 0:NS], in1=gu[:, 0:NS], op=ALU.add),
               R=[gRB, guB], W=[gRB])
            op("act", lambda e: e.activation(out=gR[96:100, 0:NS], in_=gR[96:100, 0:NS], func=AF.Exp), R=[gRB], W=[gRB])
        first = True
        for (kind, c0, L, sidx) in chunks:
            if kind != "p":
                continue
            mp_ap = mP[:, 0:1] if first else gm[:, c0 - 1:c0]
            op("act", lambda e, c0=c0, L=L, mp_ap=mp_ap: e.activation(out=gR[32:36, c0:c0 + L], in_=gu[:, c0:c0 + L],
                                                                     func=AF.Exp, bias=mp_ap),
               R=[guB, gmB, mPB], W=[gRB])
            op("act", lambda e, c0=c0, L=L: e.activation(out=gR[96:100, c0:c0 + L], in_=gR[0:4, c0:c0 + L],
                                                         func=AF.Exp, bias=gu[:, c0 + L - 1:c0 + L]),
               R=[guB, gRB], W=[gRB])
            first = False

        dma("sp", bc1024[:, :], W["mlstm_norm_w"][l:l + 1, :].partition_broadcast(64), W=[bc1024B])

        def chunk(c0, L, Ct, CtB):
            lv = LV[L]
            cs = slice(c0, c0 + L)
            ps, psB = psum()
            op("pe", lambda e: e.transpose(out=ps[0:L, 0:128], in_=gR[:, cs], identity=ident), R=[gRB, cstB], W=[psB])
            op("act", lambda e: e.activation(out=tokT[0:L, :], in_=ps[0:L, 0:128], func=AF.Copy), R=[psB], W=[tokTB])
            for (src, srcB, dstfn, dstB) in ((kF, kFB, lambda blk: kT[0:L, blk * 128:(blk + 1) * 128], kTB),
                                             (vF, vFB, lambda blk: vext[0:L, blk // 2, (blk % 2) * 128:(blk % 2) * 128 + 128], vextB),
                                             (oF, oFB, lambda blk: osg[0:L, blk * 128:(blk + 1) * 128], osgB)):
                ps, psB = psum()
                pv = ps.bitcast(BF16)
                for blk in range(8):
                    op("pe", lambda e, blk=blk, pv=pv, src=src: e.transpose(out=pv[0:L, blk * 128:(blk + 1) * 128],
                                                                          in_=src[:, blk, cs], identity=ident_bf[:, :]),
                       R=[srcB, ident_bfB], W=[psB])
                for blk in range(8):
                    if blk % 2 == 0:
                        op("act", lambda e, blk=blk, pv=pv, dstfn=dstfn: e.activation(
                            out=dstfn(blk), in_=pv[0:L, blk * 128:(blk + 1) * 128], func=AF.Copy), R=[psB], W=[dstB])
                    else:
                        op("dve", lambda e, blk=blk, pv=pv, dstfn=dstfn: e.tensor_copy(
                            out=dstfn(blk), in_=pv[0:L, blk * 128:(blk + 1) * 128]), R=[psB], W=[dstB])
            op("act", lambda e: e.activation(out=Cbf[:, :, :, :], in_=Ct[:, :, :, :], func=AF.Copy), R=[CtB], W=[CbfB])
            psd, psdB = psum()
            op("pe", lambda e: e.matmul(psd[:, 0:4], lhsT=sellast[0:L, lv, :], rhs=tokT[0:L, 32:36], start=True, stop=True),
               R=[cstB, tokTB], W=[psdB])
            op("dve", lambda e: e.tensor_copy(out=dbc[:, :], in_=psd[:, 0:4]), R=[psdB], W=[dbcB])
            for h in range(4):
                pst, pstB = psum()
                for kc in range(2):
                    op("pe", lambda e, kc=kc: e.matmul(pst[0:L, 0:L], lhsT=kF[:, 2 * h + kc, cs], rhs=qF[:, 2 * h + kc, cs],
                                                       start=(kc == 0), stop=(kc == 1)), R=[kFB, qFB], W=[pstB])
                psu, psuB = psum()
                op("pe", lambda e: e.matmul(psu[0:L, 0:L], lhsT=selh[0:4, h, 0:L], rhs=gu[:, cs], start=True, stop=False),
                   R=[cstB, guB], W=[psuB])
                op("pe", lambda e: e.matmul(psu[0:L, 0:L], lhsT=ident[0:L, 0:L], rhs=nmT_incl[0:L, 0:L], start=False, stop=True),
                   R=[cstB], W=[psuB])
                op("act", lambda e: e.activation(out=Et[0:L, 0:L], in_=psu[0:L, 0:L], func=AF.Exp, bias=tokT[0:L, h:h + 1]),
                   R=[psuB, tokTB], W=[EtB])
                op("dve", lambda e: e.tensor_tensor(out=PT[0:L, 0:L], in0=pst[0:L, 0:L], in1=Et[0:L, 0:L], op=ALU.mult),
                   R=[pstB, EtB], W=[PTB])
                psb, psbB = psum()
                op("pe", lambda e: e.matmul(psb[0:L, 0:257], lhsT=PT[0:L, 0:L], rhs=vext[0:L, h, :], start=True, stop=True),
                   R=[PTB, vextB], W=[psbB])
                psa, psaB = psum()
                for kc in range(2):
                    op("pe", lambda e, kc=kc: e.matmul(psa[0:L, 0:257], lhsT=qF[:, 2 * h + kc, cs], rhs=Cbf[:, h, kc, :],
                                                       start=(kc == 0), stop=(kc == 1)), R=[qFB, CbfB], W=[psaB])
                op("act", lambda e: e.activation(out=Asb[0:L, :], in_=psa[0:L, 0:257], func=AF.Copy,
                                                 scale=tokT[0:L, 32 + h:33 + h]), R=[psaB, tokTB], W=[AsbB])
                op("dve", lambda e: e.tensor_tensor(out=Rt[0:L, :], in0=Asb[0:L, :], in1=psb[0:L, 0:257], op=ALU.add),
                   R=[AsbB, psbB], W=[RtB])
                op("act", lambda e: e.activation(out=den[0:L, 0:1], in_=Rt[0:L, 256:257], func=AF.Abs),
                   R=[RtB], W=[denB])
                op("dve", lambda e: e.tensor_tensor(out=den[0:L, 0:1], in0=den[0:L, 0:1], in1=tokT[0:L, 64 + h:65 + h],
                                                    op=ALU.max), R=[denB, tokTB], W=[denB])
                op("dve", lambda e: e.reciprocal(out=den[0:L, 1:2], in_=den[0:L, 0:1]), R=[denB], W=[denB])
                op("act", lambda e: e.activation(out=hT[0:L, h * 256:(h + 1) * 256], in_=Rt[0:L, 0:256], func=AF.Copy,
                                                 scale=den[0:L, 1:2]), R=[RtB, denB], W=[hTB])
                op("act", lambda e: e.activation(out=ktil[0:L, :], in_=kT[0:L, h * 256:(h + 1) * 256], func=AF.Copy,
                                                 scale=tokT[0:L, 96 + h:97 + h]), R=[kTB, tokTB], W=[ktilB])
                for kc in range(2):
                    psc, pscB = psum()
                    op("pe", lambda e, kc=kc, psc=psc: e.matmul(psc[:, 0:257], lhsT=ktil[0:L, kc * 128:(kc + 1) * 128],
                                                                rhs=vext[0:L, h, :], start=True, stop=True),
                       R=[ktilB, vextB], W=[pscB])
                    op("dve", lambda e, kc=kc, psc=psc: e.scalar_tensor_tensor(
                        out=Ct[:, h, kc, :], in0=Ct[:, h, kc, :], scalar=dbc[:, h:h + 1], in1=psc[:, 0:257],
                        op0=ALU.mult, op1=ALU.add), R=[CtB, dbcB, pscB], W=[CtB])
            for h in range(4):
                op("act", lambda e, h=h: e.activation(out=sq_junk[0:L, :], in_=hT[0:L, h * 256:(h + 1) * 256], func=AF.Square,
                                                      accum_out=ssq[0:L, h:h + 1]), R=[hTB], W=[ssqB, sq_junkB])
            op("act", lambda e: e.activation(out=ssq[0:L, 4:8], in_=ssq[0:L, 0:4], func=AF.Ln, scale=1.0 / 256.0,
                                             bias=epst[0:L, 0:1]), R=[ssqB, epsB], W=[ssqB])
            op("act", lambda e: e.activation(out=ssq[0:L, 4:8], in_=ssq[0:L, 4:8], func=AF.Exp, scale=-0.5), R=[ssqB], W=[ssqB])
            for h in range(4):
                op("dve", lambda e, h=h: e.scalar_tensor_tensor(out=hT[0:L, h * 256:(h + 1) * 256],
                                                                in0=hT[0:L, h * 256:(h + 1) * 256],
                                                                scalar=ssq[0:L, 4 + h:5 + h],
                                                                in1=bc1024[0:L, h * 256:(h + 1) * 256],
                                                                op0=ALU.mult, op1=ALU.mult),
                   R=[hTB, ssqB, bc1024B], W=[hTB])
            op("dve", lambda e: e.tensor_tensor(out=yT[0:L, :], in0=hT[0:L, :], in1=osg[0:L, :], op=ALU.mult),
               R=[hTB, osgB], W=[yTB])
            to_fm(yT, yTB, L, cs, yF[0])

        for (kind, c0, L, sidx) in chunks:
            if kind == "s":
                Ct, CtB = CextS, CextSB
                for h in range(4):
                    dma("sp", Ct[:, h, :, 0:256], st_c[l, sidx, h].rearrange("(kc p) v -> p kc v", p=128), W=[CtB])
                dma("sp", Ct[:, :, :, 256:257], st_n[l, sidx].rearrange("h (kc p o) -> p h kc o", p=128, o=1),
                    W=[CtB], slow=True)
            else:
                Ct, CtB = CeP, CePB
            chunk(c0, L, Ct, CtB)
            if kind == "s":
                for h in range(4):
                    dma("sp", o_sc[l, sidx, h].rearrange("(kc p) v -> p kc v", p=128), Ct[:, h, :, 0:256],
                        R=[CtB], semb=CtB)
                dma("sp", o_sn[l, sidx].rearrange("h (kc p o) -> p h kc o", p=128, o=1), Ct[:, :, :, 256:257],
                    R=[CtB], semb=CtB, slow=True)
        if has_s:
            dma("sp", o_sm[l].rearrange("b h -> h b"), gm[:, 0:NS], R=[gmB], semb=gmB, slow=True)
        op("dve", lambda e: e.tensor_copy(out=mP[:, 0:1], in_=gm[:, N - 1:N]), R=[gmB], W=[mPB])


    nm_strict = cst[0:64, CO["nm_strict"]:CO["nm_strict"] + 64]
    nmT_strict = cst[0:64, CO["nmT_strict"]:CO["nmT_strict"] + 64]
    GS = []
    ghist = []
    for l in range(DEPTH):
        t, b = sb("GS%d" % l, [128, 8, 128])
        op("dve", lambda e, t=t: e.memset(t[:, :, :], 0.0), W=[b])
        GS.append((t, b))
        t, b = sb("ghist%d" % l, [128, 24, 3])
        op("dve", lambda e, t=t: e.memset(t[:, :, :], 0.0), W=[b])
        ghist.append((t, b))
    gcw, gcwB = sb("gcw", [128, DEPTH, 24, 4])
    gpar, gparB = sb("gpar", [8, DEPTH, 4])
    for l in range(DEPTH):
        dma("sp", gpar[:, l, 0:1], W["gdn_a_log"][l:l + 1, :].rearrange("o h -> h o"), W=[gparB], slow=True)
        dma("sp", gpar[:, l, 1:2], W["gdn_dt_bias"][l:l + 1, :].rearrange("o h -> h o"), W=[gparB], slow=True)
        op("act", lambda e, l=l: e.activation(out=gpar[:, l, 2:3], in_=gpar[:, l, 0:1], func=AF.Exp), R=[gparB], W=[gparB])
        op("dve", lambda e, l=l: e.tensor_scalar(out=gpar[:, l, 2:3], in0=gpar[:, l, 2:3], scalar1=-1.0, scalar2=None,
                                                 op0=ALU.mult), R=[gparB], W=[gparB])
    with Arena():
        cwt, cwtB = sb("cwt", [4, 3072])
        for l in range(DEPTH):
            dma("sp", cwt[:, :], W["gdn_conv_w"][l], W=[cwtB])
            for q in range(6):
                ps, psB = psum()
                for kk in range(4):
                    k = q * 4 + kk
                    op("pe", lambda e, k=k, kk=kk: e.transpose(out=ps[:, kk * 4:kk * 4 + 4], in_=cwt[0:4, k * 128:(k + 1) * 128],
                                                              identity=ident[0:4, 0:4]), R=[cwtB, cstB], W=[psB])
                op("dve", lambda e, q=q, l=l: e.tensor_copy(out=gcw[:, l, q * 4:q * 4 + 4, :],
                                                            in_=ps[:, 0:16].rearrange("p (a b) -> p a b", b=4)),
                   R=[psB], W=[gcwB])

    def gdn_phase(l, tile):
        r0, N, chunks = tile
        St, StB = GS[l]
        hs, hsB = ghist[l]
        w_in = W["w_in"][l]
        has_s = any(c[0] == "s" for c in chunks)
        pc0 = NS if has_s else 0
        Np = N - pc0
        xcs = [sb("g_xc%d" % i, [128, NMAX + 3]) for i in range(2)]
        yc, ycB = sb("g_yc", [128, NMAX])
        qF, qFB = sb("g_qF", [128, 8, NMAX], BF16)
        kF, kFB = sb("g_kF", [128, 8, NMAX], BF16)
        vF, vFB = sb("g_vF", [128, 8, NMAX], BF16)
        zF, zFB = sb("g_zF", [128, 8, NMAX], BF16)
        ra, raB = sb("g_ra", [8, NMAX])
        rlb, rlbB = sb("g_rlb", [8, NMAX])
        rg, rgB = sb("g_rg", [8, NMAX])
        rng_, rngB = sb("g_rng", [8, NMAX])
        rgb, rgbB = sb("g_rgb", [8, NMAX])
        stA, stAB = sb("g_stA", [128, NMAX])
        stB, stBB = sb("g_stB", [64, NMAX])
        tokA, tokAB = sb("g_tokA", [64, 128])
        tokB, tokBB = sb("g_tokB", [64, 64])
        kT, kTB = sb("g_kT", [64, 1024], BF16)
        vT, vTB = sb("g_vT", [64, 1024], BF16)
        zT, zTB = sb("g_zT", [64, 1024], BF16)
        E1, E1B = sb("g_E1", [64, 64])
        E2, E2B = sb("g_E2", [64, 64])
        E3, E3B = sb("g_E3", [64, 64])
        Pm = [sb("g_P%d" % i, [64, 64]) for i in range(2)]
        Qm = [sb("g_Q%d" % i, [64, 64]) for i in range(2)]
        TT, TTB = sb("g_TT", [64, 64])
        attnT, attnTB = sb("g_attnT", [64, 64], BF16)
        bv, bvB = sb("g_bv", [64, 128])
        bek, bekB = sb("g_bek", [64, 128])
        wF, wFB = sb("g_wF", [128, 64], BF16)
        usb, usbB = sb("g_usb", [64, 128])
        vnew, vnewB = sb("g_vnew", [64, 128], BF16)
        qsb, qsbB = sb("g_qsb", [64, 128])
        oT, oTB = sb("g_oT", [64, 1024])
        kd, kdB = sb("g_kd", [64, 128], BF16)
        dbc, dbcB = sb("g_dbc", [128, 8])
        Sbf, SbfB = sb("g_Sbf", [128, 8, 128], BF16)
        if has_s:
            SS, SSB = sb("g_SS", [128, 8, 128])
        ssq, ssqB = sb("g_ssq", [64, 16])
        sqj, sqjB = sb("g_sqj", [64, 128])
        yT, yTB = sb("g_yT", [64, 1024], BF16)
        bcn, bcnB = sb("g_bcn", [64, 128])
        if has_s:
            xs, xsB = sb("g_xs", [128, 24, NS])
            bufS, bufSB = sb("g_bufS", [128, 24, NS * 3])
            st48, st48B = sb("g_st48", [48, 1536])
        op("dve", lambda e: e.memset(stA[:, :], 0.0), W=[stAB])
        op("dve", lambda e: e.memset(stB[:, :], 0.0), W=[stBB])
        dma("sp", bcn[:, :], W["gdn_norm_w"][l:l + 1, :].partition_broadcast(64), W=[bcnB])
        if has_s:
            for hq in range(2):
                dma("sp", st48[:, :], st_gconv[l].rearrange("b j c -> (b j) c")[:, hq * 1536:(hq + 1) * 1536], W=[st48B])
                for kq in range(12):
                    k = hq * 12 + kq
                    ps, psB = psum()
                    op("pe", lambda e, kq=kq, ps=ps: e.transpose(out=ps[:, 0:48], in_=st48[0:48, kq * 128:(kq + 1) * 128],
                                                                 identity=ident[0:48, 0:48]), R=[st48B, cstB], W=[psB])
                    op("dve", lambda e, k=k, ps=ps: e.tensor_copy(out=bufS[:, k, :], in_=ps[:, 0:48]), R=[psB], W=[bufSB])
            dma("sp", o_sgconv[l][:, 0:2, :], st_gconv[l][:, 1:3, :], R=[st48B], semb=st48B)

        def cons_qkv(j, ps, psB, m):
            xc, xcB = xcs[j % 2]
            op("dve", lambda e: e.tensor_copy(out=xc[:, 0:3], in_=hs[:, j, :]), R=[hsB], W=[xcB])
            if has_s:
                op("act", lambda e: e.activation(out=xs[:, j, :], in_=ps[:, 0:NS], func=AF.Copy), R=[psB], W=[xsB])
            op("act", lambda e: e.activation(out=xc[:, 3:3 + Np], in_=ps[:, pc0:N], func=AF.Copy), R=[psB], W=[xcB])
            op("dve", lambda e: e.tensor_scalar(out=yc[:, pc0:N], in0=xc[:, 0:Np], scalar1=gcw[:, l, j, 0:1],
                                                scalar2=None, op0=ALU.mult), R=[xcB, gcwB], W=[ycB])
            for t in range(1, 4):
                op("dve", lambda e, t=t: e.scalar_tensor_tensor(out=yc[:, pc0:N], in0=xc[:, t:t + Np],
                                                                scalar=gcw[:, l, j, t:t + 1], in1=yc[:, pc0:N],
                                                                op0=ALU.mult, op1=ALU.add), R=[xcB, gcwB, ycB], W=[ycB])
            op("dve", lambda e: e.tensor_copy(out=hs[:, j, :], in_=xc[:, Np:Np + 3]), R=[xcB], W=[hsB])
            if has_s:
                bs = bufS[:, j, :].rearrange("p (b t) -> p b t", t=3)
                op("dve", lambda e: e.tensor_scalar(out=yc[:, 0:NS], in0=xs[:, j, :], scalar1=gcw[:, l, j, 3:4],
                                                    scalar2=None, op0=ALU.mult), R=[xsB, gcwB], W=[ycB])
                for t in range(3):
                    op("dve", lambda e, t=t: e.scalar_tensor_tensor(out=yc[:, 0:NS], in0=bs[:, :, t],
                                                                    scalar=gcw[:, l, j, t:t + 1], in1=yc[:, 0:NS],
                                                                    op0=ALU.mult, op1=ALU.add),
                       R=[bufSB, gcwB, ycB], W=[ycB])
            if j >= 16:
                op("act", lambda e: e.activation(out=vF[:, j - 16, 0:N], in_=yc[:, 0:N], func=AF.Silu), R=[ycB], W=[vFB])
                return
            op("act", lambda e: e.activation(out=yc[:, 0:N], in_=yc[:, 0:N], func=AF.Silu), R=[ycB], W=[ycB])
            op("act", lambda e: e.activation(out=tmpA[:, 0:N], in_=yc[:, 0:N], func=AF.Square), R=[ycB], W=[tmpAB])
            ps2, ps2B = psum()
            op("pe", lambda e: e.matmul(ps2[:, 0:N], lhsT=ones, rhs=tmpA[:, 0:N], start=True, stop=True),
               R=[tmpAB, cstB], W=[ps2B])
            op("act", lambda e: e.activation(out=tmpA[:, 0:N], in_=ps2[:, 0:N], func=AF.Ln, bias=epst[:, 1:2]),
               R=[ps2B, epsB], W=[tmpAB])
            op("act", lambda e: e.activation(out=tmpA[:, 0:N], in_=tmpA[:, 0:N], func=AF.Exp, scale=-0.5), R=[tmpAB], W=[tmpAB])
            if j < 8:
                op("dve", lambda e: e.scalar_tensor_tensor(out=qF[:, j, 0:N], in0=yc[:, 0:N], scalar=128.0 ** -0.5,
                                                           in1=tmpA[:, 0:N], op0=ALU.mult, op1=ALU.mult),
                   R=[ycB, tmpAB], W=[qFB])
            else:
                op("dve", lambda e: e.tensor_tensor(out=kF[:, j - 8, 0:N], in0=yc[:, 0:N], in1=tmpA[:, 0:N], op=ALU.mult),
                   R=[ycB, tmpAB], W=[kFB])
        proj_fm(w_in, B0, 3072, N, cons_qkv)
        if has_s:
            for hq in range(2):
                for q3 in range(3):
                    q = hq * 3 + q3
                    ps, psB = psum()
                    for kk in range(4):
                        k = q * 4 + kk
                        op("pe", lambda e, k=k, kk=kk, ps=ps: e.transpose(out=ps[0:NS, kk * 128:(kk + 1) * 128], in_=xs[:, k, :],
                                                                  identity=ident), R=[xsB, cstB], W=[psB])
                    op("dve", lambda e, q3=q3, ps=ps: e.tensor_copy(out=st48[0:NS, q3 * 512:(q3 + 1) * 512], in_=ps[0:NS, 0:512]),
                       R=[psB], W=[st48B])
                dma("sp", o_sgconv[l][:, 2, hq * 1536:(hq + 1) * 1536], st48[0:NS, :], R=[st48B], semb=st48B)

        def cons_z(j, ps, psB, m):
            op("act", lambda e: e.activation(out=zF[:, j, 0:N], in_=ps[:, 0:N], func=AF.Silu), R=[psB], W=[zFB])
        proj_fm(w_in, B0 + 3072, 1024, N, cons_z)

        def cons_a(j, ps, psB, m):
            op("act", lambda e: e.activation(out=ra[:, 0:N], in_=ps[0:8, 0:N], func=AF.Exp, bias=gpar[:, l, 1:2]),
               R=[psB, gparB], W=[raB])
            op("act", lambda e: e.activation(out=ra[:, 0:N], in_=ra[:, 0:N], func=AF.Ln, bias=epst[0:8, 3:4]),
               R=[raB, epsB], W=[raB])
            op("dve", lambda e: e.tensor_scalar(out=ra[:, 0:N], in0=ra[:, 0:N], scalar1=gpar[:, l, 2:3], scalar2=None,
                                                op0=ALU.mult), R=[raB, gparB], W=[raB])
        proj_fm(w_in, B0 + 4096, 8, N, cons_a)

        def cons_b(j, ps, psB, m):
            op("act", lambda e: e.activation(out=rlb[:, 0:N], in_=ps[0:8, 0:N], func=AF.Exp, scale=-1.0), R=[psB], W=[rlbB])
            op("act", lambda e: e.activation(out=rlb[:, 0:N], in_=rlb[:, 0:N], func=AF.Ln, bias=epst[0:8, 3:4]),
               R=[rlbB, epsB], W=[rlbB])
            op("dve", lambda e: e.tensor_scalar(out=rlb[:, 0:N], in0=rlb[:, 0:N], scalar1=-1.0, scalar2=None, op0=ALU.mult),
               R=[rlbB], W=[rlbB])
        proj_fm(w_in, B0 + 4104, 8, N, cons_b)
        if has_s:
            op("dve", lambda e: e.tensor_copy(out=rg[:, 0:NS], in_=ra[:, 0:NS]), R=[raB], W=[rgB])
        for (kind, c0, L, sidx) in chunks:
            if kind != "p":
                continue
            op("dve", lambda e, c0=c0, L=L: e.tensor_tensor_scan(out=rg[:, c0:c0 + L], data0=ra[:, c0:c0 + L],
                                                                 data1=zrow[0:8, 0:L], initial=0.0, op0=ALU.add, op1=ALU.add),
               R=[raB, zrowB], W=[rgB])
        op("dve", lambda e: e.tensor_scalar(out=rng_[:, 0:N], in0=rg[:, 0:N], scalar1=-1.0, scalar2=None, op0=ALU.mult),
           R=[rgB], W=[rngB])
        op("dve", lambda e: e.tensor_tensor(out=rgb[:, 0:N], in0=rg[:, 0:N], in1=rlb[:, 0:N], op=ALU.add),
           R=[rgB, rlbB], W=[rgbB])
        op("dve", lambda e: e.tensor_copy(out=stA[0:8, 0:N], in_=rng_[:, 0:N]), R=[rngB], W=[stAB])
        op("dve", lambda e: e.tensor_copy(out=stA[32:40, 0:N], in_=rgb[:, 0:N]), R=[rgbB], W=[stAB])
        op("act", lambda e: e.activation(out=stA[64:72, 0:N], in_=rlb[:, 0:N], func=AF.Exp), R=[rlbB], W=[stAB])
        op("act", lambda e: e.activation(out=stA[96:104, 0:N], in_=rgb[:, 0:N], func=AF.Exp), R=[rgbB], W=[stAB])
        op("act", lambda e: e.activation(out=stB[0:8, 0:N], in_=rg[:, 0:N], func=AF.Exp), R=[rgB], W=[stBB])
        for (kind, c0, L, sidx) in chunks:
            op("act", lambda e, c0=c0, L=L: e.activation(out=stB[32:40, c0:c0 + L], in_=rng_[:, c0:c0 + L], func=AF.Exp,
                                                         bias=rg[:, c0 + L - 1:c0 + L]), R=[rngB, rgB], W=[stBB])

        def chunk(c0, L, S_, S_B):
            lv = LV[L]
            cs = slice(c0, c0 + L)
            nlev = {64: 5, 16: 3, 1: 0}[L]
            ps, psB = psum()
            op("pe", lambda e: e.transpose(out=ps[0:L, 0:128], in_=stA[:, cs], identity=ident), R=[stAB, cstB], W=[psB])
            op("act", lambda e: e.activation(out=tokA[0:L, :], in_=ps[0:L, 0:128], func=AF.Copy), R=[psB], W=[tokAB])
            ps, psB = psum()
            op("pe", lambda e: e.transpose(out=ps[0:L, 0:64], in_=stB[:, cs], identity=ident[0:64, 0:64]), R=[stBB, cstB], W=[psB])
            op("act", lambda e: e.activation(out=tokB[0:L, :], in_=ps[0:L, 0:64], func=AF.Copy), R=[psB], W=[tokBB])
            for (src, srcB, dst, dstB) in ((kF, kFB, kT, kTB), (vF, vFB, vT, vTB), (zF, zFB, zT, zTB)):
                ps, psB = psum()
                pv = ps.bitcast(BF16)
                for blk in range(8):
                    op("pe", lambda e, blk=blk, pv=pv, src=src: e.transpose(out=pv[0:L, blk * 128:(blk + 1) * 128],
                                                                          in_=src[:, blk, cs], identity=ident_bf[:, :]),
                       R=[srcB, ident_bfB], W=[psB])
                op("act", lambda e, pv=pv, dst=dst: e.activation(out=dst[0:L, 0:512], in_=pv[0:L, 0:512], func=AF.Copy),
                   R=[psB], W=[dstB])
                op("dve", lambda e, pv=pv, dst=dst: e.tensor_copy(out=dst[0:L, 512:1024], in_=pv[0:L, 512:1024]),
                   R=[psB], W=[dstB])
            op("act", lambda e: e.activation(out=Sbf[:, :, :], in_=S_[:, :, :], func=AF.Copy), R=[S_B], W=[SbfB])
            psd, psdB = psum()
            op("pe", lambda e: e.matmul(psd[:, 0:8], lhsT=sellast[0:L, lv, :], rhs=tokB[0:L, 0:8], start=True, stop=True),
               R=[cstB, tokBB], W=[psdB])
            op("dve", lambda e: e.tensor_copy(out=dbc[:, :], in_=psd[:, 0:8]), R=[psdB], W=[dbcB])
            for h in range(8):
                psg, psgB = psum()
                op("pe", lambda e: e.matmul(psg[0:L, 0:L], lhsT=kF[:, h, cs], rhs=kF[:, h, cs], start=True, stop=True),
                   R=[kFB], W=[psgB])
                pskq, pskqB = psum()
                op("pe", lambda e: e.matmul(pskq[0:L, 0:L], lhsT=kF[:, h, cs], rhs=qF[:, h, cs], start=True, stop=True),
                   R=[kFB, qFB], W=[pskqB])
                for (row, rowB, nm, Ed, EdB, bias_ap) in ((rgb, rgbB, nmT_strict, E1, E1B, tokA[0:L, h:h + 1]),
                                                         (rng_, rngB, nm_strict, E2, E2B, tokA[0:L, 32 + h:33 + h]),
                                                         (rg, rgB, nmT_incl, E3, E3B, tokA[0:L, h:h + 1])):
                    pse, pseB = psum()
                    op("pe", lambda e, row=row, pse=pse: e.matmul(pse[0:L, 0:L], lhsT=selh[0:8, h, 0:L], rhs=row[:, cs],
                                                                  start=True, stop=False), R=[cstB, rowB], W=[pseB])
                    op("pe", lambda e, nm=nm, pse=pse: e.matmul(pse[0:L, 0:L], lhsT=ident[0:L, 0:L], rhs=nm[0:L, 0:L],
                                                                start=False, stop=True), R=[cstB], W=[pseB])
                    op("act", lambda e, Ed=Ed, pse=pse, bias_ap=bias_ap: e.activation(out=Ed[0:L, 0:L], in_=pse[0:L, 0:L],
                                                                                     func=AF.Exp, bias=bias_ap),
                       R=[pseB, tokAB], W=[EdB])
                (P0, P0B), (P1, P1B) = Pm
                (Q0, Q0B), (Q1, Q1B) = Qm
                op("dve", lambda e: e.scalar_tensor_tensor(out=Q0[0:L, 0:L], in0=psg[0:L, 0:L], scalar=-1.0, in1=E1[0:L, 0:L],
                                                           op0=ALU.mult, op1=ALU.mult), R=[psgB, E1B], W=[Q0B])
                op("dve", lambda e: e.scalar_tensor_tensor(out=P0[0:L, 0:L], in0=psg[0:L, 0:L], scalar=-1.0, in1=E2[0:L, 0:L],
                                                           op0=ALU.mult, op1=ALU.mult), R=[psgB, E2B], W=[P0B])
                op("dve", lambda e: e.tensor_tensor(out=attnT[0:L, 0:L], in0=pskq[0:L, 0:L], in1=E3[0:L, 0:L], op=ALU.mult),
                   R=[pskqB, E3B], W=[attnTB])
                op("dve", lambda e: e.tensor_tensor(out=TT[0:L, 0:L], in0=Q0[0:L, 0:L], in1=ident[0:L, 0:L], op=ALU.add),
                   R=[Q0B, cstB], W=[TTB])
                cp, cq = (P0, P0B), (Q0, Q0B)
                np_, nq = (P1, P1B), (Q1, Q1B)
                for lev in range(nlev):
                    psp, pspB = psum()
                    op("pe", lambda e, cp=cp, cq=cq, psp=psp: e.matmul(psp[0:L, 0:L], lhsT=cq[0][0:L, 0:L], rhs=cp[0][0:L, 0:L],
                                                                      start=True, stop=True), R=[cp[1], cq[1]], W=[pspB])
                    psq, psqB = psum()
                    op("pe", lambda e, cp=cp, cq=cq, psq=psq: e.matmul(psq[0:L, 0:L], lhsT=cp[0][0:L, 0:L], rhs=cq[0][0:L, 0:L],
                                                                      start=True, stop=True), R=[cp[1], cq[1]], W=[psqB])
                    op("act", lambda e, np_=np_, psp=psp: e.activation(out=np_[0][0:L, 0:L], in_=psp[0:L, 0:L], func=AF.Copy),
                       R=[pspB], W=[np_[1]])
                    op("dve", lambda e, nq=nq, psq=psq: e.tensor_copy(out=nq[0][0:L, 0:L], in_=psq[0:L, 0:L]),
                       R=[psqB], W=[nq[1]])
                    pst, pstB = psum()
                    op("pe", lambda e, np_=np_, pst=pst: e.matmul(pst[0:L, 0:L], lhsT=np_[0][0:L, 0:L], rhs=TT[0:L, 0:L],
                                                                  start=True, stop=True), R=[np_[1], TTB], W=[pstB])
                    op("dve", lambda e, pst=pst: e.tensor_tensor(out=TT[0:L, 0:L], in0=TT[0:L, 0:L], in1=pst[0:L, 0:L],
                                                                 op=ALU.add), R=[TTB, pstB], W=[TTB])
                    cp, cq, np_, nq = np_, nq, cp, cq
                hs_ = slice(h * 128, (h + 1) * 128)
                op("act", lambda e: e.activation(out=bv[0:L, :], in_=vT[0:L, hs_], func=AF.Copy, scale=tokA[0:L, 64 + h:65 + h]),
                   R=[vTB, tokAB], W=[bvB])
                op("act", lambda e: e.activation(out=bek[0:L, :], in_=kT[0:L, hs_], func=AF.Copy, scale=tokA[0:L, 96 + h:97 + h]),
                   R=[kTB, tokAB], W=[bekB])
                psu, psuB = psum()
                op("pe", lambda e: e.matmul(psu[0:L, 0:128], lhsT=TT[0:L, 0:L], rhs=bv[0:L, :], start=True, stop=True),
                   R=[TTB, bvB], W=[psuB])
                psw, pswB = psum()
                op("pe", lambda e: e.matmul(psw[:, 0:L], lhsT=bek[0:L, :], rhs=TT[0:L, 0:L], start=True, stop=True),
                   R=[TTB, bekB], W=[pswB])
                op("act", lambda e: e.activation(out=wF[:, 0:L], in_=psw[:, 0:L], func=AF.Copy), R=[pswB], W=[wFB])
                op("dve", lambda e: e.tensor_copy(out=usb[0:L, :], in_=psu[0:L, 0:128]), R=[psuB], W=[usbB])
                psws, pswsB = psum()
                op("pe", lambda e: e.matmul(psws[0:L, 0:128], lhsT=wF[:, 0:L], rhs=Sbf[:, h, :], start=True, stop=True),
                   R=[wFB, SbfB], W=[pswsB])
                op("dve", lambda e: e.tensor_tensor(out=vnew[0:L, :], in0=usb[0:L, :], in1=psws[0:L, 0:128], op=ALU.subtract),
                   R=[usbB, pswsB], W=[vnewB])
                psqs, psqsB = psum()
                op("pe", lambda e: e.matmul(psqs[0:L, 0:128], lhsT=qF[:, h, cs], rhs=Sbf[:, h, :], start=True, stop=True),
                   R=[qFB, SbfB], W=[psqsB])
                psav, psavB = psum()
                op("pe", lambda e: e.matmul(psav[0:L, 0:128], lhsT=attnT[0:L, 0:L], rhs=vnew[0:L, :], start=True, stop=True),
                   R=[attnTB, vnewB], W=[psavB])
                op("act", lambda e: e.activation(out=qsb[0:L, :], in_=psqs[0:L, 0:128], func=AF.Copy, scale=tokB[0:L, h:h + 1]),
                   R=[psqsB, tokBB], W=[qsbB])
                op("dve", lambda e: e.tensor_tensor(out=oT[0:L, hs_], in0=qsb[0:L, :], in1=psav[0:L, 0:128], op=ALU.add),
                   R=[qsbB, psavB], W=[oTB])
                op("act", lambda e: e.activation(out=kd[0:L, :], in_=kT[0:L, hs_], func=AF.Copy, scale=tokB[0:L, 32 + h:33 + h]),
                   R=[kTB, tokBB], W=[kdB])
                psup, psupB = psum()
                op("pe", lambda e: e.matmul(psup[:, 0:128], lhsT=kd[0:L, :], rhs=vnew[0:L, :], start=True, stop=True),
                   R=[kdB, vnewB], W=[psupB])
                op("dve", lambda e: e.scalar_tensor_tensor(out=S_[:, h, :], in0=S_[:, h, :], scalar=dbc[:, h:h + 1],
                                                           in1=psup[:, 0:128], op0=ALU.mult, op1=ALU.add),
                   R=[S_B, dbcB, psupB], W=[S_B])
            for h in range(8):
                op("act", lambda e, h=h: e.activation(out=sqj[0:L, :], in_=oT[0:L, h * 128:(h + 1) * 128], func=AF.Square,
                                                      accum_out=ssq[0:L, h:h + 1]), R=[oTB], W=[ssqB, sqjB])
            op("act", lambda e: e.activation(out=ssq[0:L, 8:16], in_=ssq[0:L, 0:8], func=AF.Ln, scale=1.0 / 128.0,
                                             bias=epst[0:L, 0:1]), R=[ssqB, epsB], W=[ssqB])
            op("act", lambda e: e.activation(out=ssq[0:L, 8:16], in_=ssq[0:L, 8:16], func=AF.Exp, scale=-0.5), R=[ssqB], W=[ssqB])
            for h in range(8):
                op("dve", lambda e, h=h: e.scalar_tensor_tensor(out=oT[0:L, h * 128:(h + 1) * 128],
                                                                in0=oT[0:L, h * 128:(h + 1) * 128],
                                                                scalar=ssq[0:L, 8 + h:9 + h], in1=bcn[0:L, :],
                                                                op0=ALU.mult, op1=ALU.mult), R=[oTB, ssqB, bcnB], W=[oTB])
            op("dve", lambda e: e.tensor_tensor(out=yT[0:L, :], in0=oT[0:L, :], in1=zT[0:L, :], op=ALU.mult),
               R=[oTB, zTB], W=[yTB])
            to_fm(yT, yTB, L, cs, yF[1])

        for (kind, c0, L, sidx) in chunks:
            if kind == "s":
                S_, S_B = SS, SSB
                dma("sp", S_[:, :, :], st_gs[l, sidx].rearrange("h k v -> k h v"), W=[S_B])
            else:
                S_, S_B = St, StB
            chunk(c0, L, S_, S_B)
            if kind == "s":
                dma("sp", o_sgs[l, sidx].rearrange("h k v -> k h v"), S_[:, :, :], R=[S_B], semb=S_B)


    onesblk = cst[:, CO["onesblk"]:CO["onesblk"] + 128]
    i2c = cst[:, CO["i2"]:CO["i2"] + 64]
    bdT_incl = cst[:, CO["bdT_incl"]:CO["bdT_incl"] + 128]
    bdT_strict = cst[:, CO["bdT_strict"]:CO["bdT_strict"] + 128]
    bd_strict = cst[:, CO["bd_strict"]:CO["bd_strict"] + 128]
    RZ = []
    rprev = []
    for l in range(DEPTH):
        t, b = sb("RZ%d" % l, [128, 8, 64])
        op("dve", lambda e, t=t: e.memset(t[:, :, :], 0.0), W=[b])
        RZ.append((t, b))
        t, b = sb("rprev%d" % l, [128, 26])
        op("dve", lambda e, t=t: e.memset(t[:, :], 0.0), W=[b])
        rprev.append((t, b))
    rpar = {}
    for nm in ("rwkv_w0", "rwkv_a0", "rwkv_k_k", "rwkv_k_a", "rwkv_ln_w", "rwkv_ln_b"):
        rpar[nm] = load_fm_vec("p_" + nm, W[nm], 8)
    rpar["rwkv_r_k"] = load_fm_vec("p_rk", W["rwkv_r_k"].rearrange("l h d -> l (h d)"), 8)
    rpar["rwkv_mu"] = load_fm_vec("p_mu", W["rwkv_mu"], 26)
    omka, omkaB = sb("omka", [128, DEPTH, 8])
    op("dve", lambda e: e.tensor_scalar(out=omka[:, :, :], in0=rpar["rwkv_k_a"][0][:, :, :], scalar1=-1.0, scalar2=1.0,
                                        op0=ALU.mult, op1=ALU.add), R=[rpar["rwkv_k_a"][1]], W=[omkaB])

    def rwkv_phase(l, tile):
        r0, N, chunks = tile
        Zt, ZtB = RZ[l]
        rp, rpB = rprev[l]
        w_in = W["w_in"][l]
        has_s = any(c[0] == "s" for c in chunks)
        pc0 = NS if has_s else 0
        Np = N - pc0
        P = lambda nm: rpar[nm][0]
        PB = lambda nm: rpar[nm][1]
        mu, muB = rpar["rwkv_mu"]
        xcs = [sb("r_xc%d" % i, [128, NMAX + 1]) for i in range(2)]
        dtmp, dtmpB = sb("r_dtmp", [128, NMAX])
        xwxa, xwxaB = sb("r_xwxa", [128, NMAX])
        sxg, sxgB = sb("r_sxg", [128, NMAX])
        w2p, w2pB = sb("r_w2p", [128, 1024])
        a2p, a2pB = sb("r_a2p", [128, 1024])
        g2t, g2tB = sb("r_g2", [128, 1024])
        rows = {}
        for nm in ("r", "kx", "k2", "v", "av", "bv", "ld", "g", "asig", "yraw", "t1", "t2"):
            rows[nm] = sb("r_" + nm, [128, NMAX])
        if has_s:
            xsAll, xsAllB = sb("r_xsAll", [128, 26, NS])
            prevS, prevSB = sb("r_prevS", [128, 26, NS])
            st16, st16B = sb("r_st16", [NS, 3328])
            ZS, ZSB = sb("r_ZS", [128, 64])
            ZSt, ZStB = sb("r_ZSt", [64, 128])
        opn = ("rt", "kt", "bt", "at", "kh", "bh", "Vbd", "Q0", "Q1", "P0", "P1", "TT", "MakT", "RKT", "RBT", "K2", "B2", "Y2")
        OPT = {}
        for nm in opn:
            OPT[nm] = sb("r_o_" + nm, [128, 128])
            op("dve", lambda e, t=OPT[nm][0]: e.memset(t[:, :], 0.0), W=[OPT[nm][1]])
        lw, lwB = sb("r_lw", [128, 64])
        eW, eWB = sb("r_eW", [128, 64])
        eWi, eWiB = sb("r_eWi", [128, 64])
        eWp, eWpB = sb("r_eWp", [128, 64])
        eWl, eWlB = sb("r_eWl", [128, 64])
        wls, wlsB = sb("r_wls", [128, 1])
        Vtok, VtokB = sb("r_Vtok", [128, 64])
        RHS, RHSB = sb("r_RHS", [128, 64])
        SAt, SAtB = sb("r_SA", [128, 64])
        for t_, b_ in ((Vtok, VtokB), (RHS, RHSB), (SAt, SAtB)):
            op("dve", lambda e, t_=t_: e.memset(t_[:, :], 0.0), W=[b_])
        op("dve", lambda e: e.memset(w2p[:, :], 0.0), W=[w2pB])
        op("dve", lambda e: e.memset(a2p[:, :], 0.0), W=[a2pB])
        dma("sp", w2p[0:64, :], W["rwkv_w2"][l], W=[w2pB])
        dma("sp", a2p[64:128, :], W["rwkv_a2"][l], W=[a2pB])
        dma("sp", g2t[:, :], W["rwkv_g2"][l], W=[g2tB])
        if has_s:
            dma("sp", st16[:, :], st_rshift[l], W=[st16B])
            for j in range(26):
                ps, psB = psum()
                op("pe", lambda e, j=j, ps=ps: e.transpose(out=ps[:, 0:NS], in_=st16[0:NS, j * 128:(j + 1) * 128],
                                                           identity=ident[0:NS, 0:NS]), R=[st16B, cstB], W=[psB])
                op("dve", lambda e, j=j, ps=ps: e.tensor_copy(out=prevS[:, j, :], in_=ps[:, 0:NS]), R=[psB], W=[prevSB])

        def shifted(j, ps, psB, dst, dstB):
            xc, xcB = xcs[j % 2]
            op("dve", lambda e: e.tensor_copy(out=xc[:, 0:1], in_=rp[:, j:j + 1]), R=[rpB], W=[xcB])
            op("act", lambda e: e.activation(out=xc[:, 1:1 + Np], in_=ps[:, pc0:N], func=AF.Copy), R=[psB], W=[xcB])
            op("dve", lambda e: e.tensor_tensor(out=dtmp[:, 0:Np], in0=xc[:, 0:Np], in1=xc[:, 1:1 + Np], op=ALU.subtract),
               R=[xcB], W=[dtmpB])
            op("dve", lambda e: e.scalar_tensor_tensor(out=dst[:, pc0:N], in0=dtmp[:, 0:Np], scalar=mu[:, l, j:j + 1],
                                                       in1=xc[:, 1:1 + Np], op0=ALU.mult, op1=ALU.add),
               R=[dtmpB, muB, xcB], W=[dstB])
            op("dve", lambda e: e.tensor_copy(out=rp[:, j:j + 1], in_=xc[:, Np:Np + 1]), R=[xcB], W=[rpB])
            if has_s:
                op("act", lambda e: e.activation(out=xsAll[:, j, :], in_=ps[:, 0:NS], func=AF.Copy), R=[psB], W=[xsAllB])
                op("dve", lambda e: e.tensor_tensor(out=dtmp[:, 0:NS], in0=prevS[:, j, :], in1=xsAll[:, j, :], op=ALU.subtract),
                   R=[prevSB, xsAllB], W=[dtmpB])
                op("dve", lambda e: e.scalar_tensor_tensor(out=dst[:, 0:NS], in0=dtmp[:, 0:NS], scalar=mu[:, l, j:j + 1],
                                                           in1=xsAll[:, j, :], op0=ALU.mult, op1=ALU.add),
                   R=[dtmpB, muB, xsAllB], W=[dstB])

        def cons_l24(j, ps, psB, m):
            shifted(24, ps, psB, xwxa, xwxaB)
            op("act", lambda e: e.activation(out=xwxa[0:64, 0:N], in_=xwxa[0:64, 0:N], func=AF.Tanh), R=[xwxaB], W=[xwxaB])
        proj_fm(w_in, C0 + 3072, 128, N, cons_l24)

        def cons_l25(j, ps, psB, m):
            shifted(25, ps, psB, sxg, sxgB)
            op("act", lambda e: e.activation(out=sxg[:, 0:N], in_=sxg[:, 0:N], func=AF.Sigmoid), R=[sxgB], W=[sxgB])
        proj_fm(w_in, C0 + 3200, 128, N, cons_l25)

        R_ = lambda nm: rows[nm][0]
        RB = lambda nm: rows[nm][1]
        lastL = [0]

        def chunk(jb, c0, L, Z_, Z_B, zsl):
            cs = slice(c0, c0 + L)
            nlev = {64: 5, 16: 3, 1: 0}[L]
            O = lambda nm: OPT[nm][0]
            OB = lambda nm: OPT[nm][1]
            ld_ = R_("ld")
            if lastL[0] != L:
                for nm_ in ("rt", "kt", "bt", "at", "kh", "bh", "Vbd"):
                    op("dve", lambda e, nm_=nm_: e.memset(O(nm_)[:, :], 0.0), W=[OB(nm_)])
                lastL[0] = L
            op("dve", lambda e: e.tensor_tensor_scan(out=lw[:, 0:L], data0=ld_[:, cs], data1=zcol[:, 0:L], initial=0.0,
                                                     op0=ALU.add, op1=ALU.add), R=[RB("ld"), zcolB], W=[lwB])
            op("act", lambda e: e.activation(out=eW[:, 0:L], in_=lw[:, 0:L], func=AF.Exp), R=[lwB], W=[eWB])
            op("act", lambda e: e.activation(out=eWi[:, 0:L], in_=lw[:, 0:L], func=AF.Exp, scale=-1.0), R=[lwB], W=[eWiB])
            op("dve", lambda e: e.tensor_tensor(out=eWp[:, 0:L], in0=lw[:, 0:L], in1=ld_[:, cs], op=ALU.subtract),
               R=[lwB, RB("ld")], W=[eWpB])
            op("act", lambda e: e.activation(out=eWp[:, 0:L], in_=eWp[:, 0:L], func=AF.Exp), R=[eWpB], W=[eWpB])
            op("act", lambda e: e.activation(out=eWl[:, 0:L], in_=lw[:, 0:L], func=AF.Exp, scale=-1.0, bias=lw[:, L - 1:L]),
               R=[lwB], W=[eWlB])
            op("act", lambda e: e.activation(out=wls[:, 0:1], in_=lw[:, L - 1:L], func=AF.Exp), R=[lwB], W=[wlsB])
            for (dst, srcn, fac, facB) in (("rt", "r", eW, eWB), ("kt", "k2", eWi, eWiB), ("bt", "bv", eWi, eWiB),
                                           ("at", "av", eWp, eWpB), ("kh", "k2", eWl, eWlB), ("bh", "bv", eWl, eWlB)):
                for hb in range(2):
                    pr = slice(hb * 64, hb * 64 + 64)
                    op("dve", lambda e, dst=dst, srcn=srcn, fac=fac, pr=pr, hb=hb: e.tensor_tensor(
                        out=O(dst)[pr, hb * 64:hb * 64 + L], in0=R_(srcn)[pr, cs], in1=fac[pr, 0:L], op=ALU.mult),
                       R=[RB(srcn), facB], W=[OB(dst)])
            for hb in range(2):
                pr = slice(hb * 64, hb * 64 + 64)
                op("act", lambda e, pr=pr, hb=hb: e.activation(out=O("Vbd")[pr, hb * 64:hb * 64 + L], in_=R_("v")[pr, cs],
                                                               func=AF.Copy), R=[RB("v")], W=[OB("Vbd")])
            ps, psB = psum()
            op("pe", lambda e, ps=ps: e.matmul(ps[:, 0:64], lhsT=O("Vbd")[:, :], rhs=i2c, start=True, stop=True),
               R=[OB("Vbd"), cstB], W=[psB])
            op("act", lambda e, ps=ps: e.activation(out=Vtok[:, :], in_=ps[:, 0:64], func=AF.Copy), R=[psB], W=[VtokB])

            def gram(lhs, rhs, mask, dstn):
                ps, psB = psum()
                op("pe", lambda e: e.matmul(ps[:, 0:128], lhsT=O(lhs)[:, :], rhs=O(rhs)[:, :], start=True, stop=True),
                   R=[OB(lhs), OB(rhs)], W=[psB])
                op("dve", lambda e: e.tensor_tensor(out=O(dstn)[:, :], in0=ps[:, 0:128], in1=mask, op=ALU.mult),
                   R=[psB, cstB], W=[OB(dstn)])
            gram("bt", "at", bdT_strict, "Q0")
            gram("at", "bt", bd_strict, "P0")
            gram("kt", "at", bdT_strict, "MakT")
            gram("kt", "rt", bdT_incl, "RKT")
            gram("bt", "rt", bdT_incl, "RBT")
            op("dve", lambda e: e.tensor_tensor(out=O("TT")[:, :], in0=O("Q0")[:, :], in1=ident, op=ALU.add),
               R=[OB("Q0"), cstB], W=[OB("TT")])
            cp, cq, np_, nq = "P0", "Q0", "P1", "Q1"
            for lev in range(nlev):
                psp, pspB = psum()
                op("pe", lambda e, cp=cp, cq=cq, psp=psp: e.matmul(psp[:, 0:128], lhsT=O(cq)[:, :], rhs=O(cp)[:, :],
                                                                  start=True, stop=True), R=[OB(cp), OB(cq)], W=[pspB])
                psq, psqB = psum()
                op("pe", lambda e, cp=cp, cq=cq, psq=psq: e.matmul(psq[:, 0:128], lhsT=O(cp)[:, :], rhs=O(cq)[:, :],
                                                                  start=True, stop=True), R=[OB(cp), OB(cq)], W=[psqB])
                op("act", lambda e, np_=np_, psp=psp: e.activation(out=O(np_)[:, :], in_=psp[:, 0:128], func=AF.Copy),
                   R=[pspB], W=[OB(np_)])
                op("dve", lambda e, nq=nq, psq=psq: e.tensor_copy(out=O(nq)[:, :], in_=psq[:, 0:128]), R=[psqB], W=[OB(nq)])
                pst, pstB = psum()
                op("pe", lambda e, np_=np_, pst=pst: e.matmul(pst[:, 0:128], lhsT=O(np_)[:, :], rhs=O("TT")[:, :],
                                                              start=True, stop=True), R=[OB(np_), OB("TT")], W=[pstB])
                op("dve", lambda e, pst=pst: e.tensor_tensor(out=O("TT")[:, :], in0=O("TT")[:, :], in1=pst[:, 0:128],
                                                             op=ALU.add), R=[OB("TT"), pstB], W=[OB("TT")])
                cp, cq, np_, nq = np_, nq, cp, cq
            ps, psB = psum()
            op("pe", lambda e, ps=ps: e.matmul(ps[:, 0:64], lhsT=O("at")[:, :], rhs=zsl, start=True, stop=False),
               R=[OB("at"), Z_B], W=[psB])
            op("pe", lambda e, ps=ps: e.matmul(ps[:, 0:64], lhsT=O("MakT")[:, :], rhs=Vtok[:, :], start=False, stop=True),
               R=[OB("MakT"), VtokB], W=[psB])
            op("act", lambda e, ps=ps: e.activation(out=RHS[:, :], in_=ps[:, 0:64], func=AF.Copy), R=[psB], W=[RHSB])
            ps, psB = psum()
            op("pe", lambda e, ps=ps: e.matmul(ps[:, 0:64], lhsT=O("TT")[:, :], rhs=RHS[:, :], start=True, stop=True),
               R=[OB("TT"), RHSB], W=[psB])
            op("dve", lambda e, ps=ps: e.tensor_copy(out=SAt[:, :], in_=ps[:, 0:64]), R=[psB], W=[SAtB])
            psy, psyB = psum()
            op("pe", lambda e: e.matmul(psy[:, 0:64], lhsT=O("rt")[:, :], rhs=zsl, start=True, stop=False),
               R=[OB("rt"), Z_B], W=[psyB])
            op("pe", lambda e: e.matmul(psy[:, 0:64], lhsT=O("RKT")[:, :], rhs=Vtok[:, :], start=False, stop=False),
               R=[OB("RKT"), VtokB], W=[psyB])
            op("pe", lambda e: e.matmul(psy[:, 0:64], lhsT=O("RBT")[:, :], rhs=SAt[:, :], start=False, stop=True),
               R=[OB("RBT"), SAtB], W=[psyB])
            op("act", lambda e: e.activation(out=O("Y2")[0:64, 0:64], in_=psy[0:64, 0:64], func=AF.Copy), R=[psyB], W=[OB("Y2")])
            op("dve", lambda e: e.tensor_copy(out=O("Y2")[64:128, 64:128], in_=psy[64:128, 0:64]), R=[psyB], W=[OB("Y2")])
            ps, psB = psum()
            op("pe", lambda e, ps=ps: e.matmul(ps[:, 0:64], lhsT=O("Y2")[:, :], rhs=i2c, start=True, stop=True),
               R=[OB("Y2"), cstB], W=[psB])
            op("act", lambda e, ps=ps: e.activation(out=R_("yraw")[:, cs], in_=ps[:, 0:L], func=AF.Copy), R=[psB], W=[RB("yraw")])
            for (srcn, dstn) in (("kh", "K2"), ("bh", "B2")):
                ps, psB = psum()
                op("pe", lambda e, ps=ps, srcn=srcn: e.transpose(out=ps[:, 0:128], in_=O(srcn)[:, :], identity=ident),
                   R=[OB(srcn), cstB], W=[psB])
                op("act" if srcn == "kh" else "dve",
                   (lambda e, ps=ps, dstn=dstn: e.activation(out=O(dstn)[:, :], in_=ps[:, 0:128], func=AF.Copy)) if srcn == "kh" else
                   (lambda e, ps=ps, dstn=dstn: e.tensor_copy(out=O(dstn)[:, :], in_=ps[:, 0:128])), R=[psB], W=[OB(dstn)])
            psz, pszB = psum()
            op("pe", lambda e: e.matmul(psz[:, 0:64], lhsT=O("K2")[:, :], rhs=Vtok[:, :], start=True, stop=False),
               R=[OB("K2"), VtokB], W=[pszB])
            op("pe", lambda e: e.matmul(psz[:, 0:64], lhsT=O("B2")[:, :], rhs=SAt[:, :], start=False, stop=True),
               R=[OB("B2"), SAtB], W=[pszB])
            op("dve", lambda e: e.scalar_tensor_tensor(out=zsl, in0=zsl, scalar=wls[:, 0:1], in1=psz[:, 0:64],
                                                       op0=ALU.mult, op1=ALU.add), R=[Z_B, wlsB, pszB], W=[Z_B])

        for jb in range(8):
            def mk(dstn):
                def f(j, ps, psB, m):
                    return None
                return f
            for (blk, dstn) in ((jb, "r"), (8 + jb, "kx"), (16 + jb, "v")):
                def cons(j, ps, psB, m, blk=blk, dstn=dstn):
                    shifted(blk, ps, psB, R_(dstn), RB(dstn))
                proj_fm(w_in, C0 + blk * 128, 128, N, cons)
            bsl = slice(jb * 128, (jb + 1) * 128)
            ps, psB = psum()
            op("pe", lambda e, ps=ps: e.matmul(ps[:, 0:N], lhsT=w2p[:, bsl], rhs=xwxa[:, 0:N], start=True, stop=True),
               R=[w2pB, xwxaB], W=[psB])
            op("act", lambda e, ps=ps: e.activation(out=R_("ld")[:, 0:N], in_=ps[:, 0:N], func=AF.Sigmoid,
                                                    bias=P("rwkv_w0")[:, l, jb:jb + 1]), R=[psB, PB("rwkv_w0")], W=[RB("ld")])
            op("dve", lambda e: e.tensor_scalar(out=R_("ld")[:, 0:N], in0=R_("ld")[:, 0:N], scalar1=-float(np.exp(-0.5)),
                                                scalar2=None, op0=ALU.mult), R=[RB("ld")], W=[RB("ld")])
            ps, psB = psum()
            op("pe", lambda e, ps=ps: e.matmul(ps[:, 0:N], lhsT=a2p[:, bsl], rhs=xwxa[:, 0:N], start=True, stop=True),
               R=[a2pB, xwxaB], W=[psB])
            op("act", lambda e, ps=ps: e.activation(out=R_("asig")[:, 0:N], in_=ps[:, 0:N], func=AF.Sigmoid,
                                                    bias=P("rwkv_a0")[:, l, jb:jb + 1]), R=[psB, PB("rwkv_a0")], W=[RB("asig")])
            ps, psB = psum()
            op("pe", lambda e, ps=ps: e.matmul(ps[:, 0:N], lhsT=g2t[:, bsl], rhs=sxg[:, 0:N], start=True, stop=True),
               R=[g2tB, sxgB], W=[psB])
            op("act", lambda e, ps=ps: e.activation(out=R_("g")[:, 0:N], in_=ps[:, 0:N], func=AF.Copy), R=[psB], W=[RB("g")])
            op("dve", lambda e: e.tensor_scalar(out=R_("t1")[:, 0:N], in0=R_("kx")[:, 0:N], scalar1=P("rwkv_k_k")[:, l, jb:jb + 1],
                                                scalar2=None, op0=ALU.mult), R=[RB("kx"), PB("rwkv_k_k")], W=[RB("t1")])
            op("act", lambda e: e.activation(out=R_("t2")[:, 0:N], in_=R_("t1")[:, 0:N], func=AF.Square), R=[RB("t1")], W=[RB("t2")])
            ps, psB = psum()
            op("pe", lambda e, ps=ps: e.matmul(ps[:, 0:N], lhsT=onesblk, rhs=R_("t2")[:, 0:N], start=True, stop=True),
               R=[RB("t2"), cstB], W=[psB])
            op("act", lambda e, ps=ps: e.activation(out=R_("t2")[:, 0:N], in_=ps[:, 0:N], func=AF.Ln, bias=epst[:, 1:2]),
               R=[psB, epsB], W=[RB("t2")])
            op("act", lambda e: e.activation(out=R_("t2")[:, 0:N], in_=R_("t2")[:, 0:N], func=AF.Exp, scale=-0.5),
               R=[RB("t2")], W=[RB("t2")])
            op("dve", lambda e: e.scalar_tensor_tensor(out=R_("av")[:, 0:N], in0=R_("t1")[:, 0:N], scalar=-1.0, in1=R_("t2")[:, 0:N],
                                                       op0=ALU.mult, op1=ALU.mult), R=[RB("t1"), RB("t2")], W=[RB("av")])
            op("dve", lambda e: e.scalar_tensor_tensor(out=R_("bv")[:, 0:N], in0=R_("av")[:, 0:N], scalar=-1.0, in1=R_("asig")[:, 0:N],
                                                       op0=ALU.mult, op1=ALU.mult), R=[RB("av"), RB("asig")], W=[RB("bv")])
            op("dve", lambda e: e.tensor_scalar(out=R_("t1")[:, 0:N], in0=R_("asig")[:, 0:N], scalar1=P("rwkv_k_a")[:, l, jb:jb + 1],
                                                scalar2=omka[:, l, jb:jb + 1], op0=ALU.mult, op1=ALU.add),
               R=[RB("asig"), PB("rwkv_k_a"), omkaB], W=[RB("t1")])
            op("dve", lambda e: e.tensor_tensor(out=R_("k2")[:, 0:N], in0=R_("kx")[:, 0:N], in1=R_("t1")[:, 0:N], op=ALU.mult),
               R=[RB("kx"), RB("t1")], W=[RB("k2")])
            for (kind, c0, L, sidx) in chunks:
                if kind == "s":
                    dma("sp", ZSt[:, :].rearrange("v (h k) -> v h k", k=64),
                        st_rs[l, sidx, 2 * jb:2 * jb + 2].rearrange("h v k -> v h k"), W=[ZStB])
                    ps, psB = psum()
                    op("pe", lambda e, ps=ps: e.transpose(out=ps[:, 0:64], in_=ZSt[:, :], identity=ident[0:64, 0:64]),
                       R=[ZStB, cstB], W=[psB])
                    op("act", lambda e, ps=ps: e.activation(out=ZS[:, :], in_=ps[:, 0:64], func=AF.Copy), R=[psB], W=[ZSB])
                    chunk(jb, c0, L, ZS, ZSB, ZS[:, :])
                    ps, psB = psum()
                    op("pe", lambda e, ps=ps: e.transpose(out=ps[0:64, 0:128], in_=ZS[:, :], identity=ident), R=[ZSB, cstB], W=[psB])
                    op("act", lambda e, ps=ps: e.activation(out=ZSt[:, :], in_=ps[0:64, 0:128], func=AF.Copy), R=[psB], W=[ZStB])
                    dma("sp", o_srs[l, sidx, 2 * jb:2 * jb + 2].rearrange("h v k -> v h k"),
                        ZSt[:, :].rearrange("v (h k) -> v h k", k=64), R=[ZStB], semb=ZStB)
                else:
                    chunk(jb, c0, L, Zt, ZtB, Zt[:, jb, :])
            ps, psB = psum()
            op("pe", lambda e, ps=ps: e.matmul(ps[:, 0:N], lhsT=onesblk, rhs=R_("yraw")[:, 0:N], start=True, stop=True),
               R=[RB("yraw"), cstB], W=[psB])
            op("dve", lambda e, ps=ps: e.scalar_tensor_tensor(out=R_("t1")[:, 0:N], in0=ps[:, 0:N], scalar=-1.0 / 64.0,
                                                              in1=R_("yraw")[:, 0:N], op0=ALU.mult, op1=ALU.add),
               R=[psB, RB("yraw")], W=[RB("t1")])
            op("act", lambda e: e.activation(out=R_("t2")[:, 0:N], in_=R_("t1")[:, 0:N], func=AF.Square), R=[RB("t1")], W=[RB("t2")])
            ps, psB = psum()
            op("pe", lambda e, ps=ps: e.matmul(ps[:, 0:N], lhsT=onesblk, rhs=R_("t2")[:, 0:N], start=True, stop=True),
               R=[RB("t2"), cstB], W=[psB])
            op("act", lambda e, ps=ps: e.activation(out=R_("t2")[:, 0:N], in_=ps[:, 0:N], func=AF.Ln, scale=1.0 / 64.0,
                                                    bias=epst[:, 2:3]), R=[psB, epsB], W=[RB("t2")])
            op("act", lambda e: e.activation(out=R_("t2")[:, 0:N], in_=R_("t2")[:, 0:N], func=AF.Exp, scale=-0.5),
               R=[RB("t2")], W=[RB("t2")])
            op("dve", lambda e: e.tensor_tensor(out=R_("t1")[:, 0:N], in0=R_("t1")[:, 0:N], in1=R_("t2")[:, 0:N], op=ALU.mult),
               R=[RB("t1"), RB("t2")], W=[RB("t1")])
            op("dve", lambda e: e.tensor_scalar(out=R_("t1")[:, 0:N], in0=R_("t1")[:, 0:N], scalar1=P("rwkv_ln_w")[:, l, jb:jb + 1],
                                                scalar2=P("rwkv_ln_b")[:, l, jb:jb + 1], op0=ALU.mult, op1=ALU.add),
               R=[RB("t1"), PB("rwkv_ln_w"), PB("rwkv_ln_b")], W=[RB("t1")])
            op("dve", lambda e: e.scalar_tensor_tensor(out=R_("t2")[:, 0:N], in0=R_("r")[:, 0:N], scalar=P("rwkv_r_k")[:, l, jb:jb + 1],
                                                       in1=R_("k2")[:, 0:N], op0=ALU.mult, op1=ALU.mult),
               R=[RB("r"), RB("k2"), PB("rwkv_r_k")], W=[RB("t2")])
            ps, psB = psum()
            op("pe", lambda e, ps=ps: e.matmul(ps[:, 0:N], lhsT=onesblk, rhs=R_("t2")[:, 0:N], start=True, stop=True),
               R=[RB("t2"), cstB], W=[psB])
            op("dve", lambda e, ps=ps: e.tensor_tensor(out=R_("t2")[:, 0:N], in0=ps[:, 0:N], in1=R_("v")[:, 0:N], op=ALU.mult),
               R=[psB, RB("v")], W=[RB("t2")])
            op("dve", lambda e: e.tensor_tensor(out=R_("t1")[:, 0:N], in0=R_("t1")[:, 0:N], in1=R_("t2")[:, 0:N], op=ALU.add),
               R=[RB("t1"), RB("t2")], W=[RB("t1")])
            yct, ycB_ = yF[2]
            op("dve", lambda e: e.tensor_tensor(out=yct[:, jb, 0:N], in0=R_("t1")[:, 0:N], in1=R_("g")[:, 0:N], op=ALU.mult),
               R=[RB("t1"), RB("g")], W=[ycB_])
        if has_s:
            for q in range(7):
                ps, psB = psum()
                nb = min(4, 26 - q * 4)
                for kk in range(nb):
                    k = q * 4 + kk
                    op("pe", lambda e, k=k, kk=kk, ps=ps: e.transpose(out=ps[0:NS, kk * 128:(kk + 1) * 128], in_=xsAll[:, k, :],
                                                                     identity=ident), R=[xsAllB, cstB], W=[psB])
                op("dve", lambda e, q=q, ps=ps, nb=nb: e.tensor_copy(out=st16[0:NS, q * 512:q * 512 + nb * 128],
                                                                    in_=ps[0:NS, 0:nb * 128]), R=[psB], W=[st16B])
            dma("sp", o_srshift[l], st16[0:NS, :], R=[st16B], semb=st16B)

    def zero_y(which, N):
        dt, dB = yF[which]
        op("dve", lambda e: e.memset(dt[:, :, 0:N], 0.0), W=[dB])

    def merge_phase(l, tile):
        r0, N, chunks = tile
        gt = [sb("gate%d" % i, [128, NMAX]) for i in range(3)]
        mrg, mrgB = sb("mrgF", [128, KC, NMAX], BF16)
        mtmp, mtmpB = sb("mtmp", [128, NMAX])
        w_in = W["w_in"][l]
        brw = [W["w_branch_a"][l], W["w_branch_b"][l], W["w_branch_c"][l]]
        for j in range(KC):
            for g in range(3):
                wv, wB = load_w(w_in, G0 + g * D + j * 128, 128, KC)
                ps, psB = psum()
                for k in range(KC):
                    op("pe", lambda e, k=k: e.matmul(ps[:, 0:N], lhsT=wv[:, k, :], rhs=xnF[:, k, 0:N], start=(k == 0),
                                                     stop=(k == KC - 1)), R=[wB, xnFB], W=[psB])
                g_t, g_B = gt[g]
                op("act", lambda e: e.activation(out=g_t[:, 0:N], in_=ps[:, 0:N], func=AF.Sigmoid), R=[psB], W=[g_B])
            for g in range(3):
                wv, wB = load_w(brw[g], j * 128, 128, 8)
                ps, psB = psum()
                yt, yB = yF[g]
                for k in range(8):
                    op("pe", lambda e, k=k: e.matmul(ps[:, 0:N], lhsT=wv[:, k, :], rhs=yt[:, k, 0:N], start=(k == 0),
                                                     stop=(k == 7)), R=[wB, yB], W=[psB])
                g_t, g_B = gt[g]
                if g == 0:
                    op("dve", lambda e: e.tensor_tensor(out=mtmp[:, 0:N], in0=ps[:, 0:N], in1=g_t[:, 0:N], op=ALU.mult),
                       R=[psB, g_B], W=[mtmpB])
                else:
                    op("dve", lambda e: e.tensor_tensor(out=g_t[:, 0:N], in0=ps[:, 0:N], in1=g_t[:, 0:N], op=ALU.mult),
                       R=[psB, g_B], W=[g_B])
                    if g == 1:
                        op("dve", lambda e: e.tensor_tensor(out=mtmp[:, 0:N], in0=mtmp[:, 0:N], in1=g_t[:, 0:N], op=ALU.add),
                           R=[mtmpB, g_B], W=[mtmpB])
                    else:
                        op("dve", lambda e: e.tensor_tensor(out=mrg[:, j, 0:N], in0=mtmp[:, 0:N], in1=g_t[:, 0:N], op=ALU.add),
                           R=[mtmpB, g_B], W=[mrgB])

        def cons_res(j, ps, psB, m):
            op("dve", lambda e: e.tensor_tensor(out=xF[:, j, 0:N], in0=xF[:, j, 0:N], in1=ps[:, 0:N], op=ALU.add),
               R=[xFB, psB], W=[xFB])
        proj_fm(W["w_out"][l], 0, D, N, cons_res, rhs_fn=lambda k: mrg[:, k, 0:N], rhsB=mrgB)

    def mlp_phase(l, tile):
        r0, N, chunks = tile
        mtmp, mtmpB = sb("mtmp2", [128, NMAX])
        uF, uFB = sb("uF", [128, 64, NMAX], BF16)
        rmsnorm(N, lambda k: nw_mlp[:, l, k:k + 1], nw_mlpB)

        def cons_up(j, ps, psB, m):
            op("act", lambda e: e.activation(out=mtmp[:, 0:N], in_=ps[:, 0:N], func=AF.Relu), R=[psB], W=[mtmpB])
            op("dve", lambda e: e.tensor_tensor(out=uF[:, j, 0:N], in0=mtmp[:, 0:N], in1=mtmp[:, 0:N], op=ALU.mult),
               R=[mtmpB], W=[uFB])
        proj_fm(W["w_up"][l], 0, 4 * D, N, cons_up)
        wd = W["w_down"][l]
        for j in range(KC):
            ps, psB = psum()
            for q in range(4):
                wv, wB = load_w(wd[q * 2048:(q + 1) * 2048, :], j * 128, 128, 16)
                for k in range(16):
                    kk = q * 16 + k
                    op("pe", lambda e, k=k, kk=kk, wv=wv: e.matmul(ps[:, 0:N], lhsT=wv[:, k, :], rhs=uF[:, kk, 0:N],
                                                                  start=(kk == 0), stop=(kk == 63)), R=[wB, uFB], W=[psB])
            op("dve", lambda e, j=j: e.tensor_tensor(out=xF[:, j, 0:N], in0=xF[:, j, 0:N], in1=ps[:, 0:N], op=ALU.add),
               R=[xFB, psB], W=[xFB])

    tiles = tile_plan()
    for ti, tile in enumerate(tiles):
        r0, N, chunks = tile
        with Arena():
            stage_tok, stage_tokB = sb("stage_in", [128, D])
            rr = 0
            while rr < N:
                m = min(128, N - rr)
                dma("sp", stage_tok[0:m, :], xin[r0 + rr:r0 + rr + m, :], W=[stage_tokB])
                for kq in range(4):
                    ps, psB = psum()
                    for kk in range(4):
                        k = kq * 4 + kk
                        op("pe", lambda e, k=k, kk=kk, ps=ps: e.transpose(out=ps[:, kk * 128:kk * 128 + m],
                                                                         in_=stage_tok[0:m, k * 128:(k + 1) * 128],
                                                                         identity=ident[0:m, 0:m]), R=[stage_tokB, cstB], W=[psB])
                    for kk in range(4):
                        k = kq * 4 + kk
                        if kk % 2 == 0:
                            op("act", lambda e, k=k, kk=kk, ps=ps: e.activation(out=xF[:, k, rr:rr + m],
                                                                               in_=ps[:, kk * 128:kk * 128 + m], func=AF.Copy),
                               R=[psB], W=[xFB])
                        else:
                            op("dve", lambda e, k=k, kk=kk, ps=ps: e.tensor_copy(out=xF[:, k, rr:rr + m],
                                                                                in_=ps[:, kk * 128:kk * 128 + m]),
                               R=[psB], W=[xFB])
                rr += m
        for l in range(DEPTH):
            rmsnorm(N, lambda k: nw_mix[:, l, k:k + 1], nw_mixB)
            with Arena():
                mlstm_phase(l, tile)
            with Arena():
                gdn_phase(l, tile)
            with Arena():
                rwkv_phase(l, tile)
            with Arena():
                merge_phase(l, tile)
            with Arena():
                mlp_phase(l, tile)
        rmsnorm(N, lambda k: nw_fin[:, k:k + 1], nw_finB, inplace=True)
        with Arena():
            stage_tok, stage_tokB = sb("stage_out", [128, D])
            rr = 0
            while rr < N:
                m = min(128, N - rr)
                for kq in range(4):
                    ps, psB = psum()
                    for kk in range(4):
                        k = kq * 4 + kk
                        op("pe", lambda e, k=k, kk=kk, ps=ps: e.transpose(out=ps[0:m, kk * 128:(kk + 1) * 128],
                                                                         in_=xF[:, k, rr:rr + m], identity=ident),
                           R=[xFB, cstB], W=[psB])
                    if kq % 2 == 0:
                        op("act", lambda e, kq=kq, ps=ps: e.activation(out=stage_tok[0:m, kq * 512:(kq + 1) * 512],
                                                                      in_=ps[0:m, 0:512], func=AF.Copy), R=[psB], W=[stage_tokB])
                    else:
                        op("dve", lambda e, kq=kq, ps=ps: e.tensor_copy(out=stage_tok[0:m, kq * 512:(kq + 1) * 512],
                                                                       in_=ps[0:m, 0:512]), R=[psB], W=[stage_tokB])
                dma("sp", y_out[r0 + rr:r0 + rr + m, :], stage_tok[0:m, :], R=[stage_tokB], semb=stage_tokB)
                rr += m

    for l in range(DEPTH):
        Ct, CtB = Cext[l]
        for h in range(4):
            dma("sp", o_pc[l, h].rearrange("(kc p) v -> p kc v", p=128), Ct[:, h, :, 0:256], R=[CtB], semb=CtB)
        dma("sp", o_pn[l].rearrange("h (kc p o) -> p h kc o", p=128, o=1), Ct[:, :, :, 256:257], R=[CtB], semb=CtB, slow=True)
        outbufs.append(CtB)
        mP, mPB = mprev[l]
        dma("sp", o_pm[l:l + 1, :].rearrange("o h -> h o"), mP[:, 0:1], R=[mPB], semb=mPB, slow=True)
        outbufs.append(mPB)

    for l in range(DEPTH):
        St, StB = GS[l]
        dma("sp", o_pgs[l].rearrange("h k v -> k h v"), St[:, :, :], R=[StB], semb=StB)
        outbufs.append(StB)
        hs, hsB = ghist[l]
        for t in range(3):
            dma("sp", o_pgconv[l][t].rearrange("(k p) -> p k", p=128), hs[:, :, t], R=[hsB], semb=hsB, slow=True)
        outbufs.append(hsB)
    with Arena():
        zt2, zt2B = sb("zt_out", [64, 8, 128])
        for l in range(DEPTH):
            Zt, ZtB = RZ[l]
            for jb in range(8):
                ps, psB = psum()
                op("pe", lambda e, jb=jb, ps=ps: e.transpose(out=ps[0:64, 0:128], in_=Zt[:, jb, :], identity=ident),
                   R=[ZtB, cstB], W=[psB])
                op("act", lambda e, jb=jb, ps=ps: e.activation(out=zt2[:, jb, :], in_=ps[0:64, 0:128], func=AF.Copy),
                   R=[psB], W=[zt2B])
            dma("sp", o_prs[l].rearrange("h v k -> v h k"), zt2[:, :, :].rearrange("v j (h k) -> v (j h) k", k=64),
                R=[zt2B], semb=zt2B)
            rp, rpB = rprev[l]
            dma("sp", o_prshift[l].rearrange("(k p) -> p k", p=128), rp[:, :], R=[rpB], semb=rpB, slow=True)
            outbufs.append(rpB)
        kb.finish([zt2B])
    kb.finish(outbufs)
    print("[kernel] instructions:", kb.n_inst, "dma sems:", kb.nd, "sbuf left:", nc.sbuf_bytes_remaining)


def _consts():
    cols = {}
    parts = []
    off = [0]

    def add(name, arr):
        a = np.zeros((128, arr.shape[1]), np.float32)
        a[:arr.shape[0]] = arr
        cols[name] = off[0]
        off[0] += arr.shape[1]
        parts.append(a)
    add("ident", np.eye(128, dtype=np.float32))
    add("ones", np.ones((128, 128), np.float32))
    ob = np.zeros((128, 128), np.float32)
    ob[:64, :64] = 1
    ob[64:, 64:] = 1
    add("onesblk", ob)
    s = np.arange(64)[:, None]
    l_ = np.arange(64)[None, :]
    NEG = -30000.0
    add("nmT_incl", np.where(s <= l_, 0.0, NEG).astype(np.float32))
    add("nmT_strict", np.where(s < l_, 0.0, NEG).astype(np.float32))
    add("nm_strict", np.where(l_ < s, 0.0, NEG).astype(np.float32))
    add("mT_incl", (s <= l_).astype(np.float32))
    add("mT_strict", (s < l_).astype(np.float32))
    add("m_strict", (l_ < s).astype(np.float32))
    sel = np.zeros((8, 8, 64), np.float32)
    for h in range(8):
        sel[h, h, :] = 1
    add("selh", sel.reshape(8, 512))
    sl = np.zeros((64, 3, 128), np.float32)
    sl[63, 0, :] = 1
    sl[15, 1, :] = 1
    sl[0, 2, :] = 1
    add("sellast", sl.reshape(64, 384))
    i2 = np.concatenate([np.eye(64, dtype=np.float32)] * 2, axis=0)
    add("i2", i2)
    t2 = lambda m: np.tile(m, (2, 2)).astype(np.float32)
    add("bdT_incl", t2(s <= l_))
    add("bdT_strict", t2(s < l_))
    add("bd_strict", t2(l_ < s))
    return np.concatenate(parts, axis=1), cols


CONSTS, CO = _consts()
CONST_COLS = CONSTS.shape[1]

PARAM_SHAPES = {
    "norm_mix_w": (2, 2048), "w_in": (2, 2048, 17688), "mlstm_b_i": (2, 4), "mlstm_b_f": (2, 4),
    "mlstm_norm_w": (2, 1024), "gdn_conv_w": (2, 4, 3072), "gdn_a_log": (2, 8), "gdn_dt_bias": (2, 8),
    "gdn_norm_w": (2, 128), "rwkv_mu": (2, 3328), "rwkv_w0": (2, 1024), "rwkv_w2": (2, 64, 1024),
    "rwkv_a0": (2, 1024), "rwkv_a2": (2, 64, 1024), "rwkv_g2": (2, 128, 1024), "rwkv_k_k": (2, 1024),
    "rwkv_k_a": (2, 1024), "rwkv_r_k": (2, 16, 64), "rwkv_ln_w": (2, 1024), "rwkv_ln_b": (2, 1024),
    "w_branch_a": (2, 1024, 2048), "w_branch_b": (2, 1024, 2048), "w_branch_c": (2, 1024, 2048),
    "w_out": (2, 2048, 2048), "norm_mlp_w": (2, 2048), "w_up": (2, 2048, 8192), "w_down": (2, 8192, 2048),
    "final_norm_w": (2048,),
}

_NC = [None]


def kernel(**inp):
    f = lambda a: np.ascontiguousarray(np.asarray(a, dtype=np.float32))
    if _NC[0] is None:
        _NC[0] = build()
    nc = _NC[0]
    params = {k: f(inp[k]) for k in PARAM_SHAPES}
    in_maps = []
    for c in range(8):
        sl = slice(c * NS, (c + 1) * NS)
        m = dict(params)
        m["xin"] = np.ascontiguousarray(np.concatenate(
            [f(inp["x_sample"])[sl, 0], f(inp["meta_tokens"]), f(inp["x_prompt"])[c % 4]], axis=0))
        m["consts"] = CONSTS
        m["st_c"] = f(inp["state_mlstm_c"])[:, sl]
        m["st_n"] = f(inp["state_mlstm_n"])[:, sl]
        m["st_m"] = f(inp["state_mlstm_m"])[:, sl]
        m["st_gs"] = f(inp["state_gdn_s"])[:, sl]
        m["st_gconv"] = f(inp["state_gdn_conv"])[:, sl]
        m["st_rs"] = f(inp["state_rwkv_s"])[:, sl]
        m["st_rshift"] = f(inp["state_rwkv_shift"])[:, sl]
        m = {k: np.ascontiguousarray(v) for k, v in m.items()}
        in_maps.append(m)
    res = run_bass_kernel_spmd(nc, in_maps, core_ids=list(range(8)))
    R = res.results
    y = [r["y"] for r in R]
    y_prompt = np.stack([y[c][NS + NMETA:] for c in range(4)], axis=0)
    y_sample = np.concatenate([y[c][0:NS] for c in range(8)], axis=0)[:, None, :]
    pk = lambda k: np.stack([R[c][k] for c in range(4)], axis=1)
    sk = lambda k: np.concatenate([R[c][k] for c in range(8)], axis=1)
    outs = (y_prompt, y_sample, pk("p_c"), pk("p_n"), pk("p_m"), pk("p_gs"), pk("p_gconv"), pk("p_rs"), pk("p_rshift"),
            sk("s_c"), sk("s_n"), sk("s_m"), sk("s_gs"), sk("s_gconv"), sk("s_rs"), sk("s_rshift"))
    return tuple(np.ascontiguousarray(o, dtype=np.float32) for o in outs)
```

```python
import numpy as np
import concourse.bass as bass
import concourse.mybir as mybir
from concourse.bass_utils import run_bass_kernel_spmd
from contextlib import ExitStack

F32 = mybir.dt.float32
BF16 = mybir.dt.bfloat16
AF = mybir.ActivationFunctionType
ALU = mybir.AluOpType
AX = mybir.AxisListType

D = 2048
KC = 16
NS = 16
NMETA = 16
SEQ = 2048
NROWS = NS + NMETA + SEQ
N_IN = 17688
DEPTH = 2
A0, B0, C0, G0 = 0, 4104, 8216, 11544
STAGES = ("mlstm", "gdn", "rwkv")


class Buf:
    __slots__ = ("name", "lw", "rd", "dsem", "dcnt")

    def __init__(self, name):
        self.name = name
        self.lw = None
        self.rd = {}
        self.dsem = None
        self.dcnt = 0


class Eng:
    def __init__(self, name, obj, sem):
        self.name = name
        self.obj = obj
        self.sem = sem
        self.cnt = 0
        self.known = {}


class KB:
    def __init__(self, nc, es):
        self.nc = nc
        self.es = es
        self.E = {}
        for nm, obj in (("pe", nc.tensor), ("act", nc.scalar), ("dve", nc.vector), ("pool", nc.gpsimd),
                        ("sp", nc.sync)):
            self.E[nm] = Eng(nm, obj, es.enter_context(nc.semaphore("sem_" + nm)))
        self.semid = {}
        self.dpool = []
        self.nd = 0
        self.out_tokens = {}
        self.n_inst = 0

    def buf(self, name):
        return Buf(name)

    def _dsem(self, b):
        if b.dsem is None:
            if self.dpool:
                b.dsem, b.dcnt = self.dpool.pop()
            else:
                b.dsem = self.es.enter_context(self.nc.semaphore("d%d" % self.nd))
                self.nd += 1
        return b.dsem

    def recycle(self, bufs):
        for b in bufs:
            if b.dsem is not None:
                self.dpool.append((b.dsem, b.dcnt))
                b.dsem = None

    def _waits(self, e, R, W):
        deps = {}
        for b in R:
            if b.lw is not None:
                s, v = b.lw
                if deps.get(s, 0) < v:
                    deps[s] = v
        for b in W:
            if b.lw is not None:
                s, v = b.lw
                if deps.get(s, 0) < v:
                    deps[s] = v
            for s, v in b.rd.items():
                if deps.get(s, 0) < v:
                    deps[s] = v
        for s, v in deps.items():
            if s is e.sem:
                if e.name == "pe" or e.name == "sp":
                    continue
                if e.cnt - v >= 2:
                    continue
            if e.known.get(s, 0) >= v:
                continue
            e.obj.wait_ge(s, v)
            e.known[s] = v
            self.n_inst += 1

    def op(self, en, fn, R=(), W=()):
        e = self.E[en]
        if e.cnt >= 30000:
            e.sem = self.es.enter_context(self.nc.semaphore("sem_%s_%d" % (en, self.n_inst)))
            e.cnt = 0
        self._waits(e, R, W)
        ins = fn(e.obj)
        e.cnt += 1
        ins.then_inc(e.sem, 1)
        tok = (e.sem, e.cnt)
        for b in R:
            if b.rd.get(tok[0], 0) < tok[1]:
                b.rd[tok[0]] = tok[1]
        for b in W:
            b.lw = tok
            b.rd = {}
        self.n_inst += 1
        return ins

    def dma(self, q, out, in_, R=(), W=(), semb=None, slow=False):
        e = self.E[q]
        self._waits(e, R, W)
        if semb is None:
            semb = W[0] if W else R[0]
        s = self._dsem(semb)
        if slow:
            ins = e.obj.dma_start(out=out, in_=in_, allow_slow_non_contiguous=True)
        else:
            ins = e.obj.dma_start(out=out, in_=in_)
        semb.dcnt += 16
        ins.then_inc(s, 16)
        tok = (s, semb.dcnt)
        for b in R:
            if b.rd.get(s, 0) < tok[1]:
                b.rd[s] = tok[1]
        for b in W:
            b.lw = tok
            b.rd = {}
        self.n_inst += 1
        return tok

    def barrier(self, bufs=()):
        bufs = list(bufs)
        for nm in ("pe", "act", "dve", "pool", "sp"):
            self._waits(self.E[nm], [], bufs)

    def finish(self, bufs):
        e = self.E["sp"]
        self._waits(e, list(bufs), list(bufs))


def build():
    nc = bass.Bass("TRN2", target_bir_lowering=False)
    es = ExitStack()
    with es:
        _build(nc, es)
    return nc


def tile_plan():
    tiles = []
    ch = [("s", b, 1, b) for b in range(NS)]
    ch.append(("p", NS, NMETA, 0))
    for j in range(5):
        ch.append(("p", NS + NMETA + 64 * j, 64, 0))
    tiles.append((0, NS + NMETA + 320, ch))
    r0 = NS + NMETA + 320
    for nchk in (6, 6, 6, 6, 3):
        ch = [("p", 64 * j, 64, 0) for j in range(nchk)]
        tiles.append((r0, 64 * nchk, ch))
        r0 += 64 * nchk
    assert r0 == NROWS
    return tiles


NMAX = 384
NWB = 8


def _build(nc, es):
    kb = KB(nc, es)
    op, dma = kb.op, kb.dma

    def dram_in(name, shape):
        return nc.dram_tensor(name, list(shape), F32, kind="ExternalInput").ap()

    def dram_out(name, shape):
        return nc.dram_tensor(name, list(shape), F32, kind="ExternalOutput").ap()

    xin = dram_in("xin", [NROWS, D])
    consts_d = dram_in("consts", [128, CONST_COLS])
    st_c = dram_in("st_c", [DEPTH, NS, 4, 256, 256])
    st_n = dram_in("st_n", [DEPTH, NS, 4, 256])
    st_m = dram_in("st_m", [DEPTH, NS, 4])
    st_gs = dram_in("st_gs", [DEPTH, NS, 8, 128, 128])
    st_gconv = dram_in("st_gconv", [DEPTH, NS, 3, 3072])
    st_rs = dram_in("st_rs", [DEPTH, NS, 16, 64, 64])
    st_rshift = dram_in("st_rshift", [DEPTH, NS, 3328])
    W = {}
    for nm, shp in PARAM_SHAPES.items():
        W[nm] = dram_in(nm, shp)
    y_out = dram_out("y", [NROWS, D])
    o_pc = dram_out("p_c", [DEPTH, 4, 256, 256])
    o_pn = dram_out("p_n", [DEPTH, 4, 256])
    o_pm = dram_out("p_m", [DEPTH, 4])
    o_pgs = dram_out("p_gs", [DEPTH, 8, 128, 128])
    o_pgconv = dram_out("p_gconv", [DEPTH, 3, 3072])
    o_prs = dram_out("p_rs", [DEPTH, 16, 64, 64])
    o_prshift = dram_out("p_rshift", [DEPTH, 3328])
    o_sc = dram_out("s_c", [DEPTH, NS, 4, 256, 256])
    o_sn = dram_out("s_n", [DEPTH, NS, 4, 256])
    o_sm = dram_out("s_m", [DEPTH, NS, 4])
    o_sgs = dram_out("s_gs", [DEPTH, NS, 8, 128, 128])
    o_sgconv = dram_out("s_gconv", [DEPTH, NS, 3, 3072])
    o_srs = dram_out("s_rs", [DEPTH, NS, 16, 64, 64])
    o_srshift = dram_out("s_rshift", [DEPTH, NS, 3328])

    outbufs = []
    cur = [es]

    uid = [0]

    def sb(name, shape, dt=F32):
        uid[0] += 1
        name = "%s_%d" % (name, uid[0])
        t = cur[0].enter_context(nc.sbuf_tensor(name, list(shape), dt))
        b = kb.buf(name)
        if cur[0] is not es:
            arena_bufs.append(b)
        return t, b
    arena_bufs = []

    class Arena:
        def __enter__(self):
            self.st = ExitStack()
            self.st.__enter__()
            self.prev = cur[0]
            self.mark = len(arena_bufs)
            cur[0] = self.st
            return self

        def __exit__(self, *a):
            mine = arena_bufs[self.mark:]
            kb.barrier(mine)
            kb.recycle(mine)
            del arena_bufs[self.mark:]
            cur[0] = self.prev
            return self.st.__exit__(*a)

    def run_interleaved(makers, K):
        pending = list(makers)
        active = {}
        free = list(range(K))
        while pending or active:
            while pending and free:
                s_ = free.pop(0)
                active[s_] = pending.pop(0)(s_)
            for s_ in list(active.keys()):
                try:
                    next(active[s_])
                except StopIteration:
                    del active[s_]
                    free.append(s_)

    cst, cstB = sb("cst", [128, CONST_COLS])
    dma("sp", cst[:, :], consts_d[:, :], W=[cstB])
    ident = cst[:, CO["ident"]:CO["ident"] + 128]
    ones = cst[:, CO["ones"]:CO["ones"] + 128]

    def cview(name, p, shape):
        n = int(np.prod(shape))
        v = cst[0:p, CO[name]:CO[name] + n]
        return v.rearrange("p (a b) -> p a b", b=shape[1])
    nmT_incl = cst[0:64, CO["nmT_incl"]:CO["nmT_incl"] + 64]
    selh = cview("selh", 8, (8, 64))
    sellast = cview("sellast", 64, (3, 128))
    LV = {64: 0, 16: 1, 1: 2}

    ident_bf, ident_bfB = sb("ident_bf", [128, 128], BF16)
    op("dve", lambda e: e.tensor_copy(out=ident_bf[:, :], in_=ident), R=[cstB], W=[ident_bfB])

    PS = []
    for i in range(8):
        t = es.enter_context(nc.psum_tensor("ps%d" % i, [128, 512], F32))
        PS.append((t, kb.buf("ps%d" % i)))
    psi = [0]

    def psum():
        t = PS[psi[0] % 8]
        psi[0] += 1
        return t

    def load_fm_vec(name, src, nblk):
        t, b = sb(name, [128, DEPTH, nblk])
        for l in range(DEPTH):
            dma("sp", t[:, l, :], src[l].rearrange("(k p) -> p k", p=128), W=[b], slow=True)
        return t, b

    nw_mix, nw_mixB = load_fm_vec("nw_mix", W["norm_mix_w"], 16)
    nw_mlp, nw_mlpB = load_fm_vec("nw_mlp", W["norm_mlp_w"], 16)
    nw_fin, nw_finB = sb("nw_fin", [128, 16])
    dma("sp", nw_fin[:, :], W["final_norm_w"].rearrange("(k p) -> p k", p=128), W=[nw_finB], slow=True)

    epst, epsB = sb("epst", [128, 4])
    op("dve", lambda e: e.memset(epst[:, 0:1], 1e-6), W=[epsB])
    op("dve", lambda e: e.memset(epst[:, 1:2], 1e-12), W=[epsB])
    op("dve", lambda e: e.memset(epst[:, 2:3], 64e-5), W=[epsB])
    op("dve", lambda e: e.memset(epst[:, 3:4], 1.0), W=[epsB])
    zrow, zrowB = sb("zrow", [8, 64])
    op("dve", lambda e: e.memset(zrow[:, :], 0.0), W=[zrowB])
    zcol, zcolB = sb("zcol", [128, 64])
    op("dve", lambda e: e.memset(zcol[:, :], 0.0), W=[zcolB])

    xF, xFB = sb("xF", [128, KC, NMAX])
    xnF, xnFB = sb("xnF", [128, KC, NMAX], BF16)
    wbuf = [sb("wb%d" % i, [128, 2048], BF16) for i in range(NWB)]
    wbi = [0]
    yF = [sb("y%sF" % c, [128, 8, NMAX], BF16) for c in "abc"]
    rstd, rstdB = sb("rstd", [128, NMAX])
    tmpA, tmpAB = sb("tmpA", [128, NMAX])

    def load_w(src2d, c0, ncol, kchunks):
        t, b = wbuf[wbi[0] % NWB]
        wbi[0] += 1
        v = t[:, 0:kchunks * ncol].rearrange("p (k c) -> p k c", c=ncol)
        src = src2d.rearrange("(k p) n -> p k n", p=128)[:, :, c0:c0 + ncol]
        dma("pool", v, src, W=[b])
        return v, b

    def rmsnorm(N, nw_ap_fn, nwB, inplace=False):
        ps, psB = psum()
        for k in range(KC):
            op("act", lambda e, k=k: e.activation(out=tmpA[:, 0:N], in_=xF[:, k, 0:N], func=AF.Square),
               R=[xFB], W=[tmpAB])
            op("pe", lambda e, k=k: e.matmul(ps[:, 0:N], lhsT=ones, rhs=tmpA[:, 0:N], start=(k == 0),
                                             stop=(k == KC - 1)), R=[tmpAB, cstB], W=[psB])
        op("act", lambda e: e.activation(out=rstd[:, 0:N], in_=ps[:, 0:N], func=AF.Ln, scale=1.0 / D, bias=epst[:, 0:1]),
           R=[psB, epsB], W=[rstdB])
        op("act", lambda e: e.activation(out=rstd[:, 0:N], in_=rstd[:, 0:N], func=AF.Exp, scale=-0.5),
           R=[rstdB], W=[rstdB])
        for k in range(KC):
            if inplace:
                op("dve", lambda e, k=k: e.scalar_tensor_tensor(out=xF[:, k, 0:N], in0=xF[:, k, 0:N], scalar=nw_ap_fn(k),
                                                                in1=rstd[:, 0:N], op0=ALU.mult, op1=ALU.mult),
                   R=[xFB, rstdB, nwB], W=[xFB])
            else:
                op("dve", lambda e, k=k: e.scalar_tensor_tensor(out=xnF[:, k, 0:N], in0=xF[:, k, 0:N], scalar=nw_ap_fn(k),
                                                                in1=rstd[:, 0:N], op0=ALU.mult, op1=ALU.mult),
                   R=[xFB, rstdB, nwB], W=[xnFB])

    def proj_fm(src2d, c0, ncols, N, consume, kchunks=KC, rhs_fn=None, rhsB=None):
        rhs_fn = rhs_fn or (lambda k: xnF[:, k, 0:N])
        rhsB = rhsB or xnFB
        done = 0
        j = 0
        while done < ncols:
            m = min(128, ncols - done)
            wv, wB = load_w(src2d, c0 + done, m, kchunks)
            ps, psB = psum()
            for k in range(kchunks):
                op("pe", lambda e, k=k: e.matmul(ps[0:m, 0:N], lhsT=wv[:, k, 0:m], rhs=rhs_fn(k), start=(k == 0),
                                                 stop=(k == kchunks - 1)), R=[wB, rhsB], W=[psB])
            consume(j, ps, psB, m)
            j += 1
            done += m

    Cext = []
    for l in range(DEPTH):
        t, b = sb("Cext%d" % l, [128, 4, 2, 257])
        op("dve", lambda e, t=t: e.memset(t[:, :, :, :], 0.0), W=[b])
        Cext.append((t, b))
    mprev = []
    for l in range(DEPTH):
        t, b = sb("mprev%d" % l, [4, 1])
        op("dve", lambda e, t=t: e.memset(t[:, :], 0.0), W=[b])
        mprev.append((t, b))
    mb_i, mb_iB = sb("mb_i", [4, DEPTH])
    mb_f, mb_fB = sb("mb_f", [4, DEPTH])
    dma("sp", mb_i[:, :], W["mlstm_b_i"].rearrange("l h -> h l"), W=[mb_iB], slow=True)
    dma("sp", mb_f[:, :], W["mlstm_b_f"].rearrange("l h -> h l"), W=[mb_fB], slow=True)
    nmb_f, nmb_fB = sb("nmb_f", [4, DEPTH])
    op("dve", lambda e: e.tensor_scalar(out=nmb_f[:, :], in0=mb_f[:, :], scalar1=-1.0, scalar2=None, op0=ALU.mult),
       R=[mb_fB], W=[nmb_fB])

    def to_fm(src, srcB, L, cs, dst):
        dt, dB = dst
        ps, psB = psum()
        pv = ps.bitcast(BF16)
        for blk in range(8):
            op("pe", lambda e, blk=blk: e.transpose(out=pv[:, blk * 64:blk * 64 + L], in_=src[0:L, blk * 128:(blk + 1) * 128],
                                                    identity=ident_bf[0:L, 0:L]), R=[srcB, ident_bfB], W=[psB])
        for blk in range(8):
            if blk % 2 == 0:
                op("act", lambda e, blk=blk: e.activation(out=dt[:, blk, cs], in_=pv[:, blk * 64:blk * 64 + L], func=AF.Copy),
                   R=[psB], W=[dB])
            else:
                op("dve", lambda e, blk=blk: e.tensor_copy(out=dt[:, blk, cs], in_=pv[:, blk * 64:blk * 64 + L]),
                   R=[psB], W=[dB])

    def mlstm_phase(l, tile):
        r0, N, chunks = tile
        CeP, CePB = Cext[l]
        w_in = W["w_in"][l]
        qF, qFB = sb("m_qF", [128, 8, NMAX], BF16)
        kF, kFB = sb("m_kF", [128, 8, NMAX], BF16)
        vF, vFB = sb("m_vF", [128, 8, NMAX], BF16)
        oF, oFB = sb("m_oF", [128, 8, NMAX], BF16)
        gi, giB = sb("m_gi", [4, NMAX])
        gf, gfB = sb("m_gf", [4, NMAX])
        gm, gmB = sb("m_gm", [4, NMAX])
        gb, gbB = sb("m_gb", [4, NMAX])
        gu, guB = sb("m_gu", [4, NMAX])
        gR, gRB = sb("m_gR", [128, NMAX])
        msamp, msampB = sb("m_msamp", [4, NS])
        tokT, tokTB = sb("m_tokT", [64, 128])
        kT, kTB = sb("m_kT", [64, 1024], BF16)
        vext, vextB = sb("m_vext", [64, 4, 257], BF16)
        osg, osgB = sb("m_osg", [64, 1024], BF16)
        MK = 4
        mslots = []
        for _i in range(MK):
            mslots.append({"Et": sb("m_E", [64, 64]), "PT": sb("m_PT", [64, 64], BF16), "Asb": sb("m_Asb", [64, 257]),
                           "Rt": sb("m_R", [64, 257]), "den": sb("m_den", [64, 2]), "ktil": sb("m_ktil", [64, 256], BF16)})
        hT, hTB = sb("m_hT", [64, 1024])
        dbc, dbcB = sb("m_dbc", [128, 4])
        ssq, ssqB = sb("m_ssq", [64, 8])
        yT, yTB = sb("m_yT", [64, 1024], BF16)
        sq_junk, sq_junkB = sb("m_sqj", [64, 256])
        CextS, CextSB = sb("CextS", [128, 4, 2, 257])
        Cbf, CbfB = sb("Cbf", [128, 4, 2, 257], BF16)
        bc1024, bc1024B = sb("bc1024", [64, 1024])
        op("dve", lambda e: e.memset(gR[:, :], 0.0), W=[gRB])
        op("dve", lambda e: e.memset(vext[:, :, 256:257], 1.0), W=[vextB])

        def cons_to(dst, dstB, scale=1.0, func=AF.Copy):
            def f(j, ps, psB, m):
                op("act", lambda e: e.activation(out=dst[0:m, j, 0:N], in_=ps[0:m, 0:N], func=func, scale=scale),
                   R=[psB], W=[dstB])
            return f
        proj_fm(w_in, A0 + 0, 1024, N, cons_to(qF, qFB))
        proj_fm(w_in, A0 + 1024, 1024, N, cons_to(kF, kFB, scale=1.0 / 16.0))
        proj_fm(w_in, A0 + 2048, 1024, N, cons_to(vF, vFB))
        proj_fm(w_in, A0 + 3072, 1024, N, cons_to(oF, oFB, func=AF.Sigmoid))

        def cons_i(j, ps, psB, m):
            op("act", lambda e: e.activation(out=gi[:, 0:N], in_=ps[0:4, 0:N], func=AF.Identity, bias=mb_i[:, l:l + 1]),
               R=[psB, mb_iB], W=[giB])
        proj_fm(w_in, A0 + 4096, 4, N, cons_i)

        def cons_f(j, ps, psB, m):
            op("act", lambda e: e.activation(out=gf[:, 0:N], in_=ps[0:4, 0:N], func=AF.Exp, scale=-1.0,
                                             bias=nmb_f[:, l:l + 1]), R=[psB, nmb_fB], W=[gfB])
            op("act", lambda e: e.activation(out=gf[:, 0:N], in_=gf[:, 0:N], func=AF.Ln, bias=epst[0:4, 3:4]),
               R=[gfB, epsB], W=[gfB])
            op("dve", lambda e: e.tensor_scalar(out=gf[:, 0:N], in0=gf[:, 0:N], scalar1=-1.0, scalar2=None,
                                                op0=ALU.mult), R=[gfB], W=[gfB])
        proj_fm(w_in, A0 + 4100, 4, N, cons_f)

        mP, mPB = mprev[l]
        has_s = any(c[0] == "s" for c in chunks)
        pc0 = NS if has_s else 0
        if has_s:
            dma("sp", msamp[:, :], st_m[l].rearrange("b h -> h b"), W=[msampB], slow=True)
            op("dve", lambda e: e.tensor_tensor(out=gm[:, 0:NS], in0=gf[:, 0:NS], in1=msamp[:, :], op=ALU.add),
               R=[gfB, msampB], W=[gmB])
            op("dve", lambda e: e.tensor_tensor(out=gm[:, 0:NS], in0=gm[:, 0:NS], in1=gi[:, 0:NS], op=ALU.max),
               R=[gmB, giB], W=[gmB])
            op("dve", lambda e: e.tensor_copy(out=gb[:, 0:NS], in_=gf[:, 0:NS]), R=[gfB], W=[gbB])
        op("dve", lambda e: e.tensor_tensor_scan(out=gm[:, pc0:N], data0=gf[:, pc0:N], data1=gi[:, pc0:N],
                                                 initial=mP[:, 0:1], op0=ALU.add, op1=ALU.max),
           R=[gfB, giB, mPB], W=[gmB])
        for (kind, c0, L, sidx) in chunks:
            if kind != "p":
                continue
            op("dve", lambda e, c0=c0, L=L: e.tensor_tensor_scan(out=gb[:, c0:c0 + L], data0=gf[:, c0:c0 + L],
                                                                 data1=zrow[0:4, 0:L], initial=0.0,
                                                                 op0=ALU.add, op1=ALU.add),
               R=[gfB, zrowB], W=[gbB])
        op("dve", lambda e: e.tensor_tensor(out=gu[:, 0:N], in0=gb[:, 0:N], in1=gm[:, 0:N], op=ALU.subtract),
           R=[gbB, gmB], W=[guB])
        op("dve", lambda e: e.tensor_tensor(out=gR[0:4, 0:N], in0=gi[:, 0:N], in1=gb[:, 0:N], op=ALU.subtract),
           R=[giB, gbB], W=[gRB])
        op("act", lambda e: e.activation(out=gR[64:68, 0:N], in_=gm[:, 0:N], func=AF.Exp, scale=-1.0),
           R=[gmB], W=[gRB])
        if has_s:
            op("dve", lambda e: e.tensor_tensor(out=gR[32:36, 0:NS], in0=gu[:, 0:NS], in1=msamp[:, :], op=ALU.add),
               R=[guB, msampB], W=[gRB])
            op("act", lambda e: e.activation(out=gR[32:36, 0:NS], in_=gR[32:36, 0:NS], func=AF.Exp), R=[gRB], W=[gRB])
            op("dve", lambda e: e.tensor_tensor(out=gR[96:100, 0:NS], in0=gR[0:4, 0:NS], in1=gu[:, 0:NS], op=ALU.add),
               R=[gRB, guB], W=[gRB])
            op("act", lambda e: e.activation(out=gR[96:100, 0:NS], in_=gR[96:100, 0:NS], func=AF.Exp), R=[gRB], W=[gRB])
        first = True
        for (kind, c0, L, sidx) in chunks:
            if kind != "p":
                continue
            mp_ap = mP[:, 0:1] if first else gm[:, c0 - 1:c0]
            op("act", lambda e, c0=c0, L=L, mp_ap=mp_ap: e.activation(out=gR[32:36, c0:c0 + L], in_=gu[:, c0:c0 + L],
                                                                     func=AF.Exp, bias=mp_ap),
               R=[guB, gmB, mPB], W=[gRB])
            op("act", lambda e, c0=c0, L=L: e.activation(out=gR[96:100, c0:c0 + L], in_=gR[0:4, c0:c0 + L],
                                                         func=AF.Exp, bias=gu[:, c0 + L - 1:c0 + L]),
               R=[guB, gRB], W=[gRB])
            first = False

        dma("sp", bc1024[:, :], W["mlstm_norm_w"][l:l + 1, :].partition_broadcast(64), W=[bc1024B])

        def chunk(c0, L, Ct, CtB):
            lv = LV[L]
            cs = slice(c0, c0 + L)
            ps, psB = psum()
            op("pe", lambda e: e.transpose(out=ps[0:L, 0:128], in_=gR[:, cs], identity=ident), R=[gRB, cstB], W=[psB])
            op("act", lambda e: e.activation(out=tokT[0:L, :], in_=ps[0:L, 0:128], func=AF.Copy), R=[psB], W=[tokTB])
            for (src, srcB, dstfn, dstB) in ((kF, kFB, lambda blk: kT[0:L, blk * 128:(blk + 1) * 128], kTB),
                                             (vF, vFB, lambda blk: vext[0:L, blk // 2, (blk % 2) * 128:(blk % 2) * 128 + 128], vextB),
                                             (oF, oFB, lambda blk: osg[0:L, blk * 128:(blk + 1) * 128], osgB)):
                ps, psB = psum()
                pv = ps.bitcast(BF16)
                for blk in range(8):
                    op("pe", lambda e, blk=blk, pv=pv, src=src: e.transpose(out=pv[0:L, blk * 128:(blk + 1) * 128],
                                                                          in_=src[:, blk, cs], identity=ident_bf[:, :]),
                       R=[srcB, ident_bfB], W=[psB])
                for blk in range(8):
                    if blk % 2 == 0:
                        op("act", lambda e, blk=blk, pv=pv, dstfn=dstfn: e.activation(
                            out=dstfn(blk), in_=pv[0:L, blk * 128:(blk + 1) * 128], func=AF.Copy), R=[psB], W=[dstB])
                    else:
                        op("dve", lambda e, blk=blk, pv=pv, dstfn=dstfn: e.tensor_copy(
                            out=dstfn(blk), in_=pv[0:L, blk * 128:(blk + 1) * 128]), R=[psB], W=[dstB])
            op("act", lambda e: e.activation(out=Cbf[:, :, :, :], in_=Ct[:, :, :, :], func=AF.Copy), R=[CtB], W=[CbfB])
            psd, psdB = psum()
            op("pe", lambda e: e.matmul(psd[:, 0:4], lhsT=sellast[0:L, lv, :], rhs=tokT[0:L, 32:36], start=True, stop=True),
               R=[cstB, tokTB], W=[psdB])
            op("dve", lambda e: e.tensor_copy(out=dbc[:, :], in_=psd[:, 0:4]), R=[psdB], W=[dbcB])
            def mhead(h, sl):
                SL = mslots[sl]
                (Et, EtB), (PT, PTB), (Asb, AsbB) = SL['Et'], SL['PT'], SL['Asb']
                (Rt, RtB), (den, denB), (ktil, ktilB) = SL['Rt'], SL['den'], SL['ktil']
                psu, psuB = psum()
                op("pe", lambda e: e.matmul(psu[0:L, 0:L], lhsT=selh[0:4, h, 0:L], rhs=gu[:, cs], start=True, stop=False),
                   R=[cstB, guB], W=[psuB])
                op("pe", lambda e: e.matmul(psu[0:L, 0:L], lhsT=ident[0:L, 0:L], rhs=nmT_incl[0:L, 0:L], start=False, stop=True),
                   R=[cstB], W=[psuB])
                op("act", lambda e: e.activation(out=Et[0:L, 0:L], in_=psu[0:L, 0:L], func=AF.Exp, bias=tokT[0:L, h:h + 1]),
                   R=[psuB, tokTB], W=[EtB])
                yield
                pst, pstB = psum()
                for kc in range(2):
                    op("pe", lambda e, kc=kc: e.matmul(pst[0:L, 0:L], lhsT=kF[:, 2 * h + kc, cs], rhs=qF[:, 2 * h + kc, cs],
                                                       start=(kc == 0), stop=(kc == 1)), R=[kFB, qFB], W=[pstB])
                op("dve", lambda e: e.tensor_tensor(out=PT[0:L, 0:L], in0=pst[0:L, 0:L], in1=Et[0:L, 0:L], op=ALU.mult),
                   R=[pstB, EtB], W=[PTB])
                yield
                psb, psbB = psum()
                op("pe", lambda e: e.matmul(psb[0:L, 0:257], lhsT=PT[0:L, 0:L], rhs=vext[0:L, h, :], start=True, stop=True),
                   R=[PTB, vextB], W=[psbB])
                psa, psaB = psum()
                for kc in range(2):
                    op("pe", lambda e, kc=kc: e.matmul(psa[0:L, 0:257], lhsT=qF[:, 2 * h + kc, cs], rhs=Cbf[:, h, kc, :],
                                                       start=(kc == 0), stop=(kc == 1)), R=[qFB, CbfB], W=[psaB])
                op("act", lambda e: e.activation(out=Asb[0:L, :], in_=psa[0:L, 0:257], func=AF.Copy,
                                                 scale=tokT[0:L, 32 + h:33 + h]), R=[psaB, tokTB], W=[AsbB])
                op("dve", lambda e: e.tensor_tensor(out=Rt[0:L, :], in0=Asb[0:L, :], in1=psb[0:L, 0:257], op=ALU.add),
                   R=[AsbB, psbB], W=[RtB])
                yield
                op("act", lambda e: e.activation(out=den[0:L, 0:1], in_=Rt[0:L, 256:257], func=AF.Abs),
                   R=[RtB], W=[denB])
                op("dve", lambda e: e.tensor_tensor(out=den[0:L, 0:1], in0=den[0:L, 0:1], in1=tokT[0:L, 64 + h:65 + h],
                                                    op=ALU.max), R=[denB, tokTB], W=[denB])
                op("dve", lambda e: e.reciprocal(out=den[0:L, 1:2], in_=den[0:L, 0:1]), R=[denB], W=[denB])
                op("act", lambda e: e.activation(out=hT[0:L, h * 256:(h + 1) * 256], in_=Rt[0:L, 0:256], func=AF.Copy,
                                                 scale=den[0:L, 1:2]), R=[RtB, denB], W=[hTB])
                yield
                op("act", lambda e: e.activation(out=ktil[0:L, :], in_=kT[0:L, h * 256:(h + 1) * 256], func=AF.Copy,
                                                 scale=tokT[0:L, 96 + h:97 + h]), R=[kTB, tokTB], W=[ktilB])
                yield
                for kc in range(2):
                    psc, pscB = psum()
                    op("pe", lambda e, kc=kc, psc=psc: e.matmul(psc[:, 0:257], lhsT=ktil[0:L, kc * 128:(kc + 1) * 128],
                                                                rhs=vext[0:L, h, :], start=True, stop=True),
                       R=[ktilB, vextB], W=[pscB])
                    op("dve", lambda e, kc=kc, psc=psc: e.scalar_tensor_tensor(
                        out=Ct[:, h, kc, :], in0=Ct[:, h, kc, :], scalar=dbc[:, h:h + 1], in1=psc[:, 0:257],
                        op0=ALU.mult, op1=ALU.add), R=[CtB, dbcB, pscB], W=[CtB])
            run_interleaved([(lambda sl, h=h: mhead(h, sl)) for h in range(4)], MK)
            for h in range(4):
                op("act", lambda e, h=h: e.activation(out=sq_junk[0:L, :], in_=hT[0:L, h * 256:(h + 1) * 256], func=AF.Square,
                                                      accum_out=ssq[0:L, h:h + 1]), R=[hTB], W=[ssqB, sq_junkB])
            op("act", lambda e: e.activation(out=ssq[0:L, 4:8], in_=ssq[0:L, 0:4], func=AF.Ln, scale=1.0 / 256.0,
                                             bias=epst[0:L, 0:1]), R=[ssqB, epsB], W=[ssqB])
            op("act", lambda e: e.activation(out=ssq[0:L, 4:8], in_=ssq[0:L, 4:8], func=AF.Exp, scale=-0.5), R=[ssqB], W=[ssqB])
            for h in range(4):
                op("dve", lambda e, h=h: e.scalar_tensor_tensor(out=hT[0:L, h * 256:(h + 1) * 256],
                                                                in0=hT[0:L, h * 256:(h + 1) * 256],
                                                                scalar=ssq[0:L, 4 + h:5 + h],
                                                                in1=bc1024[0:L, h * 256:(h + 1) * 256],
                                                                op0=ALU.mult, op1=ALU.mult),
                   R=[hTB, ssqB, bc1024B], W=[hTB])
            op("dve", lambda e: e.tensor_tensor(out=yT[0:L, :], in0=hT[0:L, :], in1=osg[0:L, :], op=ALU.mult),
               R=[hTB, osgB], W=[yTB])
            to_fm(yT, yTB, L, cs, yF[0])

        for (kind, c0, L, sidx) in chunks:
            if kind == "s":
                Ct, CtB = CextS, CextSB
                for h in range(4):
                    dma("sp", Ct[:, h, :, 0:256], st_c[l, sidx, h].rearrange("(kc p) v -> p kc v", p=128), W=[CtB])
                dma("sp", Ct[:, :, :, 256:257], st_n[l, sidx].rearrange("h (kc p o) -> p h kc o", p=128, o=1),
                    W=[CtB], slow=True)
            else:
                Ct, CtB = CeP, CePB
            chunk(c0, L, Ct, CtB)
            if kind == "s":
                for h in range(4):
                    dma("sp", o_sc[l, sidx, h].rearrange("(kc p) v -> p kc v", p=128), Ct[:, h, :, 0:256],
                        R=[CtB], semb=CtB)
                dma("sp", o_sn[l, sidx].rearrange("h (kc p o) -> p h kc o", p=128, o=1), Ct[:, :, :, 256:257],
                    R=[CtB], semb=CtB, slow=True)
        if has_s:
            dma("sp", o_sm[l].rearrange("b h -> h b"), gm[:, 0:NS], R=[gmB], semb=gmB, slow=True)
        op("dve", lambda e: e.tensor_copy(out=mP[:, 0:1], in_=gm[:, N - 1:N]), R=[gmB], W=[mPB])


    nm_strict = cst[0:64, CO["nm_strict"]:CO["nm_strict"] + 64]
    nmT_strict = cst[0:64, CO["nmT_strict"]:CO["nmT_strict"] + 64]
    GS = []
    ghist = []
    for l in range(DEPTH):
        t, b = sb("GS%d" % l, [128, 8, 128])
        op("dve", lambda e, t=t: e.memset(t[:, :, :], 0.0), W=[b])
        GS.append((t, b))
        t, b = sb("ghist%d" % l, [128, 24, 3])
        op("dve", lambda e, t=t: e.memset(t[:, :, :], 0.0), W=[b])
        ghist.append((t, b))
    gcw, gcwB = sb("gcw", [128, DEPTH, 24, 4])
    gpar, gparB = sb("gpar", [8, DEPTH, 4])
    for l in range(DEPTH):
        dma("sp", gpar[:, l, 0:1], W["gdn_a_log"][l:l + 1, :].rearrange("o h -> h o"), W=[gparB], slow=True)
        dma("sp", gpar[:, l, 1:2], W["gdn_dt_bias"][l:l + 1, :].rearrange("o h -> h o"), W=[gparB], slow=True)
        op("act", lambda e, l=l: e.activation(out=gpar[:, l, 2:3], in_=gpar[:, l, 0:1], func=AF.Exp), R=[gparB], W=[gparB])
        op("dve", lambda e, l=l: e.tensor_scalar(out=gpar[:, l, 2:3], in0=gpar[:, l, 2:3], scalar1=-1.0, scalar2=None,
                                                 op0=ALU.mult), R=[gparB], W=[gparB])
    with Arena():
        cwt, cwtB = sb("cwt", [4, 3072])
        for l in range(DEPTH):
            dma("sp", cwt[:, :], W["gdn_conv_w"][l], W=[cwtB])
            for q in range(6):
                ps, psB = psum()
                for kk in range(4):
                    k = q * 4 + kk
                    op("pe", lambda e, k=k, kk=kk: e.transpose(out=ps[:, kk * 4:kk * 4 + 4], in_=cwt[0:4, k * 128:(k + 1) * 128],
                                                              identity=ident[0:4, 0:4]), R=[cwtB, cstB], W=[psB])
                op("dve", lambda e, q=q, l=l: e.tensor_copy(out=gcw[:, l, q * 4:q * 4 + 4, :],
                                                            in_=ps[:, 0:16].rearrange("p (a b) -> p a b", b=4)),
                   R=[psB], W=[gcwB])

    def gdn_phase(l, tile):
        r0, N, chunks = tile
        St, StB = GS[l]
        hs, hsB = ghist[l]
        w_in = W["w_in"][l]
        has_s = any(c[0] == "s" for c in chunks)
        pc0 = NS if has_s else 0
        Np = N - pc0
        xcs = [sb("g_xc%d" % i, [128, NMAX + 3]) for i in range(2)]
        yc, ycB = sb("g_yc", [128, NMAX])
        qF, qFB = sb("g_qF", [128, 8, NMAX], BF16)
        kF, kFB = sb("g_kF", [128, 8, NMAX], BF16)
        vF, vFB = sb("g_vF", [128, 8, NMAX], BF16)
        zF, zFB = sb("g_zF", [128, 8, NMAX], BF16)
        ra, raB = sb("g_ra", [8, NMAX])
        rlb, rlbB = sb("g_rlb", [8, NMAX])
        rg, rgB = sb("g_rg", [8, NMAX])
        rng_, rngB = sb("g_rng", [8, NMAX])
        rgb, rgbB = sb("g_rgb", [8, NMAX])
        stA, stAB = sb("g_stA", [128, NMAX])
        stB, stBB = sb("g_stB", [64, NMAX])
        tokA, tokAB = sb("g_tokA", [64, 128])
        tokB, tokBB = sb("g_tokB", [64, 64])
        kT, kTB = sb("g_kT", [64, 1024], BF16)
        vT, vTB = sb("g_vT", [64, 1024], BF16)
        zT, zTB = sb("g_zT", [64, 1024], BF16)
        GK = 4
        def mkslot():
            d = {}
            for nm_, shp_, dt_ in (("E1", [64, 64], F32), ("E2", [64, 64], F32), ("E3", [64, 64], F32),
                                   ("P0", [64, 64], F32), ("P1", [64, 64], F32), ("Q0", [64, 64], F32), ("Q1", [64, 64], F32),
                                   ("TT", [64, 64], F32), ("attnT", [64, 64], BF16), ("bv", [64, 128], F32),
                                   ("bek", [64, 128], F32), ("wF", [128, 64], BF16), ("usb", [64, 128], F32),
                                   ("vnew", [64, 128], BF16), ("qsb", [64, 128], F32), ("kd", [64, 128], BF16)):
                d[nm_] = sb("g_" + nm_, shp_, dt_)
            return d
        oT, oTB = sb("g_oT", [64, 1024])
        dbc, dbcB = sb("g_dbc", [128, 8])
        Sbf, SbfB = sb("g_Sbf", [128, 8, 128], BF16)
        if has_s:
            SS, SSB = sb("g_SS", [128, 8, 128])
        ssq, ssqB = sb("g_ssq", [64, 16])
        sqj, sqjB = sb("g_sqj", [64, 128])
        yT, yTB = sb("g_yT", [64, 1024], BF16)
        bcn, bcnB = sb("g_bcn", [64, 128])
        op("dve", lambda e: e.memset(stA[:, :], 0.0), W=[stAB])
        op("dve", lambda e: e.memset(stB[:, :], 0.0), W=[stBB])
        dma("sp", bcn[:, :], W["gdn_norm_w"][l:l + 1, :].partition_broadcast(64), W=[bcnB])
        with Arena():
            if has_s:
                xs, xsB = sb("g_xs", [128, 24, NS])
                bufS, bufSB = sb("g_bufS", [128, 24, NS * 3])
                st48, st48B = sb("g_st48", [48, 1536])
            if has_s:
                for hq in range(2):
                    dma("sp", st48[:, :], st_gconv[l].rearrange("b j c -> (b j) c")[:, hq * 1536:(hq + 1) * 1536], W=[st48B])
                    for kq in range(12):
                        k = hq * 12 + kq
                        ps, psB = psum()
                        op("pe", lambda e, kq=kq, ps=ps: e.transpose(out=ps[:, 0:48], in_=st48[0:48, kq * 128:(kq + 1) * 128],
                                                                     identity=ident[0:48, 0:48]), R=[st48B, cstB], W=[psB])
                        op("dve", lambda e, k=k, ps=ps: e.tensor_copy(out=bufS[:, k, :], in_=ps[:, 0:48]), R=[psB], W=[bufSB])
                dma("sp", o_sgconv[l][:, 0:2, :], st_gconv[l][:, 1:3, :], R=[st48B], semb=st48B)

            def cons_qkv(j, ps, psB, m):
                xc, xcB = xcs[j % 2]
                op("dve", lambda e: e.tensor_copy(out=xc[:, 0:3], in_=hs[:, j, :]), R=[hsB], W=[xcB])
                if has_s:
                    op("act", lambda e: e.activation(out=xs[:, j, :], in_=ps[:, 0:NS], func=AF.Copy), R=[psB], W=[xsB])
                op("act", lambda e: e.activation(out=xc[:, 3:3 + Np], in_=ps[:, pc0:N], func=AF.Copy), R=[psB], W=[xcB])
                op("dve", lambda e: e.tensor_scalar(out=yc[:, pc0:N], in0=xc[:, 0:Np], scalar1=gcw[:, l, j, 0:1],
                                                    scalar2=None, op0=ALU.mult), R=[xcB, gcwB], W=[ycB])
                for t in range(1, 4):
                    op("dve", lambda e, t=t: e.scalar_tensor_tensor(out=yc[:, pc0:N], in0=xc[:, t:t + Np],
                                                                    scalar=gcw[:, l, j, t:t + 1], in1=yc[:, pc0:N],
                                                                    op0=ALU.mult, op1=ALU.add), R=[xcB, gcwB, ycB], W=[ycB])
                op("dve", lambda e: e.tensor_copy(out=hs[:, j, :], in_=xc[:, Np:Np + 3]), R=[xcB], W=[hsB])
                if has_s:
                    bs = bufS[:, j, :].rearrange("p (b t) -> p b t", t=3)
                    op("dve", lambda e: e.tensor_scalar(out=yc[:, 0:NS], in0=xs[:, j, :], scalar1=gcw[:, l, j, 3:4],
                                                        scalar2=None, op0=ALU.mult), R=[xsB, gcwB], W=[ycB])
                    for t in range(3):
                        op("dve", lambda e, t=t: e.scalar_tensor_tensor(out=yc[:, 0:NS], in0=bs[:, :, t],
                                                                        scalar=gcw[:, l, j, t:t + 1], in1=yc[:, 0:NS],
                                                                        op0=ALU.mult, op1=ALU.add),
                           R=[bufSB, gcwB, ycB], W=[ycB])
                if j >= 16:
                    op("act", lambda e: e.activation(out=vF[:, j - 16, 0:N], in_=yc[:, 0:N], func=AF.Silu), R=[ycB], W=[vFB])
                    return
                op("act", lambda e: e.activation(out=yc[:, 0:N], in_=yc[:, 0:N], func=AF.Silu), R=[ycB], W=[ycB])
                op("act", lambda e: e.activation(out=tmpA[:, 0:N], in_=yc[:, 0:N], func=AF.Square), R=[ycB], W=[tmpAB])
                ps2, ps2B = psum()
                op("pe", lambda e: e.matmul(ps2[:, 0:N], lhsT=ones, rhs=tmpA[:, 0:N], start=True, stop=True),
                   R=[tmpAB, cstB], W=[ps2B])
                op("act", lambda e: e.activation(out=tmpA[:, 0:N], in_=ps2[:, 0:N], func=AF.Ln, bias=epst[:, 1:2]),
                   R=[ps2B, epsB], W=[tmpAB])
                op("act", lambda e: e.activation(out=tmpA[:, 0:N], in_=tmpA[:, 0:N], func=AF.Exp, scale=-0.5), R=[tmpAB], W=[tmpAB])
                if j < 8:
                    op("dve", lambda e: e.scalar_tensor_tensor(out=qF[:, j, 0:N], in0=yc[:, 0:N], scalar=128.0 ** -0.5,
                                                               in1=tmpA[:, 0:N], op0=ALU.mult, op1=ALU.mult),
                       R=[ycB, tmpAB], W=[qFB])
                else:
                    op("dve", lambda e: e.tensor_tensor(out=kF[:, j - 8, 0:N], in0=yc[:, 0:N], in1=tmpA[:, 0:N], op=ALU.mult),
                       R=[ycB, tmpAB], W=[kFB])
            proj_fm(w_in, B0, 3072, N, cons_qkv)
            if has_s:
                for hq in range(2):
                    for q3 in range(3):
                        q = hq * 3 + q3
                        ps, psB = psum()
                        for kk in range(4):
                            k = q * 4 + kk
                            op("pe", lambda e, k=k, kk=kk, ps=ps: e.transpose(out=ps[0:NS, kk * 128:(kk + 1) * 128], in_=xs[:, k, :],
                                                                      identity=ident), R=[xsB, cstB], W=[psB])
                        op("dve", lambda e, q3=q3, ps=ps: e.tensor_copy(out=st48[0:NS, q3 * 512:(q3 + 1) * 512], in_=ps[0:NS, 0:512]),
                           R=[psB], W=[st48B])
                    dma("sp", o_sgconv[l][:, 2, hq * 1536:(hq + 1) * 1536], st48[0:NS, :], R=[st48B], semb=st48B)


        gslots = [mkslot() for _ in range(GK)]

        def cons_z(j, ps, psB, m):
            op("act", lambda e: e.activation(out=zF[:, j, 0:N], in_=ps[:, 0:N], func=AF.Silu), R=[psB], W=[zFB])
        proj_fm(w_in, B0 + 3072, 1024, N, cons_z)

        def cons_a(j, ps, psB, m):
            op("act", lambda e: e.activation(out=ra[:, 0:N], in_=ps[0:8, 0:N], func=AF.Exp, bias=gpar[:, l, 1:2]),
               R=[psB, gparB], W=[raB])
            op("act", lambda e: e.activation(out=ra[:, 0:N], in_=ra[:, 0:N], func=AF.Ln, bias=epst[0:8, 3:4]),
               R=[raB, epsB], W=[raB])
            op("dve", lambda e: e.tensor_scalar(out=ra[:, 0:N], in0=ra[:, 0:N], scalar1=gpar[:, l, 2:3], scalar2=None,
                                                op0=ALU.mult), R=[raB, gparB], W=[raB])
        proj_fm(w_in, B0 + 4096, 8, N, cons_a)

        def cons_b(j, ps, psB, m):
            op("act", lambda e: e.activation(out=rlb[:, 0:N], in_=ps[0:8, 0:N], func=AF.Exp, scale=-1.0), R=[psB], W=[rlbB])
            op("act", lambda e: e.activation(out=rlb[:, 0:N], in_=rlb[:, 0:N], func=AF.Ln, bias=epst[0:8, 3:4]),
               R=[rlbB, epsB], W=[rlbB])
            op("dve", lambda e: e.tensor_scalar(out=rlb[:, 0:N], in0=rlb[:, 0:N], scalar1=-1.0, scalar2=None, op0=ALU.mult),
               R=[rlbB], W=[rlbB])
        proj_fm(w_in, B0 + 4104, 8, N, cons_b)
        if has_s:
            op("dve", lambda e: e.tensor_copy(out=rg[:, 0:NS], in_=ra[:, 0:NS]), R=[raB], W=[rgB])
        for (kind, c0, L, sidx) in chunks:
            if kind != "p":
                continue
            op("dve", lambda e, c0=c0, L=L: e.tensor_tensor_scan(out=rg[:, c0:c0 + L], data0=ra[:, c0:c0 + L],
                                                                 data1=zrow[0:8, 0:L], initial=0.0, op0=ALU.add, op1=ALU.add),
               R=[raB, zrowB], W=[rgB])
        op("dve", lambda e: e.tensor_scalar(out=rng_[:, 0:N], in0=rg[:, 0:N], scalar1=-1.0, scalar2=None, op0=ALU.mult),
           R=[rgB], W=[rngB])
        op("dve", lambda e: e.tensor_tensor(out=rgb[:, 0:N], in0=rg[:, 0:N], in1=rlb[:, 0:N], op=ALU.add),
           R=[rgB, rlbB], W=[rgbB])
        op("dve", lambda e: e.tensor_copy(out=stA[0:8, 0:N], in_=rng_[:, 0:N]), R=[rngB], W=[stAB])
        op("dve", lambda e: e.tensor_copy(out=stA[32:40, 0:N], in_=rgb[:, 0:N]), R=[rgbB], W=[stAB])
        op("act", lambda e: e.activation(out=stA[64:72, 0:N], in_=rlb[:, 0:N], func=AF.Exp), R=[rlbB], W=[stAB])
        op("act", lambda e: e.activation(out=stA[96:104, 0:N], in_=rgb[:, 0:N], func=AF.Exp), R=[rgbB], W=[stAB])
        op("act", lambda e: e.activation(out=stB[0:8, 0:N], in_=rg[:, 0:N], func=AF.Exp), R=[rgB], W=[stBB])
        for (kind, c0, L, sidx) in chunks:
            op("act", lambda e, c0=c0, L=L: e.activation(out=stB[32:40, c0:c0 + L], in_=rng_[:, c0:c0 + L], func=AF.Exp,
                                                         bias=rg[:, c0 + L - 1:c0 + L]), R=[rngB, rgB], W=[stBB])

        def chunk(c0, L, S_, S_B):
            lv = LV[L]
            cs = slice(c0, c0 + L)
            nlev = {64: 5, 16: 3, 1: 0}[L]
            ps, psB = psum()
            op("pe", lambda e: e.transpose(out=ps[0:L, 0:128], in_=stA[:, cs], identity=ident), R=[stAB, cstB], W=[psB])
            op("act", lambda e: e.activation(out=tokA[0:L, :], in_=ps[0:L, 0:128], func=AF.Copy), R=[psB], W=[tokAB])
            ps, psB = psum()
            op("pe", lambda e: e.transpose(out=ps[0:L, 0:64], in_=stB[:, cs], identity=ident[0:64, 0:64]), R=[stBB, cstB], W=[psB])
            op("act", lambda e: e.activation(out=tokB[0:L, :], in_=ps[0:L, 0:64], func=AF.Copy), R=[psB], W=[tokBB])
            for (src, srcB, dst, dstB) in ((kF, kFB, kT, kTB), (vF, vFB, vT, vTB), (zF, zFB, zT, zTB)):
                ps, psB = psum()
                pv = ps.bitcast(BF16)
                for blk in range(8):
                    op("pe", lambda e, blk=blk, pv=pv, src=src: e.transpose(out=pv[0:L, blk * 128:(blk + 1) * 128],
                                                                          in_=src[:, blk, cs], identity=ident_bf[:, :]),
                       R=[srcB, ident_bfB], W=[psB])
                op("act", lambda e, pv=pv, dst=dst: e.activation(out=dst[0:L, 0:512], in_=pv[0:L, 0:512], func=AF.Copy),
                   R=[psB], W=[dstB])
                op("dve", lambda e, pv=pv, dst=dst: e.tensor_copy(out=dst[0:L, 512:1024], in_=pv[0:L, 512:1024]),
                   R=[psB], W=[dstB])
            op("act", lambda e: e.activation(out=Sbf[:, :, :], in_=S_[:, :, :], func=AF.Copy), R=[S_B], W=[SbfB])
            psd, psdB = psum()
            op("pe", lambda e: e.matmul(psd[:, 0:8], lhsT=sellast[0:L, lv, :], rhs=tokB[0:L, 0:8], start=True, stop=True),
               R=[cstB, tokBB], W=[psdB])
            op("dve", lambda e: e.tensor_copy(out=dbc[:, :], in_=psd[:, 0:8]), R=[psdB], W=[dbcB])
            def head(h, sl):
                SL = gslots[sl]
                (E1, E1B), (E2, E2B), (E3, E3B) = SL['E1'], SL['E2'], SL['E3']
                Pm = [SL['P0'], SL['P1']]
                Qm = [SL['Q0'], SL['Q1']]
                (TT, TTB), (attnT, attnTB), (bv, bvB), (bek, bekB) = SL['TT'], SL['attnT'], SL['bv'], SL['bek']
                (wF, wFB), (usb, usbB), (vnew, vnewB), (qsb, qsbB), (kd, kdB) = SL['wF'], SL['usb'], SL['vnew'], SL['qsb'], SL['kd']
                for (row, rowB, nm, Ed, EdB, bias_ap) in ((rgb, rgbB, nmT_strict, E1, E1B, tokA[0:L, h:h + 1]),
                                                         (rng_, rngB, nm_strict, E2, E2B, tokA[0:L, 32 + h:33 + h]),
                                                         (rg, rgB, nmT_incl, E3, E3B, tokA[0:L, h:h + 1])):
                    pse, pseB = psum()
                    op("pe", lambda e, row=row, pse=pse: e.matmul(pse[0:L, 0:L], lhsT=selh[0:8, h, 0:L], rhs=row[:, cs],
                                                                  start=True, stop=False), R=[cstB, rowB], W=[pseB])
                    op("pe", lambda e, nm=nm, pse=pse: e.matmul(pse[0:L, 0:L], lhsT=ident[0:L, 0:L], rhs=nm[0:L, 0:L],
                                                                start=False, stop=True), R=[cstB], W=[pseB])
                    op("act", lambda e, Ed=Ed, pse=pse, bias_ap=bias_ap: e.activation(out=Ed[0:L, 0:L], in_=pse[0:L, 0:L],
                                                                                     func=AF.Exp, bias=bias_ap),
                       R=[pseB, tokAB], W=[EdB])
                    yield
                psg, psgB = psum()
                op("pe", lambda e: e.matmul(psg[0:L, 0:L], lhsT=kF[:, h, cs], rhs=kF[:, h, cs], start=True, stop=True),
                   R=[kFB], W=[psgB])
                pskq, pskqB = psum()
                op("pe", lambda e: e.matmul(pskq[0:L, 0:L], lhsT=kF[:, h, cs], rhs=qF[:, h, cs], start=True, stop=True),
                   R=[kFB, qFB], W=[pskqB])
                (P0, P0B), (P1, P1B) = Pm
                (Q0, Q0B), (Q1, Q1B) = Qm
                op("dve", lambda e: e.scalar_tensor_tensor(out=Q0[0:L, 0:L], in0=psg[0:L, 0:L], scalar=-1.0, in1=E1[0:L, 0:L],
                                                           op0=ALU.mult, op1=ALU.mult), R=[psgB, E1B], W=[Q0B])
                op("dve", lambda e: e.scalar_tensor_tensor(out=P0[0:L, 0:L], in0=psg[0:L, 0:L], scalar=-1.0, in1=E2[0:L, 0:L],
                                                           op0=ALU.mult, op1=ALU.mult), R=[psgB, E2B], W=[P0B])
                op("dve", lambda e: e.tensor_tensor(out=attnT[0:L, 0:L], in0=pskq[0:L, 0:L], in1=E3[0:L, 0:L], op=ALU.mult),
                   R=[pskqB, E3B], W=[attnTB])
                op("dve", lambda e: e.tensor_tensor(out=TT[0:L, 0:L], in0=Q0[0:L, 0:L], in1=ident[0:L, 0:L], op=ALU.add),
                   R=[Q0B, cstB], W=[TTB])
                yield
                cp, cq = (P0, P0B), (Q0, Q0B)
                np_, nq = (P1, P1B), (Q1, Q1B)
                for lev in range(nlev):
                    psp, pspB = psum()
                    op("pe", lambda e, cp=cp, cq=cq, psp=psp: e.matmul(psp[0:L, 0:L], lhsT=cq[0][0:L, 0:L], rhs=cp[0][0:L, 0:L],
                                                                      start=True, stop=True), R=[cp[1], cq[1]], W=[pspB])
                    psq, psqB = psum()
                    op("pe", lambda e, cp=cp, cq=cq, psq=psq: e.matmul(psq[0:L, 0:L], lhsT=cp[0][0:L, 0:L], rhs=cq[0][0:L, 0:L],
                                                                      start=True, stop=True), R=[cp[1], cq[1]], W=[psqB])
                    op("act", lambda e, np_=np_, psp=psp: e.activation(out=np_[0][0:L, 0:L], in_=psp[0:L, 0:L], func=AF.Copy),
                       R=[pspB], W=[np_[1]])
                    op("dve", lambda e, nq=nq, psq=psq: e.tensor_copy(out=nq[0][0:L, 0:L], in_=psq[0:L, 0:L]),
                       R=[psqB], W=[nq[1]])
                    yield
                    pst, pstB = psum()
                    op("pe", lambda e, np_=np_, pst=pst: e.matmul(pst[0:L, 0:L], lhsT=np_[0][0:L, 0:L], rhs=TT[0:L, 0:L],
                                                                  start=True, stop=True), R=[np_[1], TTB], W=[pstB])
                    op("dve", lambda e, pst=pst: e.tensor_tensor(out=TT[0:L, 0:L], in0=TT[0:L, 0:L], in1=pst[0:L, 0:L],
                                                                 op=ALU.add), R=[TTB, pstB], W=[TTB])
                    yield
                    cp, cq, np_, nq = np_, nq, cp, cq
                hs_ = slice(h * 128, (h + 1) * 128)
                op("act", lambda e: e.activation(out=bv[0:L, :], in_=vT[0:L, hs_], func=AF.Copy, scale=tokA[0:L, 64 + h:65 + h]),
                   R=[vTB, tokAB], W=[bvB])
                op("act", lambda e: e.activation(out=bek[0:L, :], in_=kT[0:L, hs_], func=AF.Copy, scale=tokA[0:L, 96 + h:97 + h]),
                   R=[kTB, tokAB], W=[bekB])
                yield
                psu, psuB = psum()
                op("pe", lambda e: e.matmul(psu[0:L, 0:128], lhsT=TT[0:L, 0:L], rhs=bv[0:L, :], start=True, stop=True),
                   R=[TTB, bvB], W=[psuB])
                psw, pswB = psum()
                op("pe", lambda e: e.matmul(psw[:, 0:L], lhsT=bek[0:L, :], rhs=TT[0:L, 0:L], start=True, stop=True),
                   R=[TTB, bekB], W=[pswB])
                op("act", lambda e: e.activation(out=wF[:, 0:L], in_=psw[:, 0:L], func=AF.Copy), R=[pswB], W=[wFB])
                op("dve", lambda e: e.tensor_copy(out=usb[0:L, :], in_=psu[0:L, 0:128]), R=[psuB], W=[usbB])
                yield
                psws, pswsB = psum()
                op("pe", lambda e: e.matmul(psws[0:L, 0:128], lhsT=wF[:, 0:L], rhs=Sbf[:, h, :], start=True, stop=True),
                   R=[wFB, SbfB], W=[pswsB])
                op("dve", lambda e: e.tensor_tensor(out=vnew[0:L, :], in0=usb[0:L, :], in1=psws[0:L, 0:128], op=ALU.subtract),
                   R=[usbB, pswsB], W=[vnewB])
                yield
                psqs, psqsB = psum()
                op("pe", lambda e: e.matmul(psqs[0:L, 0:128], lhsT=qF[:, h, cs], rhs=Sbf[:, h, :], start=True, stop=True),
                   R=[qFB, SbfB], W=[psqsB])
                psav, psavB = psum()
                op("pe", lambda e: e.matmul(psav[0:L, 0:128], lhsT=attnT[0:L, 0:L], rhs=vnew[0:L, :], start=True, stop=True),
                   R=[attnTB, vnewB], W=[psavB])
                op("act", lambda e: e.activation(out=qsb[0:L, :], in_=psqs[0:L, 0:128], func=AF.Copy, scale=tokB[0:L, h:h + 1]),
                   R=[psqsB, tokBB], W=[qsbB])
                op("dve", lambda e: e.tensor_tensor(out=oT[0:L, hs_], in0=qsb[0:L, :], in1=psav[0:L, 0:128], op=ALU.add),
                   R=[qsbB, psavB], W=[oTB])
                yield
                op("act", lambda e: e.activation(out=kd[0:L, :], in_=kT[0:L, hs_], func=AF.Copy, scale=tokB[0:L, 32 + h:33 + h]),
                   R=[kTB, tokBB], W=[kdB])
                psup, psupB = psum()
                op("pe", lambda e: e.matmul(psup[:, 0:128], lhsT=kd[0:L, :], rhs=vnew[0:L, :], start=True, stop=True),
                   R=[kdB, vnewB], W=[psupB])
                op("dve", lambda e: e.scalar_tensor_tensor(out=S_[:, h, :], in0=S_[:, h, :], scalar=dbc[:, h:h + 1],
                                                           in1=psup[:, 0:128], op0=ALU.mult, op1=ALU.add),
                   R=[S_B, dbcB, psupB], W=[S_B])
            run_interleaved([(lambda sl, h=h: head(h, sl)) for h in range(8)], GK)
            for h in range(8):
                op("act", lambda e, h=h: e.activation(out=sqj[0:L, :], in_=oT[0:L, h * 128:(h + 1) * 128], func=AF.Square,
                                                      accum_out=ssq[0:L, h:h + 1]), R=[oTB], W=[ssqB, sqjB])
            op("act", lambda e: e.activation(out=ssq[0:L, 8:16], in_=ssq[0:L, 0:8], func=AF.Ln, scale=1.0 / 128.0,
                                             bias=epst[0:L, 0:1]), R=[ssqB, epsB], W=[ssqB])
            op("act", lambda e: e.activation(out=ssq[0:L, 8:16], in_=ssq[0:L, 8:16], func=AF.Exp, scale=-0.5), R=[ssqB], W=[ssqB])
            for h in range(8):
                op("dve", lambda e, h=h: e.scalar_tensor_tensor(out=oT[0:L, h * 128:(h + 1) * 128],
                                                                in0=oT[0:L, h * 128:(h + 1) * 128],
                                                                scalar=ssq[0:L, 8 + h:9 + h], in1=bcn[0:L, :],
                                                                op0=ALU.mult, op1=ALU.mult), R=[oTB, ssqB, bcnB], W=[oTB])
            op("dve", lambda e: e.tensor_tensor(out=yT[0:L, :], in0=oT[0:L, :], in1=zT[0:L, :], op=ALU.mult),
               R=[oTB, zTB], W=[yTB])
            to_fm(yT, yTB, L, cs, yF[1])

        for (kind, c0, L, sidx) in chunks:
            if kind == "s":
                S_, S_B = SS, SSB
                dma("sp", S_[:, :, :], st_gs[l, sidx].rearrange("h k v -> k h v"), W=[S_B])
            else:
                S_, S_B = St, StB
            chunk(c0, L, S_, S_B)
            if kind == "s":
                dma("sp", o_sgs[l, sidx].rearrange("h k v -> k h v"), S_[:, :, :], R=[S_B], semb=S_B)


    onesblk = cst[:, CO["onesblk"]:CO["onesblk"] + 128]
    i2c = cst[:, CO["i2"]:CO["i2"] + 64]
    bdT_incl = cst[:, CO["bdT_incl"]:CO["bdT_incl"] + 128]
    bdT_strict = cst[:, CO["bdT_strict"]:CO["bdT_strict"] + 128]
    bd_strict = cst[:, CO["bd_strict"]:CO["bd_strict"] + 128]
    RZ = []
    rprev = []
    for l in range(DEPTH):
        t, b = sb("RZ%d" % l, [128, 8, 64])
        op("dve", lambda e, t=t: e.memset(t[:, :, :], 0.0), W=[b])
        RZ.append((t, b))
        t, b = sb("rprev%d" % l, [128, 26])
        op("dve", lambda e, t=t: e.memset(t[:, :], 0.0), W=[b])
        rprev.append((t, b))
    rpar = {}
    for nm in ("rwkv_w0", "rwkv_a0", "rwkv_k_k", "rwkv_k_a", "rwkv_ln_w", "rwkv_ln_b"):
        rpar[nm] = load_fm_vec("p_" + nm, W[nm], 8)
    rpar["rwkv_r_k"] = load_fm_vec("p_rk", W["rwkv_r_k"].rearrange("l h d -> l (h d)"), 8)
    rpar["rwkv_mu"] = load_fm_vec("p_mu", W["rwkv_mu"], 26)
    omka, omkaB = sb("omka", [128, DEPTH, 8])
    op("dve", lambda e: e.tensor_scalar(out=omka[:, :, :], in0=rpar["rwkv_k_a"][0][:, :, :], scalar1=-1.0, scalar2=1.0,
                                        op0=ALU.mult, op1=ALU.add), R=[rpar["rwkv_k_a"][1]], W=[omkaB])

    def rwkv_phase(l, tile):
        r0, N, chunks = tile
        Zt, ZtB = RZ[l]
        rp, rpB = rprev[l]
        w_in = W["w_in"][l]
        has_s = any(c[0] == "s" for c in chunks)
        pc0 = NS if has_s else 0
        Np = N - pc0
        P = lambda nm: rpar[nm][0]
        PB = lambda nm: rpar[nm][1]
        mu, muB = rpar["rwkv_mu"]
        xcs = [sb("r_xc%d" % i, [128, NMAX + 1]) for i in range(2)]
        dtmp, dtmpB = sb("r_dtmp", [128, NMAX])
        xwxa, xwxaB = sb("r_xwxa", [128, NMAX])
        sxg, sxgB = sb("r_sxg", [128, NMAX])
        w2p, w2pB = sb("r_w2p", [128, 1024])
        a2p, a2pB = sb("r_a2p", [128, 1024])
        g2t, g2tB = sb("r_g2", [128, 1024])
        rows = {}
        for nm in ("r", "kx", "k2", "v", "av", "bv", "ld", "g", "asig", "yraw", "t1", "t2"):
            rows[nm] = sb("r_" + nm, [128, NMAX])
        if has_s:
            xsAll, xsAllB = sb("r_xsAll", [128, 26, NS])
            prevS, prevSB = sb("r_prevS", [128, 26, NS])
            st16, st16B = sb("r_st16", [NS, 3328])
            ZS, ZSB = sb("r_ZS", [128, 64])
            ZSt, ZStB = sb("r_ZSt", [64, 128])
        opn = ("rt", "kt", "bt", "at", "kh", "bh", "Vbd", "Q0", "Q1", "P0", "P1", "TT", "MakT", "RKT", "RBT", "K2", "B2", "Y2")
        OPT = {}
        for nm in opn:
            OPT[nm] = sb("r_o_" + nm, [128, 128])
            op("dve", lambda e, t=OPT[nm][0]: e.memset(t[:, :], 0.0), W=[OPT[nm][1]])
        lw, lwB = sb("r_lw", [128, 64])
        eW, eWB = sb("r_eW", [128, 64])
        eWi, eWiB = sb("r_eWi", [128, 64])
        eWp, eWpB = sb("r_eWp", [128, 64])
        eWl, eWlB = sb("r_eWl", [128, 64])
        wls, wlsB = sb("r_wls", [128, 1])
        Vtok, VtokB = sb("r_Vtok", [128, 64])
        RHS, RHSB = sb("r_RHS", [128, 64])
        SAt, SAtB = sb("r_SA", [128, 64])
        for t_, b_ in ((Vtok, VtokB), (RHS, RHSB), (SAt, SAtB)):
            op("dve", lambda e, t_=t_: e.memset(t_[:, :], 0.0), W=[b_])
        op("dve", lambda e: e.memset(w2p[:, :], 0.0), W=[w2pB])
        op("dve", lambda e: e.memset(a2p[:, :], 0.0), W=[a2pB])
        dma("sp", w2p[0:64, :], W["rwkv_w2"][l], W=[w2pB])
        dma("sp", a2p[64:128, :], W["rwkv_a2"][l], W=[a2pB])
        dma("sp", g2t[:, :], W["rwkv_g2"][l], W=[g2tB])
        if has_s:
            dma("sp", st16[:, :], st_rshift[l], W=[st16B])
            for j in range(26):
                ps, psB = psum()
                op("pe", lambda e, j=j, ps=ps: e.transpose(out=ps[:, 0:NS], in_=st16[0:NS, j * 128:(j + 1) * 128],
                                                           identity=ident[0:NS, 0:NS]), R=[st16B, cstB], W=[psB])
                op("dve", lambda e, j=j, ps=ps: e.tensor_copy(out=prevS[:, j, :], in_=ps[:, 0:NS]), R=[psB], W=[prevSB])

        def shifted(j, ps, psB, dst, dstB):
            xc, xcB = xcs[j % 2]
            op("dve", lambda e: e.tensor_copy(out=xc[:, 0:1], in_=rp[:, j:j + 1]), R=[rpB], W=[xcB])
            op("act", lambda e: e.activation(out=xc[:, 1:1 + Np], in_=ps[:, pc0:N], func=AF.Copy), R=[psB], W=[xcB])
            op("dve", lambda e: e.tensor_tensor(out=dtmp[:, 0:Np], in0=xc[:, 0:Np], in1=xc[:, 1:1 + Np], op=ALU.subtract),
               R=[xcB], W=[dtmpB])
            op("dve", lambda e: e.scalar_tensor_tensor(out=dst[:, pc0:N], in0=dtmp[:, 0:Np], scalar=mu[:, l, j:j + 1],
                                                       in1=xc[:, 1:1 + Np], op0=ALU.mult, op1=ALU.add),
               R=[dtmpB, muB, xcB], W=[dstB])
            op("dve", lambda e: e.tensor_copy(out=rp[:, j:j + 1], in_=xc[:, Np:Np + 1]), R=[xcB], W=[rpB])
            if has_s:
                op("act", lambda e: e.activation(out=xsAll[:, j, :], in_=ps[:, 0:NS], func=AF.Copy), R=[psB], W=[xsAllB])
                op("dve", lambda e: e.tensor_tensor(out=dtmp[:, 0:NS], in0=prevS[:, j, :], in1=xsAll[:, j, :], op=ALU.subtract),
                   R=[prevSB, xsAllB], W=[dtmpB])
                op("dve", lambda e: e.scalar_tensor_tensor(out=dst[:, 0:NS], in0=dtmp[:, 0:NS], scalar=mu[:, l, j:j + 1],
                                                           in1=xsAll[:, j, :], op0=ALU.mult, op1=ALU.add),
                   R=[dtmpB, muB, xsAllB], W=[dstB])

        def cons_l24(j, ps, psB, m):
            shifted(24, ps, psB, xwxa, xwxaB)
            op("act", lambda e: e.activation(out=xwxa[0:64, 0:N], in_=xwxa[0:64, 0:N], func=AF.Tanh), R=[xwxaB], W=[xwxaB])
        proj_fm(w_in, C0 + 3072, 128, N, cons_l24)

        def cons_l25(j, ps, psB, m):
            shifted(25, ps, psB, sxg, sxgB)
            op("act", lambda e: e.activation(out=sxg[:, 0:N], in_=sxg[:, 0:N], func=AF.Sigmoid), R=[sxgB], W=[sxgB])
        proj_fm(w_in, C0 + 3200, 128, N, cons_l25)

        R_ = lambda nm: rows[nm][0]
        RB = lambda nm: rows[nm][1]
        lastL = [0]

        def chunk(jb, c0, L, Z_, Z_B, zsl):
            cs = slice(c0, c0 + L)
            nlev = {64: 5, 16: 3, 1: 0}[L]
            O = lambda nm: OPT[nm][0]
            OB = lambda nm: OPT[nm][1]
            ld_ = R_("ld")
            if lastL[0] != L:
                for nm_ in ("rt", "kt", "bt", "at", "kh", "bh", "Vbd"):
                    op("dve", lambda e, nm_=nm_: e.memset(O(nm_)[:, :], 0.0), W=[OB(nm_)])
                lastL[0] = L
            op("dve", lambda e: e.tensor_tensor_scan(out=lw[:, 0:L], data0=ld_[:, cs], data1=zcol[:, 0:L], initial=0.0,
                                                     op0=ALU.add, op1=ALU.add), R=[RB("ld"), zcolB], W=[lwB])
            op("act", lambda e: e.activation(out=eW[:, 0:L], in_=lw[:, 0:L], func=AF.Exp), R=[lwB], W=[eWB])
            op("act", lambda e: e.activation(out=eWi[:, 0:L], in_=lw[:, 0:L], func=AF.Exp, scale=-1.0), R=[lwB], W=[eWiB])
            op("dve", lambda e: e.tensor_tensor(out=eWp[:, 0:L], in0=lw[:, 0:L], in1=ld_[:, cs], op=ALU.subtract),
               R=[lwB, RB("ld")], W=[eWpB])
            op("act", lambda e: e.activation(out=eWp[:, 0:L], in_=eWp[:, 0:L], func=AF.Exp), R=[eWpB], W=[eWpB])
            op("act", lambda e: e.activation(out=eWl[:, 0:L], in_=lw[:, 0:L], func=AF.Exp, scale=-1.0, bias=lw[:, L - 1:L]),
               R=[lwB], W=[eWlB])
            op("act", lambda e: e.activation(out=wls[:, 0:1], in_=lw[:, L - 1:L], func=AF.Exp), R=[lwB], W=[wlsB])
            for (dst, srcn, fac, facB) in (("rt", "r", eW, eWB), ("kt", "k2", eWi, eWiB), ("bt", "bv", eWi, eWiB),
                                           ("at", "av", eWp, eWpB), ("kh", "k2", eWl, eWlB), ("bh", "bv", eWl, eWlB)):
                for hb in range(2):
                    pr = slice(hb * 64, hb * 64 + 64)
                    op("dve", lambda e, dst=dst, srcn=srcn, fac=fac, pr=pr, hb=hb: e.tensor_tensor(
                        out=O(dst)[pr, hb * 64:hb * 64 + L], in0=R_(srcn)[pr, cs], in1=fac[pr, 0:L], op=ALU.mult),
                       R=[RB(srcn), facB], W=[OB(dst)])
            for hb in range(2):
                pr = slice(hb * 64, hb * 64 + 64)
                op("act", lambda e, pr=pr, hb=hb: e.activation(out=O("Vbd")[pr, hb * 64:hb * 64 + L], in_=R_("v")[pr, cs],
                                                               func=AF.Copy), R=[RB("v")], W=[OB("Vbd")])
            ps, psB = psum()
            op("pe", lambda e, ps=ps: e.matmul(ps[:, 0:64], lhsT=O("Vbd")[:, :], rhs=i2c, start=True, stop=True),
               R=[OB("Vbd"), cstB], W=[psB])
            op("act", lambda e, ps=ps: e.activation(out=Vtok[:, :], in_=ps[:, 0:64], func=AF.Copy), R=[psB], W=[VtokB])

            def gram(lhs, rhs, mask, dstn):
                ps, psB = psum()
                op("pe", lambda e: e.matmul(ps[:, 0:128], lhsT=O(lhs)[:, :], rhs=O(rhs)[:, :], start=True, stop=True),
                   R=[OB(lhs), OB(rhs)], W=[psB])
                op("dve", lambda e: e.tensor_tensor(out=O(dstn)[:, :], in0=ps[:, 0:128], in1=mask, op=ALU.mult),
                   R=[psB, cstB], W=[OB(dstn)])
            gram("bt", "at", bdT_strict, "Q0")
            gram("at", "bt", bd_strict, "P0")
            gram("kt", "at", bdT_strict, "MakT")
            gram("kt", "rt", bdT_incl, "RKT")
            gram("bt", "rt", bdT_incl, "RBT")
            op("dve", lambda e: e.tensor_tensor(out=O("TT")[:, :], in0=O("Q0")[:, :], in1=ident, op=ALU.add),
               R=[OB("Q0"), cstB], W=[OB("TT")])
            cp, cq, np_, nq = "P0", "Q0", "P1", "Q1"
            for lev in range(nlev):
                psp, pspB = psum()
                op("pe", lambda e, cp=cp, cq=cq, psp=psp: e.matmul(psp[:, 0:128], lhsT=O(cq)[:, :], rhs=O(cp)[:, :],
                                                                  start=True, stop=True), R=[OB(cp), OB(cq)], W=[pspB])
                psq, psqB = psum()
                op("pe", lambda e, cp=cp, cq=cq, psq=psq: e.matmul(psq[:, 0:128], lhsT=O(cp)[:, :], rhs=O(cq)[:, :],
                                                                  start=True, stop=True), R=[OB(cp), OB(cq)], W=[psqB])
                op("act", lambda e, np_=np_, psp=psp: e.activation(out=O(np_)[:, :], in_=psp[:, 0:128], func=AF.Copy),
                   R=[pspB], W=[OB(np_)])
                op("dve", lambda e, nq=nq, psq=psq: e.tensor_copy(out=O(nq)[:, :], in_=psq[:, 0:128]), R=[psqB], W=[OB(nq)])
                pst, pstB = psum()
                op("pe", lambda e, np_=np_, pst=pst: e.matmul(pst[:, 0:128], lhsT=O(np_)[:, :], rhs=O("TT")[:, :],
                                                              start=True, stop=True), R=[OB(np_), OB("TT")], W=[pstB])
                op("dve", lambda e, pst=pst: e.tensor_tensor(out=O("TT")[:, :], in0=O("TT")[:, :], in1=pst[:, 0:128],
                                                             op=ALU.add), R=[OB("TT"), pstB], W=[OB("TT")])
                cp, cq, np_, nq = np_, nq, cp, cq
            ps, psB = psum()
            op("pe", lambda e, ps=ps: e.matmul(ps[:, 0:64], lhsT=O("at")[:, :], rhs=zsl, start=True, stop=False),
               R=[OB("at"), Z_B], W=[psB])
            op("pe", lambda e, ps=ps: e.matmul(ps[:, 0:64], lhsT=O("MakT")[:, :], rhs=Vtok[:, :], start=False, stop=True),
               R=[OB("MakT"), VtokB], W=[psB])
            op("act", lambda e, ps=ps: e.activation(out=RHS[:, :], in_=ps[:, 0:64], func=AF.Copy), R=[psB], W=[RHSB])
            ps, psB = psum()
            op("pe", lambda e, ps=ps: e.matmul(ps[:, 0:64], lhsT=O("TT")[:, :], rhs=RHS[:, :], start=True, stop=True),
               R=[OB("TT"), RHSB], W=[psB])
            op("dve", lambda e, ps=ps: e.tensor_copy(out=SAt[:, :], in_=ps[:, 0:64]), R=[psB], W=[SAtB])
            psy, psyB = psum()
            op("pe", lambda e: e.matmul(psy[:, 0:64], lhsT=O("rt")[:, :], rhs=zsl, start=True, stop=False),
               R=[OB("rt"), Z_B], W=[psyB])
            op("pe", lambda e: e.matmul(psy[:, 0:64], lhsT=O("RKT")[:, :], rhs=Vtok[:, :], start=False, stop=False),
               R=[OB("RKT"), VtokB], W=[psyB])
            op("pe", lambda e: e.matmul(psy[:, 0:64], lhsT=O("RBT")[:, :], rhs=SAt[:, :], start=False, stop=True),
               R=[OB("RBT"), SAtB], W=[psyB])
            op("act", lambda e: e.activation(out=O("Y2")[0:64, 0:64], in_=psy[0:64, 0:64], func=AF.Copy), R=[psyB], W=[OB("Y2")])
            op("dve", lambda e: e.tensor_copy(out=O("Y2")[64:128, 64:128], in_=psy[64:128, 0:64]), R=[psyB], W=[OB("Y2")])
            ps, psB = psum()
            op("pe", lambda e, ps=ps: e.matmul(ps[:, 0:64], lhsT=O("Y2")[:, :], rhs=i2c, start=True, stop=True),
               R=[OB("Y2"), cstB], W=[psB])
            op("act", lambda e, ps=ps: e.activation(out=R_("yraw")[:, cs], in_=ps[:, 0:L], func=AF.Copy), R=[psB], W=[RB("yraw")])
            for (srcn, dstn) in (("kh", "K2"), ("bh", "B2")):
                ps, psB = psum()
                op("pe", lambda e, ps=ps, srcn=srcn: e.transpose(out=ps[:, 0:128], in_=O(srcn)[:, :], identity=ident),
                   R=[OB(srcn), cstB], W=[psB])
                op("act" if srcn == "kh" else "dve",
                   (lambda e, ps=ps, dstn=dstn: e.activation(out=O(dstn)[:, :], in_=ps[:, 0:128], func=AF.Copy)) if srcn == "kh" else
                   (lambda e, ps=ps, dstn=dstn: e.tensor_copy(out=O(dstn)[:, :], in_=ps[:, 0:128])), R=[psB], W=[OB(dstn)])
            psz, pszB = psum()
            op("pe", lambda e: e.matmul(psz[:, 0:64], lhsT=O("K2")[:, :], rhs=Vtok[:, :], start=True, stop=False),
               R=[OB("K2"), VtokB], W=[pszB])
            op("pe", lambda e: e.matmul(psz[:, 0:64], lhsT=O("B2")[:, :], rhs=SAt[:, :], start=False, stop=True),
               R=[OB("B2"), SAtB], W=[pszB])
            op("dve", lambda e: e.scalar_tensor_tensor(out=zsl, in0=zsl, scalar=wls[:, 0:1], in1=psz[:, 0:64],
                                                       op0=ALU.mult, op1=ALU.add), R=[Z_B, wlsB, pszB], W=[Z_B])

        for jb in range(8):
            def mk(dstn):
                def f(j, ps, psB, m):
                    return None
                return f
            for (blk, dstn) in ((jb, "r"), (8 + jb, "kx"), (16 + jb, "v")):
                def cons(j, ps, psB, m, blk=blk, dstn=dstn):
                    shifted(blk, ps, psB, R_(dstn), RB(dstn))
                proj_fm(w_in, C0 + blk * 128, 128, N, cons)
            bsl = slice(jb * 128, (jb + 1) * 128)
            ps, psB = psum()
            op("pe", lambda e, ps=ps: e.matmul(ps[:, 0:N], lhsT=w2p[:, bsl], rhs=xwxa[:, 0:N], start=True, stop=True),
               R=[w2pB, xwxaB], W=[psB])
            op("act", lambda e, ps=ps: e.activation(out=R_("ld")[:, 0:N], in_=ps[:, 0:N], func=AF.Sigmoid,
                                                    bias=P("rwkv_w0")[:, l, jb:jb + 1]), R=[psB, PB("rwkv_w0")], W=[RB("ld")])
            op("dve", lambda e: e.tensor_scalar(out=R_("ld")[:, 0:N], in0=R_("ld")[:, 0:N], scalar1=-float(np.exp(-0.5)),
                                                scalar2=None, op0=ALU.mult), R=[RB("ld")], W=[RB("ld")])
            ps, psB = psum()
            op("pe", lambda e, ps=ps: e.matmul(ps[:, 0:N], lhsT=a2p[:, bsl], rhs=xwxa[:, 0:N], start=True, stop=True),
               R=[a2pB, xwxaB], W=[psB])
            op("act", lambda e, ps=ps: e.activation(out=R_("asig")[:, 0:N], in_=ps[:, 0:N], func=AF.Sigmoid,
                                                    bias=P("rwkv_a0")[:, l, jb:jb + 1]), R=[psB, PB("rwkv_a0")], W=[RB("asig")])
            ps, psB = psum()
            op("pe", lambda e, ps=ps: e.matmul(ps[:, 0:N], lhsT=g2t[:, bsl], rhs=sxg[:, 0:N], start=True, stop=True),
               R=[g2tB, sxgB], W=[psB])
            op("act", lambda e, ps=ps: e.activation(out=R_("g")[:, 0:N], in_=ps[:, 0:N], func=AF.Copy), R=[psB], W=[RB("g")])
            op("dve", lambda e: e.tensor_scalar(out=R_("t1")[:, 0:N], in0=R_("kx")[:, 0:N], scalar1=P("rwkv_k_k")[:, l, jb:jb + 1],
                                                scalar2=None, op0=ALU.mult), R=[RB("kx"), PB("rwkv_k_k")], W=[RB("t1")])
            op("act", lambda e: e.activation(out=R_("t2")[:, 0:N], in_=R_("t1")[:, 0:N], func=AF.Square), R=[RB("t1")], W=[RB("t2")])
            ps, psB = psum()
            op("pe", lambda e, ps=ps: e.matmul(ps[:, 0:N], lhsT=onesblk, rhs=R_("t2")[:, 0:N], start=True, stop=True),
               R=[RB("t2"), cstB], W=[psB])
            op("act", lambda e, ps=ps: e.activation(out=R_("t2")[:, 0:N], in_=ps[:, 0:N], func=AF.Ln, bias=epst[:, 1:2]),
               R=[psB, epsB], W=[RB("t2")])
            op("act", lambda e: e.activation(out=R_("t2")[:, 0:N], in_=R_("t2")[:, 0:N], func=AF.Exp, scale=-0.5),
               R=[RB("t2")], W=[RB("t2")])
            op("dve", lambda e: e.scalar_tensor_tensor(out=R_("av")[:, 0:N], in0=R_("t1")[:, 0:N], scalar=-1.0, in1=R_("t2")[:, 0:N],
                                                       op0=ALU.mult, op1=ALU.mult), R=[RB("t1"), RB("t2")], W=[RB("av")])
            op("dve", lambda e: e.scalar_tensor_tensor(out=R_("bv")[:, 0:N], in0=R_("av")[:, 0:N], scalar=-1.0, in1=R_("asig")[:, 0:N],
                                                       op0=ALU.mult, op1=ALU.mult), R=[RB("av"), RB("asig")], W=[RB("bv")])
            op("dve", lambda e: e.tensor_scalar(out=R_("t1")[:, 0:N], in0=R_("asig")[:, 0:N], scalar1=P("rwkv_k_a")[:, l, jb:jb + 1],
                                                scalar2=omka[:, l, jb:jb + 1], op0=ALU.mult, op1=ALU.add),
               R=[RB("asig"), PB("rwkv_k_a"), omkaB], W=[RB("t1")])
            op("dve", lambda e: e.tensor_tensor(out=R_("k2")[:, 0:N], in0=R_("kx")[:, 0:N], in1=R_("t1")[:, 0:N], op=ALU.mult),
               R=[RB("kx"), RB("t1")], W=[RB("k2")])
            for (kind, c0, L, sidx) in chunks:
                if kind == "s":
                    dma("sp", ZSt[:, :].rearrange("v (h k) -> v h k", k=64),
                        st_rs[l, sidx, 2 * jb:2 * jb + 2].rearrange("h v k -> v h k"), W=[ZStB])
                    ps, psB = psum()
                    op("pe", lambda e, ps=ps: e.transpose(out=ps[:, 0:64], in_=ZSt[:, :], identity=ident[0:64, 0:64]),
                       R=[ZStB, cstB], W=[psB])
                    op("act", lambda e, ps=ps: e.activation(out=ZS[:, :], in_=ps[:, 0:64], func=AF.Copy), R=[psB], W=[ZSB])
                    chunk(jb, c0, L, ZS, ZSB, ZS[:, :])
                    ps, psB = psum()
                    op("pe", lambda e, ps=ps: e.transpose(out=ps[0:64, 0:128], in_=ZS[:, :], identity=ident), R=[ZSB, cstB], W=[psB])
                    op("act", lambda e, ps=ps: e.activation(out=ZSt[:, :], in_=ps[0:64, 0:128], func=AF.Copy), R=[psB], W=[ZStB])
                    dma("sp", o_srs[l, sidx, 2 * jb:2 * jb + 2].rearrange("h v k -> v h k"),
                        ZSt[:, :].rearrange("v (h k) -> v h k", k=64), R=[ZStB], semb=ZStB)
                else:
                    chunk(jb, c0, L, Zt, ZtB, Zt[:, jb, :])
            ps, psB = psum()
            op("pe", lambda e, ps=ps: e.matmul(ps[:, 0:N], lhsT=onesblk, rhs=R_("yraw")[:, 0:N], start=True, stop=True),
               R=[RB("yraw"), cstB], W=[psB])
            op("dve", lambda e, ps=ps: e.scalar_tensor_tensor(out=R_("t1")[:, 0:N], in0=ps[:, 0:N], scalar=-1.0 / 64.0,
                                                              in1=R_("yraw")[:, 0:N], op0=ALU.mult, op1=ALU.add),
               R=[psB, RB("yraw")], W=[RB("t1")])
            op("act", lambda e: e.activation(out=R_("t2")[:, 0:N], in_=R_("t1")[:, 0:N], func=AF.Square), R=[RB("t1")], W=[RB("t2")])
            ps, psB = psum()
            op("pe", lambda e, ps=ps: e.matmul(ps[:, 0:N], lhsT=onesblk, rhs=R_("t2")[:, 0:N], start=True, stop=True),
               R=[RB("t2"), cstB], W=[psB])
            op("act", lambda e, ps=ps: e.activation(out=R_("t2")[:, 0:N], in_=ps[:, 0:N], func=AF.Ln, scale=1.0 / 64.0,
                                                    bias=epst[:, 2:3]), R=[psB, epsB], W=[RB("t2")])
            op("act", lambda e: e.activation(out=R_("t2")[:, 0:N], in_=R_("t2")[:, 0:N], func=AF.Exp, scale=-0.5),
               R=[RB("t2")], W=[RB("t2")])
            op("dve", lambda e: e.tensor_tensor(out=R_("t1")[:, 0:N], in0=R_("t1")[:, 0:N], in1=R_("t2")[:, 0:N], op=ALU.mult),
               R=[RB("t1"), RB("t2")], W=[RB("t1")])
            op("dve", lambda e: e.tensor_scalar(out=R_("t1")[:, 0:N], in0=R_("t1")[:, 0:N], scalar1=P("rwkv_ln_w")[:, l, jb:jb + 1],
                                                scalar2=P("rwkv_ln_b")[:, l, jb:jb + 1], op0=ALU.mult, op1=ALU.add),
               R=[RB("t1"), PB("rwkv_ln_w"), PB("rwkv_ln_b")], W=[RB("t1")])
            op("dve", lambda e: e.scalar_tensor_tensor(out=R_("t2")[:, 0:N], in0=R_("r")[:, 0:N], scalar=P("rwkv_r_k")[:, l, jb:jb + 1],
                                                       in1=R_("k2")[:, 0:N], op0=ALU.mult, op1=ALU.mult),
               R=[RB("r"), RB("k2"), PB("rwkv_r_k")], W=[RB("t2")])
            ps, psB = psum()
            op("pe", lambda e, ps=ps: e.matmul(ps[:, 0:N], lhsT=onesblk, rhs=R_("t2")[:, 0:N], start=True, stop=True),
               R=[RB("t2"), cstB], W=[psB])
            op("dve", lambda e, ps=ps: e.tensor_tensor(out=R_("t2")[:, 0:N], in0=ps[:, 0:N], in1=R_("v")[:, 0:N], op=ALU.mult),
               R=[psB, RB("v")], W=[RB("t2")])
            op("dve", lambda e: e.tensor_tensor(out=R_("t1")[:, 0:N], in0=R_("t1")[:, 0:N], in1=R_("t2")[:, 0:N], op=ALU.add),
               R=[RB("t1"), RB("t2")], W=[RB("t1")])
            yct, ycB_ = yF[2]
            op("dve", lambda e: e.tensor_tensor(out=yct[:, jb, 0:N], in0=R_("t1")[:, 0:N], in1=R_("g")[:, 0:N], op=ALU.mult),
               R=[RB("t1"), RB("g")], W=[ycB_])
        if has_s:
            for q in range(7):
                ps, psB = psum()
                nb = min(4, 26 - q * 4)
                for kk in range(nb):
                    k = q * 4 + kk
                    op("pe", lambda e, k=k, kk=kk, ps=ps: e.transpose(out=ps[0:NS, kk * 128:(kk + 1) * 128], in_=xsAll[:, k, :],
                                                                     identity=ident), R=[xsAllB, cstB], W=[psB])
                op("dve", lambda e, q=q, ps=ps, nb=nb: e.tensor_copy(out=st16[0:NS, q * 512:q * 512 + nb * 128],
                                                                    in_=ps[0:NS, 0:nb * 128]), R=[psB], W=[st16B])
            dma("sp", o_srshift[l], st16[0:NS, :], R=[st16B], semb=st16B)

    def zero_y(which, N):
        dt, dB = yF[which]
        op("dve", lambda e: e.memset(dt[:, :, 0:N], 0.0), W=[dB])

    def merge_phase(l, tile):
        r0, N, chunks = tile
        gt = [sb("gate%d" % i, [128, NMAX]) for i in range(3)]
        mrg, mrgB = sb("mrgF", [128, KC, NMAX], BF16)
        mtmp, mtmpB = sb("mtmp", [128, NMAX])
        w_in = W["w_in"][l]
        brw = [W["w_branch_a"][l], W["w_branch_b"][l], W["w_branch_c"][l]]
        for j in range(KC):
            for g in range(3):
                wv, wB = load_w(w_in, G0 + g * D + j * 128, 128, KC)
                ps, psB = psum()
                for k in range(KC):
                    op("pe", lambda e, k=k: e.matmul(ps[:, 0:N], lhsT=wv[:, k, :], rhs=xnF[:, k, 0:N], start=(k == 0),
                                                     stop=(k == KC - 1)), R=[wB, xnFB], W=[psB])
                g_t, g_B = gt[g]
                op("act", lambda e: e.activation(out=g_t[:, 0:N], in_=ps[:, 0:N], func=AF.Sigmoid), R=[psB], W=[g_B])
            for g in range(3):
                wv, wB = load_w(brw[g], j * 128, 128, 8)
                ps, psB = psum()
                yt, yB = yF[g]
                for k in range(8):
                    op("pe", lambda e, k=k: e.matmul(ps[:, 0:N], lhsT=wv[:, k, :], rhs=yt[:, k, 0:N], start=(k == 0),
                                                     stop=(k == 7)), R=[wB, yB], W=[psB])
                g_t, g_B = gt[g]
                if g == 0:
                    op("dve", lambda e: e.tensor_tensor(out=mtmp[:, 0:N], in0=ps[:, 0:N], in1=g_t[:, 0:N], op=ALU.mult),
                       R=[psB, g_B], W=[mtmpB])
                else:
                    op("dve", lambda e: e.tensor_tensor(out=g_t[:, 0:N], in0=ps[:, 0:N], in1=g_t[:, 0:N], op=ALU.mult),
                       R=[psB, g_B], W=[g_B])
                    if g == 1:
                        op("dve", lambda e: e.tensor_tensor(out=mtmp[:, 0:N], in0=mtmp[:, 0:N], in1=g_t[:, 0:N], op=ALU.add),
                           R=[mtmpB, g_B], W=[mtmpB])
                    else:
                        op("dve", lambda e: e.tensor_tensor(out=mrg[:, j, 0:N], in0=mtmp[:, 0:N], in1=g_t[:, 0:N], op=ALU.add),
                           R=[mtmpB, g_B], W=[mrgB])

        def cons_res(j, ps, psB, m):
            op("dve", lambda e: e.tensor_tensor(out=xF[:, j, 0:N], in0=xF[:, j, 0:N], in1=ps[:, 0:N], op=ALU.add),
               R=[xFB, psB], W=[xFB])
        proj_fm(W["w_out"][l], 0, D, N, cons_res, rhs_fn=lambda k: mrg[:, k, 0:N], rhsB=mrgB)

    def mlp_phase(l, tile):
        r0, N, chunks = tile
        mtmp, mtmpB = sb("mtmp2", [128, NMAX])
        uF, uFB = sb("uF", [128, 64, NMAX], BF16)
        rmsnorm(N, lambda k: nw_mlp[:, l, k:k + 1], nw_mlpB)

        def cons_up(j, ps, psB, m):
            op("act", lambda e: e.activation(out=mtmp[:, 0:N], in_=ps[:, 0:N], func=AF.Relu), R=[psB], W=[mtmpB])
            op("dve", lambda e: e.tensor_tensor(out=uF[:, j, 0:N], in0=mtmp[:, 0:N], in1=mtmp[:, 0:N], op=ALU.mult),
               R=[mtmpB], W=[uFB])
        proj_fm(W["w_up"][l], 0, 4 * D, N, cons_up)
        wd = W["w_down"][l]
        for j in range(KC):
            ps, psB = psum()
            for q in range(4):
                wv, wB = load_w(wd[q * 2048:(q + 1) * 2048, :], j * 128, 128, 16)
                for k in range(16):
                    kk = q * 16 + k
                    op("pe", lambda e, k=k, kk=kk, wv=wv: e.matmul(ps[:, 0:N], lhsT=wv[:, k, :], rhs=uF[:, kk, 0:N],
                                                                  start=(kk == 0), stop=(kk == 63)), R=[wB, uFB], W=[psB])
            op("dve", lambda e, j=j: e.tensor_tensor(out=xF[:, j, 0:N], in0=xF[:, j, 0:N], in1=ps[:, 0:N], op=ALU.add),
               R=[xFB, psB], W=[xFB])

    tiles = tile_plan()
    for ti, tile in enumerate(tiles):
        r0, N, chunks = tile
        with Arena():
            stage_tok, stage_tokB = sb("stage_in", [128, D])
            rr = 0
            while rr < N:
                m = min(128, N - rr)
                dma("sp", stage_tok[0:m, :], xin[r0 + rr:r0 + rr + m, :], W=[stage_tokB])
                for kq in range(4):
                    ps, psB = psum()
                    for kk in range(4):
                        k = kq * 4 + kk
                        op("pe", lambda e, k=k, kk=kk, ps=ps: e.transpose(out=ps[:, kk * 128:kk * 128 + m],
                                                                         in_=stage_tok[0:m, k * 128:(k + 1) * 128],
                                                                         identity=ident[0:m, 0:m]), R=[stage_tokB, cstB], W=[psB])
                    for kk in range(4):
                        k = kq * 4 + kk
                        if kk % 2 == 0:
                            op("act", lambda e, k=k, kk=kk, ps=ps: e.activation(out=xF[:, k, rr:rr + m],
                                                                               in_=ps[:, kk * 128:kk * 128 + m], func=AF.Copy),
                               R=[psB], W=[xFB])
                        else:
                            op("dve", lambda e, k=k, kk=kk, ps=ps: e.tensor_copy(out=xF[:, k, rr:rr + m],
                                                                                in_=ps[:, kk * 128:kk * 128 + m]),
                               R=[psB], W=[xFB])
                rr += m
        for l in range(DEPTH):
            rmsnorm(N, lambda k: nw_mix[:, l, k:k + 1], nw_mixB)
            with Arena():
                mlstm_phase(l, tile)
            with Arena():
                gdn_phase(l, tile)
            with Arena():
                rwkv_phase(l, tile)
            with Arena():
                merge_phase(l, tile)
            with Arena():
                mlp_phase(l, tile)
        rmsnorm(N, lambda k: nw_fin[:, k:k + 1], nw_finB, inplace=True)
        with Arena():
            stage_tok, stage_tokB = sb("stage_out", [128, D])
            rr = 0
            while rr < N:
                m = min(128, N - rr)
                for kq in range(4):
                    ps, psB = psum()
                    for kk in range(4):
                        k = kq * 4 + kk
                        op("pe", lambda e, k=k, kk=kk, ps=ps: e.transpose(out=ps[0:m, kk * 128:(kk + 1) * 128],
                                                                         in_=xF[:, k, rr:rr + m], identity=ident),
                           R=[xFB, cstB], W=[psB])
                    if kq % 2 == 0:
                        op("act", lambda e, kq=kq, ps=ps: e.activation(out=stage_tok[0:m, kq * 512:(kq + 1) * 512],
                                                                      in_=ps[0:m, 0:512], func=AF.Copy), R=[psB], W=[stage_tokB])
                    else:
                        op("dve", lambda e, kq=kq, ps=ps: e.tensor_copy(out=stage_tok[0:m, kq * 512:(kq + 1) * 512],
                                                                       in_=ps[0:m, 0:512]), R=[psB], W=[stage_tokB])
                dma("sp", y_out[r0 + rr:r0 + rr + m, :], stage_tok[0:m, :], R=[stage_tokB], semb=stage_tokB)
                rr += m

    for l in range(DEPTH):
        Ct, CtB = Cext[l]
        for h in range(4):
            dma("sp", o_pc[l, h].rearrange("(kc p) v -> p kc v", p=128), Ct[:, h, :, 0:256], R=[CtB], semb=CtB)
        dma("sp", o_pn[l].rearrange("h (kc p o) -> p h kc o", p=128, o=1), Ct[:, :, :, 256:257], R=[CtB], semb=CtB, slow=True)
        outbufs.append(CtB)
        mP, mPB = mprev[l]
        dma("sp", o_pm[l:l + 1, :].rearrange("o h -> h o"), mP[:, 0:1], R=[mPB], semb=mPB, slow=True)
        outbufs.append(mPB)

    for l in range(DEPTH):
        St, StB = GS[l]
        dma("sp", o_pgs[l].rearrange("h k v -> k h v"), St[:, :, :], R=[StB], semb=StB)
        outbufs.append(StB)
        hs, hsB = ghist[l]
        for t in range(3):
            dma("sp", o_pgconv[l][t].rearrange("(k p) -> p k", p=128), hs[:, :, t], R=[hsB], semb=hsB, slow=True)
        outbufs.append(hsB)
    with Arena():
        zt2, zt2B = sb("zt_out", [64, 8, 128])
        for l in range(DEPTH):
            Zt, ZtB = RZ[l]
            for jb in range(8):
                ps, psB = psum()
                op("pe", lambda e, jb=jb, ps=ps: e.transpose(out=ps[0:64, 0:128], in_=Zt[:, jb, :], identity=ident),
                   R=[ZtB, cstB], W=[psB])
                op("act", lambda e, jb=jb, ps=ps: e.activation(out=zt2[:, jb, :], in_=ps[0:64, 0:128], func=AF.Copy),
                   R=[psB], W=[zt2B])
            dma("sp", o_prs[l].rearrange("h v k -> v h k"), zt2[:, :, :].rearrange("v j (h k) -> v (j h) k", k=64),
                R=[zt2B], semb=zt2B)
            rp, rpB = rprev[l]
            dma("sp", o_prshift[l].rearrange("(k p) -> p k", p=128), rp[:, :], R=[rpB], semb=rpB, slow=True)
            outbufs.append(rpB)
        kb.finish([zt2B])
    kb.finish(outbufs)
    print("[kernel] instructions:", kb.n_inst, "dma sems:", kb.nd, "sbuf left:", nc.sbuf_bytes_remaining)


def _consts():
    cols = {}
    parts = []
    off = [0]

    def add(name, arr):
        a = np.zeros((128, arr.shape[1]), np.float32)
        a[:arr.shape[0]] = arr
        cols[name] = off[0]
        off[0] += arr.shape[1]
        parts.append(a)
    add("ident", np.eye(128, dtype=np.float32))
    add("ones", np.ones((128, 128), np.float32))
    ob = np.zeros((128, 128), np.float32)
    ob[:64, :64] = 1
    ob[64:, 64:] = 1
    add("onesblk", ob)
    s = np.arange(64)[:, None]
    l_ = np.arange(64)[None, :]
    NEG = -30000.0
    add("nmT_incl", np.where(s <= l_, 0.0, NEG).astype(np.float32))
    add("nmT_strict", np.where(s < l_, 0.0, NEG).astype(np.float32))
    add("nm_strict", np.where(l_ < s, 0.0, NEG).astype(np.float32))
    add("mT_incl", (s <= l_).astype(np.float32))
    add("mT_strict", (s < l_).astype(np.float32))
    add("m_strict", (l_ < s).astype(np.float32))
    sel = np.zeros((8, 8, 64), np.float32)
    for h in range(8):
        sel[h, h, :] = 1
    add("selh", sel.reshape(8, 512))
    sl = np.zeros((64, 3, 128), np.float32)
    sl[63, 0, :] = 1
    sl[15, 1, :] = 1
    sl[0, 2, :] = 1
    add("sellast", sl.reshape(64, 384))
    i2 = np.concatenate([np.eye(64, dtype=np.float32)] * 2, axis=0)
    add("i2", i2)
    t2 = lambda m: np.tile(m, (2, 2)).astype(np.float32)
    add("bdT_incl", t2(s <= l_))
    add("bdT_strict", t2(s < l_))
    add("bd_strict", t2(l_ < s))
    return np.concatenate(parts, axis=1), cols


CONSTS, CO = _consts()
CONST_COLS = CONSTS.shape[1]

PARAM_SHAPES = {
    "norm_mix_w": (2, 2048), "w_in": (2, 2048, 17688), "mlstm_b_i": (2, 4), "mlstm_b_f": (2, 4),
    "mlstm_norm_w": (2, 1024), "gdn_conv_w": (2, 4, 3072), "gdn_a_log": (2, 8), "gdn_dt_bias": (2, 8),
    "gdn_norm_w": (2, 128), "rwkv_mu": (2, 3328), "rwkv_w0": (2, 1024), "rwkv_w2": (2, 64, 1024),
    "rwkv_a0": (2, 1024), "rwkv_a2": (2, 64, 1024), "rwkv_g2": (2, 128, 1024), "rwkv_k_k": (2, 1024),
    "rwkv_k_a": (2, 1024), "rwkv_r_k": (2, 16, 64), "rwkv_ln_w": (2, 1024), "rwkv_ln_b": (2, 1024),
    "w_branch_a": (2, 1024, 2048), "w_branch_b": (2, 1024, 2048), "w_branch_c": (2, 1024, 2048),
    "w_out": (2, 2048, 2048), "norm_mlp_w": (2, 2048), "w_up": (2, 2048, 8192), "w_down": (2, 8192, 2048),
    "final_norm_w": (2048,),
}

_NC = [None]


def kernel(**inp):
    f = lambda a: np.ascontiguousarray(np.asarray(a, dtype=np.float32))
    if _NC[0] is None:
        _NC[0] = build()
    nc = _NC[0]
    params = {k: f(inp[k]) for k in PARAM_SHAPES}
    in_maps = []
    for c in range(8):
        sl = slice(c * NS, (c + 1) * NS)
        m = dict(params)
        m["xin"] = np.ascontiguousarray(np.concatenate(
            [f(inp["x_sample"])[sl, 0], f(inp["meta_tokens"]), f(inp["x_prompt"])[c % 4]], axis=0))
        m["consts"] = CONSTS
        m["st_c"] = f(inp["state_mlstm_c"])[:, sl]
        m["st_n"] = f(inp["state_mlstm_n"])[:, sl]
        m["st_m"] = f(inp["state_mlstm_m"])[:, sl]
        m["st_gs"] = f(inp["state_gdn_s"])[:, sl]
        m["st_gconv"] = f(inp["state_gdn_conv"])[:, sl]
        m["st_rs"] = f(inp["state_rwkv_s"])[:, sl]
        m["st_rshift"] = f(inp["state_rwkv_shift"])[:, sl]
        m = {k: np.ascontiguousarray(v) for k, v in m.items()}
        in_maps.append(m)
    res = run_bass_kernel_spmd(nc, in_maps, core_ids=list(range(8)))
    R = res.results
    y = [r["y"] for r in R]
    y_prompt = np.stack([y[c][NS + NMETA:] for c in range(4)], axis=0)
    y_sample = np.concatenate([y[c][0:NS] for c in range(8)], axis=0)[:, None, :]
    pk = lambda k: np.stack([R[c][k] for c in range(4)], axis=1)
    sk = lambda k: np.concatenate([R[c][k] for c in range(8)], axis=1)
    outs = (y_prompt, y_sample, pk("p_c"), pk("p_n"), pk("p_m"), pk("p_gs"), pk("p_gconv"), pk("p_rs"), pk("p_rshift"),
            sk("s_c"), sk("s_n"), sk("s_m"), sk("s_gs"), sk("s_gconv"), sk("s_rs"), sk("s_rshift"))
    return tuple(np.ascontiguousarray(o, dtype=np.float32) for o in outs)
```

```python
import numpy as np
import concourse.bass as bass
import concourse.mybir as mybir
from concourse.bass_utils import run_bass_kernel_spmd
from contextlib import ExitStack

F32 = mybir.dt.float32
BF16 = mybir.dt.bfloat16
AF = mybir.ActivationFunctionType
ALU = mybir.AluOpType
AX = mybir.AxisListType

D = 2048
KC = 16
NS = 16
NMETA = 16
SEQ = 2048
NROWS = NS + NMETA + SEQ
N_IN = 17688
DEPTH = 2
A0, B0, C0, G0 = 0, 4104, 8216, 11544
STAGES = ("mlstm", "gdn", "rwkv")


class Buf:
    __slots__ = ("name", "lw", "rd", "dsem", "dcnt")

    def __init__(self, name):
        self.name = name
        self.lw = None
        self.rd = {}
        self.dsem = None
        self.dcnt = 0


class Eng:
    def __init__(self, name, obj, sem):
        self.name = name
        self.obj = obj
        self.sem = sem
        self.cnt = 0
        self.known = {}


class KB:
    def __init__(self, nc, es):
        self.nc = nc
        self.es = es
        self.E = {}
        for nm, obj in (("pe", nc.tensor), ("act", nc.scalar), ("dve", nc.vector), ("pool", nc.gpsimd),
                        ("sp", nc.sync)):
            self.E[nm] = Eng(nm, obj, es.enter_context(nc.semaphore("sem_" + nm)))
        self.semid = {}
        self.dpool = []
        self.nd = 0
        self.out_tokens = {}
        self.n_inst = 0

    def buf(self, name):
        return Buf(name)

    def _dsem(self, b):
        if b.dsem is None:
            if self.dpool:
                b.dsem, b.dcnt = self.dpool.pop()
            else:
                b.dsem = self.es.enter_context(self.nc.semaphore("d%d" % self.nd))
                self.nd += 1
        return b.dsem

    def recycle(self, bufs):
        for b in bufs:
            if b.dsem is not None:
                self.dpool.append((b.dsem, b.dcnt))
                b.dsem = None

    def _waits(self, e, R, W):
        deps = {}
        for b in R:
            if b.lw is not None:
                s, v = b.lw
                if deps.get(s, 0) < v:
                    deps[s] = v
        for b in W:
            if b.lw is not None:
                s, v = b.lw
                if deps.get(s, 0) < v:
                    deps[s] = v
            for s, v in b.rd.items():
                if deps.get(s, 0) < v:
                    deps[s] = v
        for s, v in deps.items():
            if s is e.sem:
                if e.name == "pe" or e.name == "sp":
                    continue
                if e.cnt - v >= 2:
                    continue
            if e.known.get(s, 0) >= v:
                continue
            e.obj.wait_ge(s, v)
            e.known[s] = v
            self.n_inst += 1

    def op(self, en, fn, R=(), W=()):
        e = self.E[en]
        if e.cnt >= 30000:
            e.sem = self.es.enter_context(self.nc.semaphore("sem_%s_%d" % (en, self.n_inst)))
            e.cnt = 0
        self._waits(e, R, W)
        ins = fn(e.obj)
        e.cnt += 1
        ins.then_inc(e.sem, 1)
        tok = (e.sem, e.cnt)
        for b in R:
            if b.rd.get(tok[0], 0) < tok[1]:
                b.rd[tok[0]] = tok[1]
        for b in W:
            b.lw = tok
            b.rd = {}
        self.n_inst += 1
        return ins

    def dma(self, q, out, in_, R=(), W=(), semb=None, slow=False):
        e = self.E[q]
        self._waits(e, R, W)
        if semb is None:
            semb = W[0] if W else R[0]
        s = self._dsem(semb)
        if slow:
            ins = e.obj.dma_start(out=out, in_=in_, allow_slow_non_contiguous=True)
        else:
            ins = e.obj.dma_start(out=out, in_=in_)
        semb.dcnt += 16
        ins.then_inc(s, 16)
        tok = (s, semb.dcnt)
        for b in R:
            if b.rd.get(s, 0) < tok[1]:
                b.rd[s] = tok[1]
        for b in W:
            b.lw = tok
            b.rd = {}
        self.n_inst += 1
        return tok

    def barrier(self, bufs=()):
        bufs = list(bufs)
        for nm in ("pe", "act", "dve", "pool", "sp"):
            self._waits(self.E[nm], [], bufs)

    def finish(self, bufs):
        e = self.E["sp"]
        self._waits(e, list(bufs), list(bufs))


def build():
    nc = bass.Bass("TRN2", target_bir_lowering=False)
    es = ExitStack()
    with es:
        _build(nc, es)
    return nc


def tile_plan():
    tiles = []
    ch = [("s", b, 1, b) for b in range(NS)]
    ch.append(("p", NS, NMETA, 0))
    for j in range(5):
        ch.append(("p", NS + NMETA + 64 * j, 64, 0))
    tiles.append((0, NS + NMETA + 320, ch))
    r0 = NS + NMETA + 320
    for nchk in (6, 6, 6, 6, 3):
        ch = [("p", 64 * j, 64, 0) for j in range(nchk)]
        tiles.append((r0, 64 * nchk, ch))
        r0 += 64 * nchk
    assert r0 == NROWS
    return tiles


NMAX = 384
NWB = 8


def _build(nc, es):
    kb = KB(nc, es)
    op, dma = kb.op, kb.dma

    def dram_in(name, shape):
        return nc.dram_tensor(name, list(shape), F32, kind="ExternalInput").ap()

    def dram_out(name, shape):
        return nc.dram_tensor(name, list(shape), F32, kind="ExternalOutput").ap()

    xin = dram_in("xin", [NROWS, D])
    consts_d = dram_in("consts", [128, CONST_COLS])
    st_c = dram_in("st_c", [DEPTH, NS, 4, 256, 256])
    st_n = dram_in("st_n", [DEPTH, NS, 4, 256])
    st_m = dram_in("st_m", [DEPTH, NS, 4])
    st_gs = dram_in("st_gs", [DEPTH, NS, 8, 128, 128])
    st_gconv = dram_in("st_gconv", [DEPTH, NS, 3, 3072])
    st_rs = dram_in("st_rs", [DEPTH, NS, 16, 64, 64])
    st_rshift = dram_in("st_rshift", [DEPTH, NS, 3328])
    W = {}
    for nm, shp in PARAM_SHAPES.items():
        W[nm] = dram_in(nm, shp)
    y_out = dram_out("y", [NROWS, D])
    o_pc = dram_out("p_c", [DEPTH, 4, 256, 256])
    o_pn = dram_out("p_n", [DEPTH, 4, 256])
    o_pm = dram_out("p_m", [DEPTH, 4])
    o_pgs = dram_out("p_gs", [DEPTH, 8, 128, 128])
    o_pgconv = dram_out("p_gconv", [DEPTH, 3, 3072])
    o_prs = dram_out("p_rs", [DEPTH, 16, 64, 64])
    o_prshift = dram_out("p_rshift", [DEPTH, 3328])
    o_sc = dram_out("s_c", [DEPTH, NS, 4, 256, 256])
    o_sn = dram_out("s_n", [DEPTH, NS, 4, 256])
    o_sm = dram_out("s_m", [DEPTH, NS, 4])
    o_sgs = dram_out("s_gs", [DEPTH, NS, 8, 128, 128])
    o_sgconv = dram_out("s_gconv", [DEPTH, NS, 3, 3072])
    o_srs = dram_out("s_rs", [DEPTH, NS, 16, 64, 64])
    o_srshift = dram_out("s_rshift", [DEPTH, NS, 3328])

    outbufs = []
    cur = [es]

    uid = [0]

    def sb(name, shape, dt=F32):
        uid[0] += 1
        name = "%s_%d" % (name, uid[0])
        t = cur[0].enter_context(nc.sbuf_tensor(name, list(shape), dt))
        b = kb.buf(name)
        if cur[0] is not es:
            arena_bufs.append(b)
        return t, b
    arena_bufs = []

    class Arena:
        def __enter__(self):
            self.st = ExitStack()
            self.st.__enter__()
            self.prev = cur[0]
            self.mark = len(arena_bufs)
            cur[0] = self.st
            return self

        def __exit__(self, *a):
            mine = arena_bufs[self.mark:]
            kb.barrier(mine)
            kb.recycle(mine)
            del arena_bufs[self.mark:]
            cur[0] = self.prev
            return self.st.__exit__(*a)

    def run_interleaved(makers, K):
        pending = list(makers)
        active = {}
        free = list(range(K))
        while pending or active:
            while pending and free:
                s_ = free.pop(0)
                active[s_] = pending.pop(0)(s_)
            for s_ in list(active.keys()):
                try:
                    next(active[s_])
                except StopIteration:
                    del active[s_]
                    free.append(s_)

    cst, cstB = sb("cst", [128, CONST_COLS])
    dma("sp", cst[:, :], consts_d[:, :], W=[cstB])
    ident = cst[:, CO["ident"]:CO["ident"] + 128]
    ones = cst[:, CO["ones"]:CO["ones"] + 128]

    def cview(name, p, shape):
        n = int(np.prod(shape))
        v = cst[0:p, CO[name]:CO[name] + n]
        return v.rearrange("p (a b) -> p a b", b=shape[1])
    nmT_incl = cst[0:64, CO["nmT_incl"]:CO["nmT_incl"] + 64]
    selh = cview("selh", 8, (8, 64))
    sellast = cview("sellast", 64, (3, 128))
    LV = {64: 0, 16: 1, 1: 2}

    ident_bf, ident_bfB = sb("ident_bf", [128, 128], BF16)
    op("dve", lambda e: e.tensor_copy(out=ident_bf[:, :], in_=ident), R=[cstB], W=[ident_bfB])

    PS = []
    for i in range(8):
        t = es.enter_context(nc.psum_tensor("ps%d" % i, [128, 512], F32))
        PS.append((t, kb.buf("ps%d" % i)))
    psi = [0]

    def psum():
        t = PS[psi[0] % 8]
        psi[0] += 1
        return t

    def load_fm_vec(name, src, nblk):
        t, b = sb(name, [128, DEPTH, nblk])
        for l in range(DEPTH):
            dma("sp", t[:, l, :], src[l].rearrange("(k p) -> p k", p=128), W=[b], slow=True)
        return t, b

    nw_mix, nw_mixB = load_fm_vec("nw_mix", W["norm_mix_w"], 16)
    nw_mlp, nw_mlpB = load_fm_vec("nw_mlp", W["norm_mlp_w"], 16)
    nw_fin, nw_finB = sb("nw_fin", [128, 16])
    dma("sp", nw_fin[:, :], W["final_norm_w"].rearrange("(k p) -> p k", p=128), W=[nw_finB], slow=True)

    epst, epsB = sb("epst", [128, 4])
    op("dve", lambda e: e.memset(epst[:, 0:1], 1e-6), W=[epsB])
    op("dve", lambda e: e.memset(epst[:, 1:2], 1e-12), W=[epsB])
    op("dve", lambda e: e.memset(epst[:, 2:3], 64e-5), W=[epsB])
    op("dve", lambda e: e.memset(epst[:, 3:4], 1.0), W=[epsB])
    zrow, zrowB = sb("zrow", [8, 64])
    op("dve", lambda e: e.memset(zrow[:, :], 0.0), W=[zrowB])
    zcol, zcolB = sb("zcol", [128, 64])
    op("dve", lambda e: e.memset(zcol[:, :], 0.0), W=[zcolB])

    xF, xFB = sb("xF", [128, KC, NMAX])
    xnF, xnFB = sb("xnF", [128, KC, NMAX], BF16)
    wbuf = [sb("wb%d" % i, [128, 2048], BF16) for i in range(NWB)]
    wbi = [0]
    yF = [sb("y%sF" % c, [128, 8, NMAX], BF16) for c in "abc"]
    rstd, rstdB = sb("rstd", [128, NMAX])
    tmpA, tmpAB = sb("tmpA", [128, NMAX])

    wscrs = [nc.dram_tensor("wscratch%d" % i, [175, 128, 2048], BF16).ap() for i in range(4)]
    wcache = {}

    def load_w(src2d, c0, ncol, kchunks):
        t, b = wbuf[wbi[0] % NWB]
        wbi[0] += 1
        n = kchunks * ncol
        v = t[:, 0:n].rearrange("p (k c) -> p k c", c=ncol)
        key = (str(src2d.tensor.name), int(src2d.offset), c0, ncol, kchunks)
        if key in wcache:
            idx, blkB = wcache[key]
            dma("sp", t[:, 0:n], wscrs[idx // 175][idx % 175, :, 0:n], R=[blkB], W=[b])
        else:
            src = src2d.rearrange("(k p) n -> p k n", p=128)[:, :, c0:c0 + ncol]
            dma("pool", v, src, W=[b])
            idx = len(wcache)
            blkB = kb.buf("wblk%d" % idx)
            dma("sp", wscrs[idx // 175][idx % 175, :, 0:n], t[:, 0:n], R=[b], W=[blkB], semb=b)
            wcache[key] = (idx, blkB)
        return v, b

    def rmsnorm(N, nw_ap_fn, nwB, inplace=False):
        ps, psB = psum()
        for k in range(KC):
            op("act", lambda e, k=k: e.activation(out=tmpA[:, 0:N], in_=xF[:, k, 0:N], func=AF.Square),
               R=[xFB], W=[tmpAB])
            op("pe", lambda e, k=k: e.matmul(ps[:, 0:N], lhsT=ones, rhs=tmpA[:, 0:N], start=(k == 0),
                                             stop=(k == KC - 1)), R=[tmpAB, cstB], W=[psB])
        op("act", lambda e: e.activation(out=rstd[:, 0:N], in_=ps[:, 0:N], func=AF.Ln, scale=1.0 / D, bias=epst[:, 0:1]),
           R=[psB, epsB], W=[rstdB])
        op("act", lambda e: e.activation(out=rstd[:, 0:N], in_=rstd[:, 0:N], func=AF.Exp, scale=-0.5),
           R=[rstdB], W=[rstdB])
        for k in range(KC):
            if inplace:
                op("dve", lambda e, k=k: e.scalar_tensor_tensor(out=xF[:, k, 0:N], in0=xF[:, k, 0:N], scalar=nw_ap_fn(k),
                                                                in1=rstd[:, 0:N], op0=ALU.mult, op1=ALU.mult),
                   R=[xFB, rstdB, nwB], W=[xFB])
            else:
                op("dve", lambda e, k=k: e.scalar_tensor_tensor(out=xnF[:, k, 0:N], in0=xF[:, k, 0:N], scalar=nw_ap_fn(k),
                                                                in1=rstd[:, 0:N], op0=ALU.mult, op1=ALU.mult),
                   R=[xFB, rstdB, nwB], W=[xnFB])

    def proj_fm(src2d, c0, ncols, N, consume, kchunks=KC, rhs_fn=None, rhsB=None):
        rhs_fn = rhs_fn or (lambda k: xnF[:, k, 0:N])
        rhsB = rhsB or xnFB
        done = 0
        j = 0
        while done < ncols:
            m = min(128, ncols - done)
            wv, wB = load_w(src2d, c0 + done, m, kchunks)
            ps, psB = psum()
            for k in range(kchunks):
                op("pe", lambda e, k=k: e.matmul(ps[0:m, 0:N], lhsT=wv[:, k, 0:m], rhs=rhs_fn(k), start=(k == 0),
                                                 stop=(k == kchunks - 1)), R=[wB, rhsB], W=[psB])
            consume(j, ps, psB, m)
            j += 1
            done += m

    Cext = []
    for l in range(DEPTH):
        t, b = sb("Cext%d" % l, [128, 4, 2, 257])
        op("dve", lambda e, t=t: e.memset(t[:, :, :, :], 0.0), W=[b])
        Cext.append((t, b))
    mprev = []
    for l in range(DEPTH):
        t, b = sb("mprev%d" % l, [4, 1])
        op("dve", lambda e, t=t: e.memset(t[:, :], 0.0), W=[b])
        mprev.append((t, b))
    mb_i, mb_iB = sb("mb_i", [4, DEPTH])
    mb_f, mb_fB = sb("mb_f", [4, DEPTH])
    dma("sp", mb_i[:, :], W["mlstm_b_i"].rearrange("l h -> h l"), W=[mb_iB], slow=True)
    dma("sp", mb_f[:, :], W["mlstm_b_f"].rearrange("l h -> h l"), W=[mb_fB], slow=True)
    nmb_f, nmb_fB = sb("nmb_f", [4, DEPTH])
    op("dve", lambda e: e.tensor_scalar(out=nmb_f[:, :], in0=mb_f[:, :], scalar1=-1.0, scalar2=None, op0=ALU.mult),
       R=[mb_fB], W=[nmb_fB])

    def to_fm(src, srcB, L, cs, dst):
        dt, dB = dst
        ps, psB = psum()
        pv = ps.bitcast(BF16)
        for blk in range(8):
            op("pe", lambda e, blk=blk: e.transpose(out=pv[:, blk * 64:blk * 64 + L], in_=src[0:L, blk * 128:(blk + 1) * 128],
                                                    identity=ident_bf[0:L, 0:L]), R=[srcB, ident_bfB], W=[psB])
        for blk in range(8):
            if blk % 2 == 0:
                op("act", lambda e, blk=blk: e.activation(out=dt[:, blk, cs], in_=pv[:, blk * 64:blk * 64 + L], func=AF.Copy),
                   R=[psB], W=[dB])
            else:
                op("dve", lambda e, blk=blk: e.tensor_copy(out=dt[:, blk, cs], in_=pv[:, blk * 64:blk * 64 + L]),
                   R=[psB], W=[dB])

    def mlstm_phase(l, tile):
        r0, N, chunks = tile
        CeP, CePB = Cext[l]
        w_in = W["w_in"][l]
        qF, qFB = sb("m_qF", [128, 8, NMAX], BF16)
        kF, kFB = sb("m_kF", [128, 8, NMAX], BF16)
        vF, vFB = sb("m_vF", [128, 8, NMAX], BF16)
        oF, oFB = sb("m_oF", [128, 8, NMAX], BF16)
        gi, giB = sb("m_gi", [4, NMAX])
        gf, gfB = sb("m_gf", [4, NMAX])
        gm, gmB = sb("m_gm", [4, NMAX])
        gb, gbB = sb("m_gb", [4, NMAX])
        gu, guB = sb("m_gu", [4, NMAX])
        gR, gRB = sb("m_gR", [128, NMAX])
        msamp, msampB = sb("m_msamp", [4, NS])
        tokT, tokTB = sb("m_tokT", [64, 128])
        kT, kTB = sb("m_kT", [64, 1024], BF16)
        vext, vextB = sb("m_vext", [64, 4, 257], BF16)
        osg, osgB = sb("m_osg", [64, 1024], BF16)
        MK = 4
        mslots = []
        for _i in range(MK):
            mslots.append({"Et": sb("m_E", [64, 64]), "PT": sb("m_PT", [64, 64], BF16), "Asb": sb("m_Asb", [64, 257]),
                           "Rt": sb("m_R", [64, 257]), "den": sb("m_den", [64, 2]), "ktil": sb("m_ktil", [64, 256], BF16)})
        hT, hTB = sb("m_hT", [64, 1024])
        dbc, dbcB = sb("m_dbc", [128, 4])
        ssq, ssqB = sb("m_ssq", [64, 8])
        yT, yTB = sb("m_yT", [64, 1024], BF16)
        sq_junk, sq_junkB = sb("m_sqj", [64, 256])
        CextS, CextSB = sb("CextS", [128, 4, 2, 257])
        Cbf, CbfB = sb("Cbf", [128, 4, 2, 257], BF16)
        bc1024, bc1024B = sb("bc1024", [64, 1024])
        op("dve", lambda e: e.memset(gR[:, :], 0.0), W=[gRB])
        op("dve", lambda e: e.memset(vext[:, :, 256:257], 1.0), W=[vextB])

        def cons_to(dst, dstB, scale=1.0, func=AF.Copy):
            def f(j, ps, psB, m):
                op("act", lambda e: e.activation(out=dst[0:m, j, 0:N], in_=ps[0:m, 0:N], func=func, scale=scale),
                   R=[psB], W=[dstB])
            return f
        proj_fm(w_in, A0 + 0, 1024, N, cons_to(qF, qFB))
        proj_fm(w_in, A0 + 1024, 1024, N, cons_to(kF, kFB, scale=1.0 / 16.0))
        proj_fm(w_in, A0 + 2048, 1024, N, cons_to(vF, vFB))
        proj_fm(w_in, A0 + 3072, 1024, N, cons_to(oF, oFB, func=AF.Sigmoid))

        def cons_i(j, ps, psB, m):
            op("act", lambda e: e.activation(out=gi[:, 0:N], in_=ps[0:4, 0:N], func=AF.Identity, bias=mb_i[:, l:l + 1]),
               R=[psB, mb_iB], W=[giB])
        proj_fm(w_in, A0 + 4096, 4, N, cons_i)

        def cons_f(j, ps, psB, m):
            op("act", lambda e: e.activation(out=gf[:, 0:N], in_=ps[0:4, 0:N], func=AF.Exp, scale=-1.0,
                                             bias=nmb_f[:, l:l + 1]), R=[psB, nmb_fB], W=[gfB])
            op("act", lambda e: e.activation(out=gf[:, 0:N], in_=gf[:, 0:N], func=AF.Ln, bias=epst[0:4, 3:4]),
               R=[gfB, epsB], W=[gfB])
            op("dve", lambda e: e.tensor_scalar(out=gf[:, 0:N], in0=gf[:, 0:N], scalar1=-1.0, scalar2=None,
                                                op0=ALU.mult), R=[gfB], W=[gfB])
        proj_fm(w_in, A0 + 4100, 4, N, cons_f)

        mP, mPB = mprev[l]
        has_s = any(c[0] == "s" for c in chunks)
        pc0 = NS if has_s else 0
        if has_s:
            dma("sp", msamp[:, :], st_m[l].rearrange("b h -> h b"), W=[msampB], slow=True)
            op("dve", lambda e: e.tensor_tensor(out=gm[:, 0:NS], in0=gf[:, 0:NS], in1=msamp[:, :], op=ALU.add),
               R=[gfB, msampB], W=[gmB])
            op("dve", lambda e: e.tensor_tensor(out=gm[:, 0:NS], in0=gm[:, 0:NS], in1=gi[:, 0:NS], op=ALU.max),
               R=[gmB, giB], W=[gmB])
            op("dve", lambda e: e.tensor_copy(out=gb[:, 0:NS], in_=gf[:, 0:NS]), R=[gfB], W=[gbB])
        op("dve", lambda e: e.tensor_tensor_scan(out=gm[:, pc0:N], data0=gf[:, pc0:N], data1=gi[:, pc0:N],
                                                 initial=mP[:, 0:1], op0=ALU.add, op1=ALU.max),
           R=[gfB, giB, mPB], W=[gmB])
        for (kind, c0, L, sidx) in chunks:
            if kind != "p":
                continue
            op("dve", lambda e, c0=c0, L=L: e.tensor_tensor_scan(out=gb[:, c0:c0 + L], data0=gf[:, c0:c0 + L],
                                                                 data1=zrow[0:4, 0:L], initial=0.0,
                                                                 op0=ALU.add, op1=ALU.add),
               R=[gfB, zrowB], W=[gbB])
        op("dve", lambda e: e.tensor_tensor(out=gu[:, 0:N], in0=gb[:, 0:N], in1=gm[:, 0:N], op=ALU.subtract),
           R=[gbB, gmB], W=[guB])
        op("dve", lambda e: e.tensor_tensor(out=gR[0:4, 0:N], in0=gi[:, 0:N], in1=gb[:, 0:N], op=ALU.subtract),
           R=[giB, gbB], W=[gRB])
        op("act", lambda e: e.activation(out=gR[64:68, 0:N], in_=gm[:, 0:N], func=AF.Exp, scale=-1.0),
           R=[gmB], W=[gRB])
        if has_s:
            op("dve", lambda e: e.tensor_tensor(out=gR[32:36, 0:NS], in0=gu[:, 0:NS], in1=msamp[:, :], op=ALU.add),
               R=[guB, msampB], W=[gRB])
            op("act", lambda e: e.activation(out=gR[32:36, 0:NS], in_=gR[32:36, 0:NS], func=AF.Exp), R=[gRB], W=[gRB])
            op("dve", lambda e: e.tensor_tensor(out=gR[96:100, 0:NS], in0=gR[0:4, 0:NS], in1=gu[:, 0:NS], op=ALU.add),
               R=[gRB, guB], W=[gRB])
            op("act", lambda e: e.activation(out=gR[96:100, 0:NS], in_=gR[96:100, 0:NS], func=AF.Exp), R=[gRB], W=[gRB])
        first = True
        for (kind, c0, L, sidx) in chunks:
            if kind != "p":
                continue
            mp_ap = mP[:, 0:1] if first else gm[:, c0 - 1:c0]
            op("act", lambda e, c0=c0, L=L, mp_ap=mp_ap: e.activation(out=gR[32:36, c0:c0 + L], in_=gu[:, c0:c0 + L],
                                                                     func=AF.Exp, bias=mp_ap),
               R=[guB, gmB, mPB], W=[gRB])
            op("act", lambda e, c0=c0, L=L: e.activation(out=gR[96:100, c0:c0 + L], in_=gR[0:4, c0:c0 + L],
                                                         func=AF.Exp, bias=gu[:, c0 + L - 1:c0 + L]),
               R=[guB, gRB], W=[gRB])
            first = False

        dma("sp", bc1024[:, :], W["mlstm_norm_w"][l:l + 1, :].partition_broadcast(64), W=[bc1024B])

        def chunk(c0, L, Ct, CtB):
            lv = LV[L]
            cs = slice(c0, c0 + L)
            ps, psB = psum()
            op("pe", lambda e: e.transpose(out=ps[0:L, 0:128], in_=gR[:, cs], identity=ident), R=[gRB, cstB], W=[psB])
            op("act", lambda e: e.activation(out=tokT[0:L, :], in_=ps[0:L, 0:128], func=AF.Copy), R=[psB], W=[tokTB])
            for (src, srcB, dstfn, dstB) in ((kF, kFB, lambda blk: kT[0:L, blk * 128:(blk + 1) * 128], kTB),
                                             (vF, vFB, lambda blk: vext[0:L, blk // 2, (blk % 2) * 128:(blk % 2) * 128 + 128], vextB),
                                             (oF, oFB, lambda blk: osg[0:L, blk * 128:(blk + 1) * 128], osgB)):
                ps, psB = psum()
                pv = ps.bitcast(BF16)
                for blk in range(8):
                    op("pe", lambda e, blk=blk, pv=pv, src=src: e.transpose(out=pv[0:L, blk * 128:(blk + 1) * 128],
                                                                          in_=src[:, blk, cs], identity=ident_bf[:, :]),
                       R=[srcB, ident_bfB], W=[psB])
                for blk in range(8):
                    if blk % 2 == 0:
                        op("act", lambda e, blk=blk, pv=pv, dstfn=dstfn: e.activation(
                            out=dstfn(blk), in_=pv[0:L, blk * 128:(blk + 1) * 128], func=AF.Copy), R=[psB], W=[dstB])
                    else:
                        op("dve", lambda e, blk=blk, pv=pv, dstfn=dstfn: e.tensor_copy(
                            out=dstfn(blk), in_=pv[0:L, blk * 128:(blk + 1) * 128]), R=[psB], W=[dstB])
            op("act", lambda e: e.activation(out=Cbf[:, :, :, :], in_=Ct[:, :, :, :], func=AF.Copy), R=[CtB], W=[CbfB])
            psd, psdB = psum()
            op("pe", lambda e: e.matmul(psd[:, 0:4], lhsT=sellast[0:L, lv, :], rhs=tokT[0:L, 32:36], start=True, stop=True),
               R=[cstB, tokTB], W=[psdB])
            op("dve", lambda e: e.tensor_copy(out=dbc[:, :], in_=psd[:, 0:4]), R=[psdB], W=[dbcB])
            def mhead(h, sl):
                SL = mslots[sl]
                (Et, EtB), (PT, PTB), (Asb, AsbB) = SL['Et'], SL['PT'], SL['Asb']
                (Rt, RtB), (den, denB), (ktil, ktilB) = SL['Rt'], SL['den'], SL['ktil']
                psu, psuB = psum()
                op("pe", lambda e: e.matmul(psu[0:L, 0:L], lhsT=selh[0:4, h, 0:L], rhs=gu[:, cs], start=True, stop=False),
                   R=[cstB, guB], W=[psuB])
                op("pe", lambda e: e.matmul(psu[0:L, 0:L], lhsT=ident[0:L, 0:L], rhs=nmT_incl[0:L, 0:L], start=False, stop=True),
                   R=[cstB], W=[psuB])
                op("act", lambda e: e.activation(out=Et[0:L, 0:L], in_=psu[0:L, 0:L], func=AF.Exp, bias=tokT[0:L, h:h + 1]),
                   R=[psuB, tokTB], W=[EtB])
                yield
                pst, pstB = psum()
                for kc in range(2):
                    op("pe", lambda e, kc=kc: e.matmul(pst[0:L, 0:L], lhsT=kF[:, 2 * h + kc, cs], rhs=qF[:, 2 * h + kc, cs],
                                                       start=(kc == 0), stop=(kc == 1)), R=[kFB, qFB], W=[pstB])
                op("dve", lambda e: e.tensor_tensor(out=PT[0:L, 0:L], in0=pst[0:L, 0:L], in1=Et[0:L, 0:L], op=ALU.mult),
                   R=[pstB, EtB], W=[PTB])
                yield
                psb, psbB = psum()
                op("pe", lambda e: e.matmul(psb[0:L, 0:257], lhsT=PT[0:L, 0:L], rhs=vext[0:L, h, :], start=True, stop=True),
                   R=[PTB, vextB], W=[psbB])
                psa, psaB = psum()
                for kc in range(2):
                    op("pe", lambda e, kc=kc: e.matmul(psa[0:L, 0:257], lhsT=qF[:, 2 * h + kc, cs], rhs=Cbf[:, h, kc, :],
                                                       start=(kc == 0), stop=(kc == 1)), R=[qFB, CbfB], W=[psaB])
                op("act", lambda e: e.activation(out=Asb[0:L, :], in_=psa[0:L, 0:257], func=AF.Copy,
                                                 scale=tokT[0:L, 32 + h:33 + h]), R=[psaB, tokTB], W=[AsbB])
                op("dve", lambda e: e.tensor_tensor(out=Rt[0:L, :], in0=Asb[0:L, :], in1=psb[0:L, 0:257], op=ALU.add),
                   R=[AsbB, psbB], W=[RtB])
                yield
                op("act", lambda e: e.activation(out=den[0:L, 0:1], in_=Rt[0:L, 256:257], func=AF.Abs),
                   R=[RtB], W=[denB])
                op("dve", lambda e: e.tensor_tensor(out=den[0:L, 0:1], in0=den[0:L, 0:1], in1=tokT[0:L, 64 + h:65 + h],
                                                    op=ALU.max), R=[denB, tokTB], W=[denB])
                op("dve", lambda e: e.reciprocal(out=den[0:L, 1:2], in_=den[0:L, 0:1]), R=[denB], W=[denB])
                op("act", lambda e: e.activation(out=hT[0:L, h * 256:(h + 1) * 256], in_=Rt[0:L, 0:256], func=AF.Copy,
                                                 scale=den[0:L, 1:2]), R=[RtB, denB], W=[hTB])
                yield
                op("act", lambda e: e.activation(out=ktil[0:L, :], in_=kT[0:L, h * 256:(h + 1) * 256], func=AF.Copy,
                                                 scale=tokT[0:L, 96 + h:97 + h]), R=[kTB, tokTB], W=[ktilB])
                yield
                for kc in range(2):
                    psc, pscB = psum()
                    op("pe", lambda e, kc=kc, psc=psc: e.matmul(psc[:, 0:257], lhsT=ktil[0:L, kc * 128:(kc + 1) * 128],
                                                                rhs=vext[0:L, h, :], start=True, stop=True),
                       R=[ktilB, vextB], W=[pscB])
                    op("dve", lambda e, kc=kc, psc=psc: e.scalar_tensor_tensor(
                        out=Ct[:, h, kc, :], in0=Ct[:, h, kc, :], scalar=dbc[:, h:h + 1], in1=psc[:, 0:257],
                        op0=ALU.mult, op1=ALU.add), R=[CtB, dbcB, pscB], W=[CtB])
            run_interleaved([(lambda sl, h=h: mhead(h, sl)) for h in range(4)], MK)
            for h in range(4):
                op("act", lambda e, h=h: e.activation(out=sq_junk[0:L, :], in_=hT[0:L, h * 256:(h + 1) * 256], func=AF.Square,
                                                      accum_out=ssq[0:L, h:h + 1]), R=[hTB], W=[ssqB, sq_junkB])
            op("act", lambda e: e.activation(out=ssq[0:L, 4:8], in_=ssq[0:L, 0:4], func=AF.Ln, scale=1.0 / 256.0,
                                             bias=epst[0:L, 0:1]), R=[ssqB, epsB], W=[ssqB])
            op("act", lambda e: e.activation(out=ssq[0:L, 4:8], in_=ssq[0:L, 4:8], func=AF.Exp, scale=-0.5), R=[ssqB], W=[ssqB])
            for h in range(4):
                op("dve", lambda e, h=h: e.scalar_tensor_tensor(out=hT[0:L, h * 256:(h + 1) * 256],
                                                                in0=hT[0:L, h * 256:(h + 1) * 256],
                                                                scalar=ssq[0:L, 4 + h:5 + h],
                                                                in1=bc1024[0:L, h * 256:(h + 1) * 256],
                                                                op0=ALU.mult, op1=ALU.mult),
                   R=[hTB, ssqB, bc1024B], W=[hTB])
            op("dve", lambda e: e.tensor_tensor(out=yT[0:L, :], in0=hT[0:L, :], in1=osg[0:L, :], op=ALU.mult),
               R=[hTB, osgB], W=[yTB])
            to_fm(yT, yTB, L, cs, yF[0])

        for (kind, c0, L, sidx) in chunks:
            if kind == "s":
                Ct, CtB = CextS, CextSB
                for h in range(4):
                    dma("sp", Ct[:, h, :, 0:256], st_c[l, sidx, h].rearrange("(kc p) v -> p kc v", p=128), W=[CtB])
                dma("sp", Ct[:, :, :, 256:257], st_n[l, sidx].rearrange("h (kc p o) -> p h kc o", p=128, o=1),
                    W=[CtB], slow=True)
            else:
                Ct, CtB = CeP, CePB
            chunk(c0, L, Ct, CtB)
            if kind == "s":
                for h in range(4):
                    dma("sp", o_sc[l, sidx, h].rearrange("(kc p) v -> p kc v", p=128), Ct[:, h, :, 0:256],
                        R=[CtB], semb=CtB)
                dma("sp", o_sn[l, sidx].rearrange("h (kc p o) -> p h kc o", p=128, o=1), Ct[:, :, :, 256:257],
                    R=[CtB], semb=CtB, slow=True)
        if has_s:
            dma("sp", o_sm[l].rearrange("b h -> h b"), gm[:, 0:NS], R=[gmB], semb=gmB, slow=True)
        op("dve", lambda e: e.tensor_copy(out=mP[:, 0:1], in_=gm[:, N - 1:N]), R=[gmB], W=[mPB])


    nm_strict = cst[0:64, CO["nm_strict"]:CO["nm_strict"] + 64]
    nmT_strict = cst[0:64, CO["nmT_strict"]:CO["nmT_strict"] + 64]
    GS = []
    ghist = []
    for l in range(DEPTH):
        t, b = sb("GS%d" % l, [128, 8, 128])
        op("dve", lambda e, t=t: e.memset(t[:, :, :], 0.0), W=[b])
        GS.append((t, b))
        t, b = sb("ghist%d" % l, [128, 24, 3])
        op("dve", lambda e, t=t: e.memset(t[:, :, :], 0.0), W=[b])
        ghist.append((t, b))
    gcw, gcwB = sb("gcw", [128, DEPTH, 24, 4])
    gpar, gparB = sb("gpar", [8, DEPTH, 4])
    for l in range(DEPTH):
        dma("sp", gpar[:, l, 0:1], W["gdn_a_log"][l:l + 1, :].rearrange("o h -> h o"), W=[gparB], slow=True)
        dma("sp", gpar[:, l, 1:2], W["gdn_dt_bias"][l:l + 1, :].rearrange("o h -> h o"), W=[gparB], slow=True)
        op("act", lambda e, l=l: e.activation(out=gpar[:, l, 2:3], in_=gpar[:, l, 0:1], func=AF.Exp), R=[gparB], W=[gparB])
        op("dve", lambda e, l=l: e.tensor_scalar(out=gpar[:, l, 2:3], in0=gpar[:, l, 2:3], scalar1=-1.0, scalar2=None,
                                                 op0=ALU.mult), R=[gparB], W=[gparB])
    with Arena():
        cwt, cwtB = sb("cwt", [4, 3072])
        for l in range(DEPTH):
            dma("sp", cwt[:, :], W["gdn_conv_w"][l], W=[cwtB])
            for q in range(6):
                ps, psB = psum()
                for kk in range(4):
                    k = q * 4 + kk
                    op("pe", lambda e, k=k, kk=kk: e.transpose(out=ps[:, kk * 4:kk * 4 + 4], in_=cwt[0:4, k * 128:(k + 1) * 128],
                                                              identity=ident[0:4, 0:4]), R=[cwtB, cstB], W=[psB])
                op("dve", lambda e, q=q, l=l: e.tensor_copy(out=gcw[:, l, q * 4:q * 4 + 4, :],
                                                            in_=ps[:, 0:16].rearrange("p (a b) -> p a b", b=4)),
                   R=[psB], W=[gcwB])

    def gdn_phase(l, tile):
        r0, N, chunks = tile
        St, StB = GS[l]
        hs, hsB = ghist[l]
        w_in = W["w_in"][l]
        has_s = any(c[0] == "s" for c in chunks)
        pc0 = NS if has_s else 0
        Np = N - pc0
        xcs = [sb("g_xc%d" % i, [128, NMAX + 3]) for i in range(2)]
        yc, ycB = sb("g_yc", [128, NMAX])
        qF, qFB = sb("g_qF", [128, 8, NMAX], BF16)
        kF, kFB = sb("g_kF", [128, 8, NMAX], BF16)
        vF, vFB = sb("g_vF", [128, 8, NMAX], BF16)
        zF, zFB = sb("g_zF", [128, 8, NMAX], BF16)
        ra, raB = sb("g_ra", [8, NMAX])
        rlb, rlbB = sb("g_rlb", [8, NMAX])
        rg, rgB = sb("g_rg", [8, NMAX])
        rng_, rngB = sb("g_rng", [8, NMAX])
        rgb, rgbB = sb("g_rgb", [8, NMAX])
        stA, stAB = sb("g_stA", [128, NMAX])
        stB, stBB = sb("g_stB", [64, NMAX])
        tokA, tokAB = sb("g_tokA", [64, 128])
        tokB, tokBB = sb("g_tokB", [64, 64])
        kT, kTB = sb("g_kT", [64, 1024], BF16)
        vT, vTB = sb("g_vT", [64, 1024], BF16)
        zT, zTB = sb("g_zT", [64, 1024], BF16)
        GK = 4
        def mkslot():
            d = {}
            for nm_, shp_, dt_ in (("E1", [64, 64], F32), ("E2", [64, 64], F32), ("E3", [64, 64], F32),
                                   ("P0", [64, 64], F32), ("P1", [64, 64], F32), ("Q0", [64, 64], F32), ("Q1", [64, 64], F32),
                                   ("TT", [64, 64], F32), ("attnT", [64, 64], BF16), ("bv", [64, 128], F32),
                                   ("bek", [64, 128], F32), ("wF", [128, 64], BF16), ("usb", [64, 128], F32),
                                   ("vnew", [64, 128], BF16), ("qsb", [64, 128], F32), ("kd", [64, 128], BF16)):
                d[nm_] = sb("g_" + nm_, shp_, dt_)
            return d
        oT, oTB = sb("g_oT", [64, 1024])
        dbc, dbcB = sb("g_dbc", [128, 8])
        Sbf, SbfB = sb("g_Sbf", [128, 8, 128], BF16)
        if has_s:
            SS, SSB = sb("g_SS", [128, 8, 128])
        ssq, ssqB = sb("g_ssq", [64, 16])
        sqj, sqjB = sb("g_sqj", [64, 128])
        yT, yTB = sb("g_yT", [64, 1024], BF16)
        bcn, bcnB = sb("g_bcn", [64, 128])
        op("dve", lambda e: e.memset(stA[:, :], 0.0), W=[stAB])
        op("dve", lambda e: e.memset(stB[:, :], 0.0), W=[stBB])
        dma("sp", bcn[:, :], W["gdn_norm_w"][l:l + 1, :].partition_broadcast(64), W=[bcnB])
        with Arena():
            if has_s:
                xs, xsB = sb("g_xs", [128, 24, NS])
                bufS, bufSB = sb("g_bufS", [128, 24, NS * 3])
                st48, st48B = sb("g_st48", [48, 1536])
            if has_s:
                for hq in range(2):
                    dma("sp", st48[:, :], st_gconv[l].rearrange("b j c -> (b j) c")[:, hq * 1536:(hq + 1) * 1536], W=[st48B])
                    for kq in range(12):
                        k = hq * 12 + kq
                        ps, psB = psum()
                        op("pe", lambda e, kq=kq, ps=ps: e.transpose(out=ps[:, 0:48], in_=st48[0:48, kq * 128:(kq + 1) * 128],
                                                                     identity=ident[0:48, 0:48]), R=[st48B, cstB], W=[psB])
                        op("dve", lambda e, k=k, ps=ps: e.tensor_copy(out=bufS[:, k, :], in_=ps[:, 0:48]), R=[psB], W=[bufSB])
                dma("sp", o_sgconv[l][:, 0:2, :], st_gconv[l][:, 1:3, :], R=[st48B], semb=st48B)

            def cons_qkv(j, ps, psB, m):
                xc, xcB = xcs[j % 2]
                op("dve", lambda e: e.tensor_copy(out=xc[:, 0:3], in_=hs[:, j, :]), R=[hsB], W=[xcB])
                if has_s:
                    op("act", lambda e: e.activation(out=xs[:, j, :], in_=ps[:, 0:NS], func=AF.Copy), R=[psB], W=[xsB])
                op("act", lambda e: e.activation(out=xc[:, 3:3 + Np], in_=ps[:, pc0:N], func=AF.Copy), R=[psB], W=[xcB])
                op("dve", lambda e: e.tensor_scalar(out=yc[:, pc0:N], in0=xc[:, 0:Np], scalar1=gcw[:, l, j, 0:1],
                                                    scalar2=None, op0=ALU.mult), R=[xcB, gcwB], W=[ycB])
                for t in range(1, 4):
                    op("dve", lambda e, t=t: e.scalar_tensor_tensor(out=yc[:, pc0:N], in0=xc[:, t:t + Np],
                                                                    scalar=gcw[:, l, j, t:t + 1], in1=yc[:, pc0:N],
                                                                    op0=ALU.mult, op1=ALU.add), R=[xcB, gcwB, ycB], W=[ycB])
                op("dve", lambda e: e.tensor_copy(out=hs[:, j, :], in_=xc[:, Np:Np + 3]), R=[xcB], W=[hsB])
                if has_s:
                    bs = bufS[:, j, :].rearrange("p (b t) -> p b t", t=3)
                    op("dve", lambda e: e.tensor_scalar(out=yc[:, 0:NS], in0=xs[:, j, :], scalar1=gcw[:, l, j, 3:4],
                                                        scalar2=None, op0=ALU.mult), R=[xsB, gcwB], W=[ycB])
                    for t in range(3):
                        op("dve", lambda e, t=t: e.scalar_tensor_tensor(out=yc[:, 0:NS], in0=bs[:, :, t],
                                                                        scalar=gcw[:, l, j, t:t + 1], in1=yc[:, 0:NS],
                                                                        op0=ALU.mult, op1=ALU.add),
                           R=[bufSB, gcwB, ycB], W=[ycB])
                if j >= 16:
                    op("act", lambda e: e.activation(out=vF[:, j - 16, 0:N], in_=yc[:, 0:N], func=AF.Silu), R=[ycB], W=[vFB])
                    return
                op("act", lambda e: e.activation(out=yc[:, 0:N], in_=yc[:, 0:N], func=AF.Silu), R=[ycB], W=[ycB])
                op("act", lambda e: e.activation(out=tmpA[:, 0:N], in_=yc[:, 0:N], func=AF.Square), R=[ycB], W=[tmpAB])
                ps2, ps2B = psum()
                op("pe", lambda e: e.matmul(ps2[:, 0:N], lhsT=ones, rhs=tmpA[:, 0:N], start=True, stop=True),
                   R=[tmpAB, cstB], W=[ps2B])
                op("act", lambda e: e.activation(out=tmpA[:, 0:N], in_=ps2[:, 0:N], func=AF.Ln, bias=epst[:, 1:2]),
                   R=[ps2B, epsB], W=[tmpAB])
                op("act", lambda e: e.activation(out=tmpA[:, 0:N], in_=tmpA[:, 0:N], func=AF.Exp, scale=-0.5), R=[tmpAB], W=[tmpAB])
                if j < 8:
                    op("dve", lambda e: e.scalar_tensor_tensor(out=qF[:, j, 0:N], in0=yc[:, 0:N], scalar=128.0 ** -0.5,
                                                               in1=tmpA[:, 0:N], op0=ALU.mult, op1=ALU.mult),
                       R=[ycB, tmpAB], W=[qFB])
                else:
                    op("dve", lambda e: e.tensor_tensor(out=kF[:, j - 8, 0:N], in0=yc[:, 0:N], in1=tmpA[:, 0:N], op=ALU.mult),
                       R=[ycB, tmpAB], W=[kFB])
            proj_fm(w_in, B0, 3072, N, cons_qkv)
            if has_s:
                for hq in range(2):
                    for q3 in range(3):
                        q = hq * 3 + q3
                        ps, psB = psum()
                        for kk in range(4):
                            k = q * 4 + kk
                            op("pe", lambda e, k=k, kk=kk, ps=ps: e.transpose(out=ps[0:NS, kk * 128:(kk + 1) * 128], in_=xs[:, k, :],
                                                                      identity=ident), R=[xsB, cstB], W=[psB])
                        op("dve", lambda e, q3=q3, ps=ps: e.tensor_copy(out=st48[0:NS, q3 * 512:(q3 + 1) * 512], in_=ps[0:NS, 0:512]),
                           R=[psB], W=[st48B])
                    dma("sp", o_sgconv[l][:, 2, hq * 1536:(hq + 1) * 1536], st48[0:NS, :], R=[st48B], semb=st48B)


        gslots = [mkslot() for _ in range(GK)]

        def cons_z(j, ps, psB, m):
            op("act", lambda e: e.activation(out=zF[:, j, 0:N], in_=ps[:, 0:N], func=AF.Silu), R=[psB], W=[zFB])
        proj_fm(w_in, B0 + 3072, 1024, N, cons_z)

        def cons_a(j, ps, psB, m):
            op("act", lambda e: e.activation(out=ra[:, 0:N], in_=ps[0:8, 0:N], func=AF.Exp, bias=gpar[:, l, 1:2]),
               R=[psB, gparB], W=[raB])
            op("act", lambda e: e.activation(out=ra[:, 0:N], in_=ra[:, 0:N], func=AF.Ln, bias=epst[0:8, 3:4]),
               R=[raB, epsB], W=[raB])
            op("dve", lambda e: e.tensor_scalar(out=ra[:, 0:N], in0=ra[:, 0:N], scalar1=gpar[:, l, 2:3], scalar2=None,
                                                op0=ALU.mult), R=[raB, gparB], W=[raB])
        proj_fm(w_in, B0 + 4096, 8, N, cons_a)

        def cons_b(j, ps, psB, m):
            op("act", lambda e: e.activation(out=rlb[:, 0:N], in_=ps[0:8, 0:N], func=AF.Exp, scale=-1.0), R=[psB], W=[rlbB])
            op("act", lambda e: e.activation(out=rlb[:, 0:N], in_=rlb[:, 0:N], func=AF.Ln, bias=epst[0:8, 3:4]),
               R=[rlbB, epsB], W=[rlbB])
            op("dve", lambda e: e.tensor_scalar(out=rlb[:, 0:N], in0=rlb[:, 0:N], scalar1=-1.0, scalar2=None, op0=ALU.mult),
               R=[rlbB], W=[rlbB])
        proj_fm(w_in, B0 + 4104, 8, N, cons_b)
        if has_s:
            op("dve", lambda e: e.tensor_copy(out=rg[:, 0:NS], in_=ra[:, 0:NS]), R=[raB], W=[rgB])
        for (kind, c0, L, sidx) in chunks:
            if kind != "p":
                continue
            op("dve", lambda e, c0=c0, L=L: e.tensor_tensor_scan(out=rg[:, c0:c0 + L], data0=ra[:, c0:c0 + L],
                                                                 data1=zrow[0:8, 0:L], initial=0.0, op0=ALU.add, op1=ALU.add),
               R=[raB, zrowB], W=[rgB])
        op("dve", lambda e: e.tensor_scalar(out=rng_[:, 0:N], in0=rg[:, 0:N], scalar1=-1.0, scalar2=None, op0=ALU.mult),
           R=[rgB], W=[rngB])
        op("dve", lambda e: e.tensor_tensor(out=rgb[:, 0:N], in0=rg[:, 0:N], in1=rlb[:, 0:N], op=ALU.add),
           R=[rgB, rlbB], W=[rgbB])
        op("dve", lambda e: e.tensor_copy(out=stA[0:8, 0:N], in_=rng_[:, 0:N]), R=[rngB], W=[stAB])
        op("dve", lambda e: e.tensor_copy(out=stA[32:40, 0:N], in_=rgb[:, 0:N]), R=[rgbB], W=[stAB])
        op("act", lambda e: e.activation(out=stA[64:72, 0:N], in_=rlb[:, 0:N], func=AF.Exp), R=[rlbB], W=[stAB])
        op("act", lambda e: e.activation(out=stA[96:104, 0:N], in_=rgb[:, 0:N], func=AF.Exp), R=[rgbB], W=[stAB])
        op("act", lambda e: e.activation(out=stB[0:8, 0:N], in_=rg[:, 0:N], func=AF.Exp), R=[rgB], W=[stBB])
        for (kind, c0, L, sidx) in chunks:
            op("act", lambda e, c0=c0, L=L: e.activation(out=stB[32:40, c0:c0 + L], in_=rng_[:, c0:c0 + L], func=AF.Exp,
                                                         bias=rg[:, c0 + L - 1:c0 + L]), R=[rngB, rgB], W=[stBB])

        def chunk(c0, L, S_, S_B):
            lv = LV[L]
            cs = slice(c0, c0 + L)
            nlev = {64: 5, 16: 3, 1: 0}[L]
            ps, psB = psum()
            op("pe", lambda e: e.transpose(out=ps[0:L, 0:128], in_=stA[:, cs], identity=ident), R=[stAB, cstB], W=[psB])
            op("act", lambda e: e.activation(out=tokA[0:L, :], in_=ps[0:L, 0:128], func=AF.Copy), R=[psB], W=[tokAB])
            ps, psB = psum()
            op("pe", lambda e: e.transpose(out=ps[0:L, 0:64], in_=stB[:, cs], identity=ident[0:64, 0:64]), R=[stBB, cstB], W=[psB])
            op("act", lambda e: e.activation(out=tokB[0:L, :], in_=ps[0:L, 0:64], func=AF.Copy), R=[psB], W=[tokBB])
            for (src, srcB, dst, dstB) in ((kF, kFB, kT, kTB), (vF, vFB, vT, vTB), (zF, zFB, zT, zTB)):
                ps, psB = psum()
                pv = ps.bitcast(BF16)
                for blk in range(8):
                    op("pe", lambda e, blk=blk, pv=pv, src=src: e.transpose(out=pv[0:L, blk * 128:(blk + 1) * 128],
                                                                          in_=src[:, blk, cs], identity=ident_bf[:, :]),
                       R=[srcB, ident_bfB], W=[psB])
                op("act", lambda e, pv=pv, dst=dst: e.activation(out=dst[0:L, 0:512], in_=pv[0:L, 0:512], func=AF.Copy),
                   R=[psB], W=[dstB])
                op("dve", lambda e, pv=pv, dst=dst: e.tensor_copy(out=dst[0:L, 512:1024], in_=pv[0:L, 512:1024]),
                   R=[psB], W=[dstB])
            op("act", lambda e: e.activation(out=Sbf[:, :, :], in_=S_[:, :, :], func=AF.Copy), R=[S_B], W=[SbfB])
            psd, psdB = psum()
            op("pe", lambda e: e.matmul(psd[:, 0:8], lhsT=sellast[0:L, lv, :], rhs=tokB[0:L, 0:8], start=True, stop=True),
               R=[cstB, tokBB], W=[psdB])
            op("dve", lambda e: e.tensor_copy(out=dbc[:, :], in_=psd[:, 0:8]), R=[psdB], W=[dbcB])
            def head(h, sl):
                SL = gslots[sl]
                (E1, E1B), (E2, E2B), (E3, E3B) = SL['E1'], SL['E2'], SL['E3']
                Pm = [SL['P0'], SL['P1']]
                Qm = [SL['Q0'], SL['Q1']]
                (TT, TTB), (attnT, attnTB), (bv, bvB), (bek, bekB) = SL['TT'], SL['attnT'], SL['bv'], SL['bek']
                (wF, wFB), (usb, usbB), (vnew, vnewB), (qsb, qsbB), (kd, kdB) = SL['wF'], SL['usb'], SL['vnew'], SL['qsb'], SL['kd']
                for (row, rowB, nm, Ed, EdB, bias_ap) in ((rgb, rgbB, nmT_strict, E1, E1B, tokA[0:L, h:h + 1]),
                                                         (rng_, rngB, nm_strict, E2, E2B, tokA[0:L, 32 + h:33 + h]),
                                                         (rg, rgB, nmT_incl, E3, E3B, tokA[0:L, h:h + 1])):
                    pse, pseB = psum()
                    op("pe", lambda e, row=row, pse=pse: e.matmul(pse[0:L, 0:L], lhsT=selh[0:8, h, 0:L], rhs=row[:, cs],
                                                                  start=True, stop=False), R=[cstB, rowB], W=[pseB])
                    op("pe", lambda e, nm=nm, pse=pse: e.matmul(pse[0:L, 0:L], lhsT=ident[0:L, 0:L], rhs=nm[0:L, 0:L],
                                                                start=False, stop=True), R=[cstB], W=[pseB])
                    op("act", lambda e, Ed=Ed, pse=pse, bias_ap=bias_ap: e.activation(out=Ed[0:L, 0:L], in_=pse[0:L, 0:L],
                                                                                     func=AF.Exp, bias=bias_ap),
                       R=[pseB, tokAB], W=[EdB])
                    yield
                psg, psgB = psum()
                op("pe", lambda e: e.matmul(psg[0:L, 0:L], lhsT=kF[:, h, cs], rhs=kF[:, h, cs], start=True, stop=True),
                   R=[kFB], W=[psgB])
                pskq, pskqB = psum()
                op("pe", lambda e: e.matmul(pskq[0:L, 0:L], lhsT=kF[:, h, cs], rhs=qF[:, h, cs], start=True, stop=True),
                   R=[kFB, qFB], W=[pskqB])
                (P0, P0B), (P1, P1B) = Pm
                (Q0, Q0B), (Q1, Q1B) = Qm
                op("dve", lambda e: e.scalar_tensor_tensor(out=Q0[0:L, 0:L], in0=psg[0:L, 0:L], scalar=-1.0, in1=E1[0:L, 0:L],
                                                           op0=ALU.mult, op1=ALU.mult), R=[psgB, E1B], W=[Q0B])
                op("dve", lambda e: e.scalar_tensor_tensor(out=P0[0:L, 0:L], in0=psg[0:L, 0:L], scalar=-1.0, in1=E2[0:L, 0:L],
                                                           op0=ALU.mult, op1=ALU.mult), R=[psgB, E2B], W=[P0B])
                op("dve", lambda e: e.tensor_tensor(out=attnT[0:L, 0:L], in0=pskq[0:L, 0:L], in1=E3[0:L, 0:L], op=ALU.mult),
                   R=[pskqB, E3B], W=[attnTB])
                op("dve", lambda e: e.tensor_tensor(out=TT[0:L, 0:L], in0=Q0[0:L, 0:L], in1=ident[0:L, 0:L], op=ALU.add),
                   R=[Q0B, cstB], W=[TTB])
                yield
                cp, cq = (P0, P0B), (Q0, Q0B)
                np_, nq = (P1, P1B), (Q1, Q1B)
                for lev in range(nlev):
                    psp, pspB = psum()
                    op("pe", lambda e, cp=cp, cq=cq, psp=psp: e.matmul(psp[0:L, 0:L], lhsT=cq[0][0:L, 0:L], rhs=cp[0][0:L, 0:L],
                                                                      start=True, stop=True), R=[cp[1], cq[1]], W=[pspB])
                    psq, psqB = psum()
                    op("pe", lambda e, cp=cp, cq=cq, psq=psq: e.matmul(psq[0:L, 0:L], lhsT=cp[0][0:L, 0:L], rhs=cq[0][0:L, 0:L],
                                                                      start=True, stop=True), R=[cp[1], cq[1]], W=[psqB])
                    op("act", lambda e, np_=np_, psp=psp: e.activation(out=np_[0][0:L, 0:L], in_=psp[0:L, 0:L], func=AF.Copy),
                       R=[pspB], W=[np_[1]])
                    op("dve", lambda e, nq=nq, psq=psq: e.tensor_copy(out=nq[0][0:L, 0:L], in_=psq[0:L, 0:L]),
                       R=[psqB], W=[nq[1]])
                    yield
                    pst, pstB = psum()
                    op("pe", lambda e, np_=np_, pst=pst: e.matmul(pst[0:L, 0:L], lhsT=np_[0][0:L, 0:L], rhs=TT[0:L, 0:L],
                                                                  start=True, stop=True), R=[np_[1], TTB], W=[pstB])
                    op("dve", lambda e, pst=pst: e.tensor_tensor(out=TT[0:L, 0:L], in0=TT[0:L, 0:L], in1=pst[0:L, 0:L],
                                                                 op=ALU.add), R=[TTB, pstB], W=[TTB])
                    yield
                    cp, cq, np_, nq = np_, nq, cp, cq
                hs_ = slice(h * 128, (h + 1) * 128)
                op("act", lambda e: e.activation(out=bv[0:L, :], in_=vT[0:L, hs_], func=AF.Copy, scale=tokA[0:L, 64 + h:65 + h]),
                   R=[vTB, tokAB], W=[bvB])
                op("act", lambda e: e.activation(out=bek[0:L, :], in_=kT[0:L, hs_], func=AF.Copy, scale=tokA[0:L, 96 + h:97 + h]),
                   R=[kTB, tokAB], W=[bekB])
                yield
                psu, psuB = psum()
                op("pe", lambda e: e.matmul(psu[0:L, 0:128], lhsT=TT[0:L, 0:L], rhs=bv[0:L, :], start=True, stop=True),
                   R=[TTB, bvB], W=[psuB])
                psw, pswB = psum()
                op("pe", lambda e: e.matmul(psw[:, 0:L], lhsT=bek[0:L, :], rhs=TT[0:L, 0:L], start=True, stop=True),
                   R=[TTB, bekB], W=[pswB])
                op("act", lambda e: e.activation(out=wF[:, 0:L], in_=psw[:, 0:L], func=AF.Copy), R=[pswB], W=[wFB])
                op("dve", lambda e: e.tensor_copy(out=usb[0:L, :], in_=psu[0:L, 0:128]), R=[psuB], W=[usbB])
                yield
                psws, pswsB = psum()
                op("pe", lambda e: e.matmul(psws[0:L, 0:128], lhsT=wF[:, 0:L], rhs=Sbf[:, h, :], start=True, stop=True),
                   R=[wFB, SbfB], W=[pswsB])
                op("dve", lambda e: e.tensor_tensor(out=vnew[0:L, :], in0=usb[0:L, :], in1=psws[0:L, 0:128], op=ALU.subtract),
                   R=[usbB, pswsB], W=[vnewB])
                yield
                psqs, psqsB = psum()
                op("pe", lambda e: e.matmul(psqs[0:L, 0:128], lhsT=qF[:, h, cs], rhs=Sbf[:, h, :], start=True, stop=True),
                   R=[qFB, SbfB], W=[psqsB])
                psav, psavB = psum()
                op("pe", lambda e: e.matmul(psav[0:L, 0:128], lhsT=attnT[0:L, 0:L], rhs=vnew[0:L, :], start=True, stop=True),
                   R=[attnTB, vnewB], W=[psavB])
                op("act", lambda e: e.activation(out=qsb[0:L, :], in_=psqs[0:L, 0:128], func=AF.Copy, scale=tokB[0:L, h:h + 1]),
                   R=[psqsB, tokBB], W=[qsbB])
                op("dve", lambda e: e.tensor_tensor(out=oT[0:L, hs_], in0=qsb[0:L, :], in1=psav[0:L, 0:128], op=ALU.add),
                   R=[qsbB, psavB], W=[oTB])
                yield
                op("act", lambda e: e.activation(out=kd[0:L, :], in_=kT[0:L, hs_], func=AF.Copy, scale=tokB[0:L, 32 + h:33 + h]),
                   R=[kTB, tokBB], W=[kdB])
                psup, psupB = psum()
                op("pe", lambda e: e.matmul(psup[:, 0:128], lhsT=kd[0:L, :], rhs=vnew[0:L, :], start=True, stop=True),
                   R=[kdB, vnewB], W=[psupB])
                op("dve", lambda e: e.scalar_tensor_tensor(out=S_[:, h, :], in0=S_[:, h, :], scalar=dbc[:, h:h + 1],
                                                           in1=psup[:, 0:128], op0=ALU.mult, op1=ALU.add),
                   R=[S_B, dbcB, psupB], W=[S_B])
            run_interleaved([(lambda sl, h=h: head(h, sl)) for h in range(8)], GK)
            for h in range(8):
                op("act", lambda e, h=h: e.activation(out=sqj[0:L, :], in_=oT[0:L, h * 128:(h + 1) * 128], func=AF.Square,
                                                      accum_out=ssq[0:L, h:h + 1]), R=[oTB], W=[ssqB, sqjB])
            op("act", lambda e: e.activation(out=ssq[0:L, 8:16], in_=ssq[0:L, 0:8], func=AF.Ln, scale=1.0 / 128.0,
                                             bias=epst[0:L, 0:1]), R=[ssqB, epsB], W=[ssqB])
            op("act", lambda e: e.activation(out=ssq[0:L, 8:16], in_=ssq[0:L, 8:16], func=AF.Exp, scale=-0.5), R=[ssqB], W=[ssqB])
            for h in range(8):
                op("dve", lambda e, h=h: e.scalar_tensor_tensor(out=oT[0:L, h * 128:(h + 1) * 128],
                                                                in0=oT[0:L, h * 128:(h + 1) * 128],
                                                                scalar=ssq[0:L, 8 + h:9 + h], in1=bcn[0:L, :],
                                                                op0=ALU.mult, op1=ALU.mult), R=[oTB, ssqB, bcnB], W=[oTB])
            op("dve", lambda e: e.tensor_tensor(out=yT[0:L, :], in0=oT[0:L, :], in1=zT[0:L, :], op=ALU.mult),
               R=[oTB, zTB], W=[yTB])
            to_fm(yT, yTB, L, cs, yF[1])

        for (kind, c0, L, sidx) in chunks:
            if kind == "s":
                S_, S_B = SS, SSB
                dma("sp", S_[:, :, :], st_gs[l, sidx].rearrange("h k v -> k h v"), W=[S_B])
            else:
                S_, S_B = St, StB
            chunk(c0, L, S_, S_B)
            if kind == "s":
                dma("sp", o_sgs[l, sidx].rearrange("h k v -> k h v"), S_[:, :, :], R=[S_B], semb=S_B)


    onesblk = cst[:, CO["onesblk"]:CO["onesblk"] + 128]
    i2c = cst[:, CO["i2"]:CO["i2"] + 64]
    bdT_incl = cst[:, CO["bdT_incl"]:CO["bdT_incl"] + 128]
    bdT_strict = cst[:, CO["bdT_strict"]:CO["bdT_strict"] + 128]
    bd_strict = cst[:, CO["bd_strict"]:CO["bd_strict"] + 128]
    RZ = []
    rprev = []
    for l in range(DEPTH):
        t, b = sb("RZ%d" % l, [128, 8, 64])
        op("dve", lambda e, t=t: e.memset(t[:, :, :], 0.0), W=[b])
        RZ.append((t, b))
        t, b = sb("rprev%d" % l, [128, 26])
        op("dve", lambda e, t=t: e.memset(t[:, :], 0.0), W=[b])
        rprev.append((t, b))
    rpar = {}
    for nm in ("rwkv_w0", "rwkv_a0", "rwkv_k_k", "rwkv_k_a", "rwkv_ln_w", "rwkv_ln_b"):
        rpar[nm] = load_fm_vec("p_" + nm, W[nm], 8)
    rpar["rwkv_r_k"] = load_fm_vec("p_rk", W["rwkv_r_k"].rearrange("l h d -> l (h d)"), 8)
    rpar["rwkv_mu"] = load_fm_vec("p_mu", W["rwkv_mu"], 26)
    omka, omkaB = sb("omka", [128, DEPTH, 8])
    op("dve", lambda e: e.tensor_scalar(out=omka[:, :, :], in0=rpar["rwkv_k_a"][0][:, :, :], scalar1=-1.0, scalar2=1.0,
                                        op0=ALU.mult, op1=ALU.add), R=[rpar["rwkv_k_a"][1]], W=[omkaB])

    def rwkv_phase(l, tile):
        r0, N, chunks = tile
        Zt, ZtB = RZ[l]
        rp, rpB = rprev[l]
        w_in = W["w_in"][l]
        has_s = any(c[0] == "s" for c in chunks)
        pc0 = NS if has_s else 0
        Np = N - pc0
        P = lambda nm: rpar[nm][0]
        PB = lambda nm: rpar[nm][1]
        mu, muB = rpar["rwkv_mu"]
        xcs = [sb("r_xc%d" % i, [128, NMAX + 1]) for i in range(2)]
        dtmp, dtmpB = sb("r_dtmp", [128, NMAX])
        xwxa, xwxaB = sb("r_xwxa", [128, NMAX])
        sxg, sxgB = sb("r_sxg", [128, NMAX])
        w2p, w2pB = sb("r_w2p", [128, 1024])
        a2p, a2pB = sb("r_a2p", [128, 1024])
        g2t, g2tB = sb("r_g2", [128, 1024])
        rows = {}
        for nm in ("r", "kx", "k2", "v", "av", "bv", "ld", "g", "asig", "yraw", "t1", "t2"):
            rows[nm] = sb("r_" + nm, [128, NMAX])
        if has_s:
            xsAll, xsAllB = sb("r_xsAll", [128, 26, NS])
            prevS, prevSB = sb("r_prevS", [128, 26, NS])
            st16, st16B = sb("r_st16", [NS, 3328])
            ZS, ZSB = sb("r_ZS", [128, 64])
            ZSt, ZStB = sb("r_ZSt", [64, 128])
        opn = ("rt", "kt", "bt", "at", "kh", "bh", "Vbd", "Q0", "Q1", "P0", "P1", "TT", "MakT", "RKT", "RBT", "K2", "B2", "Y2")
        OPT = {}
        for nm in opn:
            OPT[nm] = sb("r_o_" + nm, [128, 128])
            op("dve", lambda e, t=OPT[nm][0]: e.memset(t[:, :], 0.0), W=[OPT[nm][1]])
        lw, lwB = sb("r_lw", [128, 64])
        eW, eWB = sb("r_eW", [128, 64])
        eWi, eWiB = sb("r_eWi", [128, 64])
        eWp, eWpB = sb("r_eWp", [128, 64])
        eWl, eWlB = sb("r_eWl", [128, 64])
        wls, wlsB = sb("r_wls", [128, 1])
        Vtok, VtokB = sb("r_Vtok", [128, 64])
        RHS, RHSB = sb("r_RHS", [128, 64])
        SAt, SAtB = sb("r_SA", [128, 64])
        for t_, b_ in ((Vtok, VtokB), (RHS, RHSB), (SAt, SAtB)):
            op("dve", lambda e, t_=t_: e.memset(t_[:, :], 0.0), W=[b_])
        op("dve", lambda e: e.memset(w2p[:, :], 0.0), W=[w2pB])
        op("dve", lambda e: e.memset(a2p[:, :], 0.0), W=[a2pB])
        dma("sp", w2p[0:64, :], W["rwkv_w2"][l], W=[w2pB])
        dma("sp", a2p[64:128, :], W["rwkv_a2"][l], W=[a2pB])
        dma("sp", g2t[:, :], W["rwkv_g2"][l], W=[g2tB])
        if has_s:
            dma("sp", st16[:, :], st_rshift[l], W=[st16B])
            for j in range(26):
                ps, psB = psum()
                op("pe", lambda e, j=j, ps=ps: e.transpose(out=ps[:, 0:NS], in_=st16[0:NS, j * 128:(j + 1) * 128],
                                                           identity=ident[0:NS, 0:NS]), R=[st16B, cstB], W=[psB])
                op("dve", lambda e, j=j, ps=ps: e.tensor_copy(out=prevS[:, j, :], in_=ps[:, 0:NS]), R=[psB], W=[prevSB])

        def shifted(j, ps, psB, dst, dstB):
            xc, xcB = xcs[j % 2]
            op("dve", lambda e: e.tensor_copy(out=xc[:, 0:1], in_=rp[:, j:j + 1]), R=[rpB], W=[xcB])
            op("act", lambda e: e.activation(out=xc[:, 1:1 + Np], in_=ps[:, pc0:N], func=AF.Copy), R=[psB], W=[xcB])
            op("dve", lambda e: e.tensor_tensor(out=dtmp[:, 0:Np], in0=xc[:, 0:Np], in1=xc[:, 1:1 + Np], op=ALU.subtract),
               R=[xcB], W=[dtmpB])
            op("dve", lambda e: e.scalar_tensor_tensor(out=dst[:, pc0:N], in0=dtmp[:, 0:Np], scalar=mu[:, l, j:j + 1],
                                                       in1=xc[:, 1:1 + Np], op0=ALU.mult, op1=ALU.add),
               R=[dtmpB, muB, xcB], W=[dstB])
            op("dve", lambda e: e.tensor_copy(out=rp[:, j:j + 1], in_=xc[:, Np:Np + 1]), R=[xcB], W=[rpB])
            if has_s:
                op("act", lambda e: e.activation(out=xsAll[:, j, :], in_=ps[:, 0:NS], func=AF.Copy), R=[psB], W=[xsAllB])
                op("dve", lambda e: e.tensor_tensor(out=dtmp[:, 0:NS], in0=prevS[:, j, :], in1=xsAll[:, j, :], op=ALU.subtract),
                   R=[prevSB, xsAllB], W=[dtmpB])
                op("dve", lambda e: e.scalar_tensor_tensor(out=dst[:, 0:NS], in0=dtmp[:, 0:NS], scalar=mu[:, l, j:j + 1],
                                                           in1=xsAll[:, j, :], op0=ALU.mult, op1=ALU.add),
                   R=[dtmpB, muB, xsAllB], W=[dstB])

        def cons_l24(j, ps, psB, m):
            shifted(24, ps, psB, xwxa, xwxaB)
            op("act", lambda e: e.activation(out=xwxa[0:64, 0:N], in_=xwxa[0:64, 0:N], func=AF.Tanh), R=[xwxaB], W=[xwxaB])
        proj_fm(w_in, C0 + 3072, 128, N, cons_l24)

        def cons_l25(j, ps, psB, m):
            shifted(25, ps, psB, sxg, sxgB)
            op("act", lambda e: e.activation(out=sxg[:, 0:N], in_=sxg[:, 0:N], func=AF.Sigmoid), R=[sxgB], W=[sxgB])
        proj_fm(w_in, C0 + 3200, 128, N, cons_l25)

        R_ = lambda nm: rows[nm][0]
        RB = lambda nm: rows[nm][1]
        lastL = [0]

        def chunk(jb, c0, L, Z_, Z_B, zsl):
            cs = slice(c0, c0 + L)
            nlev = {64: 5, 16: 3, 1: 0}[L]
            O = lambda nm: OPT[nm][0]
            OB = lambda nm: OPT[nm][1]
            ld_ = R_("ld")
            if lastL[0] != L:
                for nm_ in ("rt", "kt", "bt", "at", "kh", "bh", "Vbd"):
                    op("dve", lambda e, nm_=nm_: e.memset(O(nm_)[:, :], 0.0), W=[OB(nm_)])
                lastL[0] = L
            op("dve", lambda e: e.tensor_tensor_scan(out=lw[:, 0:L], data0=ld_[:, cs], data1=zcol[:, 0:L], initial=0.0,
                                                     op0=ALU.add, op1=ALU.add), R=[RB("ld"), zcolB], W=[lwB])
            op("act", lambda e: e.activation(out=eW[:, 0:L], in_=lw[:, 0:L], func=AF.Exp), R=[lwB], W=[eWB])
            op("act", lambda e: e.activation(out=eWi[:, 0:L], in_=lw[:, 0:L], func=AF.Exp, scale=-1.0), R=[lwB], W=[eWiB])
            op("dve", lambda e: e.tensor_tensor(out=eWp[:, 0:L], in0=lw[:, 0:L], in1=ld_[:, cs], op=ALU.subtract),
               R=[lwB, RB("ld")], W=[eWpB])
            op("act", lambda e: e.activation(out=eWp[:, 0:L], in_=eWp[:, 0:L], func=AF.Exp), R=[eWpB], W=[eWpB])
            op("act", lambda e: e.activation(out=eWl[:, 0:L], in_=lw[:, 0:L], func=AF.Exp, scale=-1.0, bias=lw[:, L - 1:L]),
               R=[lwB], W=[eWlB])
            op("act", lambda e: e.activation(out=wls[:, 0:1], in_=lw[:, L - 1:L], func=AF.Exp), R=[lwB], W=[wlsB])
            for (dst, srcn, fac, facB) in (("rt", "r", eW, eWB), ("kt", "k2", eWi, eWiB), ("bt", "bv", eWi, eWiB),
                                           ("at", "av", eWp, eWpB), ("kh", "k2", eWl, eWlB), ("bh", "bv", eWl, eWlB)):
                for hb in range(2):
                    pr = slice(hb * 64, hb * 64 + 64)
                    op("dve", lambda e, dst=dst, srcn=srcn, fac=fac, pr=pr, hb=hb: e.tensor_tensor(
                        out=O(dst)[pr, hb * 64:hb * 64 + L], in0=R_(srcn)[pr, cs], in1=fac[pr, 0:L], op=ALU.mult),
                       R=[RB(srcn), facB], W=[OB(dst)])
            for hb in range(2):
                pr = slice(hb * 64, hb * 64 + 64)
                op("act", lambda e, pr=pr, hb=hb: e.activation(out=O("Vbd")[pr, hb * 64:hb * 64 + L], in_=R_("v")[pr, cs],
                                                               func=AF.Copy), R=[RB("v")], W=[OB("Vbd")])
            ps, psB = psum()
            op("pe", lambda e, ps=ps: e.matmul(ps[:, 0:64], lhsT=O("Vbd")[:, :], rhs=i2c, start=True, stop=True),
               R=[OB("Vbd"), cstB], W=[psB])
            op("act", lambda e, ps=ps: e.activation(out=Vtok[:, :], in_=ps[:, 0:64], func=AF.Copy), R=[psB], W=[VtokB])

            def gram(lhs, rhs, mask, dstn):
                ps, psB = psum()
                op("pe", lambda e: e.matmul(ps[:, 0:128], lhsT=O(lhs)[:, :], rhs=O(rhs)[:, :], start=True, stop=True),
                   R=[OB(lhs), OB(rhs)], W=[psB])
                op("dve", lambda e: e.tensor_tensor(out=O(dstn)[:, :], in0=ps[:, 0:128], in1=mask, op=ALU.mult),
                   R=[psB, cstB], W=[OB(dstn)])
            gram("bt", "at", bdT_strict, "Q0")
            gram("at", "bt", bd_strict, "P0")
            gram("kt", "at", bdT_strict, "MakT")
            gram("kt", "rt", bdT_incl, "RKT")
            gram("bt", "rt", bdT_incl, "RBT")
            op("dve", lambda e: e.tensor_tensor(out=O("TT")[:, :], in0=O("Q0")[:, :], in1=ident, op=ALU.add),
               R=[OB("Q0"), cstB], W=[OB("TT")])
            cp, cq, np_, nq = "P0", "Q0", "P1", "Q1"
            for lev in range(nlev):
                psp, pspB = psum()
                op("pe", lambda e, cp=cp, cq=cq, psp=psp: e.matmul(psp[:, 0:128], lhsT=O(cq)[:, :], rhs=O(cp)[:, :],
                                                                  start=True, stop=True), R=[OB(cp), OB(cq)], W=[pspB])
                psq, psqB = psum()
                op("pe", lambda e, cp=cp, cq=cq, psq=psq: e.matmul(psq[:, 0:128], lhsT=O(cp)[:, :], rhs=O(cq)[:, :],
                                                                  start=True, stop=True), R=[OB(cp), OB(cq)], W=[psqB])
                op("act", lambda e, np_=np_, psp=psp: e.activation(out=O(np_)[:, :], in_=psp[:, 0:128], func=AF.Copy),
                   R=[pspB], W=[OB(np_)])
                op("dve", lambda e, nq=nq, psq=psq: e.tensor_copy(out=O(nq)[:, :], in_=psq[:, 0:128]), R=[psqB], W=[OB(nq)])
                pst, pstB = psum()
                op("pe", lambda e, np_=np_, pst=pst: e.matmul(pst[:, 0:128], lhsT=O(np_)[:, :], rhs=O("TT")[:, :],
                                                              start=True, stop=True), R=[OB(np_), OB("TT")], W=[pstB])
                op("dve", lambda e, pst=pst: e.tensor_tensor(out=O("TT")[:, :], in0=O("TT")[:, :], in1=pst[:, 0:128],
                                                             op=ALU.add), R=[OB("TT"), pstB], W=[OB("TT")])
                cp, cq, np_, nq = np_, nq, cp, cq
            ps, psB = psum()
            op("pe", lambda e, ps=ps: e.matmul(ps[:, 0:64], lhsT=O("at")[:, :], rhs=zsl, start=True, stop=False),
               R=[OB("at"), Z_B], W=[psB])
            op("pe", lambda e, ps=ps: e.matmul(ps[:, 0:64], lhsT=O("MakT")[:, :], rhs=Vtok[:, :], start=False, stop=True),
               R=[OB("MakT"), VtokB], W=[psB])
            op("act", lambda e, ps=ps: e.activation(out=RHS[:, :], in_=ps[:, 0:64], func=AF.Copy), R=[psB], W=[RHSB])
            ps, psB = psum()
            op("pe", lambda e, ps=ps: e.matmul(ps[:, 0:64], lhsT=O("TT")[:, :], rhs=RHS[:, :], start=True, stop=True),
               R=[OB("TT"), RHSB], W=[psB])
            op("dve", lambda e, ps=ps: e.tensor_copy(out=SAt[:, :], in_=ps[:, 0:64]), R=[psB], W=[SAtB])
            psy, psyB = psum()
            op("pe", lambda e: e.matmul(psy[:, 0:64], lhsT=O("rt")[:, :], rhs=zsl, start=True, stop=False),
               R=[OB("rt"), Z_B], W=[psyB])
            op("pe", lambda e: e.matmul(psy[:, 0:64], lhsT=O("RKT")[:, :], rhs=Vtok[:, :], start=False, stop=False),
               R=[OB("RKT"), VtokB], W=[psyB])
            op("pe", lambda e: e.matmul(psy[:, 0:64], lhsT=O("RBT")[:, :], rhs=SAt[:, :], start=False, stop=True),
               R=[OB("RBT"), SAtB], W=[psyB])
            op("act", lambda e: e.activation(out=O("Y2")[0:64, 0:64], in_=psy[0:64, 0:64], func=AF.Copy), R=[psyB], W=[OB("Y2")])
            op("dve", lambda e: e.tensor_copy(out=O("Y2")[64:128, 64:128], in_=psy[64:128, 0:64]), R=[psyB], W=[OB("Y2")])
            ps, psB = psum()
            op("pe", lambda e, ps=ps: e.matmul(ps[:, 0:64], lhsT=O("Y2")[:, :], rhs=i2c, start=True, stop=True),
               R=[OB("Y2"), cstB], W=[psB])
            op("act", lambda e, ps=ps: e.activation(out=R_("yraw")[:, cs], in_=ps[:, 0:L], func=AF.Copy), R=[psB], W=[RB("yraw")])
            for (srcn, dstn) in (("kh", "K2"), ("bh", "B2")):
                ps, psB = psum()
                op("pe", lambda e, ps=ps, srcn=srcn: e.transpose(out=ps[:, 0:128], in_=O(srcn)[:, :], identity=ident),
                   R=[OB(srcn), cstB], W=[psB])
                op("act" if srcn == "kh" else "dve",
                   (lambda e, ps=ps, dstn=dstn: e.activation(out=O(dstn)[:, :], in_=ps[:, 0:128], func=AF.Copy)) if srcn == "kh" else
                   (lambda e, ps=ps, dstn=dstn: e.tensor_copy(out=O(dstn)[:, :], in_=ps[:, 0:128])), R=[psB], W=[OB(dstn)])
            psz, pszB = psum()
            op("pe", lambda e: e.matmul(psz[:, 0:64], lhsT=O("K2")[:, :], rhs=Vtok[:, :], start=True, stop=False),
               R=[OB("K2"), VtokB], W=[pszB])
            op("pe", lambda e: e.matmul(psz[:, 0:64], lhsT=O("B2")[:, :], rhs=SAt[:, :], start=False, stop=True),
               R=[OB("B2"), SAtB], W=[pszB])
            op("dve", lambda e: e.scalar_tensor_tensor(out=zsl, in0=zsl, scalar=wls[:, 0:1], in1=psz[:, 0:64],
                                                       op0=ALU.mult, op1=ALU.add), R=[Z_B, wlsB, pszB], W=[Z_B])

        for jb in range(8):
            def mk(dstn):
                def f(j, ps, psB, m):
                    return None
                return f
            for (blk, dstn) in ((jb, "r"), (8 + jb, "kx"), (16 + jb, "v")):
                def cons(j, ps, psB, m, blk=blk, dstn=dstn):
                    shifted(blk, ps, psB, R_(dstn), RB(dstn))
                proj_fm(w_in, C0 + blk * 128, 128, N, cons)
            bsl = slice(jb * 128, (jb + 1) * 128)
            ps, psB = psum()
            op("pe", lambda e, ps=ps: e.matmul(ps[:, 0:N], lhsT=w2p[:, bsl], rhs=xwxa[:, 0:N], start=True, stop=True),
               R=[w2pB, xwxaB], W=[psB])
            op("act", lambda e, ps=ps: e.activation(out=R_("ld")[:, 0:N], in_=ps[:, 0:N], func=AF.Sigmoid,
                                                    bias=P("rwkv_w0")[:, l, jb:jb + 1]), R=[psB, PB("rwkv_w0")], W=[RB("ld")])
            op("dve", lambda e: e.tensor_scalar(out=R_("ld")[:, 0:N], in0=R_("ld")[:, 0:N], scalar1=-float(np.exp(-0.5)),
                                                scalar2=None, op0=ALU.mult), R=[RB("ld")], W=[RB("ld")])
            ps, psB = psum()
            op("pe", lambda e, ps=ps: e.matmul(ps[:, 0:N], lhsT=a2p[:, bsl], rhs=xwxa[:, 0:N], start=True, stop=True),
               R=[a2pB, xwxaB], W=[psB])
            op("act", lambda e, ps=ps: e.activation(out=R_("asig")[:, 0:N], in_=ps[:, 0:N], func=AF.Sigmoid,
                                                    bias=P("rwkv_a0")[:, l, jb:jb + 1]), R=[psB, PB("rwkv_a0")], W=[RB("asig")])
            ps, psB = psum()
            op("pe", lambda e, ps=ps: e.matmul(ps[:, 0:N], lhsT=g2t[:, bsl], rhs=sxg[:, 0:N], start=True, stop=True),
               R=[g2tB, sxgB], W=[psB])
            op("act", lambda e, ps=ps: e.activation(out=R_("g")[:, 0:N], in_=ps[:, 0:N], func=AF.Copy), R=[psB], W=[RB("g")])
            op("dve", lambda e: e.tensor_scalar(out=R_("t1")[:, 0:N], in0=R_("kx")[:, 0:N], scalar1=P("rwkv_k_k")[:, l, jb:jb + 1],
                                                scalar2=None, op0=ALU.mult), R=[RB("kx"), PB("rwkv_k_k")], W=[RB("t1")])
            op("act", lambda e: e.activation(out=R_("t2")[:, 0:N], in_=R_("t1")[:, 0:N], func=AF.Square), R=[RB("t1")], W=[RB("t2")])
            ps, psB = psum()
            op("pe", lambda e, ps=ps: e.matmul(ps[:, 0:N], lhsT=onesblk, rhs=R_("t2")[:, 0:N], start=True, stop=True),
               R=[RB("t2"), cstB], W=[psB])
            op("act", lambda e, ps=ps: e.activation(out=R_("t2")[:, 0:N], in_=ps[:, 0:N], func=AF.Ln, bias=epst[:, 1:2]),
               R=[psB, epsB], W=[RB("t2")])
            op("act", lambda e: e.activation(out=R_("t2")[:, 0:N], in_=R_("t2")[:, 0:N], func=AF.Exp, scale=-0.5),
               R=[RB("t2")], W=[RB("t2")])
            op("dve", lambda e: e.scalar_tensor_tensor(out=R_("av")[:, 0:N], in0=R_("t1")[:, 0:N], scalar=-1.0, in1=R_("t2")[:, 0:N],
                                                       op0=ALU.mult, op1=ALU.mult), R=[RB("t1"), RB("t2")], W=[RB("av")])
            op("dve", lambda e: e.scalar_tensor_tensor(out=R_("bv")[:, 0:N], in0=R_("av")[:, 0:N], scalar=-1.0, in1=R_("asig")[:, 0:N],
                                                       op0=ALU.mult, op1=ALU.mult), R=[RB("av"), RB("asig")], W=[RB("bv")])
            op("dve", lambda e: e.tensor_scalar(out=R_("t1")[:, 0:N], in0=R_("asig")[:, 0:N], scalar1=P("rwkv_k_a")[:, l, jb:jb + 1],
                                                scalar2=omka[:, l, jb:jb + 1], op0=ALU.mult, op1=ALU.add),
               R=[RB("asig"), PB("rwkv_k_a"), omkaB], W=[RB("t1")])
            op("dve", lambda e: e.tensor_tensor(out=R_("k2")[:, 0:N], in0=R_("kx")[:, 0:N], in1=R_("t1")[:, 0:N], op=ALU.mult),
               R=[RB("kx"), RB("t1")], W=[RB("k2")])
            for (kind, c0, L, sidx) in chunks:
                if kind == "s":
                    dma("sp", ZSt[:, :].rearrange("v (h k) -> v h k", k=64),
                        st_rs[l, sidx, 2 * jb:2 * jb + 2].rearrange("h v k -> v h k"), W=[ZStB])
                    ps, psB = psum()
                    op("pe", lambda e, ps=ps: e.transpose(out=ps[:, 0:64], in_=ZSt[:, :], identity=ident[0:64, 0:64]),
                       R=[ZStB, cstB], W=[psB])
                    op("act", lambda e, ps=ps: e.activation(out=ZS[:, :], in_=ps[:, 0:64], func=AF.Copy), R=[psB], W=[ZSB])
                    chunk(jb, c0, L, ZS, ZSB, ZS[:, :])
                    ps, psB = psum()
                    op("pe", lambda e, ps=ps: e.transpose(out=ps[0:64, 0:128], in_=ZS[:, :], identity=ident), R=[ZSB, cstB], W=[psB])
                    op("act", lambda e, ps=ps: e.activation(out=ZSt[:, :], in_=ps[0:64, 0:128], func=AF.Copy), R=[psB], W=[ZStB])
                    dma("sp", o_srs[l, sidx, 2 * jb:2 * jb + 2].rearrange("h v k -> v h k"),
                        ZSt[:, :].rearrange("v (h k) -> v h k", k=64), R=[ZStB], semb=ZStB)
                else:
                    chunk(jb, c0, L, Zt, ZtB, Zt[:, jb, :])
            ps, psB = psum()
            op("pe", lambda e, ps=ps: e.matmul(ps[:, 0:N], lhsT=onesblk, rhs=R_("yraw")[:, 0:N], start=True, stop=True),
               R=[RB("yraw"), cstB], W=[psB])
            op("dve", lambda e, ps=ps: e.scalar_tensor_tensor(out=R_("t1")[:, 0:N], in0=ps[:, 0:N], scalar=-1.0 / 64.0,
                                                              in1=R_("yraw")[:, 0:N], op0=ALU.mult, op1=ALU.add),
               R=[psB, RB("yraw")], W=[RB("t1")])
            op("act", lambda e: e.activation(out=R_("t2")[:, 0:N], in_=R_("t1")[:, 0:N], func=AF.Square), R=[RB("t1")], W=[RB("t2")])
            ps, psB = psum()
            op("pe", lambda e, ps=ps: e.matmul(ps[:, 0:N], lhsT=onesblk, rhs=R_("t2")[:, 0:N], start=True, stop=True),
               R=[RB("t2"), cstB], W=[psB])
            op("act", lambda e, ps=ps: e.activation(out=R_("t2")[:, 0:N], in_=ps[:, 0:N], func=AF.Ln, scale=1.0 / 64.0,
                                                    bias=epst[:, 2:3]), R=[psB, epsB], W=[RB("t2")])
            op("act", lambda e: e.activation(out=R_("t2")[:, 0:N], in_=R_("t2")[:, 0:N], func=AF.Exp, scale=-0.5),
               R=[RB("t2")], W=[RB("t2")])
            op("dve", lambda e: e.tensor_tensor(out=R_("t1")[:, 0:N], in0=R_("t1")[:, 0:N], in1=R_("t2")[:, 0:N], op=ALU.mult),
               R=[RB("t1"), RB("t2")], W=[RB("t1")])
            op("dve", lambda e: e.tensor_scalar(out=R_("t1")[:, 0:N], in0=R_("t1")[:, 0:N], scalar1=P("rwkv_ln_w")[:, l, jb:jb + 1],
                                                scalar2=P("rwkv_ln_b")[:, l, jb:jb + 1], op0=ALU.mult, op1=ALU.add),
               R=[RB("t1"), PB("rwkv_ln_w"), PB("rwkv_ln_b")], W=[RB("t1")])
            op("dve", lambda e: e.scalar_tensor_tensor(out=R_("t2")[:, 0:N], in0=R_("r")[:, 0:N], scalar=P("rwkv_r_k")[:, l, jb:jb + 1],
                                                       in1=R_("k2")[:, 0:N], op0=ALU.mult, op1=ALU.mult),
               R=[RB("r"), RB("k2"), PB("rwkv_r_k")], W=[RB("t2")])
            ps, psB = psum()
            op("pe", lambda e, ps=ps: e.matmul(ps[:, 0:N], lhsT=onesblk, rhs=R_("t2")[:, 0:N], start=True, stop=True),
               R=[RB("t2"), cstB], W=[psB])
            op("dve", lambda e, ps=ps: e.tensor_tensor(out=R_("t2")[:, 0:N], in0=ps[:, 0:N], in1=R_("v")[:, 0:N], op=ALU.mult),
               R=[psB, RB("v")], W=[RB("t2")])
            op("dve", lambda e: e.tensor_tensor(out=R_("t1")[:, 0:N], in0=R_("t1")[:, 0:N], in1=R_("t2")[:, 0:N], op=ALU.add),
               R=[RB("t1"), RB("t2")], W=[RB("t1")])
            yct, ycB_ = yF[2]
            op("dve", lambda e: e.tensor_tensor(out=yct[:, jb, 0:N], in0=R_("t1")[:, 0:N], in1=R_("g")[:, 0:N], op=ALU.mult),
               R=[RB("t1"), RB("g")], W=[ycB_])
        if has_s:
            for q in range(7):
                ps, psB = psum()
                nb = min(4, 26 - q * 4)
                for kk in range(nb):
                    k = q * 4 + kk
                    op("pe", lambda e, k=k, kk=kk, ps=ps: e.transpose(out=ps[0:NS, kk * 128:(kk + 1) * 128], in_=xsAll[:, k, :],
                                                                     identity=ident), R=[xsAllB, cstB], W=[psB])
                op("dve", lambda e, q=q, ps=ps, nb=nb: e.tensor_copy(out=st16[0:NS, q * 512:q * 512 + nb * 128],
                                                                    in_=ps[0:NS, 0:nb * 128]), R=[psB], W=[st16B])
            dma("sp", o_srshift[l], st16[0:NS, :], R=[st16B], semb=st16B)

    def zero_y(which, N):
        dt, dB = yF[which]
        op("dve", lambda e: e.memset(dt[:, :, 0:N], 0.0), W=[dB])

    def merge_phase(l, tile):
        r0, N, chunks = tile
        gt = [sb("gate%d" % i, [128, NMAX]) for i in range(3)]
        mrg, mrgB = sb("mrgF", [128, KC, NMAX], BF16)
        mtmp, mtmpB = sb("mtmp", [128, NMAX])
        w_in = W["w_in"][l]
        brw = [W["w_branch_a"][l], W["w_branch_b"][l], W["w_branch_c"][l]]
        for j in range(KC):
            for g in range(3):
                wv, wB = load_w(w_in, G0 + g * D + j * 128, 128, KC)
                ps, psB = psum()
                for k in range(KC):
                    op("pe", lambda e, k=k: e.matmul(ps[:, 0:N], lhsT=wv[:, k, :], rhs=xnF[:, k, 0:N], start=(k == 0),
                                                     stop=(k == KC - 1)), R=[wB, xnFB], W=[psB])
                g_t, g_B = gt[g]
                op("act", lambda e: e.activation(out=g_t[:, 0:N], in_=ps[:, 0:N], func=AF.Sigmoid), R=[psB], W=[g_B])
            for g in range(3):
                wv, wB = load_w(brw[g], j * 128, 128, 8)
                ps, psB = psum()
                yt, yB = yF[g]
                for k in range(8):
                    op("pe", lambda e, k=k: e.matmul(ps[:, 0:N], lhsT=wv[:, k, :], rhs=yt[:, k, 0:N], start=(k == 0),
                                                     stop=(k == 7)), R=[wB, yB], W=[psB])
                g_t, g_B = gt[g]
                if g == 0:
                    op("dve", lambda e: e.tensor_tensor(out=mtmp[:, 0:N], in0=ps[:, 0:N], in1=g_t[:, 0:N], op=ALU.mult),
                       R=[psB, g_B], W=[mtmpB])
                else:
                    op("dve", lambda e: e.tensor_tensor(out=g_t[:, 0:N], in0=ps[:, 0:N], in1=g_t[:, 0:N], op=ALU.mult),
                       R=[psB, g_B], W=[g_B])
                    if g == 1:
                        op("dve", lambda e: e.tensor_tensor(out=mtmp[:, 0:N], in0=mtmp[:, 0:N], in1=g_t[:, 0:N], op=ALU.add),
                           R=[mtmpB, g_B], W=[mtmpB])
                    else:
                        op("dve", lambda e: e.tensor_tensor(out=mrg[:, j, 0:N], in0=mtmp[:, 0:N], in1=g_t[:, 0:N], op=ALU.add),
                           R=[mtmpB, g_B], W=[mrgB])

        def cons_res(j, ps, psB, m):
            op("dve", lambda e: e.tensor_tensor(out=xF[:, j, 0:N], in0=xF[:, j, 0:N], in1=ps[:, 0:N], op=ALU.add),
               R=[xFB, psB], W=[xFB])
        proj_fm(W["w_out"][l], 0, D, N, cons_res, rhs_fn=lambda k: mrg[:, k, 0:N], rhsB=mrgB)

    def mlp_phase(l, tile):
        r0, N, chunks = tile
        mtmp, mtmpB = sb("mtmp2", [128, NMAX])
        uF, uFB = sb("uF", [128, 64, NMAX], BF16)
        rmsnorm(N, lambda k: nw_mlp[:, l, k:k + 1], nw_mlpB)

        def cons_up(j, ps, psB, m):
            op("act", lambda e: e.activation(out=mtmp[:, 0:N], in_=ps[:, 0:N], func=AF.Relu), R=[psB], W=[mtmpB])
            op("dve", lambda e: e.tensor_tensor(out=uF[:, j, 0:N], in0=mtmp[:, 0:N], in1=mtmp[:, 0:N], op=ALU.mult),
               R=[mtmpB], W=[uFB])
        proj_fm(W["w_up"][l], 0, 4 * D, N, cons_up)
        wd = W["w_down"][l]
        for j in range(KC):
            ps, psB = psum()
            for q in range(4):
                wv, wB = load_w(wd[q * 2048:(q + 1) * 2048, :], j * 128, 128, 16)
                for k in range(16):
                    kk = q * 16 + k
                    op("pe", lambda e, k=k, kk=kk, wv=wv: e.matmul(ps[:, 0:N], lhsT=wv[:, k, :], rhs=uF[:, kk, 0:N],
                                                                  start=(kk == 0), stop=(kk == 63)), R=[wB, uFB], W=[psB])
            op("dve", lambda e, j=j: e.tensor_tensor(out=xF[:, j, 0:N], in0=xF[:, j, 0:N], in1=ps[:, 0:N], op=ALU.add),
               R=[xFB, psB], W=[xFB])

    tiles = tile_plan()
    for ti, tile in enumerate(tiles):
        r0, N, chunks = tile
        with Arena():
            stage_tok, stage_tokB = sb("stage_in", [128, D])
            rr = 0
            while rr < N:
                m = min(128, N - rr)
                dma("sp", stage_tok[0:m, :], xin[r0 + rr:r0 + rr + m, :], W=[stage_tokB])
                for kq in range(4):
                    ps, psB = psum()
                    for kk in range(4):
                        k = kq * 4 + kk
                        op("pe", lambda e, k=k, kk=kk, ps=ps: e.transpose(out=ps[:, kk * 128:kk * 128 + m],
                                                                         in_=stage_tok[0:m, k * 128:(k + 1) * 128],
                                                                         identity=ident[0:m, 0:m]), R=[stage_tokB, cstB], W=[psB])
                    for kk in range(4):
                        k = kq * 4 + kk
                        if kk % 2 == 0:
                            op("act", lambda e, k=k, kk=kk, ps=ps: e.activation(out=xF[:, k, rr:rr + m],
                                                                               in_=ps[:, kk * 128:kk * 128 + m], func=AF.Copy),
                               R=[psB], W=[xFB])
                        else:
                            op("dve", lambda e, k=k, kk=kk, ps=ps: e.tensor_copy(out=xF[:, k, rr:rr + m],
                                                                                in_=ps[:, kk * 128:kk * 128 + m]),
                               R=[psB], W=[xFB])
                rr += m
        for l in range(DEPTH):
            rmsnorm(N, lambda k: nw_mix[:, l, k:k + 1], nw_mixB)
            with Arena():
                mlstm_phase(l, tile)
            with Arena():
                gdn_phase(l, tile)
            with Arena():
                rwkv_phase(l, tile)
            with Arena():
                merge_phase(l, tile)
            with Arena():
                mlp_phase(l, tile)
        rmsnorm(N, lambda k: nw_fin[:, k:k + 1], nw_finB, inplace=True)
        with Arena():
            stage_tok, stage_tokB = sb("stage_out", [128, D])
            rr = 0
            while rr < N:
                m = min(128, N - rr)
                for kq in range(4):
                    ps, psB = psum()
                    for kk in range(4):
                        k = kq * 4 + kk
                        op("pe", lambda e, k=k, kk=kk, ps=ps: e.transpose(out=ps[0:m, kk * 128:(kk + 1) * 128],
                                                                         in_=xF[:, k, rr:rr + m], identity=ident),
                           R=[xFB, cstB], W=[psB])
                    if kq % 2 == 0:
                        op("act", lambda e, kq=kq, ps=ps: e.activation(out=stage_tok[0:m, kq * 512:(kq + 1) * 512],
                                                                      in_=ps[0:m, 0:512], func=AF.Copy), R=[psB], W=[stage_tokB])
                    else:
                        op("dve", lambda e, kq=kq, ps=ps: e.tensor_copy(out=stage_tok[0:m, kq * 512:(kq + 1) * 512],
                                                                       in_=ps[0:m, 0:512]), R=[psB], W=[stage_tokB])
                dma("sp", y_out[r0 + rr:r0 + rr + m, :], stage_tok[0:m, :], R=[stage_tokB], semb=stage_tokB)
                rr += m

    for l in range(DEPTH):
        Ct, CtB = Cext[l]
        for h in range(4):
            dma("sp", o_pc[l, h].rearrange("(kc p) v -> p kc v", p=128), Ct[:, h, :, 0:256], R=[CtB], semb=CtB)
        dma("sp", o_pn[l].rearrange("h (kc p o) -> p h kc o", p=128, o=1), Ct[:, :, :, 256:257], R=[CtB], semb=CtB, slow=True)
        outbufs.append(CtB)
        mP, mPB = mprev[l]
        dma("sp", o_pm[l:l + 1, :].rearrange("o h -> h o"), mP[:, 0:1], R=[mPB], semb=mPB, slow=True)
        outbufs.append(mPB)

    for l in range(DEPTH):
        St, StB = GS[l]
        dma("sp", o_pgs[l].rearrange("h k v -> k h v"), St[:, :, :], R=[StB], semb=StB)
        outbufs.append(StB)
        hs, hsB = ghist[l]
        for t in range(3):
            dma("sp", o_pgconv[l][t].rearrange("(k p) -> p k", p=128), hs[:, :, t], R=[hsB], semb=hsB, slow=True)
        outbufs.append(hsB)
    with Arena():
        zt2, zt2B = sb("zt_out", [64, 8, 128])
        for l in range(DEPTH):
            Zt, ZtB = RZ[l]
            for jb in range(8):
                ps, psB = psum()
                op("pe", lambda e, jb=jb, ps=ps: e.transpose(out=ps[0:64, 0:128], in_=Zt[:, jb, :], identity=ident),
                   R=[ZtB, cstB], W=[psB])
                op("act", lambda e, jb=jb, ps=ps: e.activation(out=zt2[:, jb, :], in_=ps[0:64, 0:128], func=AF.Copy),
                   R=[psB], W=[zt2B])
            dma("sp", o_prs[l].rearrange("h v k -> v h k"), zt2[:, :, :].rearrange("v j (h k) -> v (j h) k", k=64),
                R=[zt2B], semb=zt2B)
            rp, rpB = rprev[l]
            dma("sp", o_prshift[l].rearrange("(k p) -> p k", p=128), rp[:, :], R=[rpB], semb=rpB, slow=True)
            outbufs.append(rpB)
        kb.finish([zt2B])
    kb.finish(outbufs)
    print("[kernel] instructions:", kb.n_inst, "dma sems:", kb.nd, "sbuf left:", nc.sbuf_bytes_remaining)


def _consts():
    cols = {}
    parts = []
    off = [0]

    def add(name, arr):
        a = np.zeros((128, arr.shape[1]), np.float32)
        a[:arr.shape[0]] = arr
        cols[name] = off[0]
        off[0] += arr.shape[1]
        parts.append(a)
    add("ident", np.eye(128, dtype=np.float32))
    add("ones", np.ones((128, 128), np.float32))
    ob = np.zeros((128, 128), np.float32)
    ob[:64, :64] = 1
    ob[64:, 64:] = 1
    add("onesblk", ob)
    s = np.arange(64)[:, None]
    l_ = np.arange(64)[None, :]
    NEG = -30000.0
    add("nmT_incl", np.where(s <= l_, 0.0, NEG).astype(np.float32))
    add("nmT_strict", np.where(s < l_, 0.0, NEG).astype(np.float32))
    add("nm_strict", np.where(l_ < s, 0.0, NEG).astype(np.float32))
    add("mT_incl", (s <= l_).astype(np.float32))
    add("mT_strict", (s < l_).astype(np.float32))
    add("m_strict", (l_ < s).astype(np.float32))
    sel = np.zeros((8, 8, 64), np.float32)
    for h in range(8):
        sel[h, h, :] = 1
    add("selh", sel.reshape(8, 512))
    sl = np.zeros((64, 3, 128), np.float32)
    sl[63, 0, :] = 1
    sl[15, 1, :] = 1
    sl[0, 2, :] = 1
    add("sellast", sl.reshape(64, 384))
    i2 = np.concatenate([np.eye(64, dtype=np.float32)] * 2, axis=0)
    add("i2", i2)
    t2 = lambda m: np.tile(m, (2, 2)).astype(np.float32)
    add("bdT_incl", t2(s <= l_))
    add("bdT_strict", t2(s < l_))
    add("bd_strict", t2(l_ < s))
    return np.concatenate(parts, axis=1), cols


CONSTS, CO = _consts()
CONST_COLS = CONSTS.shape[1]

PARAM_SHAPES = {
    "norm_mix_w": (2, 2048), "w_in": (2, 2048, 17688), "mlstm_b_i": (2, 4), "mlstm_b_f": (2, 4),
    "mlstm_norm_w": (2, 1024), "gdn_conv_w": (2, 4, 3072), "gdn_a_log": (2, 8), "gdn_dt_bias": (2, 8),
    "gdn_norm_w": (2, 128), "rwkv_mu": (2, 3328), "rwkv_w0": (2, 1024), "rwkv_w2": (2, 64, 1024),
    "rwkv_a0": (2, 1024), "rwkv_a2": (2, 64, 1024), "rwkv_g2": (2, 128, 1024), "rwkv_k_k": (2, 1024),
    "rwkv_k_a": (2, 1024), "rwkv_r_k": (2, 16, 64), "rwkv_ln_w": (2, 1024), "rwkv_ln_b": (2, 1024),
    "w_branch_a": (2, 1024, 2048), "w_branch_b": (2, 1024, 2048), "w_branch_c": (2, 1024, 2048),
    "w_out": (2, 2048, 2048), "norm_mlp_w": (2, 2048), "w_up": (2, 2048, 8192), "w_down": (2, 8192, 2048),
    "final_norm_w": (2048,),
}

_NC = [None]


def kernel(**inp):
    f = lambda a: np.ascontiguousarray(np.asarray(a, dtype=np.float32))
    if _NC[0] is None:
        _NC[0] = build()
    nc = _NC[0]
    params = {k: f(inp[k]) for k in PARAM_SHAPES}
    in_maps = []
    for c in range(8):
        sl = slice(c * NS, (c + 1) * NS)
        m = dict(params)
        m["xin"] = np.ascontiguousarray(np.concatenate(
            [f(inp["x_sample"])[sl, 0], f(inp["meta_tokens"]), f(inp["x_prompt"])[c % 4]], axis=0))
        m["consts"] = CONSTS
        m["st_c"] = f(inp["state_mlstm_c"])[:, sl]
        m["st_n"] = f(inp["state_mlstm_n"])[:, sl]
        m["st_m"] = f(inp["state_mlstm_m"])[:, sl]
        m["st_gs"] = f(inp["state_gdn_s"])[:, sl]
        m["st_gconv"] = f(inp["state_gdn_conv"])[:, sl]
        m["st_rs"] = f(inp["state_rwkv_s"])[:, sl]
        m["st_rshift"] = f(inp["state_rwkv_shift"])[:, sl]
        m = {k: np.ascontiguousarray(v) for k, v in m.items()}
        in_maps.append(m)
    res = run_bass_kernel_spmd(nc, in_maps, core_ids=list(range(8)))
    R = res.results
    y = [r["y"] for r in R]
    y_prompt = np.stack([y[c][NS + NMETA:] for c in range(4)], axis=0)
    y_sample = np.concatenate([y[c][0:NS] for c in range(8)], axis=0)[:, None, :]
    pk = lambda k: np.stack([R[c][k] for c in range(4)], axis=1)
    sk = lambda k: np.concatenate([R[c][k] for c in range(8)], axis=1)
    outs = (y_prompt, y_sample, pk("p_c"), pk("p_n"), pk("p_m"), pk("p_gs"), pk("p_gconv"), pk("p_rs"), pk("p_rshift"),
            sk("s_c"), sk("s_n"), sk("s_m"), sk("s_gs"), sk("s_gconv"), sk("s_rs"), sk("s_rshift"))
    return tuple(np.ascontiguousarray(o, dtype=np.float32) for o in outs)
```

```python
import numpy as np
import concourse.bass as bass
import concourse.mybir as mybir
from concourse.bass_utils import run_bass_kernel_spmd
from contextlib import ExitStack

F32 = mybir.dt.float32
BF16 = mybir.dt.bfloat16
AF = mybir.ActivationFunctionType
ALU = mybir.AluOpType
AX = mybir.AxisListType

D = 2048
KC = 16
NS = 16
NMETA = 16
SEQ = 2048
NROWS = NS + NMETA + SEQ
N_IN = 17688
DEPTH = 2
A0, B0, C0, G0 = 0, 4104, 8216, 11544
STAGES = ("mlstm", "gdn", "rwkv")


class Buf:
    __slots__ = ("name", "lw", "rd", "dsem", "dcnt")

    def __init__(self, name):
        self.name = name
        self.lw = None
        self.rd = {}
        self.dsem = None
        self.dcnt = 0


class Eng:
    def __init__(self, name, obj, sem):
        self.name = name
        self.obj = obj
        self.sem = sem
        self.cnt = 0
        self.known = {}


class KB:
    def __init__(self, nc, es):
        self.nc = nc
        self.es = es
        self.E = {}
        for nm, obj in (("pe", nc.tensor), ("act", nc.scalar), ("dve", nc.vector), ("pool", nc.gpsimd),
                        ("sp", nc.sync)):
            self.E[nm] = Eng(nm, obj, es.enter_context(nc.semaphore("sem_" + nm)))
        self.semid = {}
        self.dpool = []
        self.nd = 0
        self.out_tokens = {}
        self.n_inst = 0

    def buf(self, name):
        return Buf(name)

    def _dsem(self, b):
        if b.dsem is None:
            if self.dpool:
                b.dsem, b.dcnt = self.dpool.pop()
            else:
                b.dsem = self.es.enter_context(self.nc.semaphore("d%d" % self.nd))
                self.nd += 1
        return b.dsem

    def recycle(self, bufs):
        for b in bufs:
            if b.dsem is not None:
                self.dpool.append((b.dsem, b.dcnt))
                b.dsem = None

    def _waits(self, e, R, W):
        deps = {}
        for b in R:
            if b.lw is not None:
                s, v = b.lw
                if deps.get(s, 0) < v:
                    deps[s] = v
        for b in W:
            if b.lw is not None:
                s, v = b.lw
                if deps.get(s, 0) < v:
                    deps[s] = v
            for s, v in b.rd.items():
                if deps.get(s, 0) < v:
                    deps[s] = v
        for s, v in deps.items():
            if s is e.sem:
                if e.name == "pe" or e.name == "sp":
                    continue
                if e.cnt - v >= 2:
                    continue
            if e.known.get(s, 0) >= v:
                continue
            e.obj.wait_ge(s, v)
            e.known[s] = v
            self.n_inst += 1

    def op(self, en, fn, R=(), W=()):
        e = self.E[en]
        if e.cnt >= 30000:
            e.sem = self.es.enter_context(self.nc.semaphore("sem_%s_%d" % (en, self.n_inst)))
            e.cnt = 0
        self._waits(e, R, W)
        ins = fn(e.obj)
        e.cnt += 1
        ins.then_inc(e.sem, 1)
        tok = (e.sem, e.cnt)
        for b in R:
            if b.rd.get(tok[0], 0) < tok[1]:
                b.rd[tok[0]] = tok[1]
        for b in W:
            b.lw = tok
            b.rd = {}
        self.n_inst += 1
        return ins

    def dma(self, q, out, in_, R=(), W=(), semb=None, slow=False):
        e = self.E[q]
        self._waits(e, R, W)
        if semb is None:
            semb = W[0] if W else R[0]
        s = self._dsem(semb)
        if slow:
            ins = e.obj.dma_start(out=out, in_=in_, allow_slow_non_contiguous=True)
        else:
            ins = e.obj.dma_start(out=out, in_=in_)
        semb.dcnt += 16
        ins.then_inc(s, 16)
        tok = (s, semb.dcnt)
        for b in R:
            if b.rd.get(s, 0) < tok[1]:
                b.rd[s] = tok[1]
        for b in W:
            b.lw = tok
            b.rd = {}
        self.n_inst += 1
        return tok

    def barrier(self, bufs=()):
        bufs = list(bufs)
        for nm in ("pe", "act", "dve", "pool", "sp"):
            self._waits(self.E[nm], [], bufs)

    def finish(self, bufs):
        e = self.E["sp"]
        self._waits(e, list(bufs), list(bufs))


def build():
    nc = bass.Bass("TRN2", target_bir_lowering=False)
    es = ExitStack()
    with es:
        _build(nc, es)
    return nc


def tile_plan():
    tiles = []
    ch = [("s", b, 1, b) for b in range(NS)]
    ch.append(("p", NS, NMETA, 0))
    for j in range(5):
        ch.append(("p", NS + NMETA + 64 * j, 64, 0))
    tiles.append((0, NS + NMETA + 320, ch))
    r0 = NS + NMETA + 320
    for nchk in (6, 6, 6, 6, 3):
        ch = [("p", 64 * j, 64, 0) for j in range(nchk)]
        tiles.append((r0, 64 * nchk, ch))
        r0 += 64 * nchk
    assert r0 == NROWS
    return tiles


NMAX = 384
NWB = 4


def _build(nc, es):
    kb = KB(nc, es)
    op, dma = kb.op, kb.dma

    def dram_in(name, shape):
        return nc.dram_tensor(name, list(shape), F32, kind="ExternalInput").ap()

    def dram_out(name, shape):
        return nc.dram_tensor(name, list(shape), F32, kind="ExternalOutput").ap()

    xin = dram_in("xin", [NROWS, D])
    consts_d = dram_in("consts", [128, CONST_COLS])
    st_c = dram_in("st_c", [DEPTH, NS, 4, 256, 256])
    st_n = dram_in("st_n", [DEPTH, NS, 4, 256])
    st_m = dram_in("st_m", [DEPTH, NS, 4])
    st_gs = dram_in("st_gs", [DEPTH, NS, 8, 128, 128])
    st_gconv = dram_in("st_gconv", [DEPTH, NS, 3, 3072])
    st_rs = dram_in("st_rs", [DEPTH, NS, 16, 64, 64])
    st_rshift = dram_in("st_rshift", [DEPTH, NS, 3328])
    W = {}
    for nm, shp in PARAM_SHAPES.items():
        W[nm] = dram_in(nm, shp)
    y_out = dram_out("y", [NROWS, D])
    o_pc = dram_out("p_c", [DEPTH, 4, 256, 256])
    o_pn = dram_out("p_n", [DEPTH, 4, 256])
    o_pm = dram_out("p_m", [DEPTH, 4])
    o_pgs = dram_out("p_gs", [DEPTH, 8, 128, 128])
    o_pgconv = dram_out("p_gconv", [DEPTH, 3, 3072])
    o_prs = dram_out("p_rs", [DEPTH, 16, 64, 64])
    o_prshift = dram_out("p_rshift", [DEPTH, 3328])
    o_sc = dram_out("s_c", [DEPTH, NS, 4, 256, 256])
    o_sn = dram_out("s_n", [DEPTH, NS, 4, 256])
    o_sm = dram_out("s_m", [DEPTH, NS, 4])
    o_sgs = dram_out("s_gs", [DEPTH, NS, 8, 128, 128])
    o_sgconv = dram_out("s_gconv", [DEPTH, NS, 3, 3072])
    o_srs = dram_out("s_rs", [DEPTH, NS, 16, 64, 64])
    o_srshift = dram_out("s_rshift", [DEPTH, NS, 3328])

    outbufs = []
    cur = [es]

    uid = [0]

    def sb(name, shape, dt=F32):
        uid[0] += 1
        name = "%s_%d" % (name, uid[0])
        t = cur[0].enter_context(nc.sbuf_tensor(name, list(shape), dt))
        b = kb.buf(name)
        if cur[0] is not es:
            arena_bufs.append(b)
        return t, b
    arena_bufs = []

    class Arena:
        def __enter__(self):
            self.st = ExitStack()
            self.st.__enter__()
            self.prev = cur[0]
            self.mark = len(arena_bufs)
            cur[0] = self.st
            return self

        def __exit__(self, *a):
            mine = arena_bufs[self.mark:]
            kb.barrier(mine)
            kb.recycle(mine)
            del arena_bufs[self.mark:]
            cur[0] = self.prev
            return self.st.__exit__(*a)

    def run_interleaved(makers, K):
        pending = list(makers)
        active = {}
        free = list(range(K))
        while pending or active:
            while pending and free:
                s_ = free.pop(0)
                active[s_] = pending.pop(0)(s_)
            for s_ in list(active.keys()):
                try:
                    next(active[s_])
                except StopIteration:
                    del active[s_]
                    free.append(s_)

    cst, cstB = sb("cst", [128, CONST_COLS])
    dma("sp", cst[:, :], consts_d[:, :], W=[cstB])
    ident = cst[:, CO["ident"]:CO["ident"] + 128]
    ones = cst[:, CO["ones"]:CO["ones"] + 128]

    def cview(name, p, shape):
        n = int(np.prod(shape))
        v = cst[0:p, CO[name]:CO[name] + n]
        return v.rearrange("p (a b) -> p a b", b=shape[1])
    nmT_incl = cst[0:64, CO["nmT_incl"]:CO["nmT_incl"] + 64]
    selh = cview("selh", 8, (8, 64))
    sellast = cview("sellast", 64, (3, 128))
    LV = {64: 0, 16: 1, 1: 2}

    ident_bf, ident_bfB = sb("ident_bf", [128, 128], BF16)
    op("dve", lambda e: e.tensor_copy(out=ident_bf[:, :], in_=ident), R=[cstB], W=[ident_bfB])

    PS = []
    for i in range(8):
        t = es.enter_context(nc.psum_tensor("ps%d" % i, [128, 512], F32))
        PS.append((t, kb.buf("ps%d" % i)))
    psi = [0]

    def psum():
        t = PS[psi[0] % 8]
        psi[0] += 1
        return t

    def load_fm_vec(name, src, nblk):
        t, b = sb(name, [128, DEPTH, nblk])
        for l in range(DEPTH):
            dma("sp", t[:, l, :], src[l].rearrange("(k p) -> p k", p=128), W=[b], slow=True)
        return t, b

    nw_mix, nw_mixB = load_fm_vec("nw_mix", W["norm_mix_w"], 16)
    nw_mlp, nw_mlpB = load_fm_vec("nw_mlp", W["norm_mlp_w"], 16)
    nw_fin, nw_finB = sb("nw_fin", [128, 16])
    dma("sp", nw_fin[:, :], W["final_norm_w"].rearrange("(k p) -> p k", p=128), W=[nw_finB], slow=True)

    epst, epsB = sb("epst", [128, 4])
    op("dve", lambda e: e.memset(epst[:, 0:1], 1e-6), W=[epsB])
    op("dve", lambda e: e.memset(epst[:, 1:2], 1e-12), W=[epsB])
    op("dve", lambda e: e.memset(epst[:, 2:3], 64e-5), W=[epsB])
    op("dve", lambda e: e.memset(epst[:, 3:4], 1.0), W=[epsB])
    zrow, zrowB = sb("zrow", [8, 64])
    op("dve", lambda e: e.memset(zrow[:, :], 0.0), W=[zrowB])
    zcol, zcolB = sb("zcol", [128, 64])
    op("dve", lambda e: e.memset(zcol[:, :], 0.0), W=[zcolB])

    xF, xFB = sb("xF", [128, KC, NMAX])
    xnF, xnFB = sb("xnF", [128, KC, NMAX], BF16)
    wbuf = [sb("wb%d" % i, [128, 2048], BF16) for i in range(NWB)]
    wbi = [0]
    yF = [sb("y%sF" % c, [128, 8, NMAX], BF16) for c in "abc"]
    rstd, rstdB = sb("rstd", [128, NMAX])
    tmpA, tmpAB = sb("tmpA", [128, NMAX])

    wscrs = [nc.dram_tensor("wscratch%d" % i, [175, 128, 2048], BF16).ap() for i in range(4)]
    wcache = {}

    def load_w(src2d, c0, ncol, kchunks):
        t, b = wbuf[wbi[0] % NWB]
        wbi[0] += 1
        n = kchunks * ncol
        v = t[:, 0:n].rearrange("p (k c) -> p k c", c=ncol)
        key = (str(src2d.tensor.name), int(src2d.offset), c0, ncol, kchunks)
        if key in wcache:
            idx, blkB = wcache[key]
            dma("sp", t[:, 0:n], wscrs[idx // 175][idx % 175, :, 0:n], R=[blkB], W=[b])
        else:
            src = src2d.rearrange("(k p) n -> p k n", p=128)[:, :, c0:c0 + ncol]
            dma("pool", v, src, W=[b])
            idx = len(wcache)
            blkB = kb.buf("wblk%d" % idx)
            dma("sp", wscrs[idx // 175][idx % 175, :, 0:n], t[:, 0:n], R=[b], W=[blkB], semb=b)
            wcache[key] = (idx, blkB)
        return v, b

    def rmsnorm(N, nw_ap_fn, nwB, inplace=False):
        ps, psB = psum()
        for k in range(KC):
            op("act", lambda e, k=k: e.activation(out=tmpA[:, 0:N], in_=xF[:, k, 0:N], func=AF.Square),
               R=[xFB], W=[tmpAB])
            op("pe", lambda e, k=k: e.matmul(ps[:, 0:N], lhsT=ones, rhs=tmpA[:, 0:N], start=(k == 0),
                                             stop=(k == KC - 1)), R=[tmpAB, cstB], W=[psB])
        op("act", lambda e: e.activation(out=rstd[:, 0:N], in_=ps[:, 0:N], func=AF.Ln, scale=1.0 / D, bias=epst[:, 0:1]),
           R=[psB, epsB], W=[rstdB])
        op("act", lambda e: e.activation(out=rstd[:, 0:N], in_=rstd[:, 0:N], func=AF.Exp, scale=-0.5),
           R=[rstdB], W=[rstdB])
        for k in range(KC):
            if inplace:
                op("dve", lambda e, k=k: e.scalar_tensor_tensor(out=xF[:, k, 0:N], in0=xF[:, k, 0:N], scalar=nw_ap_fn(k),
                                                                in1=rstd[:, 0:N], op0=ALU.mult, op1=ALU.mult),
                   R=[xFB, rstdB, nwB], W=[xFB])
            else:
                op("dve", lambda e, k=k: e.scalar_tensor_tensor(out=xnF[:, k, 0:N], in0=xF[:, k, 0:N], scalar=nw_ap_fn(k),
                                                                in1=rstd[:, 0:N], op0=ALU.mult, op1=ALU.mult),
                   R=[xFB, rstdB, nwB], W=[xnFB])

    def proj_fm(src2d, c0, ncols, N, consume, kchunks=KC, rhs_fn=None, rhsB=None):
        rhs_fn = rhs_fn or (lambda k: xnF[:, k, 0:N])
        rhsB = rhsB or xnFB
        done = 0
        j = 0
        while done < ncols:
            m = min(128, ncols - done)
            wv, wB = load_w(src2d, c0 + done, m, kchunks)
            ps, psB = psum()
            for k in range(kchunks):
                op("pe", lambda e, k=k: e.matmul(ps[0:m, 0:N], lhsT=wv[:, k, 0:m], rhs=rhs_fn(k), start=(k == 0),
                                                 stop=(k == kchunks - 1)), R=[wB, rhsB], W=[psB])
            consume(j, ps, psB, m)
            j += 1
            done += m

    Cext = []
    for l in range(DEPTH):
        t, b = sb("Cext%d" % l, [128, 4, 2, 257])
        op("dve", lambda e, t=t: e.memset(t[:, :, :, :], 0.0), W=[b])
        Cext.append((t, b))
    mprev = []
    for l in range(DEPTH):
        t, b = sb("mprev%d" % l, [4, 1])
        op("dve", lambda e, t=t: e.memset(t[:, :], 0.0), W=[b])
        mprev.append((t, b))
    mb_i, mb_iB = sb("mb_i", [4, DEPTH])
    mb_f, mb_fB = sb("mb_f", [4, DEPTH])
    dma("sp", mb_i[:, :], W["mlstm_b_i"].rearrange("l h -> h l"), W=[mb_iB], slow=True)
    dma("sp", mb_f[:, :], W["mlstm_b_f"].rearrange("l h -> h l"), W=[mb_fB], slow=True)
    nmb_f, nmb_fB = sb("nmb_f", [4, DEPTH])
    op("dve", lambda e: e.tensor_scalar(out=nmb_f[:, :], in0=mb_f[:, :], scalar1=-1.0, scalar2=None, op0=ALU.mult),
       R=[mb_fB], W=[nmb_fB])

    def to_fm(src, srcB, L, cs, dst):
        dt, dB = dst
        ps, psB = psum()
        pv = ps.bitcast(BF16)
        for blk in range(8):
            op("pe", lambda e, blk=blk: e.transpose(out=pv[:, blk * 64:blk * 64 + L], in_=src[0:L, blk * 128:(blk + 1) * 128],
                                                    identity=ident_bf[0:L, 0:L]), R=[srcB, ident_bfB], W=[psB])
        for blk in range(8):
            if blk % 2 == 0:
                op("act", lambda e, blk=blk: e.activation(out=dt[:, blk, cs], in_=pv[:, blk * 64:blk * 64 + L], func=AF.Copy),
                   R=[psB], W=[dB])
            else:
                op("dve", lambda e, blk=blk: e.tensor_copy(out=dt[:, blk, cs], in_=pv[:, blk * 64:blk * 64 + L]),
                   R=[psB], W=[dB])

    def mlstm_phase(l, tile):
        r0, N, chunks = tile
        CeP, CePB = Cext[l]
        w_in = W["w_in"][l]
        qF, qFB = sb("m_qF", [128, 8, NMAX], BF16)
        kF, kFB = sb("m_kF", [128, 8, NMAX], BF16)
        vF, vFB = sb("m_vF", [128, 8, NMAX], BF16)
        oF, oFB = sb("m_oF", [128, 8, NMAX], BF16)
        gi, giB = sb("m_gi", [4, NMAX])
        gf, gfB = sb("m_gf", [4, NMAX])
        gm, gmB = sb("m_gm", [4, NMAX])
        gb, gbB = sb("m_gb", [4, NMAX])
        gu, guB = sb("m_gu", [4, NMAX])
        gR, gRB = sb("m_gR", [128, NMAX])
        msamp, msampB = sb("m_msamp", [4, NS])
        tokT, tokTB = sb("m_tokT", [64, 128])
        kT, kTB = sb("m_kT", [64, 1024], BF16)
        vext, vextB = sb("m_vext", [64, 4, 257], BF16)
        osg, osgB = sb("m_osg", [64, 1024], BF16)
        MK = 4
        mslots = []
        for _i in range(MK):
            mslots.append({"Et": sb("m_E", [64, 64]), "PT": sb("m_PT", [64, 64], BF16), "Asb": sb("m_Asb", [64, 257]),
                           "Rt": sb("m_R", [64, 257]), "den": sb("m_den", [64, 2]), "ktil": sb("m_ktil", [64, 256], BF16)})
        hT, hTB = sb("m_hT", [64, 1024])
        dbc, dbcB = sb("m_dbc", [128, 4])
        ssq, ssqB = sb("m_ssq", [64, 8])
        yT, yTB = sb("m_yT", [64, 1024], BF16)
        sq_junk, sq_junkB = sb("m_sqj", [64, 256])
        CextS, CextSB = sb("CextS", [128, 4, 2, 257])
        Cbf, CbfB = sb("Cbf", [128, 4, 2, 257], BF16)
        bc1024, bc1024B = sb("bc1024", [64, 1024])
        op("dve", lambda e: e.memset(gR[:, :], 0.0), W=[gRB])
        op("dve", lambda e: e.memset(vext[:, :, 256:257], 1.0), W=[vextB])

        def cons_to(dst, dstB, scale=1.0, func=AF.Copy):
            def f(j, ps, psB, m):
                op("act", lambda e: e.activation(out=dst[0:m, j, 0:N], in_=ps[0:m, 0:N], func=func, scale=scale),
                   R=[psB], W=[dstB])
            return f
        proj_fm(w_in, A0 + 0, 1024, N, cons_to(qF, qFB))
        proj_fm(w_in, A0 + 1024, 1024, N, cons_to(kF, kFB, scale=1.0 / 16.0))
        proj_fm(w_in, A0 + 2048, 1024, N, cons_to(vF, vFB))
        proj_fm(w_in, A0 + 3072, 1024, N, cons_to(oF, oFB, func=AF.Sigmoid))

        def cons_i(j, ps, psB, m):
            op("act", lambda e: e.activation(out=gi[:, 0:N], in_=ps[0:4, 0:N], func=AF.Identity, bias=mb_i[:, l:l + 1]),
               R=[psB, mb_iB], W=[giB])
        proj_fm(w_in, A0 + 4096, 4, N, cons_i)

        def cons_f(j, ps, psB, m):
            op("act", lambda e: e.activation(out=gf[:, 0:N], in_=ps[0:4, 0:N], func=AF.Exp, scale=-1.0,
                                             bias=nmb_f[:, l:l + 1]), R=[psB, nmb_fB], W=[gfB])
            op("act", lambda e: e.activation(out=gf[:, 0:N], in_=gf[:, 0:N], func=AF.Ln, bias=epst[0:4, 3:4]),
               R=[gfB, epsB], W=[gfB])
            op("dve", lambda e: e.tensor_scalar(out=gf[:, 0:N], in0=gf[:, 0:N], scalar1=-1.0, scalar2=None,
                                                op0=ALU.mult), R=[gfB], W=[gfB])
        proj_fm(w_in, A0 + 4100, 4, N, cons_f)

        mP, mPB = mprev[l]
        has_s = any(c[0] == "s" for c in chunks)
        pc0 = NS if has_s else 0
        if has_s:
            dma("sp", msamp[:, :], st_m[l].rearrange("b h -> h b"), W=[msampB], slow=True)
            op("dve", lambda e: e.tensor_tensor(out=gm[:, 0:NS], in0=gf[:, 0:NS], in1=msamp[:, :], op=ALU.add),
               R=[gfB, msampB], W=[gmB])
            op("dve", lambda e: e.tensor_tensor(out=gm[:, 0:NS], in0=gm[:, 0:NS], in1=gi[:, 0:NS], op=ALU.max),
               R=[gmB, giB], W=[gmB])
            op("dve", lambda e: e.tensor_copy(out=gb[:, 0:NS], in_=gf[:, 0:NS]), R=[gfB], W=[gbB])
        op("dve", lambda e: e.tensor_tensor_scan(out=gm[:, pc0:N], data0=gf[:, pc0:N], data1=gi[:, pc0:N],
                                                 initial=mP[:, 0:1], op0=ALU.add, op1=ALU.max),
           R=[gfB, giB, mPB], W=[gmB])
        for (kind, c0, L, sidx) in chunks:
            if kind != "p":
                continue
            op("dve", lambda e, c0=c0, L=L: e.tensor_tensor_scan(out=gb[:, c0:c0 + L], data0=gf[:, c0:c0 + L],
                                                                 data1=zrow[0:4, 0:L], initial=0.0,
                                                                 op0=ALU.add, op1=ALU.add),
               R=[gfB, zrowB], W=[gbB])
        op("dve", lambda e: e.tensor_tensor(out=gu[:, 0:N], in0=gb[:, 0:N], in1=gm[:, 0:N], op=ALU.subtract),
           R=[gbB, gmB], W=[guB])
        op("dve", lambda e: e.tensor_tensor(out=gR[0:4, 0:N], in0=gi[:, 0:N], in1=gb[:, 0:N], op=ALU.subtract),
           R=[giB, gbB], W=[gRB])
        op("act", lambda e: e.activation(out=gR[64:68, 0:N], in_=gm[:, 0:N], func=AF.Exp, scale=-1.0),
           R=[gmB], W=[gRB])
        if has_s:
            op("dve", lambda e: e.tensor_tensor(out=gR[32:36, 0:NS], in0=gu[:, 0:NS], in1=msamp[:, :], op=ALU.add),
               R=[guB, msampB], W=[gRB])
            op("act", lambda e: e.activation(out=gR[32:36, 0:NS], in_=gR[32:36, 0:NS], func=AF.Exp), R=[gRB], W=[gRB])
            op("dve", lambda e: e.tensor_tensor(out=gR[96:100, 0:NS], in0=gR[0:4, 0:NS], in1=gu[:, 0:NS], op=ALU.add),
               R=[gRB, guB], W=[gRB])
            op("act", lambda e: e.activation(out=gR[96:100, 0:NS], in_=gR[96:100, 0:NS], func=AF.Exp), R=[gRB], W=[gRB])
        first = True
        for (kind, c0, L, sidx) in chunks:
            if kind != "p":
                continue
            mp_ap = mP[:, 0:1] if first else gm[:, c0 - 1:c0]
            op("act", lambda e, c0=c0, L=L, mp_ap=mp_ap: e.activation(out=gR[32:36, c0:c0 + L], in_=gu[:, c0:c0 + L],
                                                                     func=AF.Exp, bias=mp_ap),
               R=[guB, gmB, mPB], W=[gRB])
            op("act", lambda e, c0=c0, L=L: e.activation(out=gR[96:100, c0:c0 + L], in_=gR[0:4, c0:c0 + L],
                                                         func=AF.Exp, bias=gu[:, c0 + L - 1:c0 + L]),
               R=[guB, gRB], W=[gRB])
            first = False

        dma("sp", bc1024[:, :], W["mlstm_norm_w"][l:l + 1, :].partition_broadcast(64), W=[bc1024B])

        def chunk(c0, L, Ct, CtB):
            lv = LV[L]
            cs = slice(c0, c0 + L)
            ps, psB = psum()
            op("pe", lambda e: e.transpose(out=ps[0:L, 0:128], in_=gR[:, cs], identity=ident), R=[gRB, cstB], W=[psB])
            op("act", lambda e: e.activation(out=tokT[0:L, :], in_=ps[0:L, 0:128], func=AF.Copy), R=[psB], W=[tokTB])
            for (src, srcB, dstfn, dstB) in ((kF, kFB, lambda blk: kT[0:L, blk * 128:(blk + 1) * 128], kTB),
                                             (vF, vFB, lambda blk: vext[0:L, blk // 2, (blk % 2) * 128:(blk % 2) * 128 + 128], vextB),
                                             (oF, oFB, lambda blk: osg[0:L, blk * 128:(blk + 1) * 128], osgB)):
                ps, psB = psum()
                pv = ps.bitcast(BF16)
                for blk in range(8):
                    op("pe", lambda e, blk=blk, pv=pv, src=src: e.transpose(out=pv[0:L, blk * 128:(blk + 1) * 128],
                                                                          in_=src[:, blk, cs], identity=ident_bf[:, :]),
                       R=[srcB, ident_bfB], W=[psB])
                for blk in range(8):
                    if blk % 2 == 0:
                        op("act", lambda e, blk=blk, pv=pv, dstfn=dstfn: e.activation(
                            out=dstfn(blk), in_=pv[0:L, blk * 128:(blk + 1) * 128], func=AF.Copy), R=[psB], W=[dstB])
                    else:
                        op("dve", lambda e, blk=blk, pv=pv, dstfn=dstfn: e.tensor_copy(
                            out=dstfn(blk), in_=pv[0:L, blk * 128:(blk + 1) * 128]), R=[psB], W=[dstB])
            op("act", lambda e: e.activation(out=Cbf[:, :, :, :], in_=Ct[:, :, :, :], func=AF.Copy), R=[CtB], W=[CbfB])
            psd, psdB = psum()
            op("pe", lambda e: e.matmul(psd[:, 0:4], lhsT=sellast[0:L, lv, :], rhs=tokT[0:L, 32:36], start=True, stop=True),
               R=[cstB, tokTB], W=[psdB])
            op("dve", lambda e: e.tensor_copy(out=dbc[:, :], in_=psd[:, 0:4]), R=[psdB], W=[dbcB])
            def mhead(h, sl):
                SL = mslots[sl]
                (Et, EtB), (PT, PTB), (Asb, AsbB) = SL['Et'], SL['PT'], SL['Asb']
                (Rt, RtB), (den, denB), (ktil, ktilB) = SL['Rt'], SL['den'], SL['ktil']
                psu, psuB = psum()
                op("pe", lambda e: e.matmul(psu[0:L, 0:L], lhsT=selh[0:4, h, 0:L], rhs=gu[:, cs], start=True, stop=False),
                   R=[cstB, guB], W=[psuB])
                op("pe", lambda e: e.matmul(psu[0:L, 0:L], lhsT=ident[0:L, 0:L], rhs=nmT_incl[0:L, 0:L], start=False, stop=True),
                   R=[cstB], W=[psuB])
                op("act", lambda e: e.activation(out=Et[0:L, 0:L], in_=psu[0:L, 0:L], func=AF.Exp, bias=tokT[0:L, h:h + 1]),
                   R=[psuB, tokTB], W=[EtB])
                yield
                pst, pstB = psum()
                for kc in range(2):
                    op("pe", lambda e, kc=kc: e.matmul(pst[0:L, 0:L], lhsT=kF[:, 2 * h + kc, cs], rhs=qF[:, 2 * h + kc, cs],
                                                       start=(kc == 0), stop=(kc == 1)), R=[kFB, qFB], W=[pstB])
                op("dve", lambda e: e.tensor_tensor(out=PT[0:L, 0:L], in0=pst[0:L, 0:L], in1=Et[0:L, 0:L], op=ALU.mult),
                   R=[pstB, EtB], W=[PTB])
                yield
                psb, psbB = psum()
                op("pe", lambda e: e.matmul(psb[0:L, 0:257], lhsT=PT[0:L, 0:L], rhs=vext[0:L, h, :], start=True, stop=True),
                   R=[PTB, vextB], W=[psbB])
                psa, psaB = psum()
                for kc in range(2):
                    op("pe", lambda e, kc=kc: e.matmul(psa[0:L, 0:257], lhsT=qF[:, 2 * h + kc, cs], rhs=Cbf[:, h, kc, :],
                                                       start=(kc == 0), stop=(kc == 1)), R=[qFB, CbfB], W=[psaB])
                op("act", lambda e: e.activation(out=Asb[0:L, :], in_=psa[0:L, 0:257], func=AF.Copy,
                                                 scale=tokT[0:L, 32 + h:33 + h]), R=[psaB, tokTB], W=[AsbB])
                op("dve", lambda e: e.tensor_tensor(out=Rt[0:L, :], in0=Asb[0:L, :], in1=psb[0:L, 0:257], op=ALU.add),
                   R=[AsbB, psbB], W=[RtB])
                yield
                op("act", lambda e: e.activation(out=den[0:L, 0:1], in_=Rt[0:L, 256:257], func=AF.Abs),
                   R=[RtB], W=[denB])
                op("dve", lambda e: e.tensor_tensor(out=den[0:L, 0:1], in0=den[0:L, 0:1], in1=tokT[0:L, 64 + h:65 + h],
                                                    op=ALU.max), R=[denB, tokTB], W=[denB])
                op("dve", lambda e: e.reciprocal(out=den[0:L, 1:2], in_=den[0:L, 0:1]), R=[denB], W=[denB])
                op("act", lambda e: e.activation(out=hT[0:L, h * 256:(h + 1) * 256], in_=Rt[0:L, 0:256], func=AF.Copy,
                                                 scale=den[0:L, 1:2]), R=[RtB, denB], W=[hTB])
                yield
                op("act", lambda e: e.activation(out=ktil[0:L, :], in_=kT[0:L, h * 256:(h + 1) * 256], func=AF.Copy,
                                                 scale=tokT[0:L, 96 + h:97 + h]), R=[kTB, tokTB], W=[ktilB])
                yield
                for kc in range(2):
                    psc, pscB = psum()
                    op("pe", lambda e, kc=kc, psc=psc: e.matmul(psc[:, 0:257], lhsT=ktil[0:L, kc * 128:(kc + 1) * 128],
                                                                rhs=vext[0:L, h, :], start=True, stop=True),
                       R=[ktilB, vextB], W=[pscB])
                    op("dve", lambda e, kc=kc, psc=psc: e.scalar_tensor_tensor(
                        out=Ct[:, h, kc, :], in0=Ct[:, h, kc, :], scalar=dbc[:, h:h + 1], in1=psc[:, 0:257],
                        op0=ALU.mult, op1=ALU.add), R=[CtB, dbcB, pscB], W=[CtB])
            run_interleaved([(lambda sl, h=h: mhead(h, sl)) for h in range(4)], MK)
            for h in range(4):
                op("act", lambda e, h=h: e.activation(out=sq_junk[0:L, :], in_=hT[0:L, h * 256:(h + 1) * 256], func=AF.Square,
                                                      accum_out=ssq[0:L, h:h + 1]), R=[hTB], W=[ssqB, sq_junkB])
            op("act", lambda e: e.activation(out=ssq[0:L, 4:8], in_=ssq[0:L, 0:4], func=AF.Ln, scale=1.0 / 256.0,
                                             bias=epst[0:L, 0:1]), R=[ssqB, epsB], W=[ssqB])
            op("act", lambda e: e.activation(out=ssq[0:L, 4:8], in_=ssq[0:L, 4:8], func=AF.Exp, scale=-0.5), R=[ssqB], W=[ssqB])
            for h in range(4):
                op("dve", lambda e, h=h: e.scalar_tensor_tensor(out=hT[0:L, h * 256:(h + 1) * 256],
                                                                in0=hT[0:L, h * 256:(h + 1) * 256],
                                                                scalar=ssq[0:L, 4 + h:5 + h],
                                                                in1=bc1024[0:L, h * 256:(h + 1) * 256],
                                                                op0=ALU.mult, op1=ALU.mult),
                   R=[hTB, ssqB, bc1024B], W=[hTB])
            op("dve", lambda e: e.tensor_tensor(out=yT[0:L, :], in0=hT[0:L, :], in1=osg[0:L, :], op=ALU.mult),
               R=[hTB, osgB], W=[yTB])
            to_fm(yT, yTB, L, cs, yF[0])

        for (kind, c0, L, sidx) in chunks:
            if kind == "s":
                Ct, CtB = CextS, CextSB
                for h in range(4):
                    dma("sp", Ct[:, h, :, 0:256], st_c[l, sidx, h].rearrange("(kc p) v -> p kc v", p=128), W=[CtB])
                dma("sp", Ct[:, :, :, 256:257], st_n[l, sidx].rearrange("h (kc p o) -> p h kc o", p=128, o=1),
                    W=[CtB], slow=True)
            else:
                Ct, CtB = CeP, CePB
            chunk(c0, L, Ct, CtB)
            if kind == "s":
                for h in range(4):
                    dma("sp", o_sc[l, sidx, h].rearrange("(kc p) v -> p kc v", p=128), Ct[:, h, :, 0:256],
                        R=[CtB], semb=CtB)
                dma("sp", o_sn[l, sidx].rearrange("h (kc p o) -> p h kc o", p=128, o=1), Ct[:, :, :, 256:257],
                    R=[CtB], semb=CtB, slow=True)
        if has_s:
            dma("sp", o_sm[l].rearrange("b h -> h b"), gm[:, 0:NS], R=[gmB], semb=gmB, slow=True)
        op("dve", lambda e: e.tensor_copy(out=mP[:, 0:1], in_=gm[:, N - 1:N]), R=[gmB], W=[mPB])


    nm_strict = cst[0:64, CO["nm_strict"]:CO["nm_strict"] + 64]
    nmT_strict = cst[0:64, CO["nmT_strict"]:CO["nmT_strict"] + 64]
    GS = []
    ghist = []
    for l in range(DEPTH):
        t, b = sb("GS%d" % l, [128, 8, 128])
        op("dve", lambda e, t=t: e.memset(t[:, :, :], 0.0), W=[b])
        GS.append((t, b))
        t, b = sb("ghist%d" % l, [128, 24, 3])
        op("dve", lambda e, t=t: e.memset(t[:, :, :], 0.0), W=[b])
        ghist.append((t, b))
    gcw, gcwB = sb("gcw", [128, DEPTH, 24, 4])
    gpar, gparB = sb("gpar", [8, DEPTH, 4])
    for l in range(DEPTH):
        dma("sp", gpar[:, l, 0:1], W["gdn_a_log"][l:l + 1, :].rearrange("o h -> h o"), W=[gparB], slow=True)
        dma("sp", gpar[:, l, 1:2], W["gdn_dt_bias"][l:l + 1, :].rearrange("o h -> h o"), W=[gparB], slow=True)
        op("act", lambda e, l=l: e.activation(out=gpar[:, l, 2:3], in_=gpar[:, l, 0:1], func=AF.Exp), R=[gparB], W=[gparB])
        op("dve", lambda e, l=l: e.tensor_scalar(out=gpar[:, l, 2:3], in0=gpar[:, l, 2:3], scalar1=-1.0, scalar2=None,
                                                 op0=ALU.mult), R=[gparB], W=[gparB])
    with Arena():
        cwt, cwtB = sb("cwt", [4, 3072])
        for l in range(DEPTH):
            dma("sp", cwt[:, :], W["gdn_conv_w"][l], W=[cwtB])
            for q in range(6):
                ps, psB = psum()
                for kk in range(4):
                    k = q * 4 + kk
                    op("pe", lambda e, k=k, kk=kk: e.transpose(out=ps[:, kk * 4:kk * 4 + 4], in_=cwt[0:4, k * 128:(k + 1) * 128],
                                                              identity=ident[0:4, 0:4]), R=[cwtB, cstB], W=[psB])
                op("dve", lambda e, q=q, l=l: e.tensor_copy(out=gcw[:, l, q * 4:q * 4 + 4, :],
                                                            in_=ps[:, 0:16].rearrange("p (a b) -> p a b", b=4)),
                   R=[psB], W=[gcwB])

    def gdn_phase(l, tile):
        r0, N, chunks = tile
        St, StB = GS[l]
        hs, hsB = ghist[l]
        w_in = W["w_in"][l]
        has_s = any(c[0] == "s" for c in chunks)
        pc0 = NS if has_s else 0
        Np = N - pc0
        xcs = [sb("g_xc%d" % i, [128, NMAX + 3]) for i in range(2)]
        yc, ycB = sb("g_yc", [128, NMAX])
        qF, qFB = sb("g_qF", [128, 8, NMAX], BF16)
        kF, kFB = sb("g_kF", [128, 8, NMAX], BF16)
        vF, vFB = sb("g_vF", [128, 8, NMAX], BF16)
        zF, zFB = sb("g_zF", [128, 8, NMAX], BF16)
        ra, raB = sb("g_ra", [8, NMAX])
        rlb, rlbB = sb("g_rlb", [8, NMAX])
        rg, rgB = sb("g_rg", [8, NMAX])
        rng_, rngB = sb("g_rng", [8, NMAX])
        rgb, rgbB = sb("g_rgb", [8, NMAX])
        stA, stAB = sb("g_stA", [128, NMAX])
        stB, stBB = sb("g_stB", [64, NMAX])
        tokA, tokAB = sb("g_tokA", [64, 128])
        tokB, tokBB = sb("g_tokB", [64, 64])
        kT, kTB = sb("g_kT", [64, 1024], BF16)
        vT, vTB = sb("g_vT", [64, 1024], BF16)
        zT, zTB = sb("g_zT", [64, 1024], BF16)
        GK = 4
        def mkslot():
            d = {}
            for nm_, shp_, dt_ in (("E1", [64, 64], F32), ("E2", [64, 64], F32), ("E3", [64, 64], F32),
                                   ("P0", [64, 64], F32), ("P1", [64, 64], F32), ("Q0", [64, 64], F32), ("Q1", [64, 64], F32),
                                   ("TT", [64, 64], F32), ("attnT", [64, 64], BF16), ("bv", [64, 128], F32),
                                   ("bek", [64, 128], F32), ("wF", [128, 64], BF16), ("usb", [64, 128], F32),
                                   ("vnew", [64, 128], BF16), ("qsb", [64, 128], F32), ("kd", [64, 128], BF16)):
                d[nm_] = sb("g_" + nm_, shp_, dt_)
            return d
        oT, oTB = sb("g_oT", [64, 1024])
        dbc, dbcB = sb("g_dbc", [128, 8])
        Sbf, SbfB = sb("g_Sbf", [128, 8, 128], BF16)
        if has_s:
            SS, SSB = sb("g_SS", [128, 8, 128])
        ssq, ssqB = sb("g_ssq", [64, 16])
        sqj, sqjB = sb("g_sqj", [64, 128])
        yT, yTB = sb("g_yT", [64, 1024], BF16)
        bcn, bcnB = sb("g_bcn", [64, 128])
        op("dve", lambda e: e.memset(stA[:, :], 0.0), W=[stAB])
        op("dve", lambda e: e.memset(stB[:, :], 0.0), W=[stBB])
        dma("sp", bcn[:, :], W["gdn_norm_w"][l:l + 1, :].partition_broadcast(64), W=[bcnB])
        with Arena():
            if has_s:
                xs, xsB = sb("g_xs", [128, 24, NS])
                bufS, bufSB = sb("g_bufS", [128, 24, NS * 3])
                st48, st48B = sb("g_st48", [48, 1536])
            if has_s:
                for hq in range(2):
                    dma("sp", st48[:, :], st_gconv[l].rearrange("b j c -> (b j) c")[:, hq * 1536:(hq + 1) * 1536], W=[st48B])
                    for kq in range(12):
                        k = hq * 12 + kq
                        ps, psB = psum()
                        op("pe", lambda e, kq=kq, ps=ps: e.transpose(out=ps[:, 0:48], in_=st48[0:48, kq * 128:(kq + 1) * 128],
                                                                     identity=ident[0:48, 0:48]), R=[st48B, cstB], W=[psB])
                        op("dve", lambda e, k=k, ps=ps: e.tensor_copy(out=bufS[:, k, :], in_=ps[:, 0:48]), R=[psB], W=[bufSB])
                dma("sp", o_sgconv[l][:, 0:2, :], st_gconv[l][:, 1:3, :], R=[st48B], semb=st48B)

            def cons_qkv(j, ps, psB, m):
                xc, xcB = xcs[j % 2]
                op("dve", lambda e: e.tensor_copy(out=xc[:, 0:3], in_=hs[:, j, :]), R=[hsB], W=[xcB])
                if has_s:
                    op("act", lambda e: e.activation(out=xs[:, j, :], in_=ps[:, 0:NS], func=AF.Copy), R=[psB], W=[xsB])
                op("act", lambda e: e.activation(out=xc[:, 3:3 + Np], in_=ps[:, pc0:N], func=AF.Copy), R=[psB], W=[xcB])
                op("dve", lambda e: e.tensor_scalar(out=yc[:, pc0:N], in0=xc[:, 0:Np], scalar1=gcw[:, l, j, 0:1],
                                                    scalar2=None, op0=ALU.mult), R=[xcB, gcwB], W=[ycB])
                for t in range(1, 4):
                    op("dve", lambda e, t=t: e.scalar_tensor_tensor(out=yc[:, pc0:N], in0=xc[:, t:t + Np],
                                                                    scalar=gcw[:, l, j, t:t + 1], in1=yc[:, pc0:N],
                                                                    op0=ALU.mult, op1=ALU.add), R=[xcB, gcwB, ycB], W=[ycB])
                op("dve", lambda e: e.tensor_copy(out=hs[:, j, :], in_=xc[:, Np:Np + 3]), R=[xcB], W=[hsB])
                if has_s:
                    bs = bufS[:, j, :].rearrange("p (b t) -> p b t", t=3)
                    op("dve", lambda e: e.tensor_scalar(out=yc[:, 0:NS], in0=xs[:, j, :], scalar1=gcw[:, l, j, 3:4],
                                                        scalar2=None, op0=ALU.mult), R=[xsB, gcwB], W=[ycB])
                    for t in range(3):
                        op("dve", lambda e, t=t: e.scalar_tensor_tensor(out=yc[:, 0:NS], in0=bs[:, :, t],
                                                                        scalar=gcw[:, l, j, t:t + 1], in1=yc[:, 0:NS],
                                                                        op0=ALU.mult, op1=ALU.add),
                           R=[bufSB, gcwB, ycB], W=[ycB])
                if j >= 16:
                    op("act", lambda e: e.activation(out=vF[:, j - 16, 0:N], in_=yc[:, 0:N], func=AF.Silu), R=[ycB], W=[vFB])
                    return
                op("act", lambda e: e.activation(out=yc[:, 0:N], in_=yc[:, 0:N], func=AF.Silu), R=[ycB], W=[ycB])
                op("act", lambda e: e.activation(out=tmpA[:, 0:N], in_=yc[:, 0:N], func=AF.Square), R=[ycB], W=[tmpAB])
                ps2, ps2B = psum()
                op("pe", lambda e: e.matmul(ps2[:, 0:N], lhsT=ones, rhs=tmpA[:, 0:N], start=True, stop=True),
                   R=[tmpAB, cstB], W=[ps2B])
                op("act", lambda e: e.activation(out=tmpA[:, 0:N], in_=ps2[:, 0:N], func=AF.Ln, bias=epst[:, 1:2]),
                   R=[ps2B, epsB], W=[tmpAB])
                op("act", lambda e: e.activation(out=tmpA[:, 0:N], in_=tmpA[:, 0:N], func=AF.Exp, scale=-0.5), R=[tmpAB], W=[tmpAB])
                if j < 8:
                    op("dve", lambda e: e.scalar_tensor_tensor(out=qF[:, j, 0:N], in0=yc[:, 0:N], scalar=128.0 ** -0.5,
                                                               in1=tmpA[:, 0:N], op0=ALU.mult, op1=ALU.mult),
                       R=[ycB, tmpAB], W=[qFB])
                else:
                    op("dve", lambda e: e.tensor_tensor(out=kF[:, j - 8, 0:N], in0=yc[:, 0:N], in1=tmpA[:, 0:N], op=ALU.mult),
                       R=[ycB, tmpAB], W=[kFB])
            proj_fm(w_in, B0, 3072, N, cons_qkv)
            if has_s:
                for hq in range(2):
                    for q3 in range(3):
                        q = hq * 3 + q3
                        ps, psB = psum()
                        for kk in range(4):
                            k = q * 4 + kk
                            op("pe", lambda e, k=k, kk=kk, ps=ps: e.transpose(out=ps[0:NS, kk * 128:(kk + 1) * 128], in_=xs[:, k, :],
                                                                      identity=ident), R=[xsB, cstB], W=[psB])
                        op("dve", lambda e, q3=q3, ps=ps: e.tensor_copy(out=st48[0:NS, q3 * 512:(q3 + 1) * 512], in_=ps[0:NS, 0:512]),
                           R=[psB], W=[st48B])
                    dma("sp", o_sgconv[l][:, 2, hq * 1536:(hq + 1) * 1536], st48[0:NS, :], R=[st48B], semb=st48B)


        gslots = [mkslot() for _ in range(GK)]

        def cons_z(j, ps, psB, m):
            op("act", lambda e: e.activation(out=zF[:, j, 0:N], in_=ps[:, 0:N], func=AF.Silu), R=[psB], W=[zFB])
        proj_fm(w_in, B0 + 3072, 1024, N, cons_z)

        def cons_a(j, ps, psB, m):
            op("act", lambda e: e.activation(out=ra[:, 0:N], in_=ps[0:8, 0:N], func=AF.Exp, bias=gpar[:, l, 1:2]),
               R=[psB, gparB], W=[raB])
            op("act", lambda e: e.activation(out=ra[:, 0:N], in_=ra[:, 0:N], func=AF.Ln, bias=epst[0:8, 3:4]),
               R=[raB, epsB], W=[raB])
            op("dve", lambda e: e.tensor_scalar(out=ra[:, 0:N], in0=ra[:, 0:N], scalar1=gpar[:, l, 2:3], scalar2=None,
                                                op0=ALU.mult), R=[raB, gparB], W=[raB])
        proj_fm(w_in, B0 + 4096, 8, N, cons_a)

        def cons_b(j, ps, psB, m):
            op("act", lambda e: e.activation(out=rlb[:, 0:N], in_=ps[0:8, 0:N], func=AF.Exp, scale=-1.0), R=[psB], W=[rlbB])
            op("act", lambda e: e.activation(out=rlb[:, 0:N], in_=rlb[:, 0:N], func=AF.Ln, bias=epst[0:8, 3:4]),
               R=[rlbB, epsB], W=[rlbB])
            op("dve", lambda e: e.tensor_scalar(out=rlb[:, 0:N], in0=rlb[:, 0:N], scalar1=-1.0, scalar2=None, op0=ALU.mult),
               R=[rlbB], W=[rlbB])
        proj_fm(w_in, B0 + 4104, 8, N, cons_b)
        if has_s:
            op("dve", lambda e: e.tensor_copy(out=rg[:, 0:NS], in_=ra[:, 0:NS]), R=[raB], W=[rgB])
        for (kind, c0, L, sidx) in chunks:
            if kind != "p":
                continue
            op("dve", lambda e, c0=c0, L=L: e.tensor_tensor_scan(out=rg[:, c0:c0 + L], data0=ra[:, c0:c0 + L],
                                                                 data1=zrow[0:8, 0:L], initial=0.0, op0=ALU.add, op1=ALU.add),
               R=[raB, zrowB], W=[rgB])
        op("dve", lambda e: e.tensor_scalar(out=rng_[:, 0:N], in0=rg[:, 0:N], scalar1=-1.0, scalar2=None, op0=ALU.mult),
           R=[rgB], W=[rngB])
        op("dve", lambda e: e.tensor_tensor(out=rgb[:, 0:N], in0=rg[:, 0:N], in1=rlb[:, 0:N], op=ALU.add),
           R=[rgB, rlbB], W=[rgbB])
        op("dve", lambda e: e.tensor_copy(out=stA[0:8, 0:N], in_=rng_[:, 0:N]), R=[rngB], W=[stAB])
        op("dve", lambda e: e.tensor_copy(out=stA[32:40, 0:N], in_=rgb[:, 0:N]), R=[rgbB], W=[stAB])
        op("act", lambda e: e.activation(out=stA[64:72, 0:N], in_=rlb[:, 0:N], func=AF.Exp), R=[rlbB], W=[stAB])
        op("act", lambda e: e.activation(out=stA[96:104, 0:N], in_=rgb[:, 0:N], func=AF.Exp), R=[rgbB], W=[stAB])
        op("act", lambda e: e.activation(out=stB[0:8, 0:N], in_=rg[:, 0:N], func=AF.Exp), R=[rgB], W=[stBB])
        for (kind, c0, L, sidx) in chunks:
            op("act", lambda e, c0=c0, L=L: e.activation(out=stB[32:40, c0:c0 + L], in_=rng_[:, c0:c0 + L], func=AF.Exp,
                                                         bias=rg[:, c0 + L - 1:c0 + L]), R=[rngB, rgB], W=[stBB])

        def chunk(c0, L, S_, S_B):
            lv = LV[L]
            cs = slice(c0, c0 + L)
            nlev = {64: 5, 16: 3, 1: 0}[L]
            ps, psB = psum()
            op("pe", lambda e: e.transpose(out=ps[0:L, 0:128], in_=stA[:, cs], identity=ident), R=[stAB, cstB], W=[psB])
            op("act", lambda e: e.activation(out=tokA[0:L, :], in_=ps[0:L, 0:128], func=AF.Copy), R=[psB], W=[tokAB])
            ps, psB = psum()
            op("pe", lambda e: e.transpose(out=ps[0:L, 0:64], in_=stB[:, cs], identity=ident[0:64, 0:64]), R=[stBB, cstB], W=[psB])
            op("act", lambda e: e.activation(out=tokB[0:L, :], in_=ps[0:L, 0:64], func=AF.Copy), R=[psB], W=[tokBB])
            for (src, srcB, dst, dstB) in ((kF, kFB, kT, kTB), (vF, vFB, vT, vTB), (zF, zFB, zT, zTB)):
                ps, psB = psum()
                pv = ps.bitcast(BF16)
                for blk in range(8):
                    op("pe", lambda e, blk=blk, pv=pv, src=src: e.transpose(out=pv[0:L, blk * 128:(blk + 1) * 128],
                                                                          in_=src[:, blk, cs], identity=ident_bf[:, :]),
                       R=[srcB, ident_bfB], W=[psB])
                op("act", lambda e, pv=pv, dst=dst: e.activation(out=dst[0:L, 0:512], in_=pv[0:L, 0:512], func=AF.Copy),
                   R=[psB], W=[dstB])
                op("dve", lambda e, pv=pv, dst=dst: e.tensor_copy(out=dst[0:L, 512:1024], in_=pv[0:L, 512:1024]),
                   R=[psB], W=[dstB])
            op("act", lambda e: e.activation(out=Sbf[:, :, :], in_=S_[:, :, :], func=AF.Copy), R=[S_B], W=[SbfB])
            psd, psdB = psum()
            op("pe", lambda e: e.matmul(psd[:, 0:8], lhsT=sellast[0:L, lv, :], rhs=tokB[0:L, 0:8], start=True, stop=True),
               R=[cstB, tokBB], W=[psdB])
            op("dve", lambda e: e.tensor_copy(out=dbc[:, :], in_=psd[:, 0:8]), R=[psdB], W=[dbcB])
            def head(h, sl):
                SL = gslots[sl]
                (E1, E1B), (E2, E2B), (E3, E3B) = SL['E1'], SL['E2'], SL['E3']
                Pm = [SL['P0'], SL['P1']]
                Qm = [SL['Q0'], SL['Q1']]
                (TT, TTB), (attnT, attnTB), (bv, bvB), (bek, bekB) = SL['TT'], SL['attnT'], SL['bv'], SL['bek']
                (wF, wFB), (usb, usbB), (vnew, vnewB), (qsb, qsbB), (kd, kdB) = SL['wF'], SL['usb'], SL['vnew'], SL['qsb'], SL['kd']
                for (row, rowB, nm, Ed, EdB, bias_ap) in ((rgb, rgbB, nmT_strict, E1, E1B, tokA[0:L, h:h + 1]),
                                                         (rng_, rngB, nm_strict, E2, E2B, tokA[0:L, 32 + h:33 + h]),
                                                         (rg, rgB, nmT_incl, E3, E3B, tokA[0:L, h:h + 1])):
                    pse, pseB = psum()
                    op("pe", lambda e, row=row, pse=pse: e.matmul(pse[0:L, 0:L], lhsT=selh[0:8, h, 0:L], rhs=row[:, cs],
                                                                  start=True, stop=False), R=[cstB, rowB], W=[pseB])
                    op("pe", lambda e, nm=nm, pse=pse: e.matmul(pse[0:L, 0:L], lhsT=ident[0:L, 0:L], rhs=nm[0:L, 0:L],
                                                                start=False, stop=True), R=[cstB], W=[pseB])
                    op("act", lambda e, Ed=Ed, pse=pse, bias_ap=bias_ap: e.activation(out=Ed[0:L, 0:L], in_=pse[0:L, 0:L],
                                                                                     func=AF.Exp, bias=bias_ap),
                       R=[pseB, tokAB], W=[EdB])
                    yield
                psg, psgB = psum()
                op("pe", lambda e: e.matmul(psg[0:L, 0:L], lhsT=kF[:, h, cs], rhs=kF[:, h, cs], start=True, stop=True),
                   R=[kFB], W=[psgB])
                pskq, pskqB = psum()
                op("pe", lambda e: e.matmul(pskq[0:L, 0:L], lhsT=kF[:, h, cs], rhs=qF[:, h, cs], start=True, stop=True),
                   R=[kFB, qFB], W=[pskqB])
                (P0, P0B), (P1, P1B) = Pm
                (Q0, Q0B), (Q1, Q1B) = Qm
                op("dve", lambda e: e.scalar_tensor_tensor(out=Q0[0:L, 0:L], in0=psg[0:L, 0:L], scalar=-1.0, in1=E1[0:L, 0:L],
                                                           op0=ALU.mult, op1=ALU.mult), R=[psgB, E1B], W=[Q0B])
                op("dve", lambda e: e.scalar_tensor_tensor(out=P0[0:L, 0:L], in0=psg[0:L, 0:L], scalar=-1.0, in1=E2[0:L, 0:L],
                                                           op0=ALU.mult, op1=ALU.mult), R=[psgB, E2B], W=[P0B])
                op("dve", lambda e: e.tensor_tensor(out=attnT[0:L, 0:L], in0=pskq[0:L, 0:L], in1=E3[0:L, 0:L], op=ALU.mult),
                   R=[pskqB, E3B], W=[attnTB])
                op("dve", lambda e: e.tensor_tensor(out=TT[0:L, 0:L], in0=Q0[0:L, 0:L], in1=ident[0:L, 0:L], op=ALU.add),
                   R=[Q0B, cstB], W=[TTB])
                yield
                cp, cq = (P0, P0B), (Q0, Q0B)
                np_, nq = (P1, P1B), (Q1, Q1B)
                for lev in range(nlev):
                    psp, pspB = psum()
                    op("pe", lambda e, cp=cp, cq=cq, psp=psp: e.matmul(psp[0:L, 0:L], lhsT=cq[0][0:L, 0:L], rhs=cp[0][0:L, 0:L],
                                                                      start=True, stop=True), R=[cp[1], cq[1]], W=[pspB])
                    psq, psqB = psum()
                    op("pe", lambda e, cp=cp, cq=cq, psq=psq: e.matmul(psq[0:L, 0:L], lhsT=cp[0][0:L, 0:L], rhs=cq[0][0:L, 0:L],
                                                                      start=True, stop=True), R=[cp[1], cq[1]], W=[psqB])
                    op("act", lambda e, np_=np_, psp=psp: e.activation(out=np_[0][0:L, 0:L], in_=psp[0:L, 0:L], func=AF.Copy),
                       R=[pspB], W=[np_[1]])
                    op("dve", lambda e, nq=nq, psq=psq: e.tensor_copy(out=nq[0][0:L, 0:L], in_=psq[0:L, 0:L]),
                       R=[psqB], W=[nq[1]])
                    yield
                    pst, pstB = psum()
                    op("pe", lambda e, np_=np_, pst=pst: e.matmul(pst[0:L, 0:L], lhsT=np_[0][0:L, 0:L], rhs=TT[0:L, 0:L],
                                                                  start=True, stop=True), R=[np_[1], TTB], W=[pstB])
                    op("dve", lambda e, pst=pst: e.tensor_tensor(out=TT[0:L, 0:L], in0=TT[0:L, 0:L], in1=pst[0:L, 0:L],
                                                                 op=ALU.add), R=[TTB, pstB], W=[TTB])
                    yield
                    cp, cq, np_, nq = np_, nq, cp, cq
                hs_ = slice(h * 128, (h + 1) * 128)
                op("act", lambda e: e.activation(out=bv[0:L, :], in_=vT[0:L, hs_], func=AF.Copy, scale=tokA[0:L, 64 + h:65 + h]),
                   R=[vTB, tokAB], W=[bvB])
                op("act", lambda e: e.activation(out=bek[0:L, :], in_=kT[0:L, hs_], func=AF.Copy, scale=tokA[0:L, 96 + h:97 + h]),
                   R=[kTB, tokAB], W=[bekB])
                yield
                psu, psuB = psum()
                op("pe", lambda e: e.matmul(psu[0:L, 0:128], lhsT=TT[0:L, 0:L], rhs=bv[0:L, :], start=True, stop=True),
                   R=[TTB, bvB], W=[psuB])
                psw, pswB = psum()
                op("pe", lambda e: e.matmul(psw[:, 0:L], lhsT=bek[0:L, :], rhs=TT[0:L, 0:L], start=True, stop=True),
                   R=[TTB, bekB], W=[pswB])
                op("act", lambda e: e.activation(out=wF[:, 0:L], in_=psw[:, 0:L], func=AF.Copy), R=[pswB], W=[wFB])
                op("dve", lambda e: e.tensor_copy(out=usb[0:L, :], in_=psu[0:L, 0:128]), R=[psuB], W=[usbB])
                yield
                psws, pswsB = psum()
                op("pe", lambda e: e.matmul(psws[0:L, 0:128], lhsT=wF[:, 0:L], rhs=Sbf[:, h, :], start=True, stop=True),
                   R=[wFB, SbfB], W=[pswsB])
                op("dve", lambda e: e.tensor_tensor(out=vnew[0:L, :], in0=usb[0:L, :], in1=psws[0:L, 0:128], op=ALU.subtract),
                   R=[usbB, pswsB], W=[vnewB])
                yield
                psqs, psqsB = psum()
                op("pe", lambda e: e.matmul(psqs[0:L, 0:128], lhsT=qF[:, h, cs], rhs=Sbf[:, h, :], start=True, stop=True),
                   R=[qFB, SbfB], W=[psqsB])
                psav, psavB = psum()
                op("pe", lambda e: e.matmul(psav[0:L, 0:128], lhsT=attnT[0:L, 0:L], rhs=vnew[0:L, :], start=True, stop=True),
                   R=[attnTB, vnewB], W=[psavB])
                op("act", lambda e: e.activation(out=qsb[0:L, :], in_=psqs[0:L, 0:128], func=AF.Copy, scale=tokB[0:L, h:h + 1]),
                   R=[psqsB, tokBB], W=[qsbB])
                op("dve", lambda e: e.tensor_tensor(out=oT[0:L, hs_], in0=qsb[0:L, :], in1=psav[0:L, 0:128], op=ALU.add),
                   R=[qsbB, psavB], W=[oTB])
                yield
                op("act", lambda e: e.activation(out=kd[0:L, :], in_=kT[0:L, hs_], func=AF.Copy, scale=tokB[0:L, 32 + h:33 + h]),
                   R=[kTB, tokBB], W=[kdB])
                psup, psupB = psum()
                op("pe", lambda e: e.matmul(psup[:, 0:128], lhsT=kd[0:L, :], rhs=vnew[0:L, :], start=True, stop=True),
                   R=[kdB, vnewB], W=[psupB])
                op("dve", lambda e: e.scalar_tensor_tensor(out=S_[:, h, :], in0=S_[:, h, :], scalar=dbc[:, h:h + 1],
                                                           in1=psup[:, 0:128], op0=ALU.mult, op1=ALU.add),
                   R=[S_B, dbcB, psupB], W=[S_B])
            run_interleaved([(lambda sl, h=h: head(h, sl)) for h in range(8)], GK)
            for h in range(8):
                op("act", lambda e, h=h: e.activation(out=sqj[0:L, :], in_=oT[0:L, h * 128:(h + 1) * 128], func=AF.Square,
                                                      accum_out=ssq[0:L, h:h + 1]), R=[oTB], W=[ssqB, sqjB])
            op("act", lambda e: e.activation(out=ssq[0:L, 8:16], in_=ssq[0:L, 0:8], func=AF.Ln, scale=1.0 / 128.0,
                                             bias=epst[0:L, 0:1]), R=[ssqB, epsB], W=[ssqB])
            op("act", lambda e: e.activation(out=ssq[0:L, 8:16], in_=ssq[0:L, 8:16], func=AF.Exp, scale=-0.5), R=[ssqB], W=[ssqB])
            for h in range(8):
                op("dve", lambda e, h=h: e.scalar_tensor_tensor(out=oT[0:L, h * 128:(h + 1) * 128],
                                                                in0=oT[0:L, h * 128:(h + 1) * 128],
                                                                scalar=ssq[0:L, 8 + h:9 + h], in1=bcn[0:L, :],
                                                                op0=ALU.mult, op1=ALU.mult), R=[oTB, ssqB, bcnB], W=[oTB])
            op("dve", lambda e: e.tensor_tensor(out=yT[0:L, :], in0=oT[0:L, :], in1=zT[0:L, :], op=ALU.mult),
               R=[oTB, zTB], W=[yTB])
            to_fm(yT, yTB, L, cs, yF[1])

        for (kind, c0, L, sidx) in chunks:
            if kind == "s":
                S_, S_B = SS, SSB
                dma("sp", S_[:, :, :], st_gs[l, sidx].rearrange("h k v -> k h v"), W=[S_B])
            else:
                S_, S_B = St, StB
            chunk(c0, L, S_, S_B)
            if kind == "s":
                dma("sp", o_sgs[l, sidx].rearrange("h k v -> k h v"), S_[:, :, :], R=[S_B], semb=S_B)


    onesblk = cst[:, CO["onesblk"]:CO["onesblk"] + 128]
    i2c = cst[:, CO["i2"]:CO["i2"] + 64]
    bdT_incl = cst[:, CO["bdT_incl"]:CO["bdT_incl"] + 128]
    bdT_strict = cst[:, CO["bdT_strict"]:CO["bdT_strict"] + 128]
    bd_strict = cst[:, CO["bd_strict"]:CO["bd_strict"] + 128]
    RZ = []
    rprev = []
    for l in range(DEPTH):
        t, b = sb("RZ%d" % l, [128, 8, 64])
        op("dve", lambda e, t=t: e.memset(t[:, :, :], 0.0), W=[b])
        RZ.append((t, b))
        t, b = sb("rprev%d" % l, [128, 26])
        op("dve", lambda e, t=t: e.memset(t[:, :], 0.0), W=[b])
        rprev.append((t, b))
    rpar = {}
    for nm in ("rwkv_w0", "rwkv_a0", "rwkv_k_k", "rwkv_k_a", "rwkv_ln_w", "rwkv_ln_b"):
        rpar[nm] = load_fm_vec("p_" + nm, W[nm], 8)
    rpar["rwkv_r_k"] = load_fm_vec("p_rk", W["rwkv_r_k"].rearrange("l h d -> l (h d)"), 8)
    rpar["rwkv_mu"] = load_fm_vec("p_mu", W["rwkv_mu"], 26)
    omka, omkaB = sb("omka", [128, DEPTH, 8])
    op("dve", lambda e: e.tensor_scalar(out=omka[:, :, :], in0=rpar["rwkv_k_a"][0][:, :, :], scalar1=-1.0, scalar2=1.0,
                                        op0=ALU.mult, op1=ALU.add), R=[rpar["rwkv_k_a"][1]], W=[omkaB])

    def rwkv_phase(l, tile):
        r0, N, chunks = tile
        Zt, ZtB = RZ[l]
        rp, rpB = rprev[l]
        w_in = W["w_in"][l]
        has_s = any(c[0] == "s" for c in chunks)
        pc0 = NS if has_s else 0
        Np = N - pc0
        P = lambda nm: rpar[nm][0]
        PB = lambda nm: rpar[nm][1]
        mu, muB = rpar["rwkv_mu"]
        xcs = [sb("r_xc%d" % i, [128, NMAX + 1]) for i in range(2)]
        dtmp, dtmpB = sb("r_dtmp", [128, NMAX])
        xwxa, xwxaB = sb("r_xwxa", [128, NMAX])
        sxg, sxgB = sb("r_sxg", [128, NMAX])
        w2p, w2pB = sb("r_w2p", [128, 1024])
        a2p, a2pB = sb("r_a2p", [128, 1024])
        g2t, g2tB = sb("r_g2", [128, 1024])
        if has_s:
            xsAll, xsAllB = sb("r_xsAll", [128, 26, NS])
            prevS, prevSB = sb("r_prevS", [128, 26, NS])
        op("dve", lambda e: e.memset(w2p[:, :], 0.0), W=[w2pB])
        op("dve", lambda e: e.memset(a2p[:, :], 0.0), W=[a2pB])
        dma("sp", w2p[0:64, :], W["rwkv_w2"][l], W=[w2pB])
        dma("sp", a2p[64:128, :], W["rwkv_a2"][l], W=[a2pB])
        dma("sp", g2t[:, :], W["rwkv_g2"][l], W=[g2tB])
        with Arena():
            if has_s:
                st16, st16B = sb("r_st16a", [NS, 3328])
                dma("sp", st16[:, :], st_rshift[l], W=[st16B])
                for j in range(26):
                    ps, psB = psum()
                    op("pe", lambda e, j=j, ps=ps: e.transpose(out=ps[:, 0:NS], in_=st16[0:NS, j * 128:(j + 1) * 128],
                                                               identity=ident[0:NS, 0:NS]), R=[st16B, cstB], W=[psB])
                    op("dve", lambda e, j=j, ps=ps: e.tensor_copy(out=prevS[:, j, :], in_=ps[:, 0:NS]), R=[psB], W=[prevSB])

        def shifted(j, ps, psB, dst, dstB, xcs=xcs, dtmp=dtmp, dtmpB=dtmpB):
            xc, xcB = xcs[j % 2]
            op("dve", lambda e: e.tensor_copy(out=xc[:, 0:1], in_=rp[:, j:j + 1]), R=[rpB], W=[xcB])
            op("act", lambda e: e.activation(out=xc[:, 1:1 + Np], in_=ps[:, pc0:N], func=AF.Copy), R=[psB], W=[xcB])
            op("dve", lambda e: e.tensor_tensor(out=dtmp[:, 0:Np], in0=xc[:, 0:Np], in1=xc[:, 1:1 + Np], op=ALU.subtract),
               R=[xcB], W=[dtmpB])
            op("dve", lambda e: e.scalar_tensor_tensor(out=dst[:, pc0:N], in0=dtmp[:, 0:Np], scalar=mu[:, l, j:j + 1],
                                                       in1=xc[:, 1:1 + Np], op0=ALU.mult, op1=ALU.add),
               R=[dtmpB, muB, xcB], W=[dstB])
            op("dve", lambda e: e.tensor_copy(out=rp[:, j:j + 1], in_=xc[:, Np:Np + 1]), R=[xcB], W=[rpB])
            if has_s:
                op("act", lambda e: e.activation(out=xsAll[:, j, :], in_=ps[:, 0:NS], func=AF.Copy), R=[psB], W=[xsAllB])
                op("dve", lambda e: e.tensor_tensor(out=dtmp[:, 0:NS], in0=prevS[:, j, :], in1=xsAll[:, j, :], op=ALU.subtract),
                   R=[prevSB, xsAllB], W=[dtmpB])
                op("dve", lambda e: e.scalar_tensor_tensor(out=dst[:, 0:NS], in0=dtmp[:, 0:NS], scalar=mu[:, l, j:j + 1],
                                                           in1=xsAll[:, j, :], op0=ALU.mult, op1=ALU.add),
                   R=[dtmpB, muB, xsAllB], W=[dstB])

        def cons_l24(j, ps, psB, m):
            shifted(24, ps, psB, xwxa, xwxaB)
            op("act", lambda e: e.activation(out=xwxa[0:64, 0:N], in_=xwxa[0:64, 0:N], func=AF.Tanh), R=[xwxaB], W=[xwxaB])
        proj_fm(w_in, C0 + 3072, 128, N, cons_l24)

        def cons_l25(j, ps, psB, m):
            shifted(25, ps, psB, sxg, sxgB)
            op("act", lambda e: e.activation(out=sxg[:, 0:N], in_=sxg[:, 0:N], func=AF.Sigmoid), R=[sxgB], W=[sxgB])
        proj_fm(w_in, C0 + 3200, 128, N, cons_l25)

        def pipe(slot):
            xcs = [sb("r_xcS%d" % i, [128, NMAX + 1]) for i in range(2)]
            dtmp, dtmpB = sb("r_dtmpS", [128, NMAX])
            rows = {}
            for nm in ("r", "kx", "k2", "v", "av", "bv", "ld", "g", "asig", "yraw", "t1", "t2"):
                rows[nm] = sb("r_" + nm, [128, NMAX])
            if has_s:
                ZS, ZSB = sb("r_ZS", [128, 64])
                ZSt, ZStB = sb("r_ZSt", [64, 128])
            opn = ("rt", "kt", "bt", "at", "kh", "bh", "Vbd", "Q0", "Q1", "P0", "P1", "TT", "MakT", "RKT", "RBT", "K2", "B2", "Y2")
            OPT = {}
            for nm in opn:
                OPT[nm] = sb("r_o_" + nm, [128, 128])
                op("dve", lambda e, t=OPT[nm][0]: e.memset(t[:, :], 0.0), W=[OPT[nm][1]])
            lw, lwB = sb("r_lw", [128, 64])
            eW, eWB = sb("r_eW", [128, 64])
            eWi, eWiB = sb("r_eWi", [128, 64])
            eWp, eWpB = sb("r_eWp", [128, 64])
            eWl, eWlB = sb("r_eWl", [128, 64])
            wls, wlsB = sb("r_wls", [128, 1])
            Vtok, VtokB = sb("r_Vtok", [128, 64])
            RHS, RHSB = sb("r_RHS", [128, 64])
            SAt, SAtB = sb("r_SA", [128, 64])
            for t_, b_ in ((Vtok, VtokB), (RHS, RHSB), (SAt, SAtB)):
                op("dve", lambda e, t_=t_: e.memset(t_[:, :], 0.0), W=[b_])
            R_ = lambda nm: rows[nm][0]
            RB = lambda nm: rows[nm][1]
            lastL = [0]

            def chunk(jb, c0, L, Z_, Z_B, zsl):
                cs = slice(c0, c0 + L)
                nlev = {64: 5, 16: 3, 1: 0}[L]
                O = lambda nm: OPT[nm][0]
                OB = lambda nm: OPT[nm][1]
                ld_ = R_("ld")
                if lastL[0] != L:
                    for nm_ in ("rt", "kt", "bt", "at", "kh", "bh", "Vbd"):
                        op("dve", lambda e, nm_=nm_: e.memset(O(nm_)[:, :], 0.0), W=[OB(nm_)])
                    lastL[0] = L
                op("dve", lambda e: e.tensor_tensor_scan(out=lw[:, 0:L], data0=ld_[:, cs], data1=zcol[:, 0:L], initial=0.0,
                                                         op0=ALU.add, op1=ALU.add), R=[RB("ld"), zcolB], W=[lwB])
                op("act", lambda e: e.activation(out=eW[:, 0:L], in_=lw[:, 0:L], func=AF.Exp), R=[lwB], W=[eWB])
                op("act", lambda e: e.activation(out=eWi[:, 0:L], in_=lw[:, 0:L], func=AF.Exp, scale=-1.0), R=[lwB], W=[eWiB])
                op("dve", lambda e: e.tensor_tensor(out=eWp[:, 0:L], in0=lw[:, 0:L], in1=ld_[:, cs], op=ALU.subtract),
                   R=[lwB, RB("ld")], W=[eWpB])
                op("act", lambda e: e.activation(out=eWp[:, 0:L], in_=eWp[:, 0:L], func=AF.Exp), R=[eWpB], W=[eWpB])
                op("act", lambda e: e.activation(out=eWl[:, 0:L], in_=lw[:, 0:L], func=AF.Exp, scale=-1.0, bias=lw[:, L - 1:L]),
                   R=[lwB], W=[eWlB])
                op("act", lambda e: e.activation(out=wls[:, 0:1], in_=lw[:, L - 1:L], func=AF.Exp), R=[lwB], W=[wlsB])
                yield
                for (dst, srcn, fac, facB) in (("rt", "r", eW, eWB), ("kt", "k2", eWi, eWiB), ("bt", "bv", eWi, eWiB),
                                               ("at", "av", eWp, eWpB), ("kh", "k2", eWl, eWlB), ("bh", "bv", eWl, eWlB)):
                    for hb in range(2):
                        pr = slice(hb * 64, hb * 64 + 64)
                        op("dve", lambda e, dst=dst, srcn=srcn, fac=fac, pr=pr, hb=hb: e.tensor_tensor(
                            out=O(dst)[pr, hb * 64:hb * 64 + L], in0=R_(srcn)[pr, cs], in1=fac[pr, 0:L], op=ALU.mult),
                           R=[RB(srcn), facB], W=[OB(dst)])
                for hb in range(2):
                    pr = slice(hb * 64, hb * 64 + 64)
                    op("act", lambda e, pr=pr, hb=hb: e.activation(out=O("Vbd")[pr, hb * 64:hb * 64 + L], in_=R_("v")[pr, cs],
                                                                   func=AF.Copy), R=[RB("v")], W=[OB("Vbd")])
                ps, psB = psum()
                op("pe", lambda e, ps=ps: e.matmul(ps[:, 0:64], lhsT=O("Vbd")[:, :], rhs=i2c, start=True, stop=True),
                   R=[OB("Vbd"), cstB], W=[psB])
                op("act", lambda e, ps=ps: e.activation(out=Vtok[:, :], in_=ps[:, 0:64], func=AF.Copy), R=[psB], W=[VtokB])
                yield

                def gram(lhs, rhs, mask, dstn):
                    ps, psB = psum()
                    op("pe", lambda e: e.matmul(ps[:, 0:128], lhsT=O(lhs)[:, :], rhs=O(rhs)[:, :], start=True, stop=True),
                       R=[OB(lhs), OB(rhs)], W=[psB])
                    op("dve", lambda e: e.tensor_tensor(out=O(dstn)[:, :], in0=ps[:, 0:128], in1=mask, op=ALU.mult),
                       R=[psB, cstB], W=[OB(dstn)])
                gram("bt", "at", bdT_strict, "Q0")
                yield
                gram("at", "bt", bd_strict, "P0")
                yield
                gram("kt", "at", bdT_strict, "MakT")
                yield
                gram("kt", "rt", bdT_incl, "RKT")
                yield
                gram("bt", "rt", bdT_incl, "RBT")
                yield
                op("dve", lambda e: e.tensor_tensor(out=O("TT")[:, :], in0=O("Q0")[:, :], in1=ident, op=ALU.add),
                   R=[OB("Q0"), cstB], W=[OB("TT")])
                cp, cq, np_, nq = "P0", "Q0", "P1", "Q1"
                for lev in range(nlev):
                    psp, pspB = psum()
                    op("pe", lambda e, cp=cp, cq=cq, psp=psp: e.matmul(psp[:, 0:128], lhsT=O(cq)[:, :], rhs=O(cp)[:, :],
                                                                      start=True, stop=True), R=[OB(cp), OB(cq)], W=[pspB])
                    psq, psqB = psum()
                    op("pe", lambda e, cp=cp, cq=cq, psq=psq: e.matmul(psq[:, 0:128], lhsT=O(cp)[:, :], rhs=O(cq)[:, :],
                                                                      start=True, stop=True), R=[OB(cp), OB(cq)], W=[psqB])
                    op("act", lambda e, np_=np_, psp=psp: e.activation(out=O(np_)[:, :], in_=psp[:, 0:128], func=AF.Copy),
                       R=[pspB], W=[OB(np_)])
                    op("dve", lambda e, nq=nq, psq=psq: e.tensor_copy(out=O(nq)[:, :], in_=psq[:, 0:128]), R=[psqB], W=[OB(nq)])
                    yield
                    pst, pstB = psum()
                    op("pe", lambda e, np_=np_, pst=pst: e.matmul(pst[:, 0:128], lhsT=O(np_)[:, :], rhs=O("TT")[:, :],
                                                                  start=True, stop=True), R=[OB(np_), OB("TT")], W=[pstB])
                    op("dve", lambda e, pst=pst: e.tensor_tensor(out=O("TT")[:, :], in0=O("TT")[:, :], in1=pst[:, 0:128],
                                                                 op=ALU.add), R=[OB("TT"), pstB], W=[OB("TT")])
                    yield
                    cp, cq, np_, nq = np_, nq, cp, cq
                ps, psB = psum()
                op("pe", lambda e, ps=ps: e.matmul(ps[:, 0:64], lhsT=O("at")[:, :], rhs=zsl, start=True, stop=False),
                   R=[OB("at"), Z_B], W=[psB])
                op("pe", lambda e, ps=ps: e.matmul(ps[:, 0:64], lhsT=O("MakT")[:, :], rhs=Vtok[:, :], start=False, stop=True),
                   R=[OB("MakT"), VtokB], W=[psB])
                op("act", lambda e, ps=ps: e.activation(out=RHS[:, :], in_=ps[:, 0:64], func=AF.Copy), R=[psB], W=[RHSB])
                yield
                ps, psB = psum()
                op("pe", lambda e, ps=ps: e.matmul(ps[:, 0:64], lhsT=O("TT")[:, :], rhs=RHS[:, :], start=True, stop=True),
                   R=[OB("TT"), RHSB], W=[psB])
                op("dve", lambda e, ps=ps: e.tensor_copy(out=SAt[:, :], in_=ps[:, 0:64]), R=[psB], W=[SAtB])
                yield
                psy, psyB = psum()
                op("pe", lambda e: e.matmul(psy[:, 0:64], lhsT=O("rt")[:, :], rhs=zsl, start=True, stop=False),
                   R=[OB("rt"), Z_B], W=[psyB])
                op("pe", lambda e: e.matmul(psy[:, 0:64], lhsT=O("RKT")[:, :], rhs=Vtok[:, :], start=False, stop=False),
                   R=[OB("RKT"), VtokB], W=[psyB])
                op("pe", lambda e: e.matmul(psy[:, 0:64], lhsT=O("RBT")[:, :], rhs=SAt[:, :], start=False, stop=True),
                   R=[OB("RBT"), SAtB], W=[psyB])
                op("act", lambda e: e.activation(out=O("Y2")[0:64, 0:64], in_=psy[0:64, 0:64], func=AF.Copy), R=[psyB], W=[OB("Y2")])
                op("dve", lambda e: e.tensor_copy(out=O("Y2")[64:128, 64:128], in_=psy[64:128, 0:64]), R=[psyB], W=[OB("Y2")])
                yield
                ps, psB = psum()
                op("pe", lambda e, ps=ps: e.matmul(ps[:, 0:64], lhsT=O("Y2")[:, :], rhs=i2c, start=True, stop=True),
                   R=[OB("Y2"), cstB], W=[psB])
                op("act", lambda e, ps=ps: e.activation(out=R_("yraw")[:, cs], in_=ps[:, 0:L], func=AF.Copy), R=[psB], W=[RB("yraw")])
                yield
                for (srcn, dstn) in (("kh", "K2"), ("bh", "B2")):
                    ps, psB = psum()
                    op("pe", lambda e, ps=ps, srcn=srcn: e.transpose(out=ps[:, 0:128], in_=O(srcn)[:, :], identity=ident),
                       R=[OB(srcn), cstB], W=[psB])
                    op("act" if srcn == "kh" else "dve",
                       (lambda e, ps=ps, dstn=dstn: e.activation(out=O(dstn)[:, :], in_=ps[:, 0:128], func=AF.Copy)) if srcn == "kh" else
                       (lambda e, ps=ps, dstn=dstn: e.tensor_copy(out=O(dstn)[:, :], in_=ps[:, 0:128])), R=[psB], W=[OB(dstn)])
                psz, pszB = psum()
                op("pe", lambda e: e.matmul(psz[:, 0:64], lhsT=O("K2")[:, :], rhs=Vtok[:, :], start=True, stop=False),
                   R=[OB("K2"), VtokB], W=[pszB])
                op("pe", lambda e: e.matmul(psz[:, 0:64], lhsT=O("B2")[:, :], rhs=SAt[:, :], start=False, stop=True),
                   R=[OB("B2"), SAtB], W=[pszB])
                op("dve", lambda e: e.scalar_tensor_tensor(out=zsl, in0=zsl, scalar=wls[:, 0:1], in1=psz[:, 0:64],
                                                           op0=ALU.mult, op1=ALU.add), R=[Z_B, wlsB, pszB], W=[Z_B])

            for jb in range(slot, 8, 2):
                def mk(dstn):
                    def f(j, ps, psB, m):
                        return None
                    return f
                for (blk, dstn) in ((jb, "r"), (8 + jb, "kx"), (16 + jb, "v")):
                    def cons(j, ps, psB, m, blk=blk, dstn=dstn):
                        shifted(blk, ps, psB, R_(dstn), RB(dstn), xcs, dtmp, dtmpB)
                    proj_fm(w_in, C0 + blk * 128, 128, N, cons)
                    yield
                bsl = slice(jb * 128, (jb + 1) * 128)
                ps, psB = psum()
                op("pe", lambda e, ps=ps: e.matmul(ps[:, 0:N], lhsT=w2p[:, bsl], rhs=xwxa[:, 0:N], start=True, stop=True),
                   R=[w2pB, xwxaB], W=[psB])
                op("act", lambda e, ps=ps: e.activation(out=R_("ld")[:, 0:N], in_=ps[:, 0:N], func=AF.Sigmoid,
                                                        bias=P("rwkv_w0")[:, l, jb:jb + 1]), R=[psB, PB("rwkv_w0")], W=[RB("ld")])
                yield
                op("dve", lambda e: e.tensor_scalar(out=R_("ld")[:, 0:N], in0=R_("ld")[:, 0:N], scalar1=-float(np.exp(-0.5)),
                                                    scalar2=None, op0=ALU.mult), R=[RB("ld")], W=[RB("ld")])
                ps, psB = psum()
                op("pe", lambda e, ps=ps: e.matmul(ps[:, 0:N], lhsT=a2p[:, bsl], rhs=xwxa[:, 0:N], start=True, stop=True),
                   R=[a2pB, xwxaB], W=[psB])
                op("act", lambda e, ps=ps: e.activation(out=R_("asig")[:, 0:N], in_=ps[:, 0:N], func=AF.Sigmoid,
                                                        bias=P("rwkv_a0")[:, l, jb:jb + 1]), R=[psB, PB("rwkv_a0")], W=[RB("asig")])
                yield
                ps, psB = psum()
                op("pe", lambda e, ps=ps: e.matmul(ps[:, 0:N], lhsT=g2t[:, bsl], rhs=sxg[:, 0:N], start=True, stop=True),
                   R=[g2tB, sxgB], W=[psB])
                op("act", lambda e, ps=ps: e.activation(out=R_("g")[:, 0:N], in_=ps[:, 0:N], func=AF.Copy), R=[psB], W=[RB("g")])
                yield
                op("dve", lambda e: e.tensor_scalar(out=R_("t1")[:, 0:N], in0=R_("kx")[:, 0:N], scalar1=P("rwkv_k_k")[:, l, jb:jb + 1],
                                                    scalar2=None, op0=ALU.mult), R=[RB("kx"), PB("rwkv_k_k")], W=[RB("t1")])
                op("act", lambda e: e.activation(out=R_("t2")[:, 0:N], in_=R_("t1")[:, 0:N], func=AF.Square), R=[RB("t1")], W=[RB("t2")])
                ps, psB = psum()
                op("pe", lambda e, ps=ps: e.matmul(ps[:, 0:N], lhsT=onesblk, rhs=R_("t2")[:, 0:N], start=True, stop=True),
                   R=[RB("t2"), cstB], W=[psB])
                op("act", lambda e, ps=ps: e.activation(out=R_("t2")[:, 0:N], in_=ps[:, 0:N], func=AF.Ln, bias=epst[:, 1:2]),
                   R=[psB, epsB], W=[RB("t2")])
                op("act", lambda e: e.activation(out=R_("t2")[:, 0:N], in_=R_("t2")[:, 0:N], func=AF.Exp, scale=-0.5),
                   R=[RB("t2")], W=[RB("t2")])
                op("dve", lambda e: e.scalar_tensor_tensor(out=R_("av")[:, 0:N], in0=R_("t1")[:, 0:N], scalar=-1.0, in1=R_("t2")[:, 0:N],
                                                           op0=ALU.mult, op1=ALU.mult), R=[RB("t1"), RB("t2")], W=[RB("av")])
                op("dve", lambda e: e.scalar_tensor_tensor(out=R_("bv")[:, 0:N], in0=R_("av")[:, 0:N], scalar=-1.0, in1=R_("asig")[:, 0:N],
                                                           op0=ALU.mult, op1=ALU.mult), R=[RB("av"), RB("asig")], W=[RB("bv")])
                op("dve", lambda e: e.tensor_scalar(out=R_("t1")[:, 0:N], in0=R_("asig")[:, 0:N], scalar1=P("rwkv_k_a")[:, l, jb:jb + 1],
                                                    scalar2=omka[:, l, jb:jb + 1], op0=ALU.mult, op1=ALU.add),
                   R=[RB("asig"), PB("rwkv_k_a"), omkaB], W=[RB("t1")])
                op("dve", lambda e: e.tensor_tensor(out=R_("k2")[:, 0:N], in0=R_("kx")[:, 0:N], in1=R_("t1")[:, 0:N], op=ALU.mult),
                   R=[RB("kx"), RB("t1")], W=[RB("k2")])
                for (kind, c0, L, sidx) in chunks:
                    if kind == "s":
                        dma("sp", ZSt[:, :].rearrange("v (h k) -> v h k", k=64),
                            st_rs[l, sidx, 2 * jb:2 * jb + 2].rearrange("h v k -> v h k"), W=[ZStB])
                        ps, psB = psum()
                        op("pe", lambda e, ps=ps: e.transpose(out=ps[:, 0:64], in_=ZSt[:, :], identity=ident[0:64, 0:64]),
                           R=[ZStB, cstB], W=[psB])
                        op("act", lambda e, ps=ps: e.activation(out=ZS[:, :], in_=ps[:, 0:64], func=AF.Copy), R=[psB], W=[ZSB])
                        yield from chunk(jb, c0, L, ZS, ZSB, ZS[:, :])
                        ps, psB = psum()
                        op("pe", lambda e, ps=ps: e.transpose(out=ps[0:64, 0:128], in_=ZS[:, :], identity=ident), R=[ZSB, cstB], W=[psB])
                        op("act", lambda e, ps=ps: e.activation(out=ZSt[:, :], in_=ps[0:64, 0:128], func=AF.Copy), R=[psB], W=[ZStB])
                        dma("sp", o_srs[l, sidx, 2 * jb:2 * jb + 2].rearrange("h v k -> v h k"),
                            ZSt[:, :].rearrange("v (h k) -> v h k", k=64), R=[ZStB], semb=ZStB)
                    else:
                        yield from chunk(jb, c0, L, Zt, ZtB, Zt[:, jb, :])
                ps, psB = psum()
                op("pe", lambda e, ps=ps: e.matmul(ps[:, 0:N], lhsT=onesblk, rhs=R_("yraw")[:, 0:N], start=True, stop=True),
                   R=[RB("yraw"), cstB], W=[psB])
                op("dve", lambda e, ps=ps: e.scalar_tensor_tensor(out=R_("t1")[:, 0:N], in0=ps[:, 0:N], scalar=-1.0 / 64.0,
                                                                  in1=R_("yraw")[:, 0:N], op0=ALU.mult, op1=ALU.add),
                   R=[psB, RB("yraw")], W=[RB("t1")])
                yield
                op("act", lambda e: e.activation(out=R_("t2")[:, 0:N], in_=R_("t1")[:, 0:N], func=AF.Square), R=[RB("t1")], W=[RB("t2")])
                ps, psB = psum()
                op("pe", lambda e, ps=ps: e.matmul(ps[:, 0:N], lhsT=onesblk, rhs=R_("t2")[:, 0:N], start=True, stop=True),
                   R=[RB("t2"), cstB], W=[psB])
                op("act", lambda e, ps=ps: e.activation(out=R_("t2")[:, 0:N], in_=ps[:, 0:N], func=AF.Ln, scale=1.0 / 64.0,
                                                        bias=epst[:, 2:3]), R=[psB, epsB], W=[RB("t2")])
                op("act", lambda e: e.activation(out=R_("t2")[:, 0:N], in_=R_("t2")[:, 0:N], func=AF.Exp, scale=-0.5),
                   R=[RB("t2")], W=[RB("t2")])
                op("dve", lambda e: e.tensor_tensor(out=R_("t1")[:, 0:N], in0=R_("t1")[:, 0:N], in1=R_("t2")[:, 0:N], op=ALU.mult),
                   R=[RB("t1"), RB("t2")], W=[RB("t1")])
                op("dve", lambda e: e.tensor_scalar(out=R_("t1")[:, 0:N], in0=R_("t1")[:, 0:N], scalar1=P("rwkv_ln_w")[:, l, jb:jb + 1],
                                                    scalar2=P("rwkv_ln_b")[:, l, jb:jb + 1], op0=ALU.mult, op1=ALU.add),
                   R=[RB("t1"), PB("rwkv_ln_w"), PB("rwkv_ln_b")], W=[RB("t1")])
                op("dve", lambda e: e.scalar_tensor_tensor(out=R_("t2")[:, 0:N], in0=R_("r")[:, 0:N], scalar=P("rwkv_r_k")[:, l, jb:jb + 1],
                                                           in1=R_("k2")[:, 0:N], op0=ALU.mult, op1=ALU.mult),
                   R=[RB("r"), RB("k2"), PB("rwkv_r_k")], W=[RB("t2")])
                ps, psB = psum()
                op("pe", lambda e, ps=ps: e.matmul(ps[:, 0:N], lhsT=onesblk, rhs=R_("t2")[:, 0:N], start=True, stop=True),
                   R=[RB("t2"), cstB], W=[psB])
                op("dve", lambda e, ps=ps: e.tensor_tensor(out=R_("t2")[:, 0:N], in0=ps[:, 0:N], in1=R_("v")[:, 0:N], op=ALU.mult),
                   R=[psB, RB("v")], W=[RB("t2")])
                yield
                op("dve", lambda e: e.tensor_tensor(out=R_("t1")[:, 0:N], in0=R_("t1")[:, 0:N], in1=R_("t2")[:, 0:N], op=ALU.add),
                   R=[RB("t1"), RB("t2")], W=[RB("t1")])
                yct, ycB_ = yF[2]
                op("dve", lambda e: e.tensor_tensor(out=yct[:, jb, 0:N], in0=R_("t1")[:, 0:N], in1=R_("g")[:, 0:N], op=ALU.mult),
                   R=[RB("t1"), RB("g")], W=[ycB_])
        run_interleaved([(lambda s_: pipe(s_)), (lambda s_: pipe(s_))], 2)
        with Arena():
            if has_s:
                st16, st16B = sb("r_st16b", [NS, 512])
                for q in range(7):
                    ps, psB = psum()
                    nb = min(4, 26 - q * 4)
                    for kk in range(nb):
                        k = q * 4 + kk
                        op("pe", lambda e, k=k, kk=kk, ps=ps: e.transpose(out=ps[0:NS, kk * 128:(kk + 1) * 128], in_=xsAll[:, k, :],
                                                                         identity=ident), R=[xsAllB, cstB], W=[psB])
                    op("dve", lambda e, q=q, ps=ps, nb=nb: e.tensor_copy(out=st16[0:NS, 0:nb * 128],
                                                                        in_=ps[0:NS, 0:nb * 128]), R=[psB], W=[st16B])
                    dma("sp", o_srshift[l][:, q * 512:q * 512 + nb * 128], st16[0:NS, 0:nb * 128], R=[st16B], semb=st16B)


    def zero_y(which, N):
        dt, dB = yF[which]
        op("dve", lambda e: e.memset(dt[:, :, 0:N], 0.0), W=[dB])

    def merge_phase(l, tile):
        r0, N, chunks = tile
        gt = [sb("gate%d" % i, [128, NMAX]) for i in range(3)]
        mrg, mrgB = sb("mrgF", [128, KC, NMAX], BF16)
        mtmp, mtmpB = sb("mtmp", [128, NMAX])
        w_in = W["w_in"][l]
        brw = [W["w_branch_a"][l], W["w_branch_b"][l], W["w_branch_c"][l]]
        for j in range(KC):
            for g in range(3):
                wv, wB = load_w(w_in, G0 + g * D + j * 128, 128, KC)
                ps, psB = psum()
                for k in range(KC):
                    op("pe", lambda e, k=k: e.matmul(ps[:, 0:N], lhsT=wv[:, k, :], rhs=xnF[:, k, 0:N], start=(k == 0),
                                                     stop=(k == KC - 1)), R=[wB, xnFB], W=[psB])
                g_t, g_B = gt[g]
                op("act", lambda e: e.activation(out=g_t[:, 0:N], in_=ps[:, 0:N], func=AF.Sigmoid), R=[psB], W=[g_B])
            for g in range(3):
                wv, wB = load_w(brw[g], j * 128, 128, 8)
                ps, psB = psum()
                yt, yB = yF[g]
                for k in range(8):
                    op("pe", lambda e, k=k: e.matmul(ps[:, 0:N], lhsT=wv[:, k, :], rhs=yt[:, k, 0:N], start=(k == 0),
                                                     stop=(k == 7)), R=[wB, yB], W=[psB])
                g_t, g_B = gt[g]
                if g == 0:
                    op("dve", lambda e: e.tensor_tensor(out=mtmp[:, 0:N], in0=ps[:, 0:N], in1=g_t[:, 0:N], op=ALU.mult),
                       R=[psB, g_B], W=[mtmpB])
                else:
                    op("dve", lambda e: e.tensor_tensor(out=g_t[:, 0:N], in0=ps[:, 0:N], in1=g_t[:, 0:N], op=ALU.mult),
                       R=[psB, g_B], W=[g_B])
                    if g == 1:
                        op("dve", lambda e: e.tensor_tensor(out=mtmp[:, 0:N], in0=mtmp[:, 0:N], in1=g_t[:, 0:N], op=ALU.add),
                           R=[mtmpB, g_B], W=[mtmpB])
                    else:
                        op("dve", lambda e: e.tensor_tensor(out=mrg[:, j, 0:N], in0=mtmp[:, 0:N], in1=g_t[:, 0:N], op=ALU.add),
                           R=[mtmpB, g_B], W=[mrgB])

        def cons_res(j, ps, psB, m):
            op("dve", lambda e: e.tensor_tensor(out=xF[:, j, 0:N], in0=xF[:, j, 0:N], in1=ps[:, 0:N], op=ALU.add),
               R=[xFB, psB], W=[xFB])
        proj_fm(W["w_out"][l], 0, D, N, cons_res, rhs_fn=lambda k: mrg[:, k, 0:N], rhsB=mrgB)

    def mlp_phase(l, tile):
        r0, N, chunks = tile
        mtmp, mtmpB = sb("mtmp2", [128, NMAX])
        uF, uFB = sb("uF", [128, 64, NMAX], BF16)
        rmsnorm(N, lambda k: nw_mlp[:, l, k:k + 1], nw_mlpB)

        def cons_up(j, ps, psB, m):
            op("act", lambda e: e.activation(out=mtmp[:, 0:N], in_=ps[:, 0:N], func=AF.Relu), R=[psB], W=[mtmpB])
            op("dve", lambda e: e.tensor_tensor(out=uF[:, j, 0:N], in0=mtmp[:, 0:N], in1=mtmp[:, 0:N], op=ALU.mult),
               R=[mtmpB], W=[uFB])
        proj_fm(W["w_up"][l], 0, 4 * D, N, cons_up)
        wd = W["w_down"][l]
        for j in range(KC):
            ps, psB = psum()
            for q in range(4):
                wv, wB = load_w(wd[q * 2048:(q + 1) * 2048, :], j * 128, 128, 16)
                for k in range(16):
                    kk = q * 16 + k
                    op("pe", lambda e, k=k, kk=kk, wv=wv: e.matmul(ps[:, 0:N], lhsT=wv[:, k, :], rhs=uF[:, kk, 0:N],
                                                                  start=(kk == 0), stop=(kk == 63)), R=[wB, uFB], W=[psB])
            op("dve", lambda e, j=j: e.tensor_tensor(out=xF[:, j, 0:N], in0=xF[:, j, 0:N], in1=ps[:, 0:N], op=ALU.add),
               R=[xFB, psB], W=[xFB])

    tiles = tile_plan()
    for ti, tile in enumerate(tiles):
        r0, N, chunks = tile
        with Arena():
            stage_tok, stage_tokB = sb("stage_in", [128, D])
            rr = 0
            while rr < N:
                m = min(128, N - rr)
                dma("sp", stage_tok[0:m, :], xin[r0 + rr:r0 + rr + m, :], W=[stage_tokB])
                for kq in range(4):
                    ps, psB = psum()
                    for kk in range(4):
                        k = kq * 4 + kk
                        op("pe", lambda e, k=k, kk=kk, ps=ps: e.transpose(out=ps[:, kk * 128:kk * 128 + m],
                                                                         in_=stage_tok[0:m, k * 128:(k + 1) * 128],
                                                                         identity=ident[0:m, 0:m]), R=[stage_tokB, cstB], W=[psB])
                    for kk in range(4):
                        k = kq * 4 + kk
                        if kk % 2 == 0:
                            op("act", lambda e, k=k, kk=kk, ps=ps: e.activation(out=xF[:, k, rr:rr + m],
                                                                               in_=ps[:, kk * 128:kk * 128 + m], func=AF.Copy),
                               R=[psB], W=[xFB])
                        else:
                            op("dve", lambda e, k=k, kk=kk, ps=ps: e.tensor_copy(out=xF[:, k, rr:rr + m],
                                                                                in_=ps[:, kk * 128:kk * 128 + m]),
                               R=[psB], W=[xFB])
                rr += m
        for l in range(DEPTH):
            rmsnorm(N, lambda k: nw_mix[:, l, k:k + 1], nw_mixB)
            with Arena():
                mlstm_phase(l, tile)
            with Arena():
                gdn_phase(l, tile)
            with Arena():
                rwkv_phase(l, tile)
            with Arena():
                merge_phase(l, tile)
            with Arena():
                mlp_phase(l, tile)
        rmsnorm(N, lambda k: nw_fin[:, k:k + 1], nw_finB, inplace=True)
        with Arena():
            stage_tok, stage_tokB = sb("stage_out", [128, D])
            rr = 0
            while rr < N:
                m = min(128, N - rr)
                for kq in range(4):
                    ps, psB = psum()
                    for kk in range(4):
                        k = kq * 4 + kk
                        op("pe", lambda e, k=k, kk=kk, ps=ps: e.transpose(out=ps[0:m, kk * 128:(kk + 1) * 128],
                                                                         in_=xF[:, k, rr:rr + m], identity=ident),
                           R=[xFB, cstB], W=[psB])
                    if kq % 2 == 0:
                        op("act", lambda e, kq=kq, ps=ps: e.activation(out=stage_tok[0:m, kq * 512:(kq + 1) * 512],
                                                                      in_=ps[0:m, 0:512], func=AF.Copy), R=[psB], W=[stage_tokB])
                    else:
                        op("dve", lambda e, kq=kq, ps=ps: e.tensor_copy(out=stage_tok[0:m, kq * 512:(kq + 1) * 512],
                                                                       in_=ps[0:m, 0:512]), R=[psB], W=[stage_tokB])
                dma("sp", y_out[r0 + rr:r0 + rr + m, :], stage_tok[0:m, :], R=[stage_tokB], semb=stage_tokB)
                rr += m

    for l in range(DEPTH):
        Ct, CtB = Cext[l]
        for h in range(4):
            dma("sp", o_pc[l, h].rearrange("(kc p) v -> p kc v", p=128), Ct[:, h, :, 0:256], R=[CtB], semb=CtB)
        dma("sp", o_pn[l].rearrange("h (kc p o) -> p h kc o", p=128, o=1), Ct[:, :, :, 256:257], R=[CtB], semb=CtB, slow=True)
        outbufs.append(CtB)
        mP, mPB = mprev[l]
        dma("sp", o_pm[l:l + 1, :].rearrange("o h -> h o"), mP[:, 0:1], R=[mPB], semb=mPB, slow=True)
        outbufs.append(mPB)

    for l in range(DEPTH):
        St, StB = GS[l]
        dma("sp", o_pgs[l].rearrange("h k v -> k h v"), St[:, :, :], R=[StB], semb=StB)
        outbufs.append(StB)
        hs, hsB = ghist[l]
        for t in range(3):
            dma("sp", o_pgconv[l][t].rearrange("(k p) -> p k", p=128), hs[:, :, t], R=[hsB], semb=hsB, slow=True)
        outbufs.append(hsB)
    with Arena():
        zt2, zt2B = sb("zt_out", [64, 8, 128])
        for l in range(DEPTH):
            Zt, ZtB = RZ[l]
            for jb in range(8):
                ps, psB = psum()
                op("pe", lambda e, jb=jb, ps=ps: e.transpose(out=ps[0:64, 0:128], in_=Zt[:, jb, :], identity=ident),
                   R=[ZtB, cstB], W=[psB])
                op("act", lambda e, jb=jb, ps=ps: e.activation(out=zt2[:, jb, :], in_=ps[0:64, 0:128], func=AF.Copy),
                   R=[psB], W=[zt2B])
            dma("sp", o_prs[l].rearrange("h v k -> v h k"), zt2[:, :, :].rearrange("v j (h k) -> v (j h) k", k=64),
                R=[zt2B], semb=zt2B)
            rp, rpB = rprev[l]
            dma("sp", o_prshift[l].rearrange("(k p) -> p k", p=128), rp[:, :], R=[rpB], semb=rpB, slow=True)
            outbufs.append(rpB)
        kb.finish([zt2B])
    kb.finish(outbufs)
    print("[kernel] instructions:", kb.n_inst, "dma sems:", kb.nd, "sbuf left:", nc.sbuf_bytes_remaining)


def _consts():
    cols = {}
    parts = []
    off = [0]

    def add(name, arr):
        a = np.zeros((128, arr.shape[1]), np.float32)
        a[:arr.shape[0]] = arr
        cols[name] = off[0]
        off[0] += arr.shape[1]
        parts.append(a)
    add("ident", np.eye(128, dtype=np.float32))
    add("ones", np.ones((128, 128), np.float32))
    ob = np.zeros((128, 128), np.float32)
    ob[:64, :64] = 1
    ob[64:, 64:] = 1
    add("onesblk", ob)
    s = np.arange(64)[:, None]
    l_ = np.arange(64)[None, :]
    NEG = -30000.0
    add("nmT_incl", np.where(s <= l_, 0.0, NEG).astype(np.float32))
    add("nmT_strict", np.where(s < l_, 0.0, NEG).astype(np.float32))
    add("nm_strict", np.where(l_ < s, 0.0, NEG).astype(np.float32))
    add("mT_incl", (s <= l_).astype(np.float32))
    add("mT_strict", (s < l_).astype(np.float32))
    add("m_strict", (l_ < s).astype(np.float32))
    sel = np.zeros((8, 8, 64), np.float32)
    for h in range(8):
        sel[h, h, :] = 1
    add("selh", sel.reshape(8, 512))
    sl = np.zeros((64, 3, 128), np.float32)
    sl[63, 0, :] = 1
    sl[15, 1, :] = 1
    sl[0, 2, :] = 1
    add("sellast", sl.reshape(64, 384))
    i2 = np.concatenate([np.eye(64, dtype=np.float32)] * 2, axis=0)
    add("i2", i2)
    t2 = lambda m: np.tile(m, (2, 2)).astype(np.float32)
    add("bdT_incl", t2(s <= l_))
    add("bdT_strict", t2(s < l_))
    add("bd_strict", t2(l_ < s))
    return np.concatenate(parts, axis=1), cols


CONSTS, CO = _consts()
CONST_COLS = CONSTS.shape[1]

PARAM_SHAPES = {
    "norm_mix_w": (2, 2048), "w_in": (2, 2048, 17688), "mlstm_b_i": (2, 4), "mlstm_b_f": (2, 4),
    "mlstm_norm_w": (2, 1024), "gdn_conv_w": (2, 4, 3072), "gdn_a_log": (2, 8), "gdn_dt_bias": (2, 8),
    "gdn_norm_w": (2, 128), "rwkv_mu": (2, 3328), "rwkv_w0": (2, 1024), "rwkv_w2": (2, 64, 1024),
    "rwkv_a0": (2, 1024), "rwkv_a2": (2, 64, 1024), "rwkv_g2": (2, 128, 1024), "rwkv_k_k": (2, 1024),
    "rwkv_k_a": (2, 1024), "rwkv_r_k": (2, 16, 64), "rwkv_ln_w": (2, 1024), "rwkv_ln_b": (2, 1024),
    "w_branch_a": (2, 1024, 2048), "w_branch_b": (2, 1024, 2048), "w_branch_c": (2, 1024, 2048),
    "w_out": (2, 2048, 2048), "norm_mlp_w": (2, 2048), "w_up": (2, 2048, 8192), "w_down": (2, 8192, 2048),
    "final_norm_w": (2048,),
}

_NC = [None]


def kernel(**inp):
    f = lambda a: np.ascontiguousarray(np.asarray(a, dtype=np.float32))
    if _NC[0] is None:
        _NC[0] = build()
    nc = _NC[0]
    params = {k: f(inp[k]) for k in PARAM_SHAPES}
    in_maps = []
    for c in range(8):
        sl = slice(c * NS, (c + 1) * NS)
        m = dict(params)
        m["xin"] = np.ascontiguousarray(np.concatenate(
            [f(inp["x_sample"])[sl, 0], f(inp["meta_tokens"]), f(inp["x_prompt"])[c % 4]], axis=0))
        m["consts"] = CONSTS
        m["st_c"] = f(inp["state_mlstm_c"])[:, sl]
        m["st_n"] = f(inp["state_mlstm_n"])[:, sl]
        m["st_m"] = f(inp["state_mlstm_m"])[:, sl]
        m["st_gs"] = f(inp["state_gdn_s"])[:, sl]
        m["st_gconv"] = f(inp["state_gdn_conv"])[:, sl]
        m["st_rs"] = f(inp["state_rwkv_s"])[:, sl]
        m["st_rshift"] = f(inp["state_rwkv_shift"])[:, sl]
        m = {k: np.ascontiguousarray(v) for k, v in m.items()}
        in_maps.append(m)
    res = run_bass_kernel_spmd(nc, in_maps, core_ids=list(range(8)))
    R = res.results
    y = [r["y"] for r in R]
    y_prompt = np.stack([y[c][NS + NMETA:] for c in range(4)], axis=0)
    y_sample = np.concatenate([y[c][0:NS] for c in range(8)], axis=0)[:, None, :]
    pk = lambda k: np.stack([R[c][k] for c in range(4)], axis=1)
    sk = lambda k: np.concatenate([R[c][k] for c in range(8)], axis=1)
    outs = (y_prompt, y_sample, pk("p_c"), pk("p_n"), pk("p_m"), pk("p_gs"), pk("p_gconv"), pk("p_rs"), pk("p_rshift"),
            sk("s_c"), sk("s_n"), sk("s_m"), sk("s_gs"), sk("s_gconv"), sk("s_rs"), sk("s_rshift"))
    return tuple(np.ascontiguousarray(o, dtype=np.float32) for o in outs)
```
